# Optimizing a Trainium2 kernel written in Bass

```python
import math
import jax
import jax.numpy as jnp
from jax import lax
import numpy as np


D_MODEL = 2048
BATCH = 2
SEQ = 4096
DEPTH = 2
DEC_BATCH = 1
DEC_SEQ = 8192
PAST_LEN = 128

NORM_EPS = 1e-6
NEG_INF = -1e30

SSM_HEADDIM = 64
SSM_INNER = 3 * D_MODEL // 4
SSM_HEADS = SSM_INNER // SSM_HEADDIM
SSM_GROUPS = 4
SSM_STATE = 128
SSM_CONV = 5
SSM_CHUNK = 128
SSM_XBC = SSM_INNER + 2 * SSM_GROUPS * SSM_STATE
SSM_DT_MIN = 1e-3
SSM_DT_MAX = 1e-1

S5_WIDTH = D_MODEL // 2
S5_GROUP = 16
S5_GROUPS = S5_WIDTH // S5_GROUP
S5_STATE = 64
S5_DT_MIN = 1e-3
S5_DT_MAX = 1e-1

ATT_HEAD_DIM = 128
ATT_HEADS_PER_GROUP = 4
ATT_PATTERNS = ((128, 1), (512, 4), (2048, 16))
ATT_GROUPS = 3
ATT_HEADS = ATT_GROUPS * ATT_HEADS_PER_GROUP
ATT_WIDTH = ATT_HEADS * ATT_HEAD_DIM
ATT_OUT_WIDTH = ATT_HEADS_PER_GROUP * ATT_HEAD_DIM
REL_BUCKETS = 32
REL_MAX_DIST = 1024

N_BRANCHES = 3
IN_WIDTH = SSM_INNER + SSM_XBC + 2 * SSM_HEADS + S5_WIDTH + 3 * ATT_WIDTH + N_BRANCHES * D_MODEL

D_FF = ((8 * D_MODEL // 3 + 127) // 128) * 128
FFN_CONV = 3

kernel_name = "hybrid_ssd_s5_dilated_encoder"


def rms_norm(x, gain):
    xf = x.astype(jnp.float32)
    xf = xf * lax.rsqrt(jnp.mean(xf * xf, axis=-1, keepdims=True) + NORM_EPS)
    return (xf * gain.astype(jnp.float32)).astype(x.dtype)


def depthwise_conv_centred(x, w, b):
    width, ch = w.shape
    pad = width // 2
    y = lax.conv_general_dilated(x, w[:, None, :].astype(x.dtype), window_strides=(1,),
                                 padding=[(pad, pad)], dimension_numbers=('NWC', 'WIO', 'NWC'),
                                 feature_group_count=ch)
    return y + b.astype(x.dtype)


def ssd_chunked(xd, a_dt, b_mat, c_mat):
    bsz, seqlen, nh, hp = xd.shape
    ng, ns = b_mat.shape[2], b_mat.shape[3]
    nr = nh // ng
    tc = SSM_CHUNK
    nc = seqlen // tc
    xd = xd.reshape(bsz, nc, tc, ng, nr, hp)
    a = a_dt.astype(jnp.float32).reshape(bsz, nc, tc, ng, nr)
    bm = b_mat.reshape(bsz, nc, tc, ng, ns)
    cm = c_mat.reshape(bsz, nc, tc, ng, ns)
    a_cs = jnp.cumsum(a, axis=2)
    seg = a_cs[:, :, :, None] - a_cs[:, :, None, :]
    tril = np.tril(np.ones((tc, tc), dtype=bool))[:, :, None, None]
    decay = jnp.exp(jnp.where(tril, seg, -jnp.inf))
    cb = jnp.einsum('bclgn,bcsgn->bclsg', cm, bm)
    y_diag = jnp.einsum('bclsgr,bcsgrp->bclgrp', cb[..., None] * decay, xd)
    decay_to_end = jnp.exp(a_cs[:, :, -1:] - a_cs)
    states = jnp.einsum('bclgn,bclgrp->bcgrpn', bm, xd * decay_to_end[..., None])
    chunk_decay = jnp.exp(a_cs[:, :, -1])

    def step(h, inp):
        st, dec = inp
        return h * dec[..., None, None] + st, h

    _, h_in = lax.scan(step, jnp.zeros_like(states[:, 0]),
                       (jnp.moveaxis(states, 1, 0), jnp.moveaxis(chunk_decay, 1, 0)))
    h_in = jnp.moveaxis(h_in, 0, 1)
    y_off = jnp.einsum('bclgn,bcgrpn->bclgrp', cm, h_in) * jnp.exp(a_cs)[..., None]
    return (y_diag + y_off).reshape(bsz, seqlen, nh, hp)


def mamba2_bidirectional(z, xbc, dt_raw, conv_w, conv_b, a_log, dt_bias, d_skip, norm_g, w_out):
    bsz, seqlen, _ = z.shape
    xbc = jax.nn.silu(depthwise_conv_centred(xbc, conv_w, conv_b))
    xs, bm, cm = jnp.split(xbc, [SSM_INNER, SSM_INNER + SSM_GROUPS * SSM_STATE], axis=-1)
    xs = xs.reshape(bsz, seqlen, SSM_HEADS, SSM_HEADDIM)
    bm = bm.reshape(bsz, seqlen, SSM_GROUPS, SSM_STATE)
    cm = cm.reshape(bsz, seqlen, SSM_GROUPS, SSM_STATE)
    dt = jax.nn.softplus(dt_raw.astype(jnp.float32).reshape(bsz, seqlen, 2, SSM_HEADS)
                         + dt_bias.astype(jnp.float32))
    a = -jnp.exp(a_log.astype(jnp.float32))
    y_fwd = ssd_chunked(xs * dt[:, :, 0, :, None], a[0] * dt[:, :, 0], bm, cm)
    flip = lambda t: jnp.flip(t, axis=1)
    y_bwd = flip(ssd_chunked(flip(xs * dt[:, :, 1, :, None]), flip(a[1] * dt[:, :, 1]), flip(bm), flip(cm)))
    y = y_fwd + y_bwd + xs * d_skip[:, None]
    y = y.reshape(bsz, seqlen, SSM_INNER).astype(z.dtype) * jax.nn.silu(z)
    return rms_norm(y, norm_g) @ w_out


def s5_bidirectional(u, a_re, a_im, log_step, b_re, b_im, c_re, c_im, d_skip, w_glu):
    bsz, seqlen, _ = u.shape
    ug = u.astype(jnp.float32).reshape(bsz, seqlen, S5_GROUPS, S5_GROUP)
    lam = lax.complex(a_re.astype(jnp.float32), a_im.astype(jnp.float32))
    step = jnp.exp(log_step.astype(jnp.float32))[..., None]
    lam_bar = jnp.exp(lam * step)
    b = lax.complex(b_re.astype(jnp.float32), b_im.astype(jnp.float32))
    c = lax.complex(c_re.astype(jnp.float32), c_im.astype(jnp.float32))
    b_bar = ((lam_bar - 1.0) / lam)[..., None] * b

    def combine(e1, e2):
        a1, s1 = e1
        a2, s2 = e2
        return a1 * a2, a2 * s1 + s2

    def run(direction, reverse):
        bu = jnp.einsum('gps,blgs->blgp', b_bar[direction], ug)
        aa = jnp.broadcast_to(lam_bar[direction], bu.shape)
        _, states = lax.associative_scan(combine, (aa, bu), reverse=reverse, axis=1)
        return jnp.real(jnp.einsum('gsp,blgp->blgs', c[direction], states))

    y = run(0, False) + run(1, True) + ug * d_skip.astype(jnp.float32).reshape(S5_GROUPS, S5_GROUP)
    y = jax.nn.gelu(y.reshape(bsz, seqlen, S5_WIDTH)).astype(u.dtype)
    val, gate = jnp.split(y @ w_glu, 2, axis=-1)
    return val * jax.nn.sigmoid(gate)


def t5_bucket(rel):
    half = REL_BUCKETS // 2
    exact = half // 2
    sign = (rel > 0).astype(np.int32) * half
    n = np.abs(rel)
    large = exact + (np.log(np.maximum(n, 1) / exact) / np.log(REL_MAX_DIST / exact)
                     * (half - exact)).astype(np.int32)
    large = np.minimum(large, half - 1)
    return sign + np.where(n < exact, n, large)


def dilated_window_attention(q, k, v, bias_table, half, dil):
    bsz, seqlen, nh, dh = q.shape
    n = seqlen // dil
    blk = half
    nb = -(-n // blk)
    pad = nb * blk - n

    def to_sub(t):
        return t.reshape(bsz, n, dil, nh, dh).transpose(0, 2, 1, 3, 4)

    qs = jnp.pad(to_sub(q), ((0, 0), (0, 0), (0, pad), (0, 0), (0, 0))).reshape(bsz, dil, nb, blk, nh, dh)

    def windows(t):
        tp = jnp.pad(to_sub(t), ((0, 0), (0, 0), (blk, pad + blk), (0, 0), (0, 0)))
        tp = tp.reshape(bsz, dil, nb + 2, blk, nh, dh)
        return jnp.concatenate([tp[:, :, :-2], tp[:, :, 1:-1], tp[:, :, 2:]], axis=3)

    ks, vs = windows(k), windows(v)
    qi = np.arange(blk)[:, None]
    kj = np.arange(3 * blk)[None, :] - blk
    rel_sub = kj - qi
    key_idx = np.arange(nb)[:, None, None] * blk + kj[None]
    valid = (np.abs(rel_sub) <= half)[None] & (key_idx >= 0) & (key_idx < n)
    bias = jnp.moveaxis(bias_table[t5_bucket(rel_sub * dil)], -1, 0).astype(jnp.float32)
    logits = jnp.einsum('brnqhd,brnkhd->brnhqk', qs, ks,
                        preferred_element_type=jnp.float32) * (dh ** -0.5) + bias
    logits = jnp.where(valid[:, None], logits, NEG_INF)
    m = jnp.max(logits, axis=-1, keepdims=True)
    p = jnp.exp(logits - m)
    s = jnp.sum(p, axis=-1)
    o = jnp.einsum('brnhqk,brnkhd->brnqhd', p, vs.astype(jnp.float32))
    o = o / jnp.swapaxes(s, -1, -2)[..., None]
    lse = jnp.swapaxes(m[..., 0] + jnp.log(s), -1, -2)

    def from_sub(t):
        t = t.reshape((bsz, dil, nb * blk) + t.shape[4:])[:, :, :n]
        return jnp.swapaxes(t, 1, 2).reshape((bsz, seqlen) + t.shape[3:])

    return from_sub(o), from_sub(lse)


def dilated_attention_mixture(q, k, v, rel_bias, w_out):
    bsz, seqlen = q.shape[:2]
    outs, lses = [], []
    for g, (window, dil) in enumerate(ATT_PATTERNS):
        heads = slice(g * ATT_HEADS_PER_GROUP, (g + 1) * ATT_HEADS_PER_GROUP)
        o, lse = dilated_window_attention(q[:, :, heads], k[:, :, heads], v[:, :, heads],
                                          rel_bias[:, heads], window // (2 * dil), dil)
        outs.append(o)
        lses.append(lse)
    wts = jax.nn.softmax(jnp.stack(lses), axis=0)
    comb = jnp.sum(wts[..., None] * jnp.stack(outs), axis=0)
    return comb.reshape(bsz, seqlen, ATT_OUT_WIDTH).astype(q.dtype) @ w_out


def trunk(x, rel_bias, norm_mix, w_in, ssm_conv_w, ssm_conv_b, ssm_a_log, ssm_dt_bias, ssm_d,
          ssm_norm, ssm_w_out, s5_a_re, s5_a_im, s5_log_step, s5_b_re, s5_b_im, s5_c_re, s5_c_im,
          s5_d, s5_w_glu, att_w_out, w_o, norm_ffn, w_up, ffn_conv_w, ffn_conv_b, w_down, final_norm):
    bsz, seqlen, _ = x.shape
    splits = np.cumsum([SSM_INNER, SSM_XBC, 2 * SSM_HEADS, S5_WIDTH, ATT_WIDTH, ATT_WIDTH, ATT_WIDTH]).tolist()
    for i in range(DEPTH):
        h = rms_norm(x, norm_mix[i])
        z, xbc, dt_raw, u, q, k, v, gates = jnp.split(h @ w_in[i], splits, axis=-1)
        a_out = mamba2_bidirectional(z, xbc, dt_raw, ssm_conv_w[i], ssm_conv_b[i], ssm_a_log[i],
                                     ssm_dt_bias[i], ssm_d[i], ssm_norm[i], ssm_w_out[i]).astype(x.dtype)
        b_out = s5_bidirectional(u, s5_a_re[i], s5_a_im[i], s5_log_step[i], s5_b_re[i], s5_b_im[i],
                                 s5_c_re[i], s5_c_im[i], s5_d[i], s5_w_glu[i]).astype(x.dtype)
        head_shape = (bsz, seqlen, ATT_HEADS, ATT_HEAD_DIM)
        c_out = dilated_attention_mixture(q.reshape(head_shape), k.reshape(head_shape), v.reshape(head_shape),
                                          rel_bias, att_w_out[i]).astype(x.dtype)
        g = jax.nn.sigmoid(gates.astype(jnp.float32)).reshape(bsz, seqlen, N_BRANCHES, D_MODEL).astype(x.dtype)
        merged = g[:, :, 0] * a_out + g[:, :, 1] * b_out + g[:, :, 2] * c_out
        x = x + merged @ w_o[i]
        h = rms_norm(x, norm_ffn[i])
        gate, val = jnp.split(depthwise_conv_centred(h @ w_up[i], ffn_conv_w[i], ffn_conv_b[i]), 2, axis=-1)
        x = x + (jax.nn.silu(gate) * val) @ w_down[i]
    return rms_norm(x, final_norm)


def setup_inputs(seed: int = 0) -> dict:
    key = jax.random.key(seed)
    keys = iter(jax.random.split(key, 48))
    nrm = lambda shape, scale: scale * jax.random.normal(next(keys), shape, jnp.float32)
    unif = lambda shape, lo, hi: jax.random.uniform(next(keys), shape, jnp.float32, lo, hi)
    ssm_dt = jnp.exp(unif((DEPTH, 2, SSM_HEADS), math.log(SSM_DT_MIN), math.log(SSM_DT_MAX)))
    s5_im = jnp.broadcast_to(math.pi * jnp.arange(S5_STATE, dtype=jnp.float32), (DEPTH, 2, S5_GROUPS, S5_STATE))
    return {
        'x_prompt': nrm((BATCH, SEQ, D_MODEL), 1.0),
        'x_sample': nrm((DEC_BATCH, DEC_SEQ, D_MODEL), 1.0),
        'rel_bias': nrm((REL_BUCKETS, ATT_HEADS), 0.5),
        'norm_mix': 1.0 + nrm((DEPTH, D_MODEL), 0.02),
        'w_in': nrm((DEPTH, D_MODEL, IN_WIDTH), D_MODEL ** -0.5),
        'ssm_conv_w': nrm((DEPTH, SSM_CONV, SSM_XBC), SSM_CONV ** -0.5),
        'ssm_conv_b': nrm((DEPTH, SSM_XBC), 0.02),
        'ssm_a_log': jnp.log(unif((DEPTH, 2, SSM_HEADS), 1.0, 16.0)),
        'ssm_dt_bias': jnp.log(jnp.expm1(ssm_dt)),
        'ssm_d': 1.0 + nrm((DEPTH, SSM_HEADS), 0.1),
        'ssm_norm': 1.0 + nrm((DEPTH, SSM_INNER), 0.02),
        'ssm_w_out': nrm((DEPTH, SSM_INNER, D_MODEL), SSM_INNER ** -0.5),
        's5_a_re': -0.5 + nrm((DEPTH, 2, S5_GROUPS, S5_STATE), 0.01),
        's5_a_im': s5_im + nrm((DEPTH, 2, S5_GROUPS, S5_STATE), 0.01),
        's5_log_step': unif((DEPTH, 2, S5_GROUPS), math.log(S5_DT_MIN), math.log(S5_DT_MAX)),
        's5_b_re': nrm((DEPTH, S5_GROUPS, S5_STATE, S5_GROUP), (2 * S5_GROUP) ** -0.5),
        's5_b_im': nrm((DEPTH, S5_GROUPS, S5_STATE, S5_GROUP), (2 * S5_GROUP) ** -0.5),
        's5_c_re': nrm((DEPTH, 2, S5_GROUPS, S5_GROUP, S5_STATE), (2 * S5_STATE) ** -0.5),
        's5_c_im': nrm((DEPTH, 2, S5_GROUPS, S5_GROUP, S5_STATE), (2 * S5_STATE) ** -0.5),
        's5_d': nrm((DEPTH, S5_WIDTH), 1.0),
        's5_w_glu': nrm((DEPTH, S5_WIDTH, 2 * D_MODEL), S5_WIDTH ** -0.5),
        'att_w_out': nrm((DEPTH, ATT_OUT_WIDTH, D_MODEL), ATT_OUT_WIDTH ** -0.5),
        'w_o': nrm((DEPTH, D_MODEL, D_MODEL), D_MODEL ** -0.5),
        'norm_ffn': 1.0 + nrm((DEPTH, D_MODEL), 0.02),
        'w_up': nrm((DEPTH, D_MODEL, 2 * D_FF), D_MODEL ** -0.5),
        'ffn_conv_w': nrm((DEPTH, FFN_CONV, 2 * D_FF), FFN_CONV ** -0.5),
        'ffn_conv_b': nrm((DEPTH, 2 * D_FF), 0.02),
        'w_down': nrm((DEPTH, D_FF, D_MODEL), D_FF ** -0.5),
        'final_norm': 1.0 + nrm((D_MODEL,), 0.02),
    }


def reference(x_prompt, x_sample, rel_bias, norm_mix, w_in, ssm_conv_w, ssm_conv_b, ssm_a_log,
              ssm_dt_bias, ssm_d, ssm_norm, ssm_w_out, s5_a_re, s5_a_im, s5_log_step, s5_b_re,
              s5_b_im, s5_c_re, s5_c_im, s5_d, s5_w_glu, att_w_out, w_o, norm_ffn, w_up,
              ffn_conv_w, ffn_conv_b, w_down, final_norm):
    y_prompt = trunk(x_prompt, rel_bias, norm_mix, w_in, ssm_conv_w, ssm_conv_b, ssm_a_log, ssm_dt_bias,
                     ssm_d, ssm_norm, ssm_w_out, s5_a_re, s5_a_im, s5_log_step, s5_b_re, s5_b_im,
                     s5_c_re, s5_c_im, s5_d, s5_w_glu, att_w_out, w_o, norm_ffn, w_up, ffn_conv_w,
                     ffn_conv_b, w_down, final_norm)
    y_sample = trunk(x_sample, rel_bias, norm_mix, w_in, ssm_conv_w, ssm_conv_b, ssm_a_log, ssm_dt_bias,
                     ssm_d, ssm_norm, ssm_w_out, s5_a_re, s5_a_im, s5_log_step, s5_b_re, s5_b_im,
                     s5_c_re, s5_c_im, s5_d, s5_w_glu, att_w_out, w_o, norm_ffn, w_up, ffn_conv_w,
                     ffn_conv_b, w_down, final_norm)
    return (y_prompt, y_sample)
```

```python
import math
import numpy as np
import concourse.bass as bass
import concourse.mybir as mybir
from concourse.bass_utils import run_bass_kernel_spmd

F32 = mybir.dt.float32
BF16 = mybir.dt.bfloat16
I32 = mybir.dt.int32
ALU = mybir.AluOpType
AF = mybir.ActivationFunctionType
AX = mybir.AxisListType
NEG = -1.0e30
TWO_PI = 2.0 * math.pi


class Buf:
    __slots__ = ("name", "w", "r", "excl")

    def __init__(self, name="", excl=False):
        self.name = name
        self.w = []
        self.r = []
        self.excl = excl


class Tile:
    def __init__(self, ap, buf):
        self.ap = ap
        self.b = buf

    def __getitem__(self, k):
        return self.ap[k]


class Prog:
    KS = 6
    NDQ = {"sp": 24, "pool": 12, "act": 8}

    def __init__(self, nc):
        self.nc = nc
        self.E = {"pe": nc.tensor, "act": nc.scalar, "dve": nc.vector, "pool": nc.gpsimd, "sp": nc.sync}
        self.csem = {e: [nc.alloc_semaphore(name=f"c_{e}{i}") for i in range(self.KS)]
                     for e in ("pe", "act", "dve", "pool")}
        self.cnt = {e: 0 for e in self.csem}
        self.dsem = {q: [nc.alloc_semaphore(name=f"d_{q}{i}") for i in range(n)] for q, n in self.NDQ.items()}
        self.dval = {q: [0] * self.NDQ[q] for q in self.dsem}
        self.dnext = {q: 0 for q in self.dsem}
        self.seen = {e: {} for e in self.E}
        self.ninst = 0
        self.nalloc = 0

    def buf(self, name=""):
        return Buf(name)

    def sb(self, name, shape, dt):
        self.nalloc += 1
        h = self.nc.alloc_sbuf_tensor(f"{name}_{self.nalloc}", list(shape), dt)
        return Tile(h.ap(), Buf(name))

    def ps(self, name, shape, dt):
        self.nalloc += 1
        h = self.nc.alloc_psum_tensor(f"{name}_{self.nalloc}", list(shape), dt)
        return Tile(h.ap(), Buf(name))

    def _wait_tok(self, eng, tok):
        if tok[0] == "c":
            _, e2, n = tok
            if e2 == eng and eng == "pe":
                return
            key = ("c", e2)
            if self.seen[eng].get(key, 0) >= n:
                return
            self.seen[eng][key] = n
            self.E[eng].wait_ge(self.csem[e2][(n - 1) % self.KS], (n - 1) // self.KS + 1)
        else:
            _, q, i, v = tok
            key = ("d", q, i)
            if self.seen[eng].get(key, 0) >= v:
                return
            self.seen[eng][key] = v
            self.E[eng].wait_ge(self.dsem[q][i], v)

    @staticmethod
    def _bufs(xs):
        out = []
        for x in xs:
            b = x.b if isinstance(x, Tile) else x
            if isinstance(b, (list, tuple)):
                out.extend(b)
            else:
                out.append(b)
        return out

    def _deps(self, reads, writes):
        deps = []
        for b in reads:
            deps.extend(b.w)
            if b.excl:
                deps.extend(b.r)
        for b in writes:
            deps.extend(b.w)
            deps.extend(b.r)
        return deps

    def _commit(self, tok, reads, writes):
        for b in writes:
            b.w = [tok]
            b.r = []
        for b in reads:
            if b not in writes:
                b.r.append(tok)
                if len(b.r) > 48:
                    best = {}
                    for t in b.r:
                        k = t[:2] if t[0] == "c" else t[:3]
                        if k not in best or best[k][-1] < t[-1]:
                            best[k] = t
                    b.r = list(best.values())

    def op(self, eng, fn, reads=(), writes=()):
        reads = self._bufs(reads)
        writes = self._bufs(writes)
        for tok in self._deps(reads, writes):
            self._wait_tok(eng, tok)
        ins = fn()
        self.cnt[eng] += 1
        n = self.cnt[eng]
        ins.then_inc(self.csem[eng][(n - 1) % self.KS], 1)
        self.ninst += 1
        self._commit(("c", eng, n), reads, writes)
        return ins

    def dma(self, q, out, in_, reads=(), writes=(), fn=None, **kw):
        reads = self._bufs(reads)
        writes = self._bufs(writes)
        for tok in self._deps(reads, writes):
            self._wait_tok(q, tok)
        i = self.dnext[q]
        self.dnext[q] = (i + 1) % self.NDQ[q]
        if self.dval[q][i] > 0:
            self._wait_tok(q, ("d", q, i, self.dval[q][i]))
        ins = self.E[q].dma_start(out=out, in_=in_, **kw) if fn is None else fn()
        ins.then_inc(self.dsem[q][i], 16)
        self.dval[q][i] += 16
        self.ninst += 1
        self._commit(("d", q, i, self.dval[q][i]), reads, writes)
        return ins

    def barrier(self):
        for e in self.E:
            for e2, n in self.cnt.items():
                if n > 0:
                    self._wait_tok(e, ("c", e2, n))
            for q in self.dsem:
                for i, v in enumerate(self.dval[q]):
                    if v > 0:
                        self._wait_tok(e, ("d", q, i, v))


class Cfg:
    def __init__(s, **kw):
        s.__dict__.update(kw)
        s.INNER = s.H * s.PH
        s.HPG = s.H // s.G
        s.XBC = s.INNER + 2 * s.G * s.NS
        s.G5 = s.W5 // s.C5
        s.AW = 3 * s.HG * s.DH
        s.INW = s.INNER + s.XBC + 2 * s.H + s.W5 + 3 * s.AW + 3 * s.D
        s.o_z = 0
        s.o_xbc = s.INNER
        s.o_dt = s.o_xbc + s.XBC
        s.o_u = s.o_dt + 2 * s.H
        s.o_q = s.o_u + s.W5
        s.o_k = s.o_q + s.AW
        s.o_v = s.o_k + s.AW
        s.o_g = s.o_v + s.AW
        s.KC = s.D // 128
        s.NCH = s.NT // 128
        s.BND = s.NT // 2
        s.PADT = [max(1, -(-(h * d) // 128)) for (h, d) in s.PAT]


FULL = Cfg(D=2048, NT=8192, TG=2048, H=24, G=4, PH=64, NS=128, CONV=5, W5=1024, C5=16, P5=64,
           DH=128, HG=4, PAT=((64, 1), (64, 4), (64, 16)), NBK=32, MAXD=1024, DFF=5504, FTG=1024,
           DEPTH=2, EPS=1e-6, SC=512, S5C=512, ATG=2)


def t5_bucket(rel, nb, maxd):
    half = nb // 2
    exact = half // 2
    sign = (rel > 0).astype(np.int32) * half
    n = np.abs(rel)
    large = exact + (np.log(np.maximum(n, 1) / exact) / np.log(maxd / exact) * (half - exact)).astype(np.int32)
    large = np.minimum(large, half - 1)
    return sign + np.where(n < exact, n, large)


def host_consts(c):
    k = {}
    k["c_ident"] = np.eye(128, dtype=np.float32)
    s = np.arange(128)[:, None]
    l = np.arange(128)[None, :]
    k["c_triu"] = (s <= l).astype(np.float32)
    k["c_tril"] = (s >= l).astype(np.float32)
    k["c_ones"] = np.ones((128, 128), np.float32)
    k["c_iota"] = np.broadcast_to(np.arange(c.S5C, dtype=np.float32), (128, c.S5C)).copy()
    for p, (half, dil) in enumerate(c.PAT):
        padt = c.PADT[p]
        W = (2 * padt + 1) * 128
        nrel = W + 127
        rel = np.arange(nrel) - 127 - padt * 128
        ok = (rel % dil == 0) & (np.abs(rel) <= half * dil)
        bk = t5_bucket(rel, c.NBK, c.MAXD)
        oh = np.zeros((c.NBK + 1, nrel), np.float32)
        for r in range(nrel):
            if ok[r]:
                oh[bk[r], r] = 1.0
            else:
                oh[c.NBK, r] = 1.0
        k[f"c_oh{p}"] = np.ascontiguousarray(oh[:, ::-1])
    return k


class Builder:
    def __init__(self, c, ncores):
        self.c = c
        nc = bass.Bass("TRN2", target_bir_lowering=False)
        self.nc = nc
        self.P = Prog(nc)
        self.din = {}
        self.dbufs = {}

    def inp(self, name, shape, dt=F32):
        self.din[name] = self.nc.dram_tensor(name, list(shape), dt, kind="ExternalInput").ap()
        return self.din[name]

    def scratch(self, name, shape, dt):
        return self.nc.dram_tensor(name, list(shape), dt).ap()

    def db(self, key):
        if key not in self.dbufs:
            self.dbufs[key] = Buf(str(key))
        return self.dbufs[key]

    def V(self, fn, r=(), w=()):
        return self.P.op("dve", fn, r, w)

    def A(self, fn, r=(), w=()):
        return self.P.op("act", fn, r, w)

    def G(self, fn, r=(), w=()):
        return self.P.op("pool", fn, r, w)

    def T(self, fn, r=(), w=()):
        return self.P.op("pe", fn, r, w)

    def mm(self, out, pairs, r, w, start=True, stop=True):
        nc = self.nc

        def fn():
            ins = None
            n = len(pairs)
            for i, (lt, rh) in enumerate(pairs):
                ins = nc.tensor.matmul(out, lhsT=lt, rhs=rh, start=(start and i == 0), stop=(stop and i == n - 1))
            return ins
        return self.P.op("pe", fn, r, w)

    def sb(self, st, name, shape, dt):
        self.P.nalloc += 1
        h = st.enter_context(self.nc.sbuf_tensor(f"{name}_{self.P.nalloc}", list(shape), dt))
        return Tile(h.ap() if hasattr(h, "ap") and callable(h.ap) else h[:], Buf(name))

    def bcast_load(self, tile, src_row_ap, n):
        self.P.dma("sp", tile.ap, src_row_ap.partition_broadcast(128), writes=[tile])

    def norm_T(self, st, xsrc, xkey, gain, t0, ntok, hT, col0=0, scale_tile=None):
        c, nc, P = self.c, self.nc, self.P
        if not hasattr(self, "_nt"):
            self._nt = None
        nt = self._norm_tiles
        ntile = -(-ntok // 128)
        for ti in range(ntile):
            rows = min(128, ntok - ti * 128)
            xt = nt["x"][self._nti % 2]
            xn = nt["xn"][self._nti % 2]
            self._nti += 1
            r0 = t0 + ti * 128
            P.dma("sp", xt.ap[0:rows, :], xsrc[r0:r0 + rows, :], reads=[self.db((xkey, r0 // 128))], writes=[xt])
            sq, ss, rt, rs = nt["sq"], nt["ss"], nt["rt"], nt["rs"]
            self.A(lambda: nc.scalar.activation(out=sq.ap[0:rows, :], in_=xt.ap[0:rows, :], func=AF.Square), [xt], [sq])
            self.V(lambda: nc.vector.reduce_sum(out=ss.ap[0:rows, :], in_=sq.ap[0:rows, :], axis=AX.X), [sq], [ss])
            self.A(lambda: nc.scalar.activation(out=rt.ap[0:rows, :], in_=ss.ap[0:rows, :], func=AF.Sqrt,
                                                scale=1.0 / c.D, bias=nt["eps"].ap[0:rows, :]), [ss, nt["eps"]], [rt])
            self.V(lambda: nc.vector.reciprocal(out=rs.ap[0:rows, :], in_=rt.ap[0:rows, :]), [rt], [rs])
            if scale_tile is not None:
                self.V(lambda: nc.vector.tensor_tensor(out=rs.ap[0:rows, :], in0=rs.ap[0:rows, :],
                                                       in1=scale_tile.ap[0:rows, :], op=ALU.mult), [rs, scale_tile], [rs])
            self.V(lambda: nc.vector.scalar_tensor_tensor(out=xn.ap[0:rows, :], in0=xt.ap[0:rows, :], scalar=rs.ap[0:rows, 0:1],
                                                          in1=gain.ap[0:rows, :], op0=ALU.mult, op1=ALU.mult),
                   [xt, rs, gain], [xn])
            pst = self.pT
            def tr():
                ins = None
                for kc in range(c.KC):
                    ins = nc.tensor.transpose(pst.ap[:, kc * 128: kc * 128 + rows], xn.ap[0:rows, kc * 128:(kc + 1) * 128],
                                              self.identb.ap[0:rows, 0:rows])
                return ins
            self.T(tr, [xn, self.identb], [pst])
            src = pst.ap[:, 0:c.KC * 128].rearrange("p (k t) -> p k t", t=128)[:, :, 0:rows]
            dst = hT.ap[:, :, col0 + ti * 128: col0 + ti * 128 + rows]
            self.A(lambda: nc.scalar.copy(out=dst, in_=src), [pst], [hT])

    def norm_setup(self, st, nbuf=2):
        c = self.c
        self._norm_tiles = {
            "x": [self.sb(st, "nx", [128, c.D], F32) for _ in range(nbuf)] * (2 // nbuf),
            "xn": [self.sb(st, "nxn", [128, c.D], BF16) for _ in range(nbuf)] * (2 // nbuf),
            "sq": self.sb(st, "nsq", [128, c.D], F32),
            "ss": self.sb(st, "nss", [128, 1], F32),
            "rt": self.sb(st, "nrt", [128, 1], F32),
            "rs": self.sb(st, "nrs", [128, 1], F32),
            "eps": self.sb(st, "neps", [128, 1], F32),
        }
        self._nti = 0
        e = self._norm_tiles["eps"]
        self.V(lambda: self.nc.vector.memset(e.ap, c.EPS), [], [e])

    def wt_setup(self, st, kmax, cw):
        self._w = {"st": [self.sb(st, "wst", [128, kmax, cw], F32) for _ in range(2)],
                   "bf": [self.sb(st, "wbf", [128, kmax, cw], BF16) for _ in range(3)], "i": 0, "kmax": kmax, "cw": cw}

    def load_w(self, Wl, k0, kn, c0, cn):
        nc, P = self.nc, self.P
        w = self._w
        ws, wb = w["st"][w["i"] % 2], w["bf"][w["i"] % 3]
        w["i"] += 1
        src = Wl[k0 * 128:(k0 + kn) * 128, c0:c0 + cn].rearrange("(k p) n -> p k n", p=128)
        P.dma("sp", ws.ap[:, 0:kn, 0:cn], src, writes=[ws])
        self.G(lambda: nc.gpsimd.tensor_copy(out=wb.ap[:, 0:kn, 0:cn], in_=ws.ap[:, 0:kn, 0:cn]), [ws], [wb])
        return wb

    def dense_feat(self, actT, KC, ntok, Wl, c0, ncols, evac, tokoff=0):
        nc = self.nc
        cw = self._w["cw"]
        ntt = -(-ntok // 512)
        for cc in range(c0, c0 + ncols, cw):
            cn = min(cw, c0 + ncols - cc)
            wb = self.load_w(Wl, 0, KC, cc, cn)
            for b0 in range(0, cn, 128):
                nb = min(128, cn - b0)
                for tt in range(ntt):
                    tw = min(512, ntok - tt * 512)
                    bank = self._bank % 8
                    self._bank += 1
                    pt = self.psb[bank]
                    pairs = [(wb.ap[:, kc, b0:b0 + nb], actT.ap[:, kc, tokoff + tt * 512: tokoff + tt * 512 + tw]) for kc in range(KC)]
                    self.mm(pt.ap[0:nb, 0:tw], pairs, [wb, actT], self.pbufs(bank))
                    evac(pt.ap[0:nb, 0:tw], cc + b0, nb, tt * 512, tw, self.pbufs(bank))

    def dense_tok(self, actT, KC, ntok, Wl, c0, ncols, CW, evac, tokoff=0):
        ntile = -(-ntok // 128)
        kmax = self._w["kmax"]
        assert CW <= self._w["cw"]
        per_bank = 512 // CW
        assert ntile <= 4 * per_bank, (ntile, CW)
        for cg, cc in enumerate(range(c0, c0 + ncols, CW)):
            cn = min(CW, c0 + ncols - cc)
            base = (self._bank % 2) * 4
            self._bank += 1
            kgs = list(range(0, KC, kmax))
            assert len(kgs) <= 3
            wbs = [(kg, min(kmax, KC - kg), self.load_w(Wl, kg, min(kmax, KC - kg), cc, cn)) for kg in kgs]
            for ti in range(ntile):
                rows = min(128, ntok - ti * 128)
                bank = base + ti // per_bank
                off = (ti % per_bank) * CW
                pt = self.psb[bank]
                pairs = []
                for (kg, kn, wb) in wbs:
                    pairs += [(actT.ap[:, kg + k, tokoff + ti * 128: tokoff + ti * 128 + rows], wb.ap[:, k, 0:cn]) for k in range(kn)]
                hb = self.phalf[bank]
                self.mm(pt.ap[0:rows, off:off + cn], pairs, [w_[2] for w_ in wbs] + [actT], hb)
                evac(pt.ap[0:rows, off:off + cn], ti * 128, rows, cc, cn, hb)

    def pbufs(self, bank):
        return self.phalf[bank]


import contextlib


def weight_specs(c):
    L = c.DEPTH
    return [
        ("norm_mix", [L, c.D]), ("w_in", [L, c.D, c.INW]), ("ssm_cw", [L, c.XBC, 8]),
        ("ssm_a_log", [L, 2 * c.H]), ("ssm_dt_bias", [L, 2 * c.H]), ("ssm_dcol", [L, 128, c.INNER // 128]),
        ("ssm_ng", [L, 128, c.INNER // 128]), ("ssm_w_out", [L, c.INNER, c.D]),
        ("s5_par", [L, 2, 128, c.TS, 3]), ("s5_b", [L, 128, c.TS, 2, c.C5]), ("s5_c", [L, 2, 128, c.TS, 2, c.C5]),
        ("s5_dcol", [L, 128, c.W5 // 128]), ("s5_w_glu", [L, c.W5, 2 * c.D]), ("att_w_out", [L, c.HG * c.DH, c.D]),
        ("w_o", [L, c.D, c.D]), ("norm_ffn", [L, c.D]), ("w_up", [L, c.D, 2 * c.DFF]),
        ("ffn_cw", [L, 128, 2 * c.DFF // 128, 4]), ("w_down", [L, c.DFF, c.D]), ("final_norm", [1, c.D]),
        ("rel_bias", [c.NBK, 3 * c.HG]),
    ]


def build(c, stop_after=None, dbg=()):
    c.TS = c.G5 * c.P5 // 128
    B = Builder(c, 1)
    nc, P = B.nc, B.P
    x_in = B.inp("x", [c.NT, c.D])
    y_out = nc.dram_tensor("y", [c.NT, c.D], F32, kind="ExternalOutput").ap()
    link = B.inp("link", [128, 1])
    nlink = B.inp("nlink", [128, 1])
    W = {n: B.inp(n, s) for n, s in weight_specs(c)}
    K = {n: B.inp(n, list(v.shape)) for n, v in host_consts(c).items()}
    S = {
        "xres": B.scratch("xres", [c.NT, c.D], F32),
        "xmid": B.scratch("xmid", [c.NT, c.D], F32),
        "zT": B.scratch("zT", [c.INNER, c.NT], BF16),
        "xbcT": B.scratch("xbcT", [c.XBC, c.NT], BF16),
        "dt": B.scratch("dt_tok", [c.NT, 2 * c.H], F32),
        "uT": B.scratch("uT", [c.W5, c.NT], BF16),
        "qT": B.scratch("qT", [c.AW, c.NT], BF16),
        "kT": B.scratch("kT", [c.AW, c.NT], BF16),
        "v": B.scratch("v_tok", [c.NT, c.AW], BF16),
        "gT": B.scratch("gT", [3 * c.D, c.NT], BF16),
        "ynT": B.scratch("ynT", [c.INNER, c.NT], BF16),
        "s5T": B.scratch("s5T", [c.W5, c.NT], BF16),
        "attT": B.scratch("attT", [c.HG * c.DH, c.NT], BF16),
        "hin": B.scratch("hin", [2, c.NCH, 128, c.INNER], BF16),
    }
    B.S, B.W, B.K = S, W, K
    B.stop_after = stop_after

    psall = nc.alloc_psum_tensor("psall", [128, 8, 512], F32).ap()
    B.phalf = [[b_, b_] for b_ in [Buf(f"ps{i}", excl=True) for i in range(8)]]
    B.psb = [Tile(psall[:, i, :], B.phalf[i]) for i in range(8)]
    B.pT = Tile(psall[:, 6:8, :].bitcast(BF16).rearrange("p a b -> p (a b)"), B.phalf[6] + B.phalf[7])
    B._bank = 0
    gst = contextlib.ExitStack()
    identf = B.sb(gst, "identf", [128, 128], F32)
    B.identb = B.sb(gst, "identb", [128, 128], BF16)
    onesf = B.sb(gst, "onesf", [128, 128], F32)
    onesb = B.sb(gst, "onesb", [128, 128], BF16)
    triu = B.sb(gst, "triu", [128, 128], F32)
    tril = B.sb(gst, "tril", [128, 128], F32)
    linkt = B.sb(gst, "linkt", [128, 1], F32)
    nlinkt = B.sb(gst, "nlinkt", [128, 1], F32)
    B.identf, B.onesf, B.onesb, B.triu, B.tril, B.linkt, B.nlinkt = identf, onesf, onesb, triu, tril, linkt, nlinkt
    P.dma("sp", identf.ap, K["c_ident"], writes=[identf])
    P.dma("sp", onesf.ap, K["c_ones"], writes=[onesf])
    P.dma("sp", triu.ap, K["c_triu"], writes=[triu])
    P.dma("sp", tril.ap, K["c_tril"], writes=[tril])
    P.dma("sp", linkt.ap, link, writes=[linkt])
    P.dma("sp", nlinkt.ap, nlink, writes=[nlinkt])
    B.V(lambda: nc.vector.tensor_copy(out=B.identb.ap, in_=identf.ap), [identf], [B.identb])
    B.V(lambda: nc.vector.tensor_copy(out=onesb.ap, in_=onesf.ap), [onesf], [onesb])

    class PTile(Tile):
        pass
    B.pT_bufs = B.phalf[6] + B.phalf[7]

    attn_setup(B)
    for l in range(c.DEPTH):
        xsrc, xkey = (x_in, "xin") if l == 0 else (S["xres"], "xres")
        phase_inproj(B, l, xsrc, xkey)
        if stop_after == ("inproj", l):
            break
        phase_ssd(B, l)
        if stop_after in (("ssd", l), ("ssd0", l), ("ssdA", l)):
            break
        phase_s5(B, l)
        if stop_after == ("s5", l):
            break
        phase_attn(B, l)
        if stop_after == ("attn", l):
            break
        phase_merge(B, l, xsrc, xkey)
        if stop_after == ("merge", l):
            break
        phase_ffn(B, l)
        if stop_after == ("ffn", l):
            break
    else:
        phase_final(B, y_out)
    for name in dbg:
        src = S[name]
        o = nc.dram_tensor("dbg_" + name, list(src.shape), src.dtype, kind="ExternalOutput").ap()
        P.dma("sp", o, src)
    P.barrier()
    gst.close()
    return nc, B


def phase_inproj(B, l, xsrc, xkey):
    c, nc, P, S, W = B.c, B.nc, B.P, B.S, B.W
    with contextlib.ExitStack() as st:
        gain = B.sb(st, "gain", [128, c.D], F32)
        B.bcast_load(gain, W["norm_mix"][l:l + 1, :], c.D)
        B.norm_setup(st)
        hT = B.sb(st, "hT", [128, c.KC, c.TG], BF16)
        B.wt_setup(st, c.KC, 256)
        stg = [B.sb(st, "stg", [128, c.TG], BF16) for _ in range(3)]
        ntile = c.TG // 128
        vst = [B.sb(st, "vst", [128, ntile, 128], BF16) for _ in range(2)]
        dst_ = B.sb(st, "dtst", [128, ntile, 2 * c.H], F32)
        segs = [("zT", c.o_z, c.INNER), ("xbcT", c.o_xbc, c.XBC), ("uT", c.o_u, c.W5), ("qT", c.o_q, c.AW),
                ("kT", c.o_k, c.AW), ("gT", c.o_g, 3 * c.D)]
        cnt = {"blk": 0, "ev": 0, "v": 0}
        ntt = -(-c.TG // 512)
        for tg in range(c.NT // c.TG):
            t0 = tg * c.TG
            B.norm_T(st, xsrc, xkey, gain, t0, c.TG, hT)
            for (dname, c0, n) in segs:
                def evac(ps, col_abs, nb, tok0, tw, pb, dname=dname, c0=c0):
                    sg = stg[cnt["blk"] % 3]
                    cnt["ev"] += 1
                    if cnt["ev"] % 2 == 0:
                        B.A(lambda: nc.scalar.copy(out=sg.ap[0:nb, tok0:tok0 + tw], in_=ps), pb, [sg])
                    else:
                        B.V(lambda: nc.vector.tensor_copy(out=sg.ap[0:nb, tok0:tok0 + tw], in_=ps), pb, [sg])
                    if tok0 + tw >= c.TG:
                        P.dma("sp", S[dname][col_abs - c0: col_abs - c0 + nb, t0:t0 + c.TG], sg.ap[0:nb, :], reads=[sg])
                        cnt["blk"] += 1
                B.dense_feat(hT, c.KC, c.TG, W["w_in"][l], c0, n, evac)

            def evac_dt(ps, tok0, rows, col_abs, cw, pb):
                ti = tok0 // 128
                B.V(lambda: nc.vector.tensor_copy(out=dst_.ap[0:rows, ti, :], in_=ps), pb, [dst_])
                if ti == ntile - 1:
                    P.dma("sp", S["dt"][t0:t0 + c.TG, :].rearrange("(t p) n -> p t n", p=128), dst_.ap, reads=[dst_])
            B.dense_tok(hT, c.KC, c.TG, W["w_in"][l], c.o_dt, 2 * c.H, 2 * c.H, evac_dt)

            def evac_v(ps, tok0, rows, col_abs, cw, pb):
                ti = tok0 // 128
                vs = vst[cnt["v"] % 2]
                B.A(lambda: nc.scalar.copy(out=vs.ap[0:rows, ti, 0:cw], in_=ps), pb, [vs])
                if ti == ntile - 1:
                    P.dma("sp", S["v"][t0:t0 + c.TG, col_abs - c.o_v: col_abs - c.o_v + cw].rearrange("(t p) n -> p t n", p=128),
                          vs.ap[:, :, 0:cw], reads=[vs])
                    cnt["v"] += 1
            B.dense_tok(hT, c.KC, c.TG, W["w_in"][l], c.o_v, c.AW, 128, evac_v)
        P.barrier()


def layout_weights(c, I):
    L = c.DEPTH
    TS = c.G5 * c.P5 // 128
    f = lambda a: np.ascontiguousarray(np.asarray(a, dtype=np.float32))
    o = {}
    o["norm_mix"] = f(I["norm_mix"])
    o["w_in"] = f(I["w_in"])
    cw = np.zeros((L, c.XBC, 8), np.float32)
    cw[:, :, 0:c.CONV] = np.asarray(I["ssm_conv_w"]).transpose(0, 2, 1)
    cw[:, :, 5] = np.asarray(I["ssm_conv_b"])
    o["ssm_cw"] = cw
    o["ssm_a_log"] = f(np.asarray(I["ssm_a_log"]).reshape(L, 2 * c.H))
    o["ssm_dt_bias"] = f(np.asarray(I["ssm_dt_bias"]).reshape(L, 2 * c.H))
    dcol = np.repeat(np.asarray(I["ssm_d"]), c.PH, axis=1)
    o["ssm_dcol"] = f(dcol.reshape(L, c.INNER // 128, 128).transpose(0, 2, 1))
    o["ssm_ng"] = f(np.asarray(I["ssm_norm"]).reshape(L, c.INNER // 128, 128).transpose(0, 2, 1))
    o["ssm_w_out"] = f(I["ssm_w_out"])
    st = lambda a: np.asarray(a).reshape(L, 2, TS, 128).transpose(0, 1, 3, 2)
    ls = np.broadcast_to(np.asarray(I["s5_log_step"])[..., None], (L, 2, c.G5, c.P5))
    o["s5_par"] = f(np.stack([st(I["s5_a_re"]), st(I["s5_a_im"]), st(ls)], axis=-1))
    sb_ = lambda b: np.asarray(b).reshape(L, TS, 128, c.C5).transpose(0, 2, 1, 3)
    o["s5_b"] = f(np.stack([sb_(I["s5_b_re"]), sb_(I["s5_b_im"])], axis=3))
    sc_ = lambda cc: np.asarray(cc).transpose(0, 1, 2, 4, 3).reshape(L, 2, TS, 128, c.C5).transpose(0, 1, 3, 2, 4)
    o["s5_c"] = f(np.stack([sc_(I["s5_c_re"]), sc_(I["s5_c_im"])], axis=4))
    o["s5_dcol"] = f(np.asarray(I["s5_d"]).reshape(L, c.W5 // 128, 128).transpose(0, 2, 1))
    o["s5_w_glu"] = f(I["s5_w_glu"])
    o["att_w_out"] = f(I["att_w_out"])
    o["w_o"] = f(I["w_o"])
    o["norm_ffn"] = f(I["norm_ffn"])
    o["w_up"] = f(I["w_up"])
    fc = np.concatenate([np.asarray(I["ffn_conv_w"]).transpose(0, 2, 1), np.asarray(I["ffn_conv_b"])[:, :, None]], axis=2)
    o["ffn_cw"] = f(fc.reshape(L, 2 * c.DFF // 128, 128, 4).transpose(0, 2, 1, 3))
    o["w_down"] = f(I["w_down"])
    o["final_norm"] = f(np.asarray(I["final_norm"]).reshape(1, c.D))
    o["rel_bias"] = f(I["rel_bias"])
    return o


def core_inputs(c, wl, consts, x_stream, link):
    m = dict(wl)
    m.update(consts)
    m["x"] = np.ascontiguousarray(x_stream, dtype=np.float32)
    m["link"] = np.full((128, 1), float(link), np.float32)
    m["nlink"] = np.full((128, 1), 0.0 if link else NEG, np.float32)
    return m


def _bc(ap, shape):
    return ap.broadcast_to(list(shape))


def phase_ssd(B, l):
    c, nc, P, S, W = B.c, B.nc, B.P, B.S, B.W
    CT = c.XBC // 128
    TI = c.INNER // 128
    H, PH, G, HPG, NS = c.H, c.PH, c.G, c.HPG, c.NS
    BND = c.BND
    SC = c.SC
    CPS = SC // 128
    NSC = c.NT // SC
    with contextlib.ExitStack() as st:
        cw = B.sb(st, "cw", [128, CT, 8], F32)
        P.dma("sp", cw.ap, W["ssm_cw"][l].rearrange("(t p) k -> p t k", p=128), writes=[cw])
        xps = [B.sb(st, "xp", [128, 2, BND + 4], BF16) for _ in range(2)]
        accs = [B.sb(st, "acc", [128, 2, BND], F32) for _ in range(2)]
        xos = [B.sb(st, "xo", [128, 2, BND], BF16) for _ in range(2)]
        for xp in xps:
            B.V(lambda: nc.vector.memset(xp.ap, 0.0), [], [xp])
        for ct in range(CT):
            xp, acc, xo = xps[ct % 2], accs[ct % 2], xos[ct % 2]
            rows = S["xbcT"][ct * 128:(ct + 1) * 128, :]
            P.dma("sp", xp.ap[:, :, 2:2 + BND], rows.rearrange("p (h t) -> p h t", h=2), writes=[xp])
            B.V(lambda: nc.vector.tensor_scalar(out=xp.ap[:, 0, BND + 2:BND + 4], in0=xp.ap[:, 1, 2:4], scalar1=B.linkt.ap[:, 0:1],
                                                scalar2=None, op0=ALU.mult), [xp, B.linkt], [xp])
            B.V(lambda: nc.vector.tensor_scalar(out=xp.ap[:, 1, 0:2], in0=xp.ap[:, 0, BND:BND + 2], scalar1=B.linkt.ap[:, 0:1],
                                                scalar2=None, op0=ALU.mult), [xp, B.linkt], [xp])
            B.V(lambda: nc.vector.tensor_scalar(out=acc.ap, in0=xp.ap[:, :, 0:BND], scalar1=cw.ap[:, ct, 0:1], scalar2=cw.ap[:, ct, 5:6],
                                                op0=ALU.mult, op1=ALU.add), [xp, cw], [acc])
            for k in range(1, c.CONV):
                B.V(lambda k=k: nc.vector.scalar_tensor_tensor(out=acc.ap, in0=xp.ap[:, :, k:k + BND], scalar=cw.ap[:, ct, k:k + 1],
                                                               in1=acc.ap, op0=ALU.mult, op1=ALU.add), [xp, cw, acc], [acc])
            B.A(lambda: nc.scalar.activation(out=xo.ap, in_=acc.ap, func=AF.Silu), [acc], [xo])
            P.dma("sp", rows.rearrange("p (h t) -> p h t", h=2), xo.ap, reads=[xo])
        P.barrier()

    if B.stop_after == ("ssd0", l):
        return

    def dt_prep(st):
        t = {}
        t["bias"] = B.sb(st, "dtb", [128, 2 * H], F32)
        t["A"] = B.sb(st, "Abc", [128, 2 * H], F32)
        B.bcast_load(t["bias"], W["ssm_dt_bias"][l:l + 1, :], 2 * H)
        B.bcast_load(t["A"], W["ssm_a_log"][l:l + 1, :], 2 * H)
        A_ = t["A"]
        B.A(lambda: nc.scalar.activation(out=A_.ap, in_=A_.ap, func=AF.Exp), [A_], [A_])
        B.V(lambda: nc.vector.tensor_scalar(out=A_.ap, in0=A_.ap, scalar1=-1.0, scalar2=None, op0=ALU.mult), [A_], [A_])
        for n in ("raw", "t", "a", "e", "dtv", "av"):
            t[n] = B.sb(st, "dt" + n, [128, CPS, 2 * H], F32)
        return t

    def dt_compute(t, sc):
        raw, tt, a, e, dtv, av = t["raw"], t["t"], t["a"], t["e"], t["dtv"], t["av"]
        P.dma("sp", raw.ap, S["dt"][sc * SC:(sc + 1) * SC, :].rearrange("(k p) n -> p k n", p=128), writes=[raw])
        bb = _bc(t["bias"].ap.unsqueeze(1), [128, CPS, 2 * H])
        B.V(lambda: nc.vector.tensor_tensor(out=tt.ap, in0=raw.ap, in1=bb, op=ALU.add), [raw, t["bias"]], [tt])
        B.A(lambda: nc.scalar.activation(out=a.ap, in_=tt.ap, func=AF.Abs), [tt], [a])
        B.A(lambda: nc.scalar.activation(out=e.ap, in_=a.ap, func=AF.Exp, scale=-1.0), [a], [e])
        B.A(lambda: nc.scalar.activation(out=e.ap, in_=e.ap, func=AF.Ln, bias=1.0), [e], [e])
        B.V(lambda: nc.vector.tensor_scalar(out=a.ap, in0=tt.ap, scalar1=0.0, scalar2=None, op0=ALU.max), [tt], [a])
        B.V(lambda: nc.vector.tensor_tensor(out=dtv.ap, in0=a.ap, in1=e.ap, op=ALU.add), [a, e], [dtv])
        ab = _bc(t["A"].ap.unsqueeze(1), [128, CPS, 2 * H])
        B.V(lambda: nc.vector.tensor_tensor(out=av.ap, in0=dtv.ap, in1=ab, op=ALU.mult), [dtv, t["A"]], [av])

    def load_xc(xc, sc):
        P.dma("sp", xc.ap, S["xbcT"][:, sc * SC:(sc + 1) * SC].rearrange("(t p) n -> p t n", p=128), writes=[xc])

    for d in range(2):
        with contextlib.ExitStack() as st:
            t = dt_prep(st)
            xcs = [B.sb(st, "xc", [128, CT, SC], BF16) for _ in range(2)]
            xsB = B.sb(st, "xsB", [128, c.INNER + G * NS], BF16)
            acs = B.sb(st, "acs", [128, H], F32)
            dte = B.sb(st, "dte", [128, H], F32)
            coef = B.sb(st, "coef", [128, H], F32)
            dec = B.sb(st, "dec", [128, H], F32)
            xdd = B.sb(st, "xdd", [128, c.INNER], BF16)
            Hst = B.sb(st, "Hst", [128, c.INNER], F32)
            hstg = [B.sb(st, "hstg", [128, c.INNER], BF16) for _ in range(2)]
            B.V(lambda: nc.vector.memset(Hst.ap, 0.0), [], [Hst])
            tri = B.triu if d == 0 else B.tril
            scs = list(range(NSC)) if d == 0 else list(range(NSC - 1, -1, -1))
            bnd_chunk = c.NCH // 2 if d == 0 else c.NCH // 2 - 1
            psS = [B.psb[0], B.psb[1], B.psb[2]]
            pm = B.psb[3]
            for si, sc in enumerate(scs):
                xc = xcs[si % 2]
                load_xc(xc, sc)
                dt_compute(t, sc)
                cks = list(range(CPS)) if d == 0 else list(range(CPS - 1, -1, -1))
                for ck in cks:
                    gck = sc * CPS + ck
                    tk = slice(ck * 128, (ck + 1) * 128)

                    def tr():
                        ins = None
                        for i in range(TI):
                            ins = nc.tensor.transpose(B.pT.ap[:, i * 128:(i + 1) * 128], xc.ap[:, i, tk], B.identb.ap)
                        for g in range(G):
                            ins = nc.tensor.transpose(B.pT.ap[:, (TI + g) * 128:(TI + g + 1) * 128], xc.ap[:, TI + g, tk], B.identb.ap)
                        return ins
                    B.T(tr, [xc, B.identb], [B.pT])
                    B.A(lambda: nc.scalar.copy(out=xsB.ap, in_=B.pT.ap[:, 0:c.INNER + G * NS]), [B.pT], [xsB])
                    a_d = t["av"].ap[:, ck, d * H:(d + 1) * H]
                    B.mm(pm.ap[:, 0:H], [(tri.ap, a_d)], [tri, t["av"]], [B.phalf[3][0]])
                    B.mm(pm.ap[:, 256:256 + H], [(B.onesf.ap, a_d)], [B.onesf, t["av"]], [B.phalf[3][1]])
                    B.A(lambda: nc.scalar.copy(out=acs.ap, in_=pm.ap[:, 0:H]), [B.phalf[3][0]], [acs])
                    B.V(lambda: nc.vector.tensor_tensor(out=dte.ap, in0=pm.ap[:, 256:256 + H], in1=acs.ap, op=ALU.subtract),
                        [B.phalf[3][1], acs], [dte])
                    B.A(lambda: nc.scalar.activation(out=dte.ap, in_=dte.ap, func=AF.Exp), [dte], [dte])
                    B.A(lambda: nc.scalar.activation(out=dec.ap, in_=pm.ap[:, 256:256 + H], func=AF.Exp), [B.phalf[3][1]], [dec])
                    B.V(lambda: nc.vector.tensor_tensor(out=coef.ap, in0=t["dtv"].ap[:, ck, d * H:(d + 1) * H], in1=dte.ap, op=ALU.mult),
                        [t["dtv"], dte], [coef])
                    B.V(lambda: nc.vector.tensor_tensor(out=xdd.ap.rearrange("p (h q) -> p h q", q=PH),
                                                        in0=xsB.ap[:, 0:c.INNER].rearrange("p (h q) -> p h q", q=PH),
                                                        in1=_bc(coef.ap.unsqueeze(2), [128, H, PH]), op=ALU.mult), [xsB, coef], [xdd])
                    for g in range(G):
                        c0 = g * HPG * PH
                        c1 = c0 + HPG * PH
                        p0 = c0
                        while p0 < c1:
                            p1 = min(c1, (p0 // 512 + 1) * 512)
                            bk = p0 // 512
                            B.mm(psS[bk].ap[:, p0 - bk * 512:p1 - bk * 512],
                                 [(xsB.ap[:, c.INNER + g * NS: c.INNER + (g + 1) * NS], xdd.ap[:, p0:p1])], [xsB, xdd], B.phalf[bk])
                            p0 = p1
                    if gck == bnd_chunk:
                        B.V(lambda: nc.vector.tensor_scalar(out=Hst.ap, in0=Hst.ap, scalar1=B.linkt.ap[:, 0:1], scalar2=None, op0=ALU.mult),
                            [Hst, B.linkt], [Hst])
                    hs = hstg[gck % 2]
                    B.A(lambda: nc.scalar.copy(out=hs.ap, in_=Hst.ap), [Hst], [hs])
                    P.dma("sp", S["hin"][d, gck], hs.ap, reads=[hs])
                    B.V(lambda: nc.vector.tensor_tensor(out=Hst.ap.rearrange("p (h q) -> p h q", q=PH),
                                                        in0=Hst.ap.rearrange("p (h q) -> p h q", q=PH),
                                                        in1=_bc(dec.ap.unsqueeze(2), [128, H, PH]), op=ALU.mult), [Hst, dec], [Hst])
                    nb = -(-c.INNER // 512)
                    for bk in range(nb):
                        w_ = min(512, c.INNER - bk * 512)
                        B.V(lambda bk=bk, w_=w_: nc.vector.tensor_tensor(out=Hst.ap[:, bk * 512:bk * 512 + w_], in0=psS[bk].ap[:, 0:w_],
                                                                         in1=Hst.ap[:, bk * 512:bk * 512 + w_], op=ALU.add),
                            B.phalf[bk] + [Hst], [Hst])
            P.barrier()

    if B.stop_after == ("ssdA", l):
        return
    HB = 3 if HPG % 3 == 0 else (2 if HPG % 2 == 0 else 1)
    with contextlib.ExitStack() as st:
        t = dt_prep(st)
        xcs = [B.sb(st, "xc", [128, CT, SC], BF16) for _ in range(2)]
        zts = [B.sb(st, "zt", [128, TI, SC], BF16) for _ in range(2)]
        yns = [B.sb(st, "ynst", [128, TI, SC], BF16) for _ in range(2)]
        dcol = B.sb(st, "dcol", [128, TI], F32)
        ng = B.sb(st, "ng", [128, TI], F32)
        P.dma("sp", dcol.ap, W["ssm_dcol"][l], writes=[dcol])
        P.dma("sp", ng.ap, W["ssm_ng"][l], writes=[ng])
        mneg = [B.sb(st, "mneg", [128, 128], F32) for _ in range(2)]
        for d, tri in enumerate((B.triu, B.tril)):
            B.V(lambda d=d, tri=tri: nc.vector.tensor_scalar(out=mneg[d].ap, in0=tri.ap, scalar1=-1.0, scalar2=-NEG, op0=ALU.add, op1=ALU.mult),
                [tri], [mneg[d]])
        epst = B.sb(st, "epst", [128, 1], F32)
        B.V(lambda: nc.vector.memset(epst.ap, c.EPS), [], [epst])
        xs_tok = B.sb(st, "xstok", [128, c.INNER], BF16)
        xd = [B.sb(st, "xd", [128, c.INNER], BF16) for _ in range(2)]
        acs2 = B.sb(st, "acs2", [128, 2 * H], F32)
        cbt = B.sb(st, "cbt", [128, G, 128], F32)
        hin = [[B.sb(st, "hin", [128, c.INNER], BF16) for _ in range(2)] for _ in range(2)]
        MT = [B.sb(st, "MT", [128, H, 128], BF16) for _ in range(2)]
        CsT = [B.sb(st, "CsT", [128, H, 128], BF16) for _ in range(2)]
        amask = [B.sb(st, "amask", [128, HB, 128], F32) for _ in range(2)]
        expA = [B.sb(st, "expA", [128, HB, 128], F32) for _ in range(2)]
        T1 = [B.sb(st, "T1", [128, HB, 128], F32) for _ in range(2)]
        yv = B.sb(st, "yv", [128, TI, 128], F32)
        sz = B.sb(st, "sz", [128, TI, 128], F32)
        sq = B.sb(st, "sq", [128, TI, 128], BF16)
        rt = B.sb(st, "rt", [128, 128], F32)
        psy = [B.psb[1], B.psb[2], B.psb[3]]
        pbc = [B.psb[0], B.psb[4]]
        pcb = B.psb[5]
        pmisc = B.psb[7]
        pmb = [B.phalf[7][1]]
        it = 0
        for sc in range(NSC):
            xc, zt, ynst = xcs[sc % 2], zts[sc % 2], yns[sc % 2]
            load_xc(xc, sc)
            P.dma("sp", zt.ap, S["zT"][:, sc * SC:(sc + 1) * SC].rearrange("(t p) n -> p t n", p=128), writes=[zt])
            dt_compute(t, sc)
            for ck in range(CPS):
                gck = sc * CPS + ck
                tk = slice(ck * 128, (ck + 1) * 128)

                def tr():
                    ins = None
                    for i in range(TI):
                        ins = nc.tensor.transpose(B.pT.ap[:, i * 128:(i + 1) * 128], xc.ap[:, i, tk], B.identb.ap)
                    return ins
                B.T(tr, [xc, B.identb], [B.phalf[6][0], B.phalf[6][1], B.phalf[7][0]])
                B.A(lambda: nc.scalar.copy(out=xs_tok.ap, in_=B.pT.ap[:, 0:c.INNER]), [B.phalf[6][0], B.phalf[6][1], B.phalf[7][0]], [xs_tok])
                for d in range(2):
                    B.V(lambda d=d: nc.vector.tensor_tensor(out=xd[d].ap.rearrange("p (h q) -> p h q", q=PH),
                                                            in0=xs_tok.ap.rearrange("p (h q) -> p h q", q=PH),
                                                            in1=_bc(t["dtv"].ap[:, ck, d * H:(d + 1) * H].unsqueeze(2), [128, H, PH]), op=ALU.mult),
                        [xs_tok, t["dtv"]], [xd[d]])
                    P.dma("sp", hin[d][gck % 2].ap, S["hin"][d, gck], writes=[hin[d][gck % 2]])
                B.mm(pmisc.ap[:, 256:256 + H], [(B.triu.ap, t["av"].ap[:, ck, 0:H])], [B.triu, t["av"]], pmb)
                B.mm(pmisc.ap[:, 256 + H:256 + 2 * H], [(B.tril.ap, t["av"].ap[:, ck, H:2 * H])], [B.tril, t["av"]], pmb)
                B.A(lambda: nc.scalar.copy(out=acs2.ap, in_=pmisc.ap[:, 256:256 + 2 * H]), pmb, [acs2])
                for g in range(G):
                    B.mm(pcb.ap[:, g * 128:(g + 1) * 128], [(xc.ap[:, TI + g, tk], xc.ap[:, TI + G + g, tk])], [xc], B.phalf[5])
                B.A(lambda: nc.scalar.copy(out=cbt.ap.rearrange("p g l -> p (g l)"), in_=pcb.ap[:, 0:G * 128]), B.phalf[5], [cbt])
                import os as _os
                for d in range(2):
                    if _os.environ.get("SSD_SKIP") == "3":
                        continue
                    tri = B.triu if d == 0 else B.tril
                    for hb in range(H // HB):
                        h0 = hb * HB
                        g = h0 // HPG
                        i2 = it % 2
                        it += 1
                        am, ea, t1, pb = amask[i2], expA[i2], T1[i2], pbc[i2]
                        pbb = B.phalf[0] if i2 == 0 else B.phalf[4]
                        a_sl = t["av"].ap[:, ck, d * H + h0: d * H + h0 + HB]
                        B.V(lambda: nc.vector.tensor_tensor(out=am.ap, in0=_bc(tri.ap.unsqueeze(1), [128, HB, 128]),
                                                            in1=_bc(a_sl.unsqueeze(2), [128, HB, 128]), op=ALU.mult), [tri, t["av"]], [am])
                        _lv = _os.environ.get("SSD_SKIP")
                        if _lv == "5":
                            continue
                        B.mm(pb.ap[:, 0:HB * 128], [(B.onesf.ap, am.ap.rearrange("p h l -> p (h l)"))], [B.onesf, am], pbb)
                        if _lv == "6":
                            continue
                        pv = pb.ap[:, 0:HB * 128].rearrange("p (h l) -> p h l", l=128)
                        B.A(lambda: nc.scalar.activation(out=ea.ap, in_=pv, func=AF.Exp), pbb, [ea])
                        if _lv == "7":
                            continue
                        B.V(lambda: nc.vector.tensor_tensor(out=t1.ap, in0=pv, in1=_bc(mneg[d].ap.unsqueeze(1), [128, HB, 128]), op=ALU.add),
                            pbb + [mneg[d]], [t1])
                        if _lv == "8":
                            continue
                        B.V(lambda: nc.vector.tensor_tensor(out=t1.ap, in0=t1.ap,
                                                            in1=_bc(acs2.ap[:, d * H + h0:d * H + h0 + HB].unsqueeze(2), [128, HB, 128]),
                                                            op=ALU.subtract), [t1, acs2], [t1])
                        B.A(lambda: nc.scalar.activation(out=t1.ap, in_=t1.ap, func=AF.Exp), [t1], [t1])
                        if _lv == "9":
                            continue
                        B.V(lambda: nc.vector.tensor_tensor(out=MT[d].ap[:, h0:h0 + HB, :], in0=t1.ap,
                                                            in1=_bc(cbt.ap[:, g:g + 1, :], [128, HB, 128]), op=ALU.mult), [t1, cbt], [MT[d]])
                        if _os.environ.get("SSD_SKIP") == "4":
                            continue
                        B.G(lambda: nc.gpsimd.tensor_tensor(out=CsT[d].ap[:, h0:h0 + HB, :], in0=ea.ap,
                                                            in1=_bc(xc.ap[:, TI + G + g, tk].unsqueeze(1), [128, HB, 128]), op=ALU.mult),
                            [ea, xc], [CsT[d]])
                if _os.environ.get("SSD_SKIP") in ("1", "3", "4", "5", "6", "7", "8", "9"):
                    continue
                hf, hb_ = hin[0][gck % 2], hin[1][gck % 2]
                for h in range(H):
                    tl = (h * PH) // 128
                    po = (h * PH) % 128
                    bk = (tl * 128) // 512
                    co = tl * 128 - bk * 512
                    hs = slice(h * PH, (h + 1) * PH)
                    B.mm(psy[bk].ap[po:po + PH, co:co + 128],
                         [(xd[0].ap[:, hs], MT[0].ap[:, h, :]), (xd[1].ap[:, hs], MT[1].ap[:, h, :]),
                          (hf.ap[:, hs], CsT[0].ap[:, h, :]), (hb_.ap[:, hs], CsT[1].ap[:, h, :])],
                         [xd[0], xd[1], MT[0], MT[1], hf, hb_, CsT[0], CsT[1]], B.phalf[1 + bk])
                if _os.environ.get("SSD_SKIP") == "2":
                    continue
                for i in range(TI):
                    bk = (i * 128) // 512
                    co = i * 128 - bk * 512
                    B.V(lambda i=i, bk=bk, co=co: nc.vector.scalar_tensor_tensor(out=yv.ap[:, i, :], in0=xc.ap[:, i, tk], scalar=dcol.ap[:, i:i + 1],
                                                                                  in1=psy[bk].ap[:, co:co + 128], op0=ALU.mult, op1=ALU.add),
                        [xc, dcol] + B.phalf[1 + bk], [yv])
                B.A(lambda: nc.scalar.activation(out=sz.ap, in_=zt.ap[:, :, tk], func=AF.Silu), [zt], [sz])
                B.V(lambda: nc.vector.tensor_tensor(out=yv.ap, in0=yv.ap, in1=sz.ap, op=ALU.mult), [yv, sz], [yv])
                B.A(lambda: nc.scalar.activation(out=sq.ap, in_=yv.ap, func=AF.Square), [yv], [sq])
                B.mm(pmisc.ap[:, 384:512], [(B.onesb.ap, sq.ap[:, i, :]) for i in range(TI)], [B.onesb, sq], pmb)
                B.A(lambda: nc.scalar.activation(out=rt.ap, in_=pmisc.ap[:, 384:512], func=AF.Sqrt, scale=1.0 / c.INNER, bias=epst.ap[:, 0:1]),
                    pmb + [epst], [rt])
                B.V(lambda: nc.vector.reciprocal(out=rt.ap, in_=rt.ap), [rt], [rt])
                B.V(lambda: nc.vector.tensor_tensor(out=yv.ap, in0=yv.ap, in1=_bc(rt.ap.unsqueeze(1), [128, TI, 128]), op=ALU.mult), [yv, rt], [yv])
                B.V(lambda: nc.vector.tensor_tensor(out=ynst.ap[:, :, tk], in0=yv.ap, in1=_bc(ng.ap.unsqueeze(2), [128, TI, 128]), op=ALU.mult),
                    [yv, ng], [ynst])
            P.dma("sp", S["ynT"][:, sc * SC:(sc + 1) * SC].rearrange("(t p) n -> p t n", p=128), ynst.ap, reads=[ynst])
        P.barrier()


def phase_s5(B, l):
    c, nc, P, S, W = B.c, B.nc, B.P, B.S, B.W
    TS, C5, NT, LC = c.TS, c.C5, c.NT, c.S5C
    KT5 = c.W5 // 128
    NLC = NT // LC
    PI = math.pi
    with contextlib.ExitStack() as st:
        par = B.sb(st, "s5par", [128, 2, TS, 3], F32)
        P.dma("sp", par.ap, W["s5_par"][l].rearrange("d p t k -> p d t k"), writes=[par])
        bb = B.sb(st, "s5b", [128, TS, 2, C5], F32)
        P.dma("sp", bb.ap, W["s5_b"][l], writes=[bb])
        cc = B.sb(st, "s5c", [128, 2, TS, 2, C5], F32)
        P.dma("sp", cc.ap, W["s5_c"][l].rearrange("d p t k o -> p d t k o"), writes=[cc])
        dcol = B.sb(st, "s5d", [128, KT5], F32)
        P.dma("sp", dcol.ap, W["s5_dcol"][l], writes=[dcol])
        iota = B.sb(st, "iota", [128, LC], F32)
        riota = B.sb(st, "riota", [128, LC], F32)
        P.dma("sp", iota.ap, B.K["c_iota"], writes=[iota])
        B.V(lambda: nc.vector.tensor_scalar(out=riota.ap, in0=iota.ap, scalar1=-1.0, scalar2=float(LC - 1), op0=ALU.mult, op1=ALU.add),
            [iota], [riota])
        hpi = B.sb(st, "hpi", [128, 1], F32)
        B.V(lambda: nc.vector.memset(hpi.ap, PI / 2), [], [hpi])
        n2 = 2 * TS
        mk = lambda n, w=n2: B.sb(st, n, [128, w], F32)
        are, aim, step, lr, th, rr, cosl, sinl = [mk(n) for n in ("are", "aim", "step", "lr", "th", "rr", "cosl", "sinl")]
        tA, tB, tC, tD = [mk(n) for n in ("tA", "tB", "tC", "tD")]
        ki = B.sb(st, "ki", [128, max(n2, LC)], I32)
        tabt = [B.sb(st, "tabt", [128, LC], F32) for _ in range(3)]

        def red_sincos(x_ap, n, sin_out, cos_out, tmp, deps, outs):
            y, g, a = tmp
            B.V(lambda: nc.vector.tensor_scalar(out=y, in0=x_ap, scalar1=1.0 / TWO_PI, scalar2=None, op0=ALU.mult), deps, [tmpb])
            B.V(lambda: nc.vector.tensor_copy(out=ki.ap[:, 0:n], in_=y), [tmpb], [ki])
            B.V(lambda: nc.vector.tensor_copy(out=y, in_=ki.ap[:, 0:n]), [ki], [tmpb])
            B.V(lambda: nc.vector.scalar_tensor_tensor(out=y, in0=y, scalar=-TWO_PI, in1=x_ap, op0=ALU.mult, op1=ALU.add), deps + [tmpb], [tmpb])
            B.V(lambda: nc.vector.tensor_single_scalar(out=g, in_=y, scalar=PI, op=ALU.is_gt), [tmpb], [tmpb])
            B.V(lambda: nc.vector.scalar_tensor_tensor(out=y, in0=g, scalar=-TWO_PI, in1=y, op0=ALU.mult, op1=ALU.add), [tmpb], [tmpb])
            B.V(lambda: nc.vector.tensor_single_scalar(out=g, in_=y, scalar=-PI, op=ALU.is_lt), [tmpb], [tmpb])
            B.V(lambda: nc.vector.scalar_tensor_tensor(out=y, in0=g, scalar=TWO_PI, in1=y, op0=ALU.mult, op1=ALU.add), [tmpb], [tmpb])
            B.V(lambda: nc.vector.tensor_scalar(out=y, in0=y, scalar1=-PI, scalar2=PI, op0=ALU.max, op1=ALU.min), [tmpb], [tmpb])
            B.A(lambda: nc.scalar.activation(out=sin_out, in_=y, func=AF.Sin), [tmpb], outs)
            B.A(lambda: nc.scalar.activation(out=a, in_=y, func=AF.Abs), [tmpb], [tmpb])
            B.A(lambda: nc.scalar.activation(out=cos_out, in_=a, func=AF.Sin, scale=-1.0, bias=hpi.ap[:, 0:1]), [tmpb, hpi], outs)
        tmpb = Buf("s5tmp")
        f2 = lambda k: par.ap[:, :, :, k]
        v2 = lambda t: t.ap.rearrange("p (d t) -> p d t", d=2)
        B.V(lambda: nc.vector.tensor_copy(out=v2(are), in_=f2(0)), [par], [are])
        B.V(lambda: nc.vector.tensor_copy(out=v2(aim), in_=f2(1)), [par], [aim])
        B.A(lambda: nc.scalar.activation(out=v2(step), in_=f2(2), func=AF.Exp), [par], [step])
        B.V(lambda: nc.vector.tensor_tensor(out=lr.ap, in0=are.ap, in1=step.ap, op=ALU.mult), [are, step], [lr])
        B.V(lambda: nc.vector.tensor_tensor(out=th.ap, in0=aim.ap, in1=step.ap, op=ALU.mult), [aim, step], [th])
        B.A(lambda: nc.scalar.activation(out=rr.ap, in_=lr.ap, func=AF.Exp), [lr], [rr])
        red_sincos(th.ap, n2, sinl.ap, cosl.ap, (tA.ap, tB.ap, tC.ap), [th], [sinl, cosl])
        cT, sT, cTl, sTl, thL = [mk(n) for n in ("cT", "sT", "cTl", "sTl", "thL")]
        B.V(lambda: nc.vector.tensor_scalar(out=thL.ap, in0=th.ap, scalar1=float(LC), scalar2=None, op0=ALU.mult), [th], [thL])
        red_sincos(thL.ap, n2, sT.ap, cT.ap, (tA.ap, tB.ap, tC.ap), [thL], [sT, cT])
        B.V(lambda: nc.vector.tensor_scalar(out=cTl.ap, in0=cT.ap, scalar1=B.linkt.ap[:, 0:1], scalar2=None, op0=ALU.mult), [cT, B.linkt], [cTl])
        B.V(lambda: nc.vector.tensor_scalar(out=sTl.ap, in0=sT.ap, scalar1=B.linkt.ap[:, 0:1], scalar2=None, op0=ALU.mult), [sT, B.linkt], [sTl])
        lbr, lbi, cr, ci = [mk(n) for n in ("lbr", "lbi", "cr", "ci")]
        B.V(lambda: nc.vector.tensor_tensor(out=lbr.ap, in0=rr.ap, in1=cosl.ap, op=ALU.mult), [rr, cosl], [lbr])
        B.V(lambda: nc.vector.tensor_tensor(out=lbi.ap, in0=rr.ap, in1=sinl.ap, op=ALU.mult), [rr, sinl], [lbi])
        B.V(lambda: nc.vector.tensor_scalar(out=lbr.ap, in0=lbr.ap, scalar1=-1.0, scalar2=None, op0=ALU.add), [lbr], [lbr])
        B.V(lambda: nc.vector.tensor_tensor(out=tA.ap, in0=are.ap, in1=are.ap, op=ALU.mult), [are, tmpb], [tmpb])
        B.V(lambda: nc.vector.tensor_tensor(out=tB.ap, in0=aim.ap, in1=aim.ap, op=ALU.mult), [aim, tmpb], [tmpb])
        B.V(lambda: nc.vector.tensor_tensor(out=tA.ap, in0=tA.ap, in1=tB.ap, op=ALU.add), [tmpb], [tmpb])
        B.V(lambda: nc.vector.reciprocal(out=tD.ap, in_=tA.ap), [tmpb], [tD])
        B.V(lambda: nc.vector.tensor_tensor(out=tA.ap, in0=lbr.ap, in1=are.ap, op=ALU.mult), [lbr, are, tmpb], [tmpb])
        B.V(lambda: nc.vector.tensor_tensor(out=tB.ap, in0=lbi.ap, in1=aim.ap, op=ALU.mult), [lbi, aim, tmpb], [tmpb])
        B.V(lambda: nc.vector.tensor_tensor(out=tA.ap, in0=tA.ap, in1=tB.ap, op=ALU.add), [tmpb], [tmpb])
        B.V(lambda: nc.vector.tensor_tensor(out=cr.ap, in0=tA.ap, in1=tD.ap, op=ALU.mult), [tmpb, tD], [cr])
        B.V(lambda: nc.vector.tensor_tensor(out=tA.ap, in0=lbi.ap, in1=are.ap, op=ALU.mult), [lbi, are, tmpb], [tmpb])
        B.V(lambda: nc.vector.tensor_tensor(out=tB.ap, in0=lbr.ap, in1=aim.ap, op=ALU.mult), [lbr, aim, tmpb], [tmpb])
        B.V(lambda: nc.vector.tensor_tensor(out=tA.ap, in0=tA.ap, in1=tB.ap, op=ALU.subtract), [tmpb], [tmpb])
        B.V(lambda: nc.vector.tensor_tensor(out=ci.ap, in0=tA.ap, in1=tD.ap, op=ALU.mult), [tmpb, tD], [ci])
        BT = B.sb(st, "BT", [128, 2, TS, 2, 128], BF16)
        Cp = B.sb(st, "Cp", [128, 2, TS, 2, 128], BF16)
        B.V(lambda: nc.vector.memset(Cp.ap, 0.0), [], [Cp])
        Bbr = B.sb(st, "Bbr", [128, TS, C5], F32)
        Bbi = B.sb(st, "Bbi", [128, TS, C5], F32)
        tE = B.sb(st, "tE", [128, TS, C5], F32)
        pad = [B.sb(st, "pad", [128, 128], BF16) for _ in range(2)]
        ipad = 0
        for d in range(2):
            crb = _bc(cr.ap[:, d * TS:(d + 1) * TS].unsqueeze(2), [128, TS, C5])
            cib = _bc(ci.ap[:, d * TS:(d + 1) * TS].unsqueeze(2), [128, TS, C5])
            bre, bim = bb.ap[:, :, 0, :], bb.ap[:, :, 1, :]
            B.V(lambda: nc.vector.tensor_tensor(out=Bbr.ap, in0=bre, in1=crb, op=ALU.mult), [bb, cr], [Bbr])
            B.V(lambda: nc.vector.tensor_tensor(out=tE.ap, in0=bim, in1=cib, op=ALU.mult), [bb, ci], [tE])
            B.V(lambda: nc.vector.tensor_tensor(out=Bbr.ap, in0=Bbr.ap, in1=tE.ap, op=ALU.subtract), [Bbr, tE], [Bbr])
            B.V(lambda: nc.vector.tensor_tensor(out=Bbi.ap, in0=bim, in1=crb, op=ALU.mult), [bb, cr], [Bbi])
            B.V(lambda: nc.vector.tensor_tensor(out=tE.ap, in0=bre, in1=cib, op=ALU.mult), [bb, ci], [tE])
            B.V(lambda: nc.vector.tensor_tensor(out=Bbi.ap, in0=Bbi.ap, in1=tE.ap, op=ALU.add), [Bbi, tE], [Bbi])
            for m in range(TS):
                off = (m % 4) * 2 * C5
                for k, src in enumerate((Bbr, Bbi)):
                    pd = pad[ipad % 2]
                    ipad += 1
                    B.V(lambda: nc.vector.memset(pd.ap, 0.0), [], [pd])
                    B.V(lambda: nc.vector.tensor_copy(out=pd.ap[0:64, off:off + C5], in_=src.ap[0:64, m, :]), [src], [pd])
                    B.V(lambda: nc.vector.tensor_copy(out=pd.ap[64:128, off + C5:off + 2 * C5], in_=src.ap[64:128, m, :]), [src], [pd])
                    B.T(lambda: nc.tensor.transpose(B.pT.ap[:, 0:128], pd.ap, B.identb.ap), [pd, B.identb], B.phalf[6])
                    B.A(lambda: nc.scalar.copy(out=BT.ap[:, d, m, k, :], in_=B.pT.ap[:, 0:128]), B.phalf[6], [BT])
                B.V(lambda: nc.vector.tensor_copy(out=Cp.ap[0:64, d, m, 0, off:off + C5], in_=cc.ap[0:64, d, m, 0, :]), [cc], [Cp])
                B.V(lambda: nc.vector.tensor_copy(out=Cp.ap[64:128, d, m, 0, off + C5:off + 2 * C5], in_=cc.ap[64:128, d, m, 0, :]), [cc], [Cp])
                B.V(lambda: nc.vector.tensor_scalar(out=Cp.ap[0:64, d, m, 1, off:off + C5], in0=cc.ap[0:64, d, m, 1, :], scalar1=-1.0, scalar2=None,
                                                    op0=ALU.mult), [cc], [Cp])
                B.V(lambda: nc.vector.tensor_scalar(out=Cp.ap[64:128, d, m, 1, off + C5:off + 2 * C5], in0=cc.ap[64:128, d, m, 1, :], scalar1=-1.0,
                                                    scalar2=None, op0=ALU.mult), [cc], [Cp])
        uT = B.sb(st, "uT", [128, NT], BF16)
        yacc = B.sb(st, "yacc", [128, NT], F32)
        nm = min(4, TS)
        cs = [B.sb(st, "cs", [128, LC], F32) for _ in range(nm)]
        sn = [B.sb(st, "sn", [128, LC], F32) for _ in range(nm)]
        z0 = [[B.sb(st, "z0", [128, 1], F32) for _ in range(2)] for _ in range(nm)]
        wr, wi, zr, zi = [[B.sb(st, n, [128, LC], F32) for _ in range(2)] for n in ("wr", "wi", "zr", "zi")]
        q1, q2 = [[B.sb(st, n, [128, LC], F32) for _ in range(2)] for n in ("q1", "q2")]
        xr = [B.sb(st, "xr", [128, LC], BF16) for _ in range(nm)]
        xi = [B.sb(st, "xi", [128, LC], BF16) for _ in range(nm)]
        c1 = B.sb(st, "c1", [128, 1], F32)
        it = 0
        for kt in range(KT5):
            ms = [m for m in range(4 * kt, 4 * kt + 4) if m < TS]
            P.dma("sp", uT.ap, S["uT"][kt * 128:(kt + 1) * 128, :], writes=[uT])
            for d in range(2):
                for j, m in enumerate(ms):
                    ang = tabt[0]
                    io = iota if d == 0 else riota
                    B.V(lambda: nc.vector.tensor_scalar(out=ang.ap, in0=io.ap, scalar1=th.ap[:, d * TS + m:d * TS + m + 1], scalar2=None, op0=ALU.mult),
                        [io, th], [ang])
                    red_sincos(ang.ap, LC, sn[j].ap, cs[j].ap, (tabt[1].ap, tabt[2].ap, tabt[2].ap), [ang], [sn[j], cs[j]])
                    for k in range(2):
                        B.V(lambda k=k: nc.vector.memset(z0[j][k].ap, 0.0), [], [z0[j][k]])
                order = list(range(NLC)) if d == 0 else list(range(NLC - 1, -1, -1))
                bnd_c = (c.BND // LC) if d == 0 else (c.BND // LC - 1)
                for ch in order:
                    tk = slice(ch * LC, (ch + 1) * LC)
                    for j, m in enumerate(ms):
                        i2 = it % 2
                        it += 1
                        pa, pb_ = B.psb[2 * i2], B.psb[2 * i2 + 1]
                        ba, bbuf = B.phalf[2 * i2], B.phalf[2 * i2 + 1]
                        B.mm(pa.ap[:, 0:LC], [(BT.ap[:, d, m, 0, :], uT.ap[:, tk])], [BT, uT], ba)
                        B.mm(pb_.ap[:, 0:LC], [(BT.ap[:, d, m, 1, :], uT.ap[:, tk])], [BT, uT], bbuf)
                        w_r, w_i, z_r, z_i, a1, a2 = wr[i2], wi[i2], zr[i2], zi[i2], q1[i2], q2[i2]
                        B.V(lambda: nc.vector.tensor_tensor(out=a1.ap, in0=pa.ap[:, 0:LC], in1=cs[j].ap, op=ALU.mult), ba + [cs[j]], [a1])
                        B.V(lambda: nc.vector.tensor_tensor(out=a2.ap, in0=pb_.ap[:, 0:LC], in1=sn[j].ap, op=ALU.mult), bbuf + [sn[j]], [a2])
                        B.V(lambda: nc.vector.tensor_tensor(out=w_r.ap, in0=a1.ap, in1=a2.ap, op=ALU.add), [a1, a2], [w_r])
                        B.V(lambda: nc.vector.tensor_tensor(out=a1.ap, in0=pb_.ap[:, 0:LC], in1=cs[j].ap, op=ALU.mult), bbuf + [cs[j]], [a1])
                        B.V(lambda: nc.vector.tensor_tensor(out=a2.ap, in0=pa.ap[:, 0:LC], in1=sn[j].ap, op=ALU.mult), ba + [sn[j]], [a2])
                        B.V(lambda: nc.vector.tensor_tensor(out=w_i.ap, in0=a1.ap, in1=a2.ap, op=ALU.subtract), [a1, a2], [w_i])
                        if ch == bnd_c:
                            for k in range(2):
                                B.V(lambda k=k: nc.vector.tensor_scalar(out=z0[j][k].ap, in0=z0[j][k].ap, scalar1=B.linkt.ap[:, 0:1], scalar2=None,
                                                                        op0=ALU.mult), [z0[j][k], B.linkt], [z0[j][k]])
                        rb = _bc(rr.ap[:, d * TS + m:d * TS + m + 1], [128, LC])
                        sl = (lambda a: a) if d == 0 else (lambda a: a[:, ::-1])
                        B.V(lambda: nc.vector.tensor_tensor_scan(out=sl(z_r.ap), data0=rb, data1=sl(w_r.ap), initial=z0[j][0].ap[:, 0:1],
                                                                 op0=ALU.mult, op1=ALU.add), [rr, w_r, z0[j][0]], [z_r])
                        B.V(lambda: nc.vector.tensor_tensor_scan(out=sl(z_i.ap), data0=rb, data1=sl(w_i.ap), initial=z0[j][1].ap[:, 0:1],
                                                                 op0=ALU.mult, op1=ALU.add), [rr, w_i, z0[j][1]], [z_i])
                        B.G(lambda: nc.gpsimd.tensor_tensor(out=a1.ap, in0=z_r.ap, in1=cs[j].ap, op=ALU.mult), [z_r, cs[j]], [a1])
                        B.G(lambda: nc.gpsimd.tensor_tensor(out=a2.ap, in0=z_i.ap, in1=sn[j].ap, op=ALU.mult), [z_i, sn[j]], [a2])
                        B.G(lambda: nc.gpsimd.tensor_tensor(out=xr[j].ap, in0=a1.ap, in1=a2.ap, op=ALU.subtract), [a1, a2], [xr[j]])
                        B.G(lambda: nc.gpsimd.tensor_tensor(out=a1.ap, in0=z_i.ap, in1=cs[j].ap, op=ALU.mult), [z_i, cs[j]], [a1])
                        B.G(lambda: nc.gpsimd.tensor_tensor(out=a2.ap, in0=z_r.ap, in1=sn[j].ap, op=ALU.mult), [z_r, sn[j]], [a2])
                        B.G(lambda: nc.gpsimd.tensor_tensor(out=xi[j].ap, in0=a1.ap, in1=a2.ap, op=ALU.add), [a1, a2], [xi[j]])
                        last = LC - 1 if d == 0 else 0
                        col = d * TS + m
                        lk = (ch == (bnd_c - 1 if d == 0 else bnd_c + 1))
                        zl_r, zl_i = z_r.ap[:, last:last + 1], z_i.ap[:, last:last + 1]
                        B.V(lambda: nc.vector.tensor_scalar(out=c1.ap, in0=zl_i, scalar1=sT.ap[:, col:col + 1], scalar2=None, op0=ALU.mult), [z_i, sT], [c1])
                        B.V(lambda: nc.vector.scalar_tensor_tensor(out=z0[j][0].ap, in0=zl_r, scalar=cT.ap[:, col:col + 1], in1=c1.ap,
                                                                   op0=ALU.mult, op1=ALU.subtract), [z_r, cT, c1], [z0[j][0]])
                        B.V(lambda: nc.vector.tensor_scalar(out=c1.ap, in0=zl_r, scalar1=sT.ap[:, col:col + 1], scalar2=None, op0=ALU.mult), [z_r, sT], [c1])
                        B.V(lambda: nc.vector.scalar_tensor_tensor(out=z0[j][1].ap, in0=zl_i, scalar=cT.ap[:, col:col + 1], in1=c1.ap,
                                                                   op0=ALU.mult, op1=ALU.add), [z_i, cT, c1], [z0[j][1]])
                    py = B.psb[4 + (ch % 2)]
                    pyb = B.phalf[4 + (ch % 2)]
                    pairs = []
                    rd = [Cp]
                    for j, m in enumerate(ms):
                        pairs += [(Cp.ap[:, d, m, 0, :], xr[j].ap), (Cp.ap[:, d, m, 1, :], xi[j].ap)]
                        rd += [xr[j], xi[j]]
                    B.mm(py.ap[:, 0:LC], pairs, rd, pyb)
                    if d == 0:
                        B.A(lambda: nc.scalar.copy(out=yacc.ap[:, tk], in_=py.ap[:, 0:LC]), pyb, [yacc])
                    else:
                        B.V(lambda: nc.vector.tensor_tensor(out=yacc.ap[:, tk], in0=py.ap[:, 0:LC], in1=yacc.ap[:, tk], op=ALU.add), pyb + [yacc], [yacc])
            for ch in range(NLC):
                tk = slice(ch * LC, (ch + 1) * LC)
                ya, tq, yoc = wr[ch % 2], q1[ch % 2], xr[ch % min(2, nm)]
                B.V(lambda: nc.vector.scalar_tensor_tensor(out=ya.ap, in0=uT.ap[:, tk], scalar=dcol.ap[:, kt:kt + 1], in1=yacc.ap[:, tk],
                                                           op0=ALU.mult, op1=ALU.add), [uT, dcol, yacc], [ya])
                B.V(lambda: nc.vector.tensor_tensor(out=tq.ap, in0=ya.ap, in1=ya.ap, op=ALU.mult), [ya], [tq])
                B.V(lambda: nc.vector.tensor_scalar(out=tq.ap, in0=tq.ap, scalar1=0.044715, scalar2=1.0, op0=ALU.mult, op1=ALU.add), [tq], [tq])
                B.V(lambda: nc.vector.tensor_tensor(out=tq.ap, in0=tq.ap, in1=ya.ap, op=ALU.mult), [tq, ya], [tq])
                B.A(lambda: nc.scalar.activation(out=tq.ap, in_=tq.ap, func=AF.Sigmoid, scale=1.5957691216057308), [tq], [tq])
                B.V(lambda: nc.vector.tensor_tensor(out=yoc.ap, in0=tq.ap, in1=ya.ap, op=ALU.mult), [tq, ya], [yoc])
                P.dma("sp", S["s5T"][kt * 128:(kt + 1) * 128, tk], yoc.ap, reads=[yoc])
        P.barrier()


def attn_setup(B):
    c, nc, P, W, K = B.c, B.nc, B.P, B.W, B.K
    NH = 3 * c.HG
    B.fd = []
    with contextlib.ExitStack() as st:
        tbl = B.sb(st, "tbl", [c.NBK + 1, NH], F32)
        P.dma("sp", tbl.ap[0:c.NBK, :], W["rel_bias"], writes=[tbl])
        B.V(lambda: nc.vector.memset(tbl.ap[c.NBK:c.NBK + 1, :], NEG), [], [tbl])
        for p in range(3):
            Wp = (2 * c.PADT[p] + 1) * 128
            nrel = Wp + 127
            fd = B.scratch(f"fd{p}", [NH, nrel], F32)
            B.fd.append(fd)
            oh = B.sb(st, "oh", [c.NBK + 1, nrel], F32)
            fs = B.sb(st, "fs", [NH, nrel], F32)
            P.dma("sp", oh.ap, K[f"c_oh{p}"], writes=[oh])
            for i, c0 in enumerate(range(0, nrel, 512)):
                w_ = min(512, nrel - c0)
                bk = i % 2
                B.mm(B.psb[bk].ap[0:NH, 0:w_], [(tbl.ap, oh.ap[:, c0:c0 + w_])], [tbl, oh], B.phalf[bk])
                B.A(lambda: nc.scalar.copy(out=fs.ap[:, c0:c0 + w_], in_=B.psb[bk].ap[0:NH, 0:w_]), B.phalf[bk], [fs])
            P.dma("sp", fd, fs.ap, reads=[fs])
        P.barrier()


def phase_attn(B, l):
    c, nc, P, S, W = B.c, B.nc, B.P, B.S, B.W
    BND, NT, HG = c.BND, c.NT, c.HG
    NQ = BND // 128
    Wp = [(2 * pt + 1) * 128 for pt in c.PADT]
    off = [0, Wp[0], Wp[0] + Wp[1]]
    Wtot = sum(Wp)
    NKT = Wtot // 128
    scale = float(c.DH) ** -0.5
    with contextlib.ExitStack() as st:
        kT = [B.sb(st, "kT", [128, BND + 2 * c.PADT[p] * 128], BF16) for p in range(3)]
        Vt = [B.sb(st, "Vt", [128, NQ + 2 * c.PADT[p], 128], BF16) for p in range(3)]
        qT = [B.sb(st, "qT", [128, BND], BF16) for p in range(3)]
        MB = [B.sb(st, "MB", [128, Wp[p]], F32) for p in range(3)]
        MBr = [B.sb(st, "MBr", [128, Wp[p]], F32) for p in range(3)]
        Sp = [B.sb(st, "Sp", [128, Wtot], F32) for _ in range(2)]
        Pm = [B.sb(st, "Pm", [128, Wtot], BF16) for _ in range(2)]
        PT = [B.sb(st, "PT", [128, NKT, 128], BF16) for _ in range(2)]
        negm = B.sb(st, "negm", [128, 1], F32)
        rs = B.sb(st, "ars", [128, 128], F32)
        ast_ = [B.sb(st, "ast", [128, BND], BF16) for _ in range(2)]
        it = 0
        for j in range(HG):
            for p in range(3):
                h = p * HG + j
                nrel = Wp[p] + 127
                src = bass.AP(tensor=B.fd[p].tensor, offset=h * nrel, ap=[[1, 128], [1, Wp[p]]])
                P.dma("sp", MBr[p].ap, src, writes=[MBr[p]])
                B.V(lambda p=p: nc.vector.tensor_copy(out=MB[p].ap, in_=MBr[p].ap[:, ::-1]), [MBr[p]], [MB[p]])
            for hf in range(2):
                hs = hf * BND
                for p in range(3):
                    h = p * HG + j
                    pad = c.PADT[p] * 128
                    lo, hi = hs - pad, hs + BND + pad
                    slo, shi = max(lo, 0), min(hi, NT)
                    B.V(lambda p=p: nc.vector.memset(kT[p].ap, 0.0), [], [kT[p]])
                    B.G(lambda p=p: nc.gpsimd.memset(Vt[p].ap, 0.0), [], [Vt[p]])
                    P.dma("sp", kT[p].ap[:, slo - lo: shi - lo], S["kT"][h * 128:(h + 1) * 128, slo:shi], writes=[kT[p]])
                    P.dma("sp", Vt[p].ap[:, (slo - lo) // 128:(shi - lo) // 128, :],
                          S["v"][slo:shi, h * 128:(h + 1) * 128].rearrange("(t p) n -> p t n", p=128), writes=[Vt[p]])
                    P.dma("sp", qT[p].ap, S["qT"][h * 128:(h + 1) * 128, hs:hs + BND], writes=[qT[p]])
                asg = ast_[(j * 2 + hf) % 2]
                for i in range(NQ):
                    sp, pm, pt = Sp[i % 2], Pm[i % 2], PT[i % 2]
                    for p in range(3):
                        pad = c.PADT[p] * 128
                        for c0 in range(0, Wp[p], 512):
                            pw = min(512, Wp[p] - c0)
                            bk = it % 2
                            it += 1
                            B.mm(B.psb[bk].ap[:, 0:pw], [(qT[p].ap[:, i * 128:(i + 1) * 128], kT[p].ap[:, i * 128 + c0: i * 128 + c0 + pw])],
                                 [qT[p], kT[p]], B.phalf[bk])
                            B.V(lambda p=p, c0=c0, pw=pw, bk=bk: nc.vector.scalar_tensor_tensor(
                                out=sp.ap[:, off[p] + c0: off[p] + c0 + pw], in0=B.psb[bk].ap[:, 0:pw], scalar=scale,
                                in1=MB[p].ap[:, c0:c0 + pw], op0=ALU.mult, op1=ALU.add), B.phalf[bk] + [MB[p]], [sp])
                        base = hs - pad + i * 128
                        if base < 0:
                            n0 = min(Wp[p], -base)
                            B.V(lambda p=p, n0=n0: nc.vector.tensor_scalar(out=sp.ap[:, off[p]:off[p] + n0], in0=sp.ap[:, off[p]:off[p] + n0],
                                                                           scalar1=NEG, scalar2=None, op0=ALU.add), [sp], [sp])
                        if base + Wp[p] > NT:
                            n0 = max(0, NT - base)
                            B.V(lambda p=p, n0=n0: nc.vector.tensor_scalar(out=sp.ap[:, off[p] + n0:off[p] + Wp[p]], in0=sp.ap[:, off[p] + n0:off[p] + Wp[p]],
                                                                           scalar1=NEG, scalar2=None, op0=ALU.add), [sp], [sp])
                        if hf == 0 and base + Wp[p] > BND:
                            n0 = max(0, BND - base)
                            B.V(lambda p=p, n0=n0: nc.vector.tensor_scalar(out=sp.ap[:, off[p] + n0:off[p] + Wp[p]], in0=sp.ap[:, off[p] + n0:off[p] + Wp[p]],
                                                                           scalar1=B.nlinkt.ap[:, 0:1], scalar2=None, op0=ALU.add), [sp, B.nlinkt], [sp])
                        if hf == 1 and base < BND:
                            n0 = min(Wp[p], BND - base)
                            B.V(lambda p=p, n0=n0: nc.vector.tensor_scalar(out=sp.ap[:, off[p]:off[p] + n0], in0=sp.ap[:, off[p]:off[p] + n0],
                                                                           scalar1=B.nlinkt.ap[:, 0:1], scalar2=None, op0=ALU.add), [sp, B.nlinkt], [sp])
                    B.V(lambda: nc.vector.tensor_reduce(out=negm.ap, in_=sp.ap, axis=AX.X, op=ALU.max, negate=True), [sp], [negm])
                    B.A(lambda: nc.scalar.activation(out=pm.ap, in_=sp.ap, func=AF.Exp, bias=negm.ap[:, 0:1]), [sp, negm], [pm])
                    for r0 in range(0, NKT, 16):
                        rn = min(16, NKT - r0)

                        def tr(r0=r0, rn=rn):
                            ins = None
                            for t_ in range(rn):
                                ins = nc.tensor.transpose(B.pT.ap[:, t_ * 128:(t_ + 1) * 128], pm.ap[:, (r0 + t_) * 128:(r0 + t_ + 1) * 128], B.identb.ap)
                            return ins
                        B.T(tr, [pm, B.identb], [B.pT])
                        if (r0 // 16) % 2 == 0:
                            B.A(lambda r0=r0, rn=rn: nc.scalar.copy(out=pt.ap[:, r0:r0 + rn, :].rearrange("p t q -> p (t q)"), in_=B.pT.ap[:, 0:rn * 128]),
                                [B.pT], [pt])
                        else:
                            B.V(lambda r0=r0, rn=rn: nc.vector.tensor_copy(out=pt.ap[:, r0:r0 + rn, :].rearrange("p t q -> p (t q)"), in_=B.pT.ap[:, 0:rn * 128]),
                                [B.pT], [pt])
                    bo, bs_ = (2, 3) if i % 2 == 0 else (4, 5)
                    pv, ps_ = [], []
                    for p in range(3):
                        for kt in range(Wp[p] // 128):
                            tile_ = off[p] // 128 + kt
                            pv.append((Vt[p].ap[:, i + kt, :], pt.ap[:, tile_, :]))
                            ps_.append((B.onesb.ap, pt.ap[:, tile_, :]))
                    B.mm(B.psb[bo].ap[:, 0:128], pv, [pt] + Vt, B.phalf[bo])
                    B.mm(B.psb[bs_].ap[:, 0:128], ps_, [pt, B.onesb], B.phalf[bs_])
                    B.V(lambda bs_=bs_: nc.vector.reciprocal(out=rs.ap, in_=B.psb[bs_].ap[:, 0:128]), B.phalf[bs_], [rs])
                    B.V(lambda bo=bo: nc.vector.tensor_tensor(out=asg.ap[:, i * 128:(i + 1) * 128], in0=B.psb[bo].ap[:, 0:128], in1=rs.ap, op=ALU.mult),
                        B.phalf[bo] + [rs], [asg])
                P.dma("sp", S["attT"][j * 128:(j + 1) * 128, hs:hs + BND], asg.ap, reads=[asg])
        P.barrier()


def phase_merge(B, l, xsrc, xkey):
    c, nc, P, S, W = B.c, B.nc, B.P, B.S, B.W
    TG, D, KC = c.FTG, c.D, c.KC
    TI, K5, KA = c.INNER // 128, c.W5 // 128, c.HG * c.DH // 128
    with contextlib.ExitStack() as st:
        B.wt_setup(st, 16, 128)
        act = B.sb(st, "act", [128, max(TI, K5, KA), TG], BF16)
        mT = B.sb(st, "mT", [128, KC, TG], BF16)
        gts = [B.sb(st, "gt", [128, TG], BF16) for _ in range(2)]
        sgs = [B.sb(st, "sg", [128, TG], F32) for _ in range(2)]
        tmp = [B.sb(st, "mtmp", [128, 512], F32) for _ in range(2)]
        tmp2 = [B.sb(st, "mtmp2", [128, 512], F32) for _ in range(2)]
        xts = [B.sb(st, "mxt", [128, 128], F32) for _ in range(4)]
        cnt = {"g": 0, "t": 0, "x": 0}
        for tg in range(c.NT // TG):
            t0 = tg * TG

            def gate_tile(b, col_abs):
                gt, sg = gts[cnt["g"] % 2], sgs[cnt["g"] % 2]
                cnt["g"] += 1
                P.dma("sp", gt.ap, S["gT"][b * D + col_abs: b * D + col_abs + 128, t0:t0 + TG], writes=[gt])
                B.A(lambda: nc.scalar.activation(out=sg.ap, in_=gt.ap, func=AF.Sigmoid), [gt], [sg])
                return sg

            def load_act(name, kn):
                P.dma("sp", act.ap[:, 0:kn, :], S[name][:, t0:t0 + TG].rearrange("(t p) n -> p t n", p=128), writes=[act])
            load_act("ynT", TI)
            cur = {}

            def evac_a(ps, col_abs, nb, tok0, tw, pb):
                if tok0 == 0:
                    cur["sg"] = gate_tile(0, col_abs)
                sg = cur["sg"]
                B.V(lambda: nc.vector.tensor_tensor(out=mT.ap[:, col_abs // 128, tok0:tok0 + tw], in0=ps, in1=sg.ap[:, tok0:tok0 + tw], op=ALU.mult),
                    pb + [sg], [mT])
            B.dense_feat(act, TI, TG, W["ssm_w_out"][l], 0, D, evac_a)
            load_act("s5T", K5)
            for db in range(KC):
                wv = B.load_w(W["s5_w_glu"][l], 0, K5, db * 128, 128)
                wg = B.load_w(W["s5_w_glu"][l], 0, K5, D + db * 128, 128)
                sg = gate_tile(1, db * 128)
                for tt in range(-(-TG // 512)):
                    tw = min(512, TG - tt * 512)
                    tks = slice(tt * 512, tt * 512 + tw)
                    ba, bb_ = (0, 1) if cnt["t"] % 2 == 0 else (2, 3)
                    tp, tp2 = tmp[cnt["t"] % 2], tmp2[cnt["t"] % 2]
                    cnt["t"] += 1
                    B.mm(B.psb[ba].ap[:, 0:tw], [(wv.ap[:, k, 0:128], act.ap[:, k, tks]) for k in range(K5)], [wv, act], B.phalf[ba])
                    B.mm(B.psb[bb_].ap[:, 0:tw], [(wg.ap[:, k, 0:128], act.ap[:, k, tks]) for k in range(K5)], [wg, act], B.phalf[bb_])
                    B.A(lambda: nc.scalar.activation(out=tp.ap[:, 0:tw], in_=B.psb[bb_].ap[:, 0:tw], func=AF.Sigmoid), B.phalf[bb_], [tp])
                    B.V(lambda: nc.vector.tensor_tensor(out=tp2.ap[:, 0:tw], in0=B.psb[ba].ap[:, 0:tw], in1=tp.ap[:, 0:tw], op=ALU.mult), B.phalf[ba] + [tp], [tp2])
                    B.V(lambda: nc.vector.tensor_tensor(out=tp2.ap[:, 0:tw], in0=tp2.ap[:, 0:tw], in1=sg.ap[:, tks], op=ALU.mult), [tp2, sg], [tp2])
                    B.V(lambda: nc.vector.tensor_tensor(out=mT.ap[:, db, tks], in0=tp2.ap[:, 0:tw], in1=mT.ap[:, db, tks], op=ALU.add), [tp2, mT], [mT])
            load_act("attT", KA)

            def evac_c(ps, col_abs, nb, tok0, tw, pb):
                if tok0 == 0:
                    cur["sg"] = gate_tile(2, col_abs)
                sg = cur["sg"]
                tp = tmp[cnt["t"] % 2]
                cnt["t"] += 1
                B.V(lambda: nc.vector.tensor_tensor(out=tp.ap[:, 0:tw], in0=ps, in1=sg.ap[:, tok0:tok0 + tw], op=ALU.mult), pb + [sg], [tp])
                B.V(lambda: nc.vector.tensor_tensor(out=mT.ap[:, col_abs // 128, tok0:tok0 + tw], in0=tp.ap[:, 0:tw],
                                                    in1=mT.ap[:, col_abs // 128, tok0:tok0 + tw], op=ALU.add), [tp, mT], [mT])
            B.dense_feat(act, KA, TG, W["att_w_out"][l], 0, D, evac_c)

            def evac_o(ps, tok0, rows, col_abs, cw, pb):
                xt = xts[cnt["x"] % 4]
                cnt["x"] += 1
                P.dma("sp", xt.ap[0:rows, 0:cw], xsrc[t0 + tok0:t0 + tok0 + rows, col_abs:col_abs + cw], writes=[xt])
                B.V(lambda: nc.vector.tensor_tensor(out=xt.ap[0:rows, 0:cw], in0=ps, in1=xt.ap[0:rows, 0:cw], op=ALU.add), pb + [xt], [xt])
                P.dma("sp", S["xmid"][t0 + tok0:t0 + tok0 + rows, col_abs:col_abs + cw], xt.ap[0:rows, 0:cw], reads=[xt])
            B.dense_tok(mT, KC, TG, W["w_o"][l], 0, D, 128, evac_o)
        P.barrier()


def phase_ffn(B, l):
    c, nc, P, S, W = B.c, B.nc, B.P, B.S, B.W
    TG, D, KC, DFF = c.FTG, c.D, c.KC, c.DFF
    KF = DFF // 128
    with contextlib.ExitStack() as st:
        gain = B.sb(st, "gain2", [128, D], F32)
        B.bcast_load(gain, W["norm_ffn"][l:l + 1, :], D)
        h2T = B.sb(st, "h2T", [128, KC, TG + 2], BF16)
        gvT = B.sb(st, "gvT", [128, KF, TG], BF16)
        cwt = B.sb(st, "fcw", [128, 2 * KF, 4], F32)
        P.dma("sp", cwt.ap, W["ffn_cw"][l], writes=[cwt])
        B.wt_setup(st, 16, 128)
        ups = [B.sb(st, "ups", [128, TG + 2], F32) for _ in range(2)]
        accg = B.sb(st, "accg", [128, TG], F32)
        accv = B.sb(st, "accv", [128, TG], F32)
        xts = [B.sb(st, "fxt", [128, 128], F32) for _ in range(4)]
        cnt = {"x": 0, "b": 0}
        for tg in range(c.NT // TG):
            t0 = tg * TG
            with contextlib.ExitStack() as st1:
                B.norm_setup(st1, nbuf=1)
                B.norm_T(st1, S["xmid"], "xmid", gain, t0, TG, h2T, col0=1)
                for (tok, col) in ((t0 - 1, 0), (t0 + TG, TG + 1)):
                    if tok < 0 or tok >= c.NT:
                        B.V(lambda col=col: nc.vector.memset(h2T.ap[:, :, col:col + 1], 0.0), [], [h2T])
                    else:
                        cross = (tok == c.BND - 1 and col == 0) or (tok == c.BND and col == TG + 1)
                        B.norm_T(st1, S["xmid"], "xmid", gain, tok, 1, h2T, col0=col, scale_tile=(B.linkt if cross else None))
                P.barrier()
            for jb in range(KF):
                accs = (accg, accv)
                for which, cbase in enumerate((jb * 128, DFF + jb * 128)):
                    wb = B.load_w(W["w_up"][l], 0, KC, cbase, 128)
                    up = ups[which]
                    ti_ = which * KF + jb
                    pieces = [(1 + q0, min(512, TG - q0)) for q0 in range(0, TG, 512)] + [(0, 1), (TG + 1, 1)]
                    for (cs_, n_) in pieces:
                        bk = cnt["b"] % 6
                        cnt["b"] += 1
                        B.mm(B.psb[bk].ap[:, 0:n_], [(wb.ap[:, k, 0:128], h2T.ap[:, k, cs_:cs_ + n_]) for k in range(KC)], [wb, h2T], B.phalf[bk])
                        if cnt["b"] % 2 == 0:
                            B.A(lambda bk=bk, cs_=cs_, n_=n_: nc.scalar.copy(out=up.ap[:, cs_:cs_ + n_], in_=B.psb[bk].ap[:, 0:n_]), B.phalf[bk], [up])
                        else:
                            B.V(lambda bk=bk, cs_=cs_, n_=n_: nc.vector.tensor_copy(out=up.ap[:, cs_:cs_ + n_], in_=B.psb[bk].ap[:, 0:n_]), B.phalf[bk], [up])
                    acc = accs[which]
                    B.V(lambda: nc.vector.tensor_scalar(out=acc.ap, in0=up.ap[:, 0:TG], scalar1=cwt.ap[:, ti_, 0:1], scalar2=cwt.ap[:, ti_, 3:4],
                                                        op0=ALU.mult, op1=ALU.add), [up, cwt], [acc])
                    for k in (1, 2):
                        B.V(lambda k=k: nc.vector.scalar_tensor_tensor(out=acc.ap, in0=up.ap[:, k:k + TG], scalar=cwt.ap[:, ti_, k:k + 1], in1=acc.ap,
                                                                       op0=ALU.mult, op1=ALU.add), [up, cwt, acc], [acc])
                B.A(lambda: nc.scalar.activation(out=accg.ap, in_=accg.ap, func=AF.Silu), [accg], [accg])
                B.V(lambda: nc.vector.tensor_tensor(out=gvT.ap[:, jb, :], in0=accg.ap, in1=accv.ap, op=ALU.mult), [accg, accv], [gvT])

            def evac_d(ps, tok0, rows, col_abs, cw, pb):
                xt = xts[cnt["x"] % 4]
                cnt["x"] += 1
                P.dma("sp", xt.ap[0:rows, 0:cw], S["xmid"][t0 + tok0:t0 + tok0 + rows, col_abs:col_abs + cw], writes=[xt])
                B.V(lambda: nc.vector.tensor_tensor(out=xt.ap[0:rows, 0:cw], in0=ps, in1=xt.ap[0:rows, 0:cw], op=ALU.add), pb + [xt], [xt])
                P.dma("sp", S["xres"][t0 + tok0:t0 + tok0 + rows, col_abs:col_abs + cw], xt.ap[0:rows, 0:cw], reads=[xt])
            B.dense_tok(gvT, KF, TG, W["w_down"][l], 0, D, 128, evac_d)
        P.barrier()


def phase_final(B, y_out):
    c, nc, P, S, W = B.c, B.nc, B.P, B.S, B.W
    with contextlib.ExitStack() as st:
        gain = B.sb(st, "gainf", [128, c.D], F32)
        B.bcast_load(gain, W["final_norm"][0:1, :], c.D)
        B.norm_setup(st)
        nt = B._norm_tiles
        outs = [B.sb(st, "fo", [128, c.D], F32) for _ in range(2)]
        for ti in range(c.NT // 128):
            xt, o = nt["x"][ti % 2], outs[ti % 2]
            sq, ss, rt, rs = nt["sq"], nt["ss"], nt["rt"], nt["rs"]
            P.dma("sp", xt.ap, S["xres"][ti * 128:(ti + 1) * 128, :], writes=[xt])
            B.A(lambda: nc.scalar.activation(out=sq.ap, in_=xt.ap, func=AF.Square), [xt], [sq])
            B.V(lambda: nc.vector.reduce_sum(out=ss.ap, in_=sq.ap, axis=AX.X), [sq], [ss])
            B.A(lambda: nc.scalar.activation(out=rt.ap, in_=ss.ap, func=AF.Sqrt, scale=1.0 / c.D, bias=nt["eps"].ap[:, 0:1]), [ss, nt["eps"]], [rt])
            B.V(lambda: nc.vector.reciprocal(out=rs.ap, in_=rt.ap), [rt], [rs])
            B.V(lambda: nc.vector.scalar_tensor_tensor(out=o.ap, in0=xt.ap, scalar=rs.ap[:, 0:1], in1=gain.ap, op0=ALU.mult, op1=ALU.mult),
                [xt, rs, gain], [o])
            P.dma("sp", y_out[ti * 128:(ti + 1) * 128, :], o.ap, reads=[o])
        P.barrier()


_CACHE = {}


def kernel(**inputs):
    c = FULL
    c.TS = c.G5 * c.P5 // 128
    xp = np.asarray(inputs["x_prompt"], dtype=np.float32)
    xs = np.asarray(inputs["x_sample"], dtype=np.float32)
    assert xp.shape == (2, c.BND, c.D) and xs.shape == (1, c.NT, c.D)
    wl = layout_weights(c, inputs)
    consts = host_consts(c)
    maps = [core_inputs(c, wl, consts, xp.reshape(c.NT, c.D), 0),
            core_inputs(c, wl, consts, xs.reshape(c.NT, c.D), 1)]
    if "nc" not in _CACHE:
        _CACHE["nc"] = build(c)[0]
    res = run_bass_kernel_spmd(_CACHE["nc"], maps, core_ids=[0, 1])
    y_prompt = np.asarray(res.results[0]["y"], dtype=np.float32).reshape(2, c.BND, c.D)
    y_sample = np.asarray(res.results[1]["y"], dtype=np.float32).reshape(1, c.NT, c.D)
    return (y_prompt, y_sample)
```

```python
import math
import numpy as np
import concourse.bass as bass
import concourse.mybir as mybir
from concourse.bass_utils import run_bass_kernel_spmd

F32 = mybir.dt.float32
BF16 = mybir.dt.bfloat16
I32 = mybir.dt.int32
ALU = mybir.AluOpType
AF = mybir.ActivationFunctionType
AX = mybir.AxisListType
NEG = -1.0e30
TWO_PI = 2.0 * math.pi


class Buf:
    __slots__ = ("name", "w", "r", "excl")

    def __init__(self, name="", excl=False):
        self.name = name
        self.w = []
        self.r = []
        self.excl = excl


class Tile:
    def __init__(self, ap, buf):
        self.ap = ap
        self.b = buf

    def __getitem__(self, k):
        return self.ap[k]


class Prog:
    KS = 6
    NDQ = {"sp": 24, "pool": 12, "act": 8}

    def __init__(self, nc):
        self.nc = nc
        self.E = {"pe": nc.tensor, "act": nc.scalar, "dve": nc.vector, "pool": nc.gpsimd, "sp": nc.sync}
        self.csem = {e: [nc.alloc_semaphore(name=f"c_{e}{i}") for i in range(self.KS)]
                     for e in ("pe", "act", "dve", "pool")}
        self.cnt = {e: 0 for e in self.csem}
        self.dsem = {q: [nc.alloc_semaphore(name=f"d_{q}{i}") for i in range(n)] for q, n in self.NDQ.items()}
        self.dval = {q: [0] * self.NDQ[q] for q in self.dsem}
        self.dnext = {q: 0 for q in self.dsem}
        self.seen = {e: {} for e in self.E}
        self.ninst = 0
        self.nalloc = 0

    def buf(self, name=""):
        return Buf(name)

    def sb(self, name, shape, dt):
        self.nalloc += 1
        h = self.nc.alloc_sbuf_tensor(f"{name}_{self.nalloc}", list(shape), dt)
        return Tile(h.ap(), Buf(name))

    def ps(self, name, shape, dt):
        self.nalloc += 1
        h = self.nc.alloc_psum_tensor(f"{name}_{self.nalloc}", list(shape), dt)
        return Tile(h.ap(), Buf(name))

    def _wait_tok(self, eng, tok):
        if tok[0] == "c":
            _, e2, n = tok
            if e2 == eng and eng == "pe":
                return
            key = ("c", e2)
            if self.seen[eng].get(key, 0) >= n:
                return
            self.seen[eng][key] = n
            self.E[eng].wait_ge(self.csem[e2][(n - 1) % self.KS], (n - 1) // self.KS + 1)
        else:
            _, q, i, v = tok
            key = ("d", q, i)
            if self.seen[eng].get(key, 0) >= v:
                return
            self.seen[eng][key] = v
            self.E[eng].wait_ge(self.dsem[q][i], v)

    @staticmethod
    def _bufs(xs):
        out = []
        for x in xs:
            b = x.b if isinstance(x, Tile) else x
            if isinstance(b, (list, tuple)):
                out.extend(b)
            else:
                out.append(b)
        return out

    def _deps(self, reads, writes):
        deps = []
        for b in reads:
            deps.extend(b.w)
            if b.excl:
                deps.extend(b.r)
        for b in writes:
            deps.extend(b.w)
            deps.extend(b.r)
        return deps

    def _commit(self, tok, reads, writes):
        for b in writes:
            b.w = [tok]
            b.r = []
        for b in reads:
            if b not in writes:
                b.r.append(tok)
                if len(b.r) > 48:
                    best = {}
                    for t in b.r:
                        k = t[:2] if t[0] == "c" else t[:3]
                        if k not in best or best[k][-1] < t[-1]:
                            best[k] = t
                    b.r = list(best.values())

    def op(self, eng, fn, reads=(), writes=()):
        reads = self._bufs(reads)
        writes = self._bufs(writes)
        for tok in self._deps(reads, writes):
            self._wait_tok(eng, tok)
        ins = fn()
        self.cnt[eng] += 1
        n = self.cnt[eng]
        ins.then_inc(self.csem[eng][(n - 1) % self.KS], 1)
        self.ninst += 1
        self._commit(("c", eng, n), reads, writes)
        return ins

    def dma(self, q, out, in_, reads=(), writes=(), fn=None, **kw):
        reads = self._bufs(reads)
        writes = self._bufs(writes)
        for tok in self._deps(reads, writes):
            self._wait_tok(q, tok)
        i = self.dnext[q]
        self.dnext[q] = (i + 1) % self.NDQ[q]
        if self.dval[q][i] > 0:
            self._wait_tok(q, ("d", q, i, self.dval[q][i]))
        ins = self.E[q].dma_start(out=out, in_=in_, **kw) if fn is None else fn()
        ins.then_inc(self.dsem[q][i], 16)
        self.dval[q][i] += 16
        self.ninst += 1
        self._commit(("d", q, i, self.dval[q][i]), reads, writes)
        return ins

    def barrier(self):
        for e in self.E:
            for e2, n in self.cnt.items():
                if n > 0:
                    self._wait_tok(e, ("c", e2, n))
            for q in self.dsem:
                for i, v in enumerate(self.dval[q]):
                    if v > 0:
                        self._wait_tok(e, ("d", q, i, v))


class Cfg:
    def __init__(s, **kw):
        s.__dict__.update(kw)
        s.INNER = s.H * s.PH
        s.HPG = s.H // s.G
        s.XBC = s.INNER + 2 * s.G * s.NS
        s.G5 = s.W5 // s.C5
        s.AW = 3 * s.HG * s.DH
        s.INW = s.INNER + s.XBC + 2 * s.H + s.W5 + 3 * s.AW + 3 * s.D
        s.o_z = 0
        s.o_xbc = s.INNER
        s.o_dt = s.o_xbc + s.XBC
        s.o_u = s.o_dt + 2 * s.H
        s.o_q = s.o_u + s.W5
        s.o_k = s.o_q + s.AW
        s.o_v = s.o_k + s.AW
        s.o_g = s.o_v + s.AW
        s.KC = s.D // 128
        s.NCH = s.NT // 128
        s.BND = s.NT // 2
        s.PADT = [max(1, -(-(h * d) // 128)) for (h, d) in s.PAT]


FULL = Cfg(D=2048, NT=8192, TG=2048, H=24, G=4, PH=64, NS=128, CONV=5, W5=1024, C5=16, P5=64,
           DH=128, HG=4, PAT=((64, 1), (64, 4), (64, 16)), NBK=32, MAXD=1024, DFF=5504, FTG=1024,
           DEPTH=2, EPS=1e-6, SC=512, S5C=512, ATG=2)


def t5_bucket(rel, nb, maxd):
    half = nb // 2
    exact = half // 2
    sign = (rel > 0).astype(np.int32) * half
    n = np.abs(rel)
    large = exact + (np.log(np.maximum(n, 1) / exact) / np.log(maxd / exact) * (half - exact)).astype(np.int32)
    large = np.minimum(large, half - 1)
    return sign + np.where(n < exact, n, large)


def host_consts(c):
    k = {}
    k["c_ident"] = np.eye(128, dtype=np.float32)
    s = np.arange(128)[:, None]
    l = np.arange(128)[None, :]
    k["c_triu"] = (s <= l).astype(np.float32)
    k["c_tril"] = (s >= l).astype(np.float32)
    k["c_ones"] = np.ones((128, 128), np.float32)
    k["c_iota"] = np.broadcast_to(np.arange(c.S5C, dtype=np.float32), (128, c.S5C)).copy()
    for p, (half, dil) in enumerate(c.PAT):
        padt = c.PADT[p]
        W = (2 * padt + 1) * 128
        nrel = W + 127
        rel = np.arange(nrel) - 127 - padt * 128
        ok = (rel % dil == 0) & (np.abs(rel) <= half * dil)
        bk = t5_bucket(rel, c.NBK, c.MAXD)
        oh = np.zeros((c.NBK + 1, nrel), np.float32)
        for r in range(nrel):
            if ok[r]:
                oh[bk[r], r] = 1.0
            else:
                oh[c.NBK, r] = 1.0
        k[f"c_oh{p}"] = np.ascontiguousarray(oh[:, ::-1])
    return k


class Builder:
    def __init__(self, c, ncores):
        self.c = c
        nc = bass.Bass("TRN2", target_bir_lowering=False)
        self.nc = nc
        self.P = Prog(nc)
        self.din = {}
        self.dbufs = {}

    def inp(self, name, shape, dt=F32):
        self.din[name] = self.nc.dram_tensor(name, list(shape), dt, kind="ExternalInput").ap()
        return self.din[name]

    def scratch(self, name, shape, dt):
        return self.nc.dram_tensor(name, list(shape), dt).ap()

    def db(self, key):
        if key not in self.dbufs:
            self.dbufs[key] = Buf(str(key))
        return self.dbufs[key]

    def V(self, fn, r=(), w=()):
        return self.P.op("dve", fn, r, w)

    def A(self, fn, r=(), w=()):
        return self.P.op("act", fn, r, w)

    def G(self, fn, r=(), w=()):
        return self.P.op("pool", fn, r, w)

    def T(self, fn, r=(), w=()):
        return self.P.op("pe", fn, r, w)

    def mm(self, out, pairs, r, w, start=True, stop=True):
        nc = self.nc

        def fn():
            ins = None
            n = len(pairs)
            for i, (lt, rh) in enumerate(pairs):
                ins = nc.tensor.matmul(out, lhsT=lt, rhs=rh, start=(start and i == 0), stop=(stop and i == n - 1))
            return ins
        return self.P.op("pe", fn, r, w)

    def sb(self, st, name, shape, dt):
        self.P.nalloc += 1
        h = st.enter_context(self.nc.sbuf_tensor(f"{name}_{self.P.nalloc}", list(shape), dt))
        return Tile(h.ap() if hasattr(h, "ap") and callable(h.ap) else h[:], Buf(name))

    def bcast_load(self, tile, src_row_ap, n):
        self.P.dma("sp", tile.ap, src_row_ap.partition_broadcast(128), writes=[tile])

    def norm_T(self, st, xsrc, xkey, gain, t0, ntok, hT, col0=0, scale_tile=None):
        c, nc, P = self.c, self.nc, self.P
        if not hasattr(self, "_nt"):
            self._nt = None
        nt = self._norm_tiles
        ntile = -(-ntok // 128)
        for ti in range(ntile):
            rows = min(128, ntok - ti * 128)
            xt = nt["x"][self._nti % 2]
            xn = nt["xn"][self._nti % 2]
            self._nti += 1
            r0 = t0 + ti * 128
            P.dma("sp", xt.ap[0:rows, :], xsrc[r0:r0 + rows, :], reads=[self.db((xkey, r0 // 128))], writes=[xt])
            sq, ss, rt, rs = nt["sq"], nt["ss"], nt["rt"], nt["rs"]
            self.A(lambda: nc.scalar.activation(out=sq.ap[0:rows, :], in_=xt.ap[0:rows, :], func=AF.Square), [xt], [sq])
            self.V(lambda: nc.vector.reduce_sum(out=ss.ap[0:rows, :], in_=sq.ap[0:rows, :], axis=AX.X), [sq], [ss])
            self.A(lambda: nc.scalar.activation(out=rt.ap[0:rows, :], in_=ss.ap[0:rows, :], func=AF.Sqrt,
                                                scale=1.0 / c.D, bias=nt["eps"].ap[0:rows, :]), [ss, nt["eps"]], [rt])
            self.V(lambda: nc.vector.reciprocal(out=rs.ap[0:rows, :], in_=rt.ap[0:rows, :]), [rt], [rs])
            if scale_tile is not None:
                self.V(lambda: nc.vector.tensor_tensor(out=rs.ap[0:rows, :], in0=rs.ap[0:rows, :],
                                                       in1=scale_tile.ap[0:rows, :], op=ALU.mult), [rs, scale_tile], [rs])
            self.V(lambda: nc.vector.scalar_tensor_tensor(out=xn.ap[0:rows, :], in0=xt.ap[0:rows, :], scalar=rs.ap[0:rows, 0:1],
                                                          in1=gain.ap[0:rows, :], op0=ALU.mult, op1=ALU.mult),
                   [xt, rs, gain], [xn])
            pst = self.pT
            def tr():
                ins = None
                for kc in range(c.KC):
                    ins = nc.tensor.transpose(pst.ap[:, kc * 128: kc * 128 + rows], xn.ap[0:rows, kc * 128:(kc + 1) * 128],
                                              self.identb.ap[0:rows, 0:rows])
                return ins
            self.T(tr, [xn, self.identb], [pst])
            src = pst.ap[:, 0:c.KC * 128].rearrange("p (k t) -> p k t", t=128)[:, :, 0:rows]
            dst = hT.ap[:, :, col0 + ti * 128: col0 + ti * 128 + rows]
            self.A(lambda: nc.scalar.copy(out=dst, in_=src), [pst], [hT])

    def norm_setup(self, st, nbuf=2):
        c = self.c
        self._norm_tiles = {
            "x": [self.sb(st, "nx", [128, c.D], F32) for _ in range(nbuf)] * (2 // nbuf),
            "xn": [self.sb(st, "nxn", [128, c.D], BF16) for _ in range(nbuf)] * (2 // nbuf),
            "sq": self.sb(st, "nsq", [128, c.D], F32),
            "ss": self.sb(st, "nss", [128, 1], F32),
            "rt": self.sb(st, "nrt", [128, 1], F32),
            "rs": self.sb(st, "nrs", [128, 1], F32),
            "eps": self.sb(st, "neps", [128, 1], F32),
        }
        self._nti = 0
        e = self._norm_tiles["eps"]
        self.V(lambda: self.nc.vector.memset(e.ap, c.EPS), [], [e])

    def wt_setup(self, st, kmax, cw, nbf=3):
        self._w = {"st": [self.sb(st, "wst", [128, kmax, cw], F32) for _ in range(2)],
                   "bf": [self.sb(st, "wbf", [128, kmax, cw], BF16) for _ in range(nbf)], "i": 0, "kmax": kmax, "cw": cw, "nbf": nbf}

    def load_w(self, Wl, k0, kn, c0, cn):
        nc, P = self.nc, self.P
        w = self._w
        ws, wb = w["st"][w["i"] % 2], w["bf"][w["i"] % w["nbf"]]
        w["i"] += 1
        src = Wl[k0 * 128:(k0 + kn) * 128, c0:c0 + cn].rearrange("(k p) n -> p k n", p=128)
        P.dma("sp", ws.ap[:, 0:kn, 0:cn], src, writes=[ws])
        self.G(lambda: nc.gpsimd.tensor_copy(out=wb.ap[:, 0:kn, 0:cn], in_=ws.ap[:, 0:kn, 0:cn]), [ws], [wb])
        return wb

    def dense_feat(self, actT, KC, ntok, Wl, c0, ncols, evac, tokoff=0):
        nc = self.nc
        cw = self._w["cw"]
        ntt = -(-ntok // 512)
        chunks = [(cc, min(cw, c0 + ncols - cc)) for cc in range(c0, c0 + ncols, cw)]
        nxt = self.load_w(Wl, 0, KC, chunks[0][0], chunks[0][1])
        for ci, (cc, cn) in enumerate(chunks):
            wb = nxt
            if ci + 1 < len(chunks):
                nxt = self.load_w(Wl, 0, KC, chunks[ci + 1][0], chunks[ci + 1][1])
            for b0 in range(0, cn, 128):
                nb = min(128, cn - b0)
                for tt in range(ntt):
                    tw = min(512, ntok - tt * 512)
                    bank = self._bank % 8
                    self._bank += 1
                    pt = self.psb[bank]
                    pairs = [(wb.ap[:, kc, b0:b0 + nb], actT.ap[:, kc, tokoff + tt * 512: tokoff + tt * 512 + tw]) for kc in range(KC)]
                    self.mm(pt.ap[0:nb, 0:tw], pairs, [wb, actT], self.pbufs(bank))
                    evac(pt.ap[0:nb, 0:tw], cc + b0, nb, tt * 512, tw, self.pbufs(bank))

    def dense_tok(self, actT, KC, ntok, Wl, c0, ncols, CW, evac, tokoff=0):
        ntile = -(-ntok // 128)
        kmax = self._w["kmax"]
        assert CW <= self._w["cw"]
        per_bank = 512 // CW
        assert ntile <= 4 * per_bank, (ntile, CW)
        kgs = list(range(0, KC, kmax))
        groups = [(cc, min(CW, c0 + ncols - cc)) for cc in range(c0, c0 + ncols, CW)]
        prefetch = self._w["nbf"] >= 2 * len(kgs)
        assert self._w["nbf"] >= len(kgs)
        ldg = lambda g: [(kg, min(kmax, KC - kg), self.load_w(Wl, kg, min(kmax, KC - kg), g[0], g[1])) for kg in kgs]
        nxt = ldg(groups[0])
        for gi, (cc, cn) in enumerate(groups):
            base = (self._bank % 2) * 4
            self._bank += 1
            wbs = nxt if (prefetch or gi == 0) else ldg(groups[gi])
            if prefetch and gi + 1 < len(groups):
                nxt = ldg(groups[gi + 1])
            for ti in range(ntile):
                rows = min(128, ntok - ti * 128)
                bank = base + ti // per_bank
                off = (ti % per_bank) * CW
                pt = self.psb[bank]
                pairs = []
                for (kg, kn, wb) in wbs:
                    pairs += [(actT.ap[:, kg + k, tokoff + ti * 128: tokoff + ti * 128 + rows], wb.ap[:, k, 0:cn]) for k in range(kn)]
                hb = self.phalf[bank]
                self.mm(pt.ap[0:rows, off:off + cn], pairs, [w_[2] for w_ in wbs] + [actT], hb)
                evac(pt.ap[0:rows, off:off + cn], ti * 128, rows, cc, cn, hb)

    def pbufs(self, bank):
        return self.phalf[bank]


import contextlib


def weight_specs(c):
    L = c.DEPTH
    return [
        ("norm_mix", [L, c.D]), ("w_in", [L, c.D, c.INW]), ("ssm_cw", [L, c.XBC, 8]),
        ("ssm_a_log", [L, 2 * c.H]), ("ssm_dt_bias", [L, 2 * c.H]), ("ssm_dcol", [L, 128, c.INNER // 128]),
        ("ssm_ng", [L, 128, c.INNER // 128]), ("ssm_w_out", [L, c.INNER, c.D]),
        ("s5_par", [L, 2, 128, c.TS, 3]), ("s5_b", [L, 128, c.TS, 2, c.C5]), ("s5_c", [L, 2, 128, c.TS, 2, c.C5]),
        ("s5_dcol", [L, 128, c.W5 // 128]), ("s5_w_glu", [L, c.W5, 2 * c.D]), ("att_w_out", [L, c.HG * c.DH, c.D]),
        ("w_o", [L, c.D, c.D]), ("norm_ffn", [L, c.D]), ("w_up", [L, c.D, 2 * c.DFF]),
        ("ffn_cw", [L, 128, 2 * c.DFF // 128, 4]), ("w_down", [L, c.DFF, c.D]), ("final_norm", [1, c.D]),
        ("rel_bias", [c.NBK, 3 * c.HG]),
    ]


def build(c, stop_after=None, dbg=()):
    c.TS = c.G5 * c.P5 // 128
    B = Builder(c, 1)
    nc, P = B.nc, B.P
    x_in = B.inp("x", [c.NT, c.D])
    y_out = nc.dram_tensor("y", [c.NT, c.D], F32, kind="ExternalOutput").ap()
    link = B.inp("link", [128, 1])
    nlink = B.inp("nlink", [128, 1])
    W = {n: B.inp(n, s) for n, s in weight_specs(c)}
    K = {n: B.inp(n, list(v.shape)) for n, v in host_consts(c).items()}
    S = {
        "xres": B.scratch("xres", [c.NT, c.D], F32),
        "xmid": B.scratch("xmid", [c.NT, c.D], F32),
        "zT": B.scratch("zT", [c.INNER, c.NT], BF16),
        "xbcT": B.scratch("xbcT", [c.XBC, c.NT], BF16),
        "dt": B.scratch("dt_tok", [c.NT, 2 * c.H], F32),
        "uT": B.scratch("uT", [c.W5, c.NT], BF16),
        "qT": B.scratch("qT", [c.AW, c.NT], BF16),
        "kT": B.scratch("kT", [c.AW, c.NT], BF16),
        "v": B.scratch("v_tok", [c.NT, c.AW], BF16),
        "gT": B.scratch("gT", [3 * c.D, c.NT], BF16),
        "ynT": B.scratch("ynT", [c.INNER, c.NT], BF16),
        "s5T": B.scratch("s5T", [c.W5, c.NT], BF16),
        "attT": B.scratch("attT", [c.HG * c.DH, c.NT], BF16),
        "hin": B.scratch("hin", [2, c.NCH, 128, c.INNER], BF16),
    }
    B.S, B.W, B.K = S, W, K
    B.stop_after = stop_after

    psall = nc.alloc_psum_tensor("psall", [128, 8, 512], F32).ap()
    B.phalf = [[b_, b_] for b_ in [Buf(f"ps{i}", excl=True) for i in range(8)]]
    B.psb = [Tile(psall[:, i, :], B.phalf[i]) for i in range(8)]
    B.pT = Tile(psall[:, 6:8, :].bitcast(BF16).rearrange("p a b -> p (a b)"), B.phalf[6] + B.phalf[7])
    B._bank = 0
    gst = contextlib.ExitStack()
    identf = B.sb(gst, "identf", [128, 128], F32)
    B.identb = B.sb(gst, "identb", [128, 128], BF16)
    onesf = B.sb(gst, "onesf", [128, 128], F32)
    onesb = B.sb(gst, "onesb", [128, 128], BF16)
    triu = B.sb(gst, "triu", [128, 128], F32)
    tril = B.sb(gst, "tril", [128, 128], F32)
    linkt = B.sb(gst, "linkt", [128, 1], F32)
    nlinkt = B.sb(gst, "nlinkt", [128, 1], F32)
    B.identf, B.onesf, B.onesb, B.triu, B.tril, B.linkt, B.nlinkt = identf, onesf, onesb, triu, tril, linkt, nlinkt
    P.dma("sp", identf.ap, K["c_ident"], writes=[identf])
    P.dma("sp", onesf.ap, K["c_ones"], writes=[onesf])
    P.dma("sp", triu.ap, K["c_triu"], writes=[triu])
    P.dma("sp", tril.ap, K["c_tril"], writes=[tril])
    P.dma("sp", linkt.ap, link, writes=[linkt])
    P.dma("sp", nlinkt.ap, nlink, writes=[nlinkt])
    B.V(lambda: nc.vector.tensor_copy(out=B.identb.ap, in_=identf.ap), [identf], [B.identb])
    B.V(lambda: nc.vector.tensor_copy(out=onesb.ap, in_=onesf.ap), [onesf], [onesb])

    class PTile(Tile):
        pass
    B.pT_bufs = B.phalf[6] + B.phalf[7]

    attn_setup(B)
    for l in range(c.DEPTH):
        xsrc, xkey = (x_in, "xin") if l == 0 else (S["xres"], "xres")
        phase_inproj(B, l, xsrc, xkey)
        if stop_after == ("inproj", l):
            break
        phase_ssd(B, l)
        if stop_after in (("ssd", l), ("ssd0", l), ("ssdA", l)):
            break
        phase_s5(B, l)
        if stop_after == ("s5", l):
            break
        phase_attn(B, l)
        if stop_after == ("attn", l):
            break
        phase_merge(B, l, xsrc, xkey)
        if stop_after == ("merge", l):
            break
        phase_ffn(B, l)
        if stop_after == ("ffn", l):
            break
    else:
        phase_final(B, y_out)
    for name in dbg:
        src = S[name]
        o = nc.dram_tensor("dbg_" + name, list(src.shape), src.dtype, kind="ExternalOutput").ap()
        P.dma("sp", o, src)
    P.barrier()
    gst.close()
    return nc, B


def phase_inproj(B, l, xsrc, xkey):
    c, nc, P, S, W = B.c, B.nc, B.P, B.S, B.W
    with contextlib.ExitStack() as st:
        gain = B.sb(st, "gain", [128, c.D], F32)
        B.bcast_load(gain, W["norm_mix"][l:l + 1, :], c.D)
        B.norm_setup(st)
        hT = B.sb(st, "hT", [128, c.KC, c.TG], BF16)
        B.wt_setup(st, c.KC, 256)
        stg = [B.sb(st, "stg", [128, c.TG], BF16) for _ in range(3)]
        ntile = c.TG // 128
        vst = [B.sb(st, "vst", [128, ntile, 128], BF16) for _ in range(2)]
        dst_ = B.sb(st, "dtst", [128, ntile, 2 * c.H], F32)
        segs = [("zT", c.o_z, c.INNER), ("xbcT", c.o_xbc, c.XBC), ("uT", c.o_u, c.W5), ("qT", c.o_q, c.AW),
                ("kT", c.o_k, c.AW), ("gT", c.o_g, 3 * c.D)]
        cnt = {"blk": 0, "ev": 0, "v": 0}
        ntt = -(-c.TG // 512)
        for tg in range(c.NT // c.TG):
            t0 = tg * c.TG
            B.norm_T(st, xsrc, xkey, gain, t0, c.TG, hT)
            for (dname, c0, n) in segs:
                def evac(ps, col_abs, nb, tok0, tw, pb, dname=dname, c0=c0):
                    sg = stg[cnt["blk"] % 3]
                    cnt["ev"] += 1
                    if cnt["ev"] % 2 == 0:
                        B.A(lambda: nc.scalar.copy(out=sg.ap[0:nb, tok0:tok0 + tw], in_=ps), pb, [sg])
                    else:
                        B.V(lambda: nc.vector.tensor_copy(out=sg.ap[0:nb, tok0:tok0 + tw], in_=ps), pb, [sg])
                    if tok0 + tw >= c.TG:
                        P.dma("sp", S[dname][col_abs - c0: col_abs - c0 + nb, t0:t0 + c.TG], sg.ap[0:nb, :], reads=[sg])
                        cnt["blk"] += 1
                B.dense_feat(hT, c.KC, c.TG, W["w_in"][l], c0, n, evac)

            def evac_dt(ps, tok0, rows, col_abs, cw, pb):
                ti = tok0 // 128
                B.V(lambda: nc.vector.tensor_copy(out=dst_.ap[0:rows, ti, :], in_=ps), pb, [dst_])
                if ti == ntile - 1:
                    P.dma("sp", S["dt"][t0:t0 + c.TG, :].rearrange("(t p) n -> p t n", p=128), dst_.ap, reads=[dst_])
            B.dense_tok(hT, c.KC, c.TG, W["w_in"][l], c.o_dt, 2 * c.H, 2 * c.H, evac_dt)

            def evac_v(ps, tok0, rows, col_abs, cw, pb):
                ti = tok0 // 128
                vs = vst[cnt["v"] % 2]
                B.A(lambda: nc.scalar.copy(out=vs.ap[0:rows, ti, 0:cw], in_=ps), pb, [vs])
                if ti == ntile - 1:
                    P.dma("sp", S["v"][t0:t0 + c.TG, col_abs - c.o_v: col_abs - c.o_v + cw].rearrange("(t p) n -> p t n", p=128),
                          vs.ap[:, :, 0:cw], reads=[vs])
                    cnt["v"] += 1
            B.dense_tok(hT, c.KC, c.TG, W["w_in"][l], c.o_v, c.AW, 128, evac_v)
        P.barrier()


def layout_weights(c, I):
    L = c.DEPTH
    TS = c.G5 * c.P5 // 128
    f = lambda a: np.ascontiguousarray(np.asarray(a, dtype=np.float32))
    o = {}
    o["norm_mix"] = f(I["norm_mix"])
    o["w_in"] = f(I["w_in"])
    cw = np.zeros((L, c.XBC, 8), np.float32)
    cw[:, :, 0:c.CONV] = np.asarray(I["ssm_conv_w"]).transpose(0, 2, 1)
    cw[:, :, 5] = np.asarray(I["ssm_conv_b"])
    o["ssm_cw"] = cw
    o["ssm_a_log"] = f(np.asarray(I["ssm_a_log"]).reshape(L, 2 * c.H))
    o["ssm_dt_bias"] = f(np.asarray(I["ssm_dt_bias"]).reshape(L, 2 * c.H))
    dcol = np.repeat(np.asarray(I["ssm_d"]), c.PH, axis=1)
    o["ssm_dcol"] = f(dcol.reshape(L, c.INNER // 128, 128).transpose(0, 2, 1))
    o["ssm_ng"] = f(np.asarray(I["ssm_norm"]).reshape(L, c.INNER // 128, 128).transpose(0, 2, 1))
    o["ssm_w_out"] = f(I["ssm_w_out"])
    st = lambda a: np.asarray(a).reshape(L, 2, TS, 128).transpose(0, 1, 3, 2)
    ls = np.broadcast_to(np.asarray(I["s5_log_step"])[..., None], (L, 2, c.G5, c.P5))
    o["s5_par"] = f(np.stack([st(I["s5_a_re"]), st(I["s5_a_im"]), st(ls)], axis=-1))
    sb_ = lambda b: np.asarray(b).reshape(L, TS, 128, c.C5).transpose(0, 2, 1, 3)
    o["s5_b"] = f(np.stack([sb_(I["s5_b_re"]), sb_(I["s5_b_im"])], axis=3))
    sc_ = lambda cc: np.asarray(cc).transpose(0, 1, 2, 4, 3).reshape(L, 2, TS, 128, c.C5).transpose(0, 1, 3, 2, 4)
    o["s5_c"] = f(np.stack([sc_(I["s5_c_re"]), sc_(I["s5_c_im"])], axis=4))
    o["s5_dcol"] = f(np.asarray(I["s5_d"]).reshape(L, c.W5 // 128, 128).transpose(0, 2, 1))
    o["s5_w_glu"] = f(I["s5_w_glu"])
    o["att_w_out"] = f(I["att_w_out"])
    o["w_o"] = f(I["w_o"])
    o["norm_ffn"] = f(I["norm_ffn"])
    o["w_up"] = f(I["w_up"])
    fc = np.concatenate([np.asarray(I["ffn_conv_w"]).transpose(0, 2, 1), np.asarray(I["ffn_conv_b"])[:, :, None]], axis=2)
    o["ffn_cw"] = f(fc.reshape(L, 2 * c.DFF // 128, 128, 4).transpose(0, 2, 1, 3))
    o["w_down"] = f(I["w_down"])
    o["final_norm"] = f(np.asarray(I["final_norm"]).reshape(1, c.D))
    o["rel_bias"] = f(I["rel_bias"])
    return o


def core_inputs(c, wl, consts, x_stream, link):
    m = dict(wl)
    m.update(consts)
    m["x"] = np.ascontiguousarray(x_stream, dtype=np.float32)
    m["link"] = np.full((128, 1), float(link), np.float32)
    m["nlink"] = np.full((128, 1), 0.0 if link else NEG, np.float32)
    return m


def _bc(ap, shape):
    return ap.broadcast_to(list(shape))


def phase_ssd(B, l):
    c, nc, P, S, W = B.c, B.nc, B.P, B.S, B.W
    CT = c.XBC // 128
    TI = c.INNER // 128
    H, PH, G, HPG, NS = c.H, c.PH, c.G, c.HPG, c.NS
    BND = c.BND
    SC = c.SC
    CPS = SC // 128
    NSC = c.NT // SC
    with contextlib.ExitStack() as st:
        cw = B.sb(st, "cw", [128, CT, 8], F32)
        P.dma("sp", cw.ap, W["ssm_cw"][l].rearrange("(t p) k -> p t k", p=128), writes=[cw])
        xps = [B.sb(st, "xp", [128, 2, BND + 4], BF16) for _ in range(2)]
        accs = [B.sb(st, "acc", [128, 2, BND], F32) for _ in range(2)]
        xos = [B.sb(st, "xo", [128, 2, BND], BF16) for _ in range(2)]
        for xp in xps:
            B.V(lambda: nc.vector.memset(xp.ap, 0.0), [], [xp])
        for ct in range(CT):
            xp, acc, xo = xps[ct % 2], accs[ct % 2], xos[ct % 2]
            rows = S["xbcT"][ct * 128:(ct + 1) * 128, :]
            P.dma("sp", xp.ap[:, :, 2:2 + BND], rows.rearrange("p (h t) -> p h t", h=2), writes=[xp])
            B.V(lambda: nc.vector.tensor_scalar(out=xp.ap[:, 0, BND + 2:BND + 4], in0=xp.ap[:, 1, 2:4], scalar1=B.linkt.ap[:, 0:1],
                                                scalar2=None, op0=ALU.mult), [xp, B.linkt], [xp])
            B.V(lambda: nc.vector.tensor_scalar(out=xp.ap[:, 1, 0:2], in0=xp.ap[:, 0, BND:BND + 2], scalar1=B.linkt.ap[:, 0:1],
                                                scalar2=None, op0=ALU.mult), [xp, B.linkt], [xp])
            B.V(lambda: nc.vector.tensor_scalar(out=acc.ap, in0=xp.ap[:, :, 0:BND], scalar1=cw.ap[:, ct, 0:1], scalar2=cw.ap[:, ct, 5:6],
                                                op0=ALU.mult, op1=ALU.add), [xp, cw], [acc])
            for k in range(1, c.CONV):
                B.V(lambda k=k: nc.vector.scalar_tensor_tensor(out=acc.ap, in0=xp.ap[:, :, k:k + BND], scalar=cw.ap[:, ct, k:k + 1],
                                                               in1=acc.ap, op0=ALU.mult, op1=ALU.add), [xp, cw, acc], [acc])
            B.A(lambda: nc.scalar.activation(out=xo.ap, in_=acc.ap, func=AF.Silu), [acc], [xo])
            P.dma("sp", rows.rearrange("p (h t) -> p h t", h=2), xo.ap, reads=[xo])
        P.barrier()

    if B.stop_after == ("ssd0", l):
        return

    def dt_prep(st):
        t = {}
        t["bias"] = B.sb(st, "dtb", [128, 2 * H], F32)
        t["A"] = B.sb(st, "Abc", [128, 2 * H], F32)
        B.bcast_load(t["bias"], W["ssm_dt_bias"][l:l + 1, :], 2 * H)
        B.bcast_load(t["A"], W["ssm_a_log"][l:l + 1, :], 2 * H)
        A_ = t["A"]
        B.A(lambda: nc.scalar.activation(out=A_.ap, in_=A_.ap, func=AF.Exp), [A_], [A_])
        B.V(lambda: nc.vector.tensor_scalar(out=A_.ap, in0=A_.ap, scalar1=-1.0, scalar2=None, op0=ALU.mult), [A_], [A_])
        for n in ("raw", "t", "a", "e", "dtv", "av"):
            t[n] = B.sb(st, "dt" + n, [128, CPS, 2 * H], F32)
        return t

    def dt_compute(t, sc):
        raw, tt, a, e, dtv, av = t["raw"], t["t"], t["a"], t["e"], t["dtv"], t["av"]
        P.dma("sp", raw.ap, S["dt"][sc * SC:(sc + 1) * SC, :].rearrange("(k p) n -> p k n", p=128), writes=[raw])
        bb = _bc(t["bias"].ap.unsqueeze(1), [128, CPS, 2 * H])
        B.V(lambda: nc.vector.tensor_tensor(out=tt.ap, in0=raw.ap, in1=bb, op=ALU.add), [raw, t["bias"]], [tt])
        B.A(lambda: nc.scalar.activation(out=a.ap, in_=tt.ap, func=AF.Abs), [tt], [a])
        B.A(lambda: nc.scalar.activation(out=e.ap, in_=a.ap, func=AF.Exp, scale=-1.0), [a], [e])
        B.A(lambda: nc.scalar.activation(out=e.ap, in_=e.ap, func=AF.Ln, bias=1.0), [e], [e])
        B.V(lambda: nc.vector.tensor_scalar(out=a.ap, in0=tt.ap, scalar1=0.0, scalar2=None, op0=ALU.max), [tt], [a])
        B.V(lambda: nc.vector.tensor_tensor(out=dtv.ap, in0=a.ap, in1=e.ap, op=ALU.add), [a, e], [dtv])
        ab = _bc(t["A"].ap.unsqueeze(1), [128, CPS, 2 * H])
        B.V(lambda: nc.vector.tensor_tensor(out=av.ap, in0=dtv.ap, in1=ab, op=ALU.mult), [dtv, t["A"]], [av])

    def load_xc(xc, sc):
        P.dma("sp", xc.ap, S["xbcT"][:, sc * SC:(sc + 1) * SC].rearrange("(t p) n -> p t n", p=128), writes=[xc])

    for d in range(2):
        with contextlib.ExitStack() as st:
            t = dt_prep(st)
            xcs = [B.sb(st, "xc", [128, CT, SC], BF16) for _ in range(2)]
            xsB = B.sb(st, "xsB", [128, c.INNER + G * NS], BF16)
            acs = B.sb(st, "acs", [128, H], F32)
            dte = B.sb(st, "dte", [128, H], F32)
            coef = B.sb(st, "coef", [128, H], F32)
            dec = B.sb(st, "dec", [128, H], F32)
            xdd = B.sb(st, "xdd", [128, c.INNER], BF16)
            Hst = B.sb(st, "Hst", [128, c.INNER], F32)
            hstg = [B.sb(st, "hstg", [128, c.INNER], BF16) for _ in range(2)]
            B.V(lambda: nc.vector.memset(Hst.ap, 0.0), [], [Hst])
            tri = B.triu if d == 0 else B.tril
            scs = list(range(NSC)) if d == 0 else list(range(NSC - 1, -1, -1))
            bnd_chunk = c.NCH // 2 if d == 0 else c.NCH // 2 - 1
            psS = [B.psb[0], B.psb[1], B.psb[2]]
            pm = B.psb[3]
            for si, sc in enumerate(scs):
                xc = xcs[si % 2]
                load_xc(xc, sc)
                dt_compute(t, sc)
                cks = list(range(CPS)) if d == 0 else list(range(CPS - 1, -1, -1))
                for ck in cks:
                    gck = sc * CPS + ck
                    tk = slice(ck * 128, (ck + 1) * 128)

                    def tr():
                        ins = None
                        for i in range(TI):
                            ins = nc.tensor.transpose(B.pT.ap[:, i * 128:(i + 1) * 128], xc.ap[:, i, tk], B.identb.ap)
                        for g in range(G):
                            ins = nc.tensor.transpose(B.pT.ap[:, (TI + g) * 128:(TI + g + 1) * 128], xc.ap[:, TI + g, tk], B.identb.ap)
                        return ins
                    B.T(tr, [xc, B.identb], [B.pT])
                    B.A(lambda: nc.scalar.copy(out=xsB.ap, in_=B.pT.ap[:, 0:c.INNER + G * NS]), [B.pT], [xsB])
                    a_d = t["av"].ap[:, ck, d * H:(d + 1) * H]
                    B.mm(pm.ap[:, 0:H], [(tri.ap, a_d)], [tri, t["av"]], [B.phalf[3][0]])
                    B.mm(pm.ap[:, 256:256 + H], [(B.onesf.ap, a_d)], [B.onesf, t["av"]], [B.phalf[3][1]])
                    B.A(lambda: nc.scalar.copy(out=acs.ap, in_=pm.ap[:, 0:H]), [B.phalf[3][0]], [acs])
                    B.V(lambda: nc.vector.tensor_tensor(out=dte.ap, in0=pm.ap[:, 256:256 + H], in1=acs.ap, op=ALU.subtract),
                        [B.phalf[3][1], acs], [dte])
                    B.A(lambda: nc.scalar.activation(out=dte.ap, in_=dte.ap, func=AF.Exp), [dte], [dte])
                    B.A(lambda: nc.scalar.activation(out=dec.ap, in_=pm.ap[:, 256:256 + H], func=AF.Exp), [B.phalf[3][1]], [dec])
                    B.V(lambda: nc.vector.tensor_tensor(out=coef.ap, in0=t["dtv"].ap[:, ck, d * H:(d + 1) * H], in1=dte.ap, op=ALU.mult),
                        [t["dtv"], dte], [coef])
                    B.V(lambda: nc.vector.tensor_tensor(out=xdd.ap.rearrange("p (h q) -> p h q", q=PH),
                                                        in0=xsB.ap[:, 0:c.INNER].rearrange("p (h q) -> p h q", q=PH),
                                                        in1=_bc(coef.ap.unsqueeze(2), [128, H, PH]), op=ALU.mult), [xsB, coef], [xdd])
                    for g in range(G):
                        c0 = g * HPG * PH
                        c1 = c0 + HPG * PH
                        p0 = c0
                        while p0 < c1:
                            p1 = min(c1, (p0 // 512 + 1) * 512)
                            bk = p0 // 512
                            B.mm(psS[bk].ap[:, p0 - bk * 512:p1 - bk * 512],
                                 [(xsB.ap[:, c.INNER + g * NS: c.INNER + (g + 1) * NS], xdd.ap[:, p0:p1])], [xsB, xdd], B.phalf[bk])
                            p0 = p1
                    if gck == bnd_chunk:
                        B.V(lambda: nc.vector.tensor_scalar(out=Hst.ap, in0=Hst.ap, scalar1=B.linkt.ap[:, 0:1], scalar2=None, op0=ALU.mult),
                            [Hst, B.linkt], [Hst])
                    hs = hstg[gck % 2]
                    B.A(lambda: nc.scalar.copy(out=hs.ap, in_=Hst.ap), [Hst], [hs])
                    P.dma("sp", S["hin"][d, gck], hs.ap, reads=[hs])
                    B.V(lambda: nc.vector.tensor_tensor(out=Hst.ap.rearrange("p (h q) -> p h q", q=PH),
                                                        in0=Hst.ap.rearrange("p (h q) -> p h q", q=PH),
                                                        in1=_bc(dec.ap.unsqueeze(2), [128, H, PH]), op=ALU.mult), [Hst, dec], [Hst])
                    nb = -(-c.INNER // 512)
                    for bk in range(nb):
                        w_ = min(512, c.INNER - bk * 512)
                        B.V(lambda bk=bk, w_=w_: nc.vector.tensor_tensor(out=Hst.ap[:, bk * 512:bk * 512 + w_], in0=psS[bk].ap[:, 0:w_],
                                                                         in1=Hst.ap[:, bk * 512:bk * 512 + w_], op=ALU.add),
                            B.phalf[bk] + [Hst], [Hst])
            P.barrier()

    if B.stop_after == ("ssdA", l):
        return
    HB = 3 if HPG % 3 == 0 else (2 if HPG % 2 == 0 else 1)
    with contextlib.ExitStack() as st:
        t = dt_prep(st)
        xcs = [B.sb(st, "xc", [128, CT, SC], BF16) for _ in range(2)]
        zts = [B.sb(st, "zt", [128, TI, SC], BF16) for _ in range(2)]
        yns = [B.sb(st, "ynst", [128, TI, SC], BF16) for _ in range(2)]
        dcol = B.sb(st, "dcol", [128, TI], F32)
        ng = B.sb(st, "ng", [128, TI], F32)
        P.dma("sp", dcol.ap, W["ssm_dcol"][l], writes=[dcol])
        P.dma("sp", ng.ap, W["ssm_ng"][l], writes=[ng])
        mneg = [B.sb(st, "mneg", [128, 128], F32) for _ in range(2)]
        for d, tri in enumerate((B.triu, B.tril)):
            B.V(lambda d=d, tri=tri: nc.vector.tensor_scalar(out=mneg[d].ap, in0=tri.ap, scalar1=-1.0, scalar2=-NEG, op0=ALU.add, op1=ALU.mult),
                [tri], [mneg[d]])
        epst = B.sb(st, "epst", [128, 1], F32)
        B.V(lambda: nc.vector.memset(epst.ap, c.EPS), [], [epst])
        xs_tok = B.sb(st, "xstok", [128, c.INNER], BF16)
        xd = [B.sb(st, "xd", [128, c.INNER], BF16) for _ in range(2)]
        acs2 = B.sb(st, "acs2", [128, 2 * H], F32)
        cbt = B.sb(st, "cbt", [128, G, 128], F32)
        hin = [[B.sb(st, "hin", [128, c.INNER], BF16) for _ in range(2)] for _ in range(2)]
        MT = [B.sb(st, "MT", [128, H, 128], BF16) for _ in range(2)]
        CsT = [B.sb(st, "CsT", [128, H, 128], BF16) for _ in range(2)]
        amask = [B.sb(st, "amask", [128, HB, 128], F32) for _ in range(2)]
        expA = [B.sb(st, "expA", [128, HB, 128], F32) for _ in range(2)]
        T1 = [B.sb(st, "T1", [128, HB, 128], F32) for _ in range(2)]
        yv = B.sb(st, "yv", [128, TI, 128], F32)
        sz = B.sb(st, "sz", [128, TI, 128], F32)
        sq = B.sb(st, "sq", [128, TI, 128], BF16)
        rt = B.sb(st, "rt", [128, 128], F32)
        psy = [B.psb[1], B.psb[2], B.psb[3]]
        pbc = [B.psb[0], B.psb[4]]
        pcb = B.psb[5]
        pmisc = B.psb[7]
        pmb = [B.phalf[7][1]]
        it = 0
        for sc in range(NSC):
            xc, zt, ynst = xcs[sc % 2], zts[sc % 2], yns[sc % 2]
            load_xc(xc, sc)
            P.dma("sp", zt.ap, S["zT"][:, sc * SC:(sc + 1) * SC].rearrange("(t p) n -> p t n", p=128), writes=[zt])
            dt_compute(t, sc)
            for ck in range(CPS):
                gck = sc * CPS + ck
                tk = slice(ck * 128, (ck + 1) * 128)

                def tr():
                    ins = None
                    for i in range(TI):
                        ins = nc.tensor.transpose(B.pT.ap[:, i * 128:(i + 1) * 128], xc.ap[:, i, tk], B.identb.ap)
                    return ins
                B.T(tr, [xc, B.identb], [B.phalf[6][0], B.phalf[6][1], B.phalf[7][0]])
                B.A(lambda: nc.scalar.copy(out=xs_tok.ap, in_=B.pT.ap[:, 0:c.INNER]), [B.phalf[6][0], B.phalf[6][1], B.phalf[7][0]], [xs_tok])
                for d in range(2):
                    B.V(lambda d=d: nc.vector.tensor_tensor(out=xd[d].ap.rearrange("p (h q) -> p h q", q=PH),
                                                            in0=xs_tok.ap.rearrange("p (h q) -> p h q", q=PH),
                                                            in1=_bc(t["dtv"].ap[:, ck, d * H:(d + 1) * H].unsqueeze(2), [128, H, PH]), op=ALU.mult),
                        [xs_tok, t["dtv"]], [xd[d]])
                    P.dma("sp", hin[d][gck % 2].ap, S["hin"][d, gck], writes=[hin[d][gck % 2]])
                B.mm(pmisc.ap[:, 256:256 + H], [(B.triu.ap, t["av"].ap[:, ck, 0:H])], [B.triu, t["av"]], pmb)
                B.mm(pmisc.ap[:, 256 + H:256 + 2 * H], [(B.tril.ap, t["av"].ap[:, ck, H:2 * H])], [B.tril, t["av"]], pmb)
                B.A(lambda: nc.scalar.copy(out=acs2.ap, in_=pmisc.ap[:, 256:256 + 2 * H]), pmb, [acs2])
                for g in range(G):
                    B.mm(pcb.ap[:, g * 128:(g + 1) * 128], [(xc.ap[:, TI + g, tk], xc.ap[:, TI + G + g, tk])], [xc], B.phalf[5])
                B.A(lambda: nc.scalar.copy(out=cbt.ap.rearrange("p g l -> p (g l)"), in_=pcb.ap[:, 0:G * 128]), B.phalf[5], [cbt])
                import os as _os
                for d in range(2):
                    if _os.environ.get("SSD_SKIP") == "3":
                        continue
                    tri = B.triu if d == 0 else B.tril
                    for hb in range(H // HB):
                        h0 = hb * HB
                        g = h0 // HPG
                        i2 = it % 2
                        it += 1
                        am, ea, t1, pb = amask[i2], expA[i2], T1[i2], pbc[i2]
                        pbb = B.phalf[0] if i2 == 0 else B.phalf[4]
                        a_sl = t["av"].ap[:, ck, d * H + h0: d * H + h0 + HB]
                        B.V(lambda: nc.vector.tensor_tensor(out=am.ap, in0=_bc(tri.ap.unsqueeze(1), [128, HB, 128]),
                                                            in1=_bc(a_sl.unsqueeze(2), [128, HB, 128]), op=ALU.mult), [tri, t["av"]], [am])
                        _lv = _os.environ.get("SSD_SKIP")
                        if _lv == "5":
                            continue
                        B.mm(pb.ap[:, 0:HB * 128], [(B.onesf.ap, am.ap.rearrange("p h l -> p (h l)"))], [B.onesf, am], pbb)
                        if _lv == "6":
                            continue
                        pv = pb.ap[:, 0:HB * 128].rearrange("p (h l) -> p h l", l=128)
                        B.A(lambda: nc.scalar.activation(out=ea.ap, in_=pv, func=AF.Exp), pbb, [ea])
                        if _lv == "7":
                            continue
                        B.V(lambda: nc.vector.tensor_tensor(out=t1.ap, in0=pv, in1=_bc(mneg[d].ap.unsqueeze(1), [128, HB, 128]), op=ALU.add),
                            pbb + [mneg[d]], [t1])
                        if _lv == "8":
                            continue
                        B.V(lambda: nc.vector.tensor_tensor(out=t1.ap, in0=t1.ap,
                                                            in1=_bc(acs2.ap[:, d * H + h0:d * H + h0 + HB].unsqueeze(2), [128, HB, 128]),
                                                            op=ALU.subtract), [t1, acs2], [t1])
                        B.A(lambda: nc.scalar.activation(out=t1.ap, in_=t1.ap, func=AF.Exp), [t1], [t1])
                        if _lv == "9":
                            continue
                        B.V(lambda: nc.vector.tensor_tensor(out=MT[d].ap[:, h0:h0 + HB, :], in0=t1.ap,
                                                            in1=_bc(cbt.ap[:, g:g + 1, :], [128, HB, 128]), op=ALU.mult), [t1, cbt], [MT[d]])
                        if _os.environ.get("SSD_SKIP") == "4":
                            continue
                        B.G(lambda: nc.gpsimd.tensor_tensor(out=CsT[d].ap[:, h0:h0 + HB, :], in0=ea.ap,
                                                            in1=_bc(xc.ap[:, TI + G + g, tk].unsqueeze(1), [128, HB, 128]), op=ALU.mult),
                            [ea, xc], [CsT[d]])
                if _os.environ.get("SSD_SKIP") in ("1", "3", "4", "5", "6", "7", "8", "9"):
                    continue
                hf, hb_ = hin[0][gck % 2], hin[1][gck % 2]
                for h in range(H):
                    tl = (h * PH) // 128
                    po = (h * PH) % 128
                    bk = (tl * 128) // 512
                    co = tl * 128 - bk * 512
                    hs = slice(h * PH, (h + 1) * PH)
                    B.mm(psy[bk].ap[po:po + PH, co:co + 128],
                         [(xd[0].ap[:, hs], MT[0].ap[:, h, :]), (xd[1].ap[:, hs], MT[1].ap[:, h, :]),
                          (hf.ap[:, hs], CsT[0].ap[:, h, :]), (hb_.ap[:, hs], CsT[1].ap[:, h, :])],
                         [xd[0], xd[1], MT[0], MT[1], hf, hb_, CsT[0], CsT[1]], B.phalf[1 + bk])
                if _os.environ.get("SSD_SKIP") == "2":
                    continue
                for i in range(TI):
                    bk = (i * 128) // 512
                    co = i * 128 - bk * 512
                    B.V(lambda i=i, bk=bk, co=co: nc.vector.scalar_tensor_tensor(out=yv.ap[:, i, :], in0=xc.ap[:, i, tk], scalar=dcol.ap[:, i:i + 1],
                                                                                  in1=psy[bk].ap[:, co:co + 128], op0=ALU.mult, op1=ALU.add),
                        [xc, dcol] + B.phalf[1 + bk], [yv])
                B.A(lambda: nc.scalar.activation(out=sz.ap, in_=zt.ap[:, :, tk], func=AF.Silu), [zt], [sz])
                B.V(lambda: nc.vector.tensor_tensor(out=yv.ap, in0=yv.ap, in1=sz.ap, op=ALU.mult), [yv, sz], [yv])
                B.A(lambda: nc.scalar.activation(out=sq.ap, in_=yv.ap, func=AF.Square), [yv], [sq])
                B.mm(pmisc.ap[:, 384:512], [(B.onesb.ap, sq.ap[:, i, :]) for i in range(TI)], [B.onesb, sq], pmb)
                B.A(lambda: nc.scalar.activation(out=rt.ap, in_=pmisc.ap[:, 384:512], func=AF.Sqrt, scale=1.0 / c.INNER, bias=epst.ap[:, 0:1]),
                    pmb + [epst], [rt])
                B.V(lambda: nc.vector.reciprocal(out=rt.ap, in_=rt.ap), [rt], [rt])
                B.V(lambda: nc.vector.tensor_tensor(out=yv.ap, in0=yv.ap, in1=_bc(rt.ap.unsqueeze(1), [128, TI, 128]), op=ALU.mult), [yv, rt], [yv])
                B.V(lambda: nc.vector.tensor_tensor(out=ynst.ap[:, :, tk], in0=yv.ap, in1=_bc(ng.ap.unsqueeze(2), [128, TI, 128]), op=ALU.mult),
                    [yv, ng], [ynst])
            P.dma("sp", S["ynT"][:, sc * SC:(sc + 1) * SC].rearrange("(t p) n -> p t n", p=128), ynst.ap, reads=[ynst])
        P.barrier()


def phase_s5(B, l):
    c, nc, P, S, W = B.c, B.nc, B.P, B.S, B.W
    TS, C5, NT, LC = c.TS, c.C5, c.NT, c.S5C
    KT5 = c.W5 // 128
    NLC = NT // LC
    PI = math.pi
    with contextlib.ExitStack() as st:
        par = B.sb(st, "s5par", [128, 2, TS, 3], F32)
        P.dma("sp", par.ap, W["s5_par"][l].rearrange("d p t k -> p d t k"), writes=[par])
        bb = B.sb(st, "s5b", [128, TS, 2, C5], F32)
        P.dma("sp", bb.ap, W["s5_b"][l], writes=[bb])
        cc = B.sb(st, "s5c", [128, 2, TS, 2, C5], F32)
        P.dma("sp", cc.ap, W["s5_c"][l].rearrange("d p t k o -> p d t k o"), writes=[cc])
        dcol = B.sb(st, "s5d", [128, KT5], F32)
        P.dma("sp", dcol.ap, W["s5_dcol"][l], writes=[dcol])
        iota = B.sb(st, "iota", [128, LC], F32)
        riota = B.sb(st, "riota", [128, LC], F32)
        P.dma("sp", iota.ap, B.K["c_iota"], writes=[iota])
        B.V(lambda: nc.vector.tensor_scalar(out=riota.ap, in0=iota.ap, scalar1=-1.0, scalar2=float(LC - 1), op0=ALU.mult, op1=ALU.add),
            [iota], [riota])
        hpi = B.sb(st, "hpi", [128, 1], F32)
        B.V(lambda: nc.vector.memset(hpi.ap, PI / 2), [], [hpi])
        n2 = 2 * TS
        mk = lambda n, w=n2: B.sb(st, n, [128, w], F32)
        are, aim, step, lr, th, rr, cosl, sinl = [mk(n) for n in ("are", "aim", "step", "lr", "th", "rr", "cosl", "sinl")]
        tA, tB, tC, tD = [mk(n) for n in ("tA", "tB", "tC", "tD")]
        ki = B.sb(st, "ki", [128, max(n2, LC)], I32)
        tabt = [B.sb(st, "tabt", [128, LC], F32) for _ in range(3)]

        def red_sincos(x_ap, n, sin_out, cos_out, tmp, deps, outs):
            y, g, a = tmp
            B.V(lambda: nc.vector.tensor_scalar(out=y, in0=x_ap, scalar1=1.0 / TWO_PI, scalar2=None, op0=ALU.mult), deps, [tmpb])
            B.V(lambda: nc.vector.tensor_copy(out=ki.ap[:, 0:n], in_=y), [tmpb], [ki])
            B.V(lambda: nc.vector.tensor_copy(out=y, in_=ki.ap[:, 0:n]), [ki], [tmpb])
            B.V(lambda: nc.vector.scalar_tensor_tensor(out=y, in0=y, scalar=-TWO_PI, in1=x_ap, op0=ALU.mult, op1=ALU.add), deps + [tmpb], [tmpb])
            B.V(lambda: nc.vector.tensor_single_scalar(out=g, in_=y, scalar=PI, op=ALU.is_gt), [tmpb], [tmpb])
            B.V(lambda: nc.vector.scalar_tensor_tensor(out=y, in0=g, scalar=-TWO_PI, in1=y, op0=ALU.mult, op1=ALU.add), [tmpb], [tmpb])
            B.V(lambda: nc.vector.tensor_single_scalar(out=g, in_=y, scalar=-PI, op=ALU.is_lt), [tmpb], [tmpb])
            B.V(lambda: nc.vector.scalar_tensor_tensor(out=y, in0=g, scalar=TWO_PI, in1=y, op0=ALU.mult, op1=ALU.add), [tmpb], [tmpb])
            B.V(lambda: nc.vector.tensor_scalar(out=y, in0=y, scalar1=-PI, scalar2=PI, op0=ALU.max, op1=ALU.min), [tmpb], [tmpb])
            B.A(lambda: nc.scalar.activation(out=sin_out, in_=y, func=AF.Sin), [tmpb], outs)
            B.A(lambda: nc.scalar.activation(out=a, in_=y, func=AF.Abs), [tmpb], [tmpb])
            B.A(lambda: nc.scalar.activation(out=cos_out, in_=a, func=AF.Sin, scale=-1.0, bias=hpi.ap[:, 0:1]), [tmpb, hpi], outs)
        tmpb = Buf("s5tmp")
        f2 = lambda k: par.ap[:, :, :, k]
        v2 = lambda t: t.ap.rearrange("p (d t) -> p d t", d=2)
        B.V(lambda: nc.vector.tensor_copy(out=v2(are), in_=f2(0)), [par], [are])
        B.V(lambda: nc.vector.tensor_copy(out=v2(aim), in_=f2(1)), [par], [aim])
        B.A(lambda: nc.scalar.activation(out=v2(step), in_=f2(2), func=AF.Exp), [par], [step])
        B.V(lambda: nc.vector.tensor_tensor(out=lr.ap, in0=are.ap, in1=step.ap, op=ALU.mult), [are, step], [lr])
        B.V(lambda: nc.vector.tensor_tensor(out=th.ap, in0=aim.ap, in1=step.ap, op=ALU.mult), [aim, step], [th])
        B.A(lambda: nc.scalar.activation(out=rr.ap, in_=lr.ap, func=AF.Exp), [lr], [rr])
        red_sincos(th.ap, n2, sinl.ap, cosl.ap, (tA.ap, tB.ap, tC.ap), [th], [sinl, cosl])
        cT, sT, cTl, sTl, thL = [mk(n) for n in ("cT", "sT", "cTl", "sTl", "thL")]
        B.V(lambda: nc.vector.tensor_scalar(out=thL.ap, in0=th.ap, scalar1=float(LC), scalar2=None, op0=ALU.mult), [th], [thL])
        red_sincos(thL.ap, n2, sT.ap, cT.ap, (tA.ap, tB.ap, tC.ap), [thL], [sT, cT])
        B.V(lambda: nc.vector.tensor_scalar(out=cTl.ap, in0=cT.ap, scalar1=B.linkt.ap[:, 0:1], scalar2=None, op0=ALU.mult), [cT, B.linkt], [cTl])
        B.V(lambda: nc.vector.tensor_scalar(out=sTl.ap, in0=sT.ap, scalar1=B.linkt.ap[:, 0:1], scalar2=None, op0=ALU.mult), [sT, B.linkt], [sTl])
        lbr, lbi, cr, ci = [mk(n) for n in ("lbr", "lbi", "cr", "ci")]
        B.V(lambda: nc.vector.tensor_tensor(out=lbr.ap, in0=rr.ap, in1=cosl.ap, op=ALU.mult), [rr, cosl], [lbr])
        B.V(lambda: nc.vector.tensor_tensor(out=lbi.ap, in0=rr.ap, in1=sinl.ap, op=ALU.mult), [rr, sinl], [lbi])
        B.V(lambda: nc.vector.tensor_scalar(out=lbr.ap, in0=lbr.ap, scalar1=-1.0, scalar2=None, op0=ALU.add), [lbr], [lbr])
        B.V(lambda: nc.vector.tensor_tensor(out=tA.ap, in0=are.ap, in1=are.ap, op=ALU.mult), [are, tmpb], [tmpb])
        B.V(lambda: nc.vector.tensor_tensor(out=tB.ap, in0=aim.ap, in1=aim.ap, op=ALU.mult), [aim, tmpb], [tmpb])
        B.V(lambda: nc.vector.tensor_tensor(out=tA.ap, in0=tA.ap, in1=tB.ap, op=ALU.add), [tmpb], [tmpb])
        B.V(lambda: nc.vector.reciprocal(out=tD.ap, in_=tA.ap), [tmpb], [tD])
        B.V(lambda: nc.vector.tensor_tensor(out=tA.ap, in0=lbr.ap, in1=are.ap, op=ALU.mult), [lbr, are, tmpb], [tmpb])
        B.V(lambda: nc.vector.tensor_tensor(out=tB.ap, in0=lbi.ap, in1=aim.ap, op=ALU.mult), [lbi, aim, tmpb], [tmpb])
        B.V(lambda: nc.vector.tensor_tensor(out=tA.ap, in0=tA.ap, in1=tB.ap, op=ALU.add), [tmpb], [tmpb])
        B.V(lambda: nc.vector.tensor_tensor(out=cr.ap, in0=tA.ap, in1=tD.ap, op=ALU.mult), [tmpb, tD], [cr])
        B.V(lambda: nc.vector.tensor_tensor(out=tA.ap, in0=lbi.ap, in1=are.ap, op=ALU.mult), [lbi, are, tmpb], [tmpb])
        B.V(lambda: nc.vector.tensor_tensor(out=tB.ap, in0=lbr.ap, in1=aim.ap, op=ALU.mult), [lbr, aim, tmpb], [tmpb])
        B.V(lambda: nc.vector.tensor_tensor(out=tA.ap, in0=tA.ap, in1=tB.ap, op=ALU.subtract), [tmpb], [tmpb])
        B.V(lambda: nc.vector.tensor_tensor(out=ci.ap, in0=tA.ap, in1=tD.ap, op=ALU.mult), [tmpb, tD], [ci])
        BT = B.sb(st, "BT", [128, 2, TS, 2, 128], BF16)
        Cp = B.sb(st, "Cp", [128, 2, TS, 2, 128], BF16)
        B.V(lambda: nc.vector.memset(Cp.ap, 0.0), [], [Cp])
        Bbr = B.sb(st, "Bbr", [128, TS, C5], F32)
        Bbi = B.sb(st, "Bbi", [128, TS, C5], F32)
        tE = B.sb(st, "tE", [128, TS, C5], F32)
        pad = [B.sb(st, "pad", [128, 128], BF16) for _ in range(2)]
        ipad = 0
        for d in range(2):
            crb = _bc(cr.ap[:, d * TS:(d + 1) * TS].unsqueeze(2), [128, TS, C5])
            cib = _bc(ci.ap[:, d * TS:(d + 1) * TS].unsqueeze(2), [128, TS, C5])
            bre, bim = bb.ap[:, :, 0, :], bb.ap[:, :, 1, :]
            B.V(lambda: nc.vector.tensor_tensor(out=Bbr.ap, in0=bre, in1=crb, op=ALU.mult), [bb, cr], [Bbr])
            B.V(lambda: nc.vector.tensor_tensor(out=tE.ap, in0=bim, in1=cib, op=ALU.mult), [bb, ci], [tE])
            B.V(lambda: nc.vector.tensor_tensor(out=Bbr.ap, in0=Bbr.ap, in1=tE.ap, op=ALU.subtract), [Bbr, tE], [Bbr])
            B.V(lambda: nc.vector.tensor_tensor(out=Bbi.ap, in0=bim, in1=crb, op=ALU.mult), [bb, cr], [Bbi])
            B.V(lambda: nc.vector.tensor_tensor(out=tE.ap, in0=bre, in1=cib, op=ALU.mult), [bb, ci], [tE])
            B.V(lambda: nc.vector.tensor_tensor(out=Bbi.ap, in0=Bbi.ap, in1=tE.ap, op=ALU.add), [Bbi, tE], [Bbi])
            for m in range(TS):
                off = (m % 4) * 2 * C5
                for k, src in enumerate((Bbr, Bbi)):
                    pd = pad[ipad % 2]
                    ipad += 1
                    B.V(lambda: nc.vector.memset(pd.ap, 0.0), [], [pd])
                    B.V(lambda: nc.vector.tensor_copy(out=pd.ap[0:64, off:off + C5], in_=src.ap[0:64, m, :]), [src], [pd])
                    B.V(lambda: nc.vector.tensor_copy(out=pd.ap[64:128, off + C5:off + 2 * C5], in_=src.ap[64:128, m, :]), [src], [pd])
                    B.T(lambda: nc.tensor.transpose(B.pT.ap[:, 0:128], pd.ap, B.identb.ap), [pd, B.identb], B.phalf[6])
                    B.A(lambda: nc.scalar.copy(out=BT.ap[:, d, m, k, :], in_=B.pT.ap[:, 0:128]), B.phalf[6], [BT])
                B.V(lambda: nc.vector.tensor_copy(out=Cp.ap[0:64, d, m, 0, off:off + C5], in_=cc.ap[0:64, d, m, 0, :]), [cc], [Cp])
                B.V(lambda: nc.vector.tensor_copy(out=Cp.ap[64:128, d, m, 0, off + C5:off + 2 * C5], in_=cc.ap[64:128, d, m, 0, :]), [cc], [Cp])
                B.V(lambda: nc.vector.tensor_scalar(out=Cp.ap[0:64, d, m, 1, off:off + C5], in0=cc.ap[0:64, d, m, 1, :], scalar1=-1.0, scalar2=None,
                                                    op0=ALU.mult), [cc], [Cp])
                B.V(lambda: nc.vector.tensor_scalar(out=Cp.ap[64:128, d, m, 1, off + C5:off + 2 * C5], in0=cc.ap[64:128, d, m, 1, :], scalar1=-1.0,
                                                    scalar2=None, op0=ALU.mult), [cc], [Cp])
        uT = B.sb(st, "uT", [128, NT], BF16)
        yacc = B.sb(st, "yacc", [128, NT], F32)
        nm = min(4, TS)
        cs = [B.sb(st, "cs", [128, LC], F32) for _ in range(nm)]
        sn = [B.sb(st, "sn", [128, LC], F32) for _ in range(nm)]
        z0 = [[B.sb(st, "z0", [128, 1], F32) for _ in range(2)] for _ in range(nm)]
        wr, wi, zr, zi = [[B.sb(st, n, [128, LC], F32) for _ in range(2)] for n in ("wr", "wi", "zr", "zi")]
        q1, q2 = [[B.sb(st, n, [128, LC], F32) for _ in range(2)] for n in ("q1", "q2")]
        xr = [B.sb(st, "xr", [128, LC], BF16) for _ in range(nm)]
        xi = [B.sb(st, "xi", [128, LC], BF16) for _ in range(nm)]
        c1 = B.sb(st, "c1", [128, 1], F32)
        it = 0
        for kt in range(KT5):
            ms = [m for m in range(4 * kt, 4 * kt + 4) if m < TS]
            P.dma("sp", uT.ap, S["uT"][kt * 128:(kt + 1) * 128, :], writes=[uT])
            for d in range(2):
                for j, m in enumerate(ms):
                    ang = tabt[0]
                    io = iota if d == 0 else riota
                    B.V(lambda: nc.vector.tensor_scalar(out=ang.ap, in0=io.ap, scalar1=th.ap[:, d * TS + m:d * TS + m + 1], scalar2=None, op0=ALU.mult),
                        [io, th], [ang])
                    red_sincos(ang.ap, LC, sn[j].ap, cs[j].ap, (tabt[1].ap, tabt[2].ap, tabt[2].ap), [ang], [sn[j], cs[j]])
                    for k in range(2):
                        B.V(lambda k=k: nc.vector.memset(z0[j][k].ap, 0.0), [], [z0[j][k]])
                order = list(range(NLC)) if d == 0 else list(range(NLC - 1, -1, -1))
                bnd_c = (c.BND // LC) if d == 0 else (c.BND // LC - 1)
                for ch in order:
                    tk = slice(ch * LC, (ch + 1) * LC)
                    for j, m in enumerate(ms):
                        i2 = it % 2
                        it += 1
                        pa, pb_ = B.psb[2 * i2], B.psb[2 * i2 + 1]
                        ba, bbuf = B.phalf[2 * i2], B.phalf[2 * i2 + 1]
                        B.mm(pa.ap[:, 0:LC], [(BT.ap[:, d, m, 0, :], uT.ap[:, tk])], [BT, uT], ba)
                        B.mm(pb_.ap[:, 0:LC], [(BT.ap[:, d, m, 1, :], uT.ap[:, tk])], [BT, uT], bbuf)
                        w_r, w_i, z_r, z_i, a1, a2 = wr[i2], wi[i2], zr[i2], zi[i2], q1[i2], q2[i2]
                        B.V(lambda: nc.vector.tensor_tensor(out=a1.ap, in0=pa.ap[:, 0:LC], in1=cs[j].ap, op=ALU.mult), ba + [cs[j]], [a1])
                        B.V(lambda: nc.vector.tensor_tensor(out=a2.ap, in0=pb_.ap[:, 0:LC], in1=sn[j].ap, op=ALU.mult), bbuf + [sn[j]], [a2])
                        B.V(lambda: nc.vector.tensor_tensor(out=w_r.ap, in0=a1.ap, in1=a2.ap, op=ALU.add), [a1, a2], [w_r])
                        B.V(lambda: nc.vector.tensor_tensor(out=a1.ap, in0=pb_.ap[:, 0:LC], in1=cs[j].ap, op=ALU.mult), bbuf + [cs[j]], [a1])
                        B.V(lambda: nc.vector.tensor_tensor(out=a2.ap, in0=pa.ap[:, 0:LC], in1=sn[j].ap, op=ALU.mult), ba + [sn[j]], [a2])
                        B.V(lambda: nc.vector.tensor_tensor(out=w_i.ap, in0=a1.ap, in1=a2.ap, op=ALU.subtract), [a1, a2], [w_i])
                        if ch == bnd_c:
                            for k in range(2):
                                B.V(lambda k=k: nc.vector.tensor_scalar(out=z0[j][k].ap, in0=z0[j][k].ap, scalar1=B.linkt.ap[:, 0:1], scalar2=None,
                                                                        op0=ALU.mult), [z0[j][k], B.linkt], [z0[j][k]])
                        rb = _bc(rr.ap[:, d * TS + m:d * TS + m + 1], [128, LC])
                        sl = (lambda a: a) if d == 0 else (lambda a: a[:, ::-1])
                        B.V(lambda: nc.vector.tensor_tensor_scan(out=sl(z_r.ap), data0=rb, data1=sl(w_r.ap), initial=z0[j][0].ap[:, 0:1],
                                                                 op0=ALU.mult, op1=ALU.add), [rr, w_r, z0[j][0]], [z_r])
                        B.V(lambda: nc.vector.tensor_tensor_scan(out=sl(z_i.ap), data0=rb, data1=sl(w_i.ap), initial=z0[j][1].ap[:, 0:1],
                                                                 op0=ALU.mult, op1=ALU.add), [rr, w_i, z0[j][1]], [z_i])
                        B.G(lambda: nc.gpsimd.tensor_tensor(out=a1.ap, in0=z_r.ap, in1=cs[j].ap, op=ALU.mult), [z_r, cs[j]], [a1])
                        B.G(lambda: nc.gpsimd.tensor_tensor(out=a2.ap, in0=z_i.ap, in1=sn[j].ap, op=ALU.mult), [z_i, sn[j]], [a2])
                        B.G(lambda: nc.gpsimd.tensor_tensor(out=xr[j].ap, in0=a1.ap, in1=a2.ap, op=ALU.subtract), [a1, a2], [xr[j]])
                        B.G(lambda: nc.gpsimd.tensor_tensor(out=a1.ap, in0=z_i.ap, in1=cs[j].ap, op=ALU.mult), [z_i, cs[j]], [a1])
                        B.G(lambda: nc.gpsimd.tensor_tensor(out=a2.ap, in0=z_r.ap, in1=sn[j].ap, op=ALU.mult), [z_r, sn[j]], [a2])
                        B.G(lambda: nc.gpsimd.tensor_tensor(out=xi[j].ap, in0=a1.ap, in1=a2.ap, op=ALU.add), [a1, a2], [xi[j]])
                        last = LC - 1 if d == 0 else 0
                        col = d * TS + m
                        lk = (ch == (bnd_c - 1 if d == 0 else bnd_c + 1))
                        zl_r, zl_i = z_r.ap[:, last:last + 1], z_i.ap[:, last:last + 1]
                        B.V(lambda: nc.vector.tensor_scalar(out=c1.ap, in0=zl_i, scalar1=sT.ap[:, col:col + 1], scalar2=None, op0=ALU.mult), [z_i, sT], [c1])
                        B.V(lambda: nc.vector.scalar_tensor_tensor(out=z0[j][0].ap, in0=zl_r, scalar=cT.ap[:, col:col + 1], in1=c1.ap,
                                                                   op0=ALU.mult, op1=ALU.subtract), [z_r, cT, c1], [z0[j][0]])
                        B.V(lambda: nc.vector.tensor_scalar(out=c1.ap, in0=zl_r, scalar1=sT.ap[:, col:col + 1], scalar2=None, op0=ALU.mult), [z_r, sT], [c1])
                        B.V(lambda: nc.vector.scalar_tensor_tensor(out=z0[j][1].ap, in0=zl_i, scalar=cT.ap[:, col:col + 1], in1=c1.ap,
                                                                   op0=ALU.mult, op1=ALU.add), [z_i, cT, c1], [z0[j][1]])
                    py = B.psb[4 + (ch % 2)]
                    pyb = B.phalf[4 + (ch % 2)]
                    pairs = []
                    rd = [Cp]
                    for j, m in enumerate(ms):
                        pairs += [(Cp.ap[:, d, m, 0, :], xr[j].ap), (Cp.ap[:, d, m, 1, :], xi[j].ap)]
                        rd += [xr[j], xi[j]]
                    B.mm(py.ap[:, 0:LC], pairs, rd, pyb)
                    if d == 0:
                        B.A(lambda: nc.scalar.copy(out=yacc.ap[:, tk], in_=py.ap[:, 0:LC]), pyb, [yacc])
                    else:
                        B.V(lambda: nc.vector.tensor_tensor(out=yacc.ap[:, tk], in0=py.ap[:, 0:LC], in1=yacc.ap[:, tk], op=ALU.add), pyb + [yacc], [yacc])
            for ch in range(NLC):
                tk = slice(ch * LC, (ch + 1) * LC)
                ya, tq, yoc = wr[ch % 2], q1[ch % 2], xr[ch % min(2, nm)]
                B.V(lambda: nc.vector.scalar_tensor_tensor(out=ya.ap, in0=uT.ap[:, tk], scalar=dcol.ap[:, kt:kt + 1], in1=yacc.ap[:, tk],
                                                           op0=ALU.mult, op1=ALU.add), [uT, dcol, yacc], [ya])
                B.V(lambda: nc.vector.tensor_tensor(out=tq.ap, in0=ya.ap, in1=ya.ap, op=ALU.mult), [ya], [tq])
                B.V(lambda: nc.vector.tensor_scalar(out=tq.ap, in0=tq.ap, scalar1=0.044715, scalar2=1.0, op0=ALU.mult, op1=ALU.add), [tq], [tq])
                B.V(lambda: nc.vector.tensor_tensor(out=tq.ap, in0=tq.ap, in1=ya.ap, op=ALU.mult), [tq, ya], [tq])
                B.A(lambda: nc.scalar.activation(out=tq.ap, in_=tq.ap, func=AF.Sigmoid, scale=1.5957691216057308), [tq], [tq])
                B.V(lambda: nc.vector.tensor_tensor(out=yoc.ap, in0=tq.ap, in1=ya.ap, op=ALU.mult), [tq, ya], [yoc])
                P.dma("sp", S["s5T"][kt * 128:(kt + 1) * 128, tk], yoc.ap, reads=[yoc])
        P.barrier()


def attn_setup(B):
    c, nc, P, W, K = B.c, B.nc, B.P, B.W, B.K
    NH = 3 * c.HG
    B.fd = []
    with contextlib.ExitStack() as st:
        tbl = B.sb(st, "tbl", [c.NBK + 1, NH], F32)
        P.dma("sp", tbl.ap[0:c.NBK, :], W["rel_bias"], writes=[tbl])
        B.V(lambda: nc.vector.memset(tbl.ap[c.NBK:c.NBK + 1, :], NEG), [], [tbl])
        for p in range(3):
            Wp = (2 * c.PADT[p] + 1) * 128
            nrel = Wp + 127
            fd = B.scratch(f"fd{p}", [NH, nrel], F32)
            B.fd.append(fd)
            oh = B.sb(st, "oh", [c.NBK + 1, nrel], F32)
            fs = B.sb(st, "fs", [NH, nrel], F32)
            P.dma("sp", oh.ap, K[f"c_oh{p}"], writes=[oh])
            for i, c0 in enumerate(range(0, nrel, 512)):
                w_ = min(512, nrel - c0)
                bk = i % 2
                B.mm(B.psb[bk].ap[0:NH, 0:w_], [(tbl.ap, oh.ap[:, c0:c0 + w_])], [tbl, oh], B.phalf[bk])
                B.A(lambda: nc.scalar.copy(out=fs.ap[:, c0:c0 + w_], in_=B.psb[bk].ap[0:NH, 0:w_]), B.phalf[bk], [fs])
            P.dma("sp", fd, fs.ap, reads=[fs])
        P.barrier()


def phase_attn(B, l):
    c, nc, P, S, W = B.c, B.nc, B.P, B.S, B.W
    BND, NT, HG = c.BND, c.NT, c.HG
    NQ = BND // 128
    Wp = [(2 * pt + 1) * 128 for pt in c.PADT]
    off = [0, Wp[0], Wp[0] + Wp[1]]
    Wtot = sum(Wp)
    NKT = Wtot // 128
    scale = float(c.DH) ** -0.5
    with contextlib.ExitStack() as st:
        kT = [B.sb(st, "kT", [128, BND + 2 * c.PADT[p] * 128], BF16) for p in range(3)]
        Vt = [B.sb(st, "Vt", [128, NQ + 2 * c.PADT[p], 128], BF16) for p in range(3)]
        qT = [B.sb(st, "qT", [128, BND], BF16) for p in range(3)]
        MB = [B.sb(st, "MB", [128, Wp[p]], F32) for p in range(3)]
        MBr = [B.sb(st, "MBr", [128, Wp[p]], F32) for p in range(3)]
        Sp = [B.sb(st, "Sp", [128, Wtot], F32) for _ in range(2)]
        Pm = [B.sb(st, "Pm", [128, Wtot], BF16) for _ in range(2)]
        PT = [B.sb(st, "PT", [128, NKT, 128], BF16) for _ in range(2)]
        negm = B.sb(st, "negm", [128, 1], F32)
        rs = B.sb(st, "ars", [128, 128], F32)
        ast_ = [B.sb(st, "ast", [128, BND], BF16) for _ in range(2)]
        it = 0
        for j in range(HG):
            for p in range(3):
                h = p * HG + j
                nrel = Wp[p] + 127
                src = bass.AP(tensor=B.fd[p].tensor, offset=h * nrel, ap=[[1, 128], [1, Wp[p]]])
                P.dma("sp", MBr[p].ap, src, writes=[MBr[p]])
                B.V(lambda p=p: nc.vector.tensor_copy(out=MB[p].ap, in_=MBr[p].ap[:, ::-1]), [MBr[p]], [MB[p]])
            for hf in range(2):
                hs = hf * BND
                for p in range(3):
                    h = p * HG + j
                    pad = c.PADT[p] * 128
                    lo, hi = hs - pad, hs + BND + pad
                    slo, shi = max(lo, 0), min(hi, NT)
                    B.V(lambda p=p: nc.vector.memset(kT[p].ap, 0.0), [], [kT[p]])
                    B.G(lambda p=p: nc.gpsimd.memset(Vt[p].ap, 0.0), [], [Vt[p]])
                    P.dma("sp", kT[p].ap[:, slo - lo: shi - lo], S["kT"][h * 128:(h + 1) * 128, slo:shi], writes=[kT[p]])
                    P.dma("sp", Vt[p].ap[:, (slo - lo) // 128:(shi - lo) // 128, :],
                          S["v"][slo:shi, h * 128:(h + 1) * 128].rearrange("(t p) n -> p t n", p=128), writes=[Vt[p]])
                    P.dma("sp", qT[p].ap, S["qT"][h * 128:(h + 1) * 128, hs:hs + BND], writes=[qT[p]])
                asg = ast_[(j * 2 + hf) % 2]
                for i in range(NQ):
                    sp, pm, pt = Sp[i % 2], Pm[i % 2], PT[i % 2]
                    for p in range(3):
                        pad = c.PADT[p] * 128
                        for c0 in range(0, Wp[p], 512):
                            pw = min(512, Wp[p] - c0)
                            bk = it % 2
                            it += 1
                            B.mm(B.psb[bk].ap[:, 0:pw], [(qT[p].ap[:, i * 128:(i + 1) * 128], kT[p].ap[:, i * 128 + c0: i * 128 + c0 + pw])],
                                 [qT[p], kT[p]], B.phalf[bk])
                            B.V(lambda p=p, c0=c0, pw=pw, bk=bk: nc.vector.scalar_tensor_tensor(
                                out=sp.ap[:, off[p] + c0: off[p] + c0 + pw], in0=B.psb[bk].ap[:, 0:pw], scalar=scale,
                                in1=MB[p].ap[:, c0:c0 + pw], op0=ALU.mult, op1=ALU.add), B.phalf[bk] + [MB[p]], [sp])
                        base = hs - pad + i * 128
                        if base < 0:
                            n0 = min(Wp[p], -base)
                            B.V(lambda p=p, n0=n0: nc.vector.tensor_scalar(out=sp.ap[:, off[p]:off[p] + n0], in0=sp.ap[:, off[p]:off[p] + n0],
                                                                           scalar1=NEG, scalar2=None, op0=ALU.add), [sp], [sp])
                        if base + Wp[p] > NT:
                            n0 = max(0, NT - base)
                            B.V(lambda p=p, n0=n0: nc.vector.tensor_scalar(out=sp.ap[:, off[p] + n0:off[p] + Wp[p]], in0=sp.ap[:, off[p] + n0:off[p] + Wp[p]],
                                                                           scalar1=NEG, scalar2=None, op0=ALU.add), [sp], [sp])
                        if hf == 0 and base + Wp[p] > BND:
                            n0 = max(0, BND - base)
                            B.V(lambda p=p, n0=n0: nc.vector.tensor_scalar(out=sp.ap[:, off[p] + n0:off[p] + Wp[p]], in0=sp.ap[:, off[p] + n0:off[p] + Wp[p]],
                                                                           scalar1=B.nlinkt.ap[:, 0:1], scalar2=None, op0=ALU.add), [sp, B.nlinkt], [sp])
                        if hf == 1 and base < BND:
                            n0 = min(Wp[p], BND - base)
                            B.V(lambda p=p, n0=n0: nc.vector.tensor_scalar(out=sp.ap[:, off[p]:off[p] + n0], in0=sp.ap[:, off[p]:off[p] + n0],
                                                                           scalar1=B.nlinkt.ap[:, 0:1], scalar2=None, op0=ALU.add), [sp, B.nlinkt], [sp])
                    B.V(lambda: nc.vector.tensor_reduce(out=negm.ap, in_=sp.ap, axis=AX.X, op=ALU.max, negate=True), [sp], [negm])
                    B.A(lambda: nc.scalar.activation(out=pm.ap, in_=sp.ap, func=AF.Exp, bias=negm.ap[:, 0:1]), [sp, negm], [pm])
                    for r0 in range(0, NKT, 16):
                        rn = min(16, NKT - r0)

                        def tr(r0=r0, rn=rn):
                            ins = None
                            for t_ in range(rn):
                                ins = nc.tensor.transpose(B.pT.ap[:, t_ * 128:(t_ + 1) * 128], pm.ap[:, (r0 + t_) * 128:(r0 + t_ + 1) * 128], B.identb.ap)
                            return ins
                        B.T(tr, [pm, B.identb], [B.pT])
                        if (r0 // 16) % 2 == 0:
                            B.A(lambda r0=r0, rn=rn: nc.scalar.copy(out=pt.ap[:, r0:r0 + rn, :].rearrange("p t q -> p (t q)"), in_=B.pT.ap[:, 0:rn * 128]),
                                [B.pT], [pt])
                        else:
                            B.V(lambda r0=r0, rn=rn: nc.vector.tensor_copy(out=pt.ap[:, r0:r0 + rn, :].rearrange("p t q -> p (t q)"), in_=B.pT.ap[:, 0:rn * 128]),
                                [B.pT], [pt])
                    bo, bs_ = (2, 3) if i % 2 == 0 else (4, 5)
                    pv, ps_ = [], []
                    for p in range(3):
                        for kt in range(Wp[p] // 128):
                            tile_ = off[p] // 128 + kt
                            pv.append((Vt[p].ap[:, i + kt, :], pt.ap[:, tile_, :]))
                            ps_.append((B.onesb.ap, pt.ap[:, tile_, :]))
                    B.mm(B.psb[bo].ap[:, 0:128], pv, [pt] + Vt, B.phalf[bo])
                    B.mm(B.psb[bs_].ap[:, 0:128], ps_, [pt, B.onesb], B.phalf[bs_])
                    B.V(lambda bs_=bs_: nc.vector.reciprocal(out=rs.ap, in_=B.psb[bs_].ap[:, 0:128]), B.phalf[bs_], [rs])
                    B.V(lambda bo=bo: nc.vector.tensor_tensor(out=asg.ap[:, i * 128:(i + 1) * 128], in0=B.psb[bo].ap[:, 0:128], in1=rs.ap, op=ALU.mult),
                        B.phalf[bo] + [rs], [asg])
                P.dma("sp", S["attT"][j * 128:(j + 1) * 128, hs:hs + BND], asg.ap, reads=[asg])
        P.barrier()


def phase_merge(B, l, xsrc, xkey):
    c, nc, P, S, W = B.c, B.nc, B.P, B.S, B.W
    TG, D, KC = c.FTG, c.D, c.KC
    TI, K5, KA = c.INNER // 128, c.W5 // 128, c.HG * c.DH // 128
    with contextlib.ExitStack() as st:
        B.wt_setup(st, 16, 128, nbf=6)
        act = B.sb(st, "act", [128, max(TI, K5, KA), TG], BF16)
        mT = B.sb(st, "mT", [128, KC, TG], BF16)
        gts = [B.sb(st, "gt", [128, TG], BF16) for _ in range(2)]
        sgs = [B.sb(st, "sg", [128, TG], F32) for _ in range(2)]
        tmp = [B.sb(st, "mtmp", [128, 512], F32) for _ in range(2)]
        tmp2 = [B.sb(st, "mtmp2", [128, 512], F32) for _ in range(2)]
        xts = [B.sb(st, "mxt", [128, 128], F32) for _ in range(4)]
        cnt = {"g": 0, "t": 0, "x": 0}
        for tg in range(c.NT // TG):
            t0 = tg * TG

            def gate_tile(b, col_abs):
                gt, sg = gts[cnt["g"] % 2], sgs[cnt["g"] % 2]
                cnt["g"] += 1
                P.dma("sp", gt.ap, S["gT"][b * D + col_abs: b * D + col_abs + 128, t0:t0 + TG], writes=[gt])
                B.A(lambda: nc.scalar.activation(out=sg.ap, in_=gt.ap, func=AF.Sigmoid), [gt], [sg])
                return sg

            def load_act(name, kn):
                P.dma("sp", act.ap[:, 0:kn, :], S[name][:, t0:t0 + TG].rearrange("(t p) n -> p t n", p=128), writes=[act])
            load_act("ynT", TI)
            cur = {}

            def evac_a(ps, col_abs, nb, tok0, tw, pb):
                if tok0 == 0:
                    cur["sg"] = gate_tile(0, col_abs)
                sg = cur["sg"]
                B.V(lambda: nc.vector.tensor_tensor(out=mT.ap[:, col_abs // 128, tok0:tok0 + tw], in0=ps, in1=sg.ap[:, tok0:tok0 + tw], op=ALU.mult),
                    pb + [sg], [mT])
            B.dense_feat(act, TI, TG, W["ssm_w_out"][l], 0, D, evac_a)
            load_act("s5T", K5)
            ldp = lambda db_: (B.load_w(W["s5_w_glu"][l], 0, K5, db_ * 128, 128), B.load_w(W["s5_w_glu"][l], 0, K5, D + db_ * 128, 128))
            nxt = ldp(0)
            for db in range(KC):
                wv, wg = nxt
                if db + 1 < KC:
                    nxt = ldp(db + 1)
                sg = gate_tile(1, db * 128)
                for tt in range(-(-TG // 512)):
                    tw = min(512, TG - tt * 512)
                    tks = slice(tt * 512, tt * 512 + tw)
                    ba, bb_ = (0, 1) if cnt["t"] % 2 == 0 else (2, 3)
                    tp, tp2 = tmp[cnt["t"] % 2], tmp2[cnt["t"] % 2]
                    cnt["t"] += 1
                    B.mm(B.psb[ba].ap[:, 0:tw], [(wv.ap[:, k, 0:128], act.ap[:, k, tks]) for k in range(K5)], [wv, act], B.phalf[ba])
                    B.mm(B.psb[bb_].ap[:, 0:tw], [(wg.ap[:, k, 0:128], act.ap[:, k, tks]) for k in range(K5)], [wg, act], B.phalf[bb_])
                    B.A(lambda: nc.scalar.activation(out=tp.ap[:, 0:tw], in_=B.psb[bb_].ap[:, 0:tw], func=AF.Sigmoid), B.phalf[bb_], [tp])
                    B.V(lambda: nc.vector.tensor_tensor(out=tp2.ap[:, 0:tw], in0=B.psb[ba].ap[:, 0:tw], in1=tp.ap[:, 0:tw], op=ALU.mult), B.phalf[ba] + [tp], [tp2])
                    B.V(lambda: nc.vector.tensor_tensor(out=tp2.ap[:, 0:tw], in0=tp2.ap[:, 0:tw], in1=sg.ap[:, tks], op=ALU.mult), [tp2, sg], [tp2])
                    B.V(lambda: nc.vector.tensor_tensor(out=mT.ap[:, db, tks], in0=tp2.ap[:, 0:tw], in1=mT.ap[:, db, tks], op=ALU.add), [tp2, mT], [mT])
            load_act("attT", KA)

            def evac_c(ps, col_abs, nb, tok0, tw, pb):
                if tok0 == 0:
                    cur["sg"] = gate_tile(2, col_abs)
                sg = cur["sg"]
                tp = tmp[cnt["t"] % 2]
                cnt["t"] += 1
                B.V(lambda: nc.vector.tensor_tensor(out=tp.ap[:, 0:tw], in0=ps, in1=sg.ap[:, tok0:tok0 + tw], op=ALU.mult), pb + [sg], [tp])
                B.V(lambda: nc.vector.tensor_tensor(out=mT.ap[:, col_abs // 128, tok0:tok0 + tw], in0=tp.ap[:, 0:tw],
                                                    in1=mT.ap[:, col_abs // 128, tok0:tok0 + tw], op=ALU.add), [tp, mT], [mT])
            B.dense_feat(act, KA, TG, W["att_w_out"][l], 0, D, evac_c)

            def evac_o(ps, tok0, rows, col_abs, cw, pb):
                xt = xts[cnt["x"] % 4]
                cnt["x"] += 1
                P.dma("sp", xt.ap[0:rows, 0:cw], xsrc[t0 + tok0:t0 + tok0 + rows, col_abs:col_abs + cw], writes=[xt])
                B.V(lambda: nc.vector.tensor_tensor(out=xt.ap[0:rows, 0:cw], in0=ps, in1=xt.ap[0:rows, 0:cw], op=ALU.add), pb + [xt], [xt])
                P.dma("sp", S["xmid"][t0 + tok0:t0 + tok0 + rows, col_abs:col_abs + cw], xt.ap[0:rows, 0:cw], reads=[xt])
            B.dense_tok(mT, KC, TG, W["w_o"][l], 0, D, 128, evac_o)
        P.barrier()


def phase_ffn(B, l):
    c, nc, P, S, W = B.c, B.nc, B.P, B.S, B.W
    TG, D, KC, DFF = c.FTG, c.D, c.KC, c.DFF
    KF = DFF // 128
    with contextlib.ExitStack() as st:
        gain = B.sb(st, "gain2", [128, D], F32)
        B.bcast_load(gain, W["norm_ffn"][l:l + 1, :], D)
        h2T = B.sb(st, "h2T", [128, KC, TG + 2], BF16)
        gvT = B.sb(st, "gvT", [128, KF, TG], BF16)
        cwt = B.sb(st, "fcw", [128, 2 * KF, 4], F32)
        P.dma("sp", cwt.ap, W["ffn_cw"][l], writes=[cwt])
        ups = [B.sb(st, "ups", [128, TG + 2], F32) for _ in range(2)]
        accg = B.sb(st, "accg", [128, TG], F32)
        accv = B.sb(st, "accv", [128, TG], F32)
        xts = [B.sb(st, "fxt", [128, 128], F32) for _ in range(4)]
        cnt = {"x": 0, "b": 0}
        for tg in range(c.NT // TG):
            t0 = tg * TG
            with contextlib.ExitStack() as st1:
                B.norm_setup(st1, nbuf=1)
                B.norm_T(st1, S["xmid"], "xmid", gain, t0, TG, h2T, col0=1)
                for (tok, col) in ((t0 - 1, 0), (t0 + TG, TG + 1)):
                    if tok < 0 or tok >= c.NT:
                        B.V(lambda col=col: nc.vector.memset(h2T.ap[:, :, col:col + 1], 0.0), [], [h2T])
                    else:
                        cross = (tok == c.BND - 1 and col == 0) or (tok == c.BND and col == TG + 1)
                        B.norm_T(st1, S["xmid"], "xmid", gain, tok, 1, h2T, col0=col, scale_tile=(B.linkt if cross else None))
                P.barrier()
            st2 = contextlib.ExitStack()
            B.wt_setup(st2, 16, 128, nbf=6)
            loads = [(jb, which, cbase) for jb in range(KF) for which, cbase in enumerate((jb * 128, DFF + jb * 128))]
            nxt = B.load_w(W["w_up"][l], 0, KC, loads[0][2], 128)
            for li, (jb, which, cbase) in enumerate(loads):
                accs = (accg, accv)
                if True:
                    wb = nxt
                    if li + 1 < len(loads):
                        nxt = B.load_w(W["w_up"][l], 0, KC, loads[li + 1][2], 128)
                    up = ups[which]
                    ti_ = which * KF + jb
                    pieces = [(1 + q0, min(512, TG - q0)) for q0 in range(0, TG, 512)] + [(0, 1), (TG + 1, 1)]
                    for (cs_, n_) in pieces:
                        bk = cnt["b"] % 6
                        cnt["b"] += 1
                        B.mm(B.psb[bk].ap[:, 0:n_], [(wb.ap[:, k, 0:128], h2T.ap[:, k, cs_:cs_ + n_]) for k in range(KC)], [wb, h2T], B.phalf[bk])
                        if cnt["b"] % 2 == 0:
                            B.A(lambda bk=bk, cs_=cs_, n_=n_: nc.scalar.copy(out=up.ap[:, cs_:cs_ + n_], in_=B.psb[bk].ap[:, 0:n_]), B.phalf[bk], [up])
                        else:
                            B.V(lambda bk=bk, cs_=cs_, n_=n_: nc.vector.tensor_copy(out=up.ap[:, cs_:cs_ + n_], in_=B.psb[bk].ap[:, 0:n_]), B.phalf[bk], [up])
                    acc = accs[which]
                    B.V(lambda: nc.vector.tensor_scalar(out=acc.ap, in0=up.ap[:, 0:TG], scalar1=cwt.ap[:, ti_, 0:1], scalar2=cwt.ap[:, ti_, 3:4],
                                                        op0=ALU.mult, op1=ALU.add), [up, cwt], [acc])
                    for k in (1, 2):
                        B.V(lambda k=k: nc.vector.scalar_tensor_tensor(out=acc.ap, in0=up.ap[:, k:k + TG], scalar=cwt.ap[:, ti_, k:k + 1], in1=acc.ap,
                                                                       op0=ALU.mult, op1=ALU.add), [up, cwt, acc], [acc])
                if which == 1:
                    B.A(lambda: nc.scalar.activation(out=accg.ap, in_=accg.ap, func=AF.Silu), [accg], [accg])
                    B.V(lambda: nc.vector.tensor_tensor(out=gvT.ap[:, jb, :], in0=accg.ap, in1=accv.ap, op=ALU.mult), [accg, accv], [gvT])

            def evac_d(ps, tok0, rows, col_abs, cw, pb):
                xt = xts[cnt["x"] % 4]
                cnt["x"] += 1
                P.dma("sp", xt.ap[0:rows, 0:cw], S["xmid"][t0 + tok0:t0 + tok0 + rows, col_abs:col_abs + cw], writes=[xt])
                B.V(lambda: nc.vector.tensor_tensor(out=xt.ap[0:rows, 0:cw], in0=ps, in1=xt.ap[0:rows, 0:cw], op=ALU.add), pb + [xt], [xt])
                P.dma("sp", S["xres"][t0 + tok0:t0 + tok0 + rows, col_abs:col_abs + cw], xt.ap[0:rows, 0:cw], reads=[xt])
            B.dense_tok(gvT, KF, TG, W["w_down"][l], 0, D, 128, evac_d)
            P.barrier()
            st2.close()
        P.barrier()


def phase_final(B, y_out):
    c, nc, P, S, W = B.c, B.nc, B.P, B.S, B.W
    with contextlib.ExitStack() as st:
        gain = B.sb(st, "gainf", [128, c.D], F32)
        B.bcast_load(gain, W["final_norm"][0:1, :], c.D)
        B.norm_setup(st)
        nt = B._norm_tiles
        outs = [B.sb(st, "fo", [128, c.D], F32) for _ in range(2)]
        for ti in range(c.NT // 128):
            xt, o = nt["x"][ti % 2], outs[ti % 2]
            sq, ss, rt, rs = nt["sq"], nt["ss"], nt["rt"], nt["rs"]
            P.dma("sp", xt.ap, S["xres"][ti * 128:(ti + 1) * 128, :], writes=[xt])
            B.A(lambda: nc.scalar.activation(out=sq.ap, in_=xt.ap, func=AF.Square), [xt], [sq])
            B.V(lambda: nc.vector.reduce_sum(out=ss.ap, in_=sq.ap, axis=AX.X), [sq], [ss])
            B.A(lambda: nc.scalar.activation(out=rt.ap, in_=ss.ap, func=AF.Sqrt, scale=1.0 / c.D, bias=nt["eps"].ap[:, 0:1]), [ss, nt["eps"]], [rt])
            B.V(lambda: nc.vector.reciprocal(out=rs.ap, in_=rt.ap), [rt], [rs])
            B.V(lambda: nc.vector.scalar_tensor_tensor(out=o.ap, in0=xt.ap, scalar=rs.ap[:, 0:1], in1=gain.ap, op0=ALU.mult, op1=ALU.mult),
                [xt, rs, gain], [o])
            P.dma("sp", y_out[ti * 128:(ti + 1) * 128, :], o.ap, reads=[o])
        P.barrier()


_CACHE = {}


def kernel(**inputs):
    c = FULL
    c.TS = c.G5 * c.P5 // 128
    xp = np.asarray(inputs["x_prompt"], dtype=np.float32)
    xs = np.asarray(inputs["x_sample"], dtype=np.float32)
    assert xp.shape == (2, c.BND, c.D) and xs.shape == (1, c.NT, c.D)
    wl = layout_weights(c, inputs)
    consts = host_consts(c)
    maps = [core_inputs(c, wl, consts, xp.reshape(c.NT, c.D), 0),
            core_inputs(c, wl, consts, xs.reshape(c.NT, c.D), 1)]
    if "nc" not in _CACHE:
        _CACHE["nc"] = build(c)[0]
    res = run_bass_kernel_spmd(_CACHE["nc"], maps, core_ids=[0, 1])
    y_prompt = np.asarray(res.results[0]["y"], dtype=np.float32).reshape(2, c.BND, c.D)
    y_sample = np.asarray(res.results[1]["y"], dtype=np.float32).reshape(1, c.NT, c.D)
    return (y_prompt, y_sample)
```

```python
import math
import numpy as np
import concourse.bass as bass
import concourse.mybir as mybir
from concourse.bass_utils import run_bass_kernel_spmd

F32 = mybir.dt.float32
BF16 = mybir.dt.bfloat16
I32 = mybir.dt.int32
ALU = mybir.AluOpType
AF = mybir.ActivationFunctionType
AX = mybir.AxisListType
NEG = -1.0e30
TWO_PI = 2.0 * math.pi


class Buf:
    __slots__ = ("name", "w", "r", "excl")

    def __init__(self, name="", excl=False):
        self.name = name
        self.w = []
        self.r = []
        self.excl = excl


class Tile:
    def __init__(self, ap, buf):
        self.ap = ap
        self.b = buf

    def __getitem__(self, k):
        return self.ap[k]


class Prog:
    KS = 6
    NDQ = {"sp": 24, "pool": 12, "act": 8}

    def __init__(self, nc):
        self.nc = nc
        self.E = {"pe": nc.tensor, "act": nc.scalar, "dve": nc.vector, "pool": nc.gpsimd, "sp": nc.sync}
        self.csem = {e: [nc.alloc_semaphore(name=f"c_{e}{i}") for i in range(self.KS)]
                     for e in ("pe", "act", "dve", "pool")}
        self.cnt = {e: 0 for e in self.csem}
        self.dsem = {q: [nc.alloc_semaphore(name=f"d_{q}{i}") for i in range(n)] for q, n in self.NDQ.items()}
        self.dval = {q: [0] * self.NDQ[q] for q in self.dsem}
        self.dnext = {q: 0 for q in self.dsem}
        self.seen = {e: {} for e in self.E}
        self.ninst = 0
        self.nalloc = 0

    def buf(self, name=""):
        return Buf(name)

    def sb(self, name, shape, dt):
        self.nalloc += 1
        h = self.nc.alloc_sbuf_tensor(f"{name}_{self.nalloc}", list(shape), dt)
        return Tile(h.ap(), Buf(name))

    def ps(self, name, shape, dt):
        self.nalloc += 1
        h = self.nc.alloc_psum_tensor(f"{name}_{self.nalloc}", list(shape), dt)
        return Tile(h.ap(), Buf(name))

    def _wait_tok(self, eng, tok):
        if tok[0] == "c":
            _, e2, n = tok
            if e2 == eng and eng == "pe":
                return
            key = ("c", e2)
            if self.seen[eng].get(key, 0) >= n:
                return
            self.seen[eng][key] = n
            self.E[eng].wait_ge(self.csem[e2][(n - 1) % self.KS], (n - 1) // self.KS + 1)
        else:
            _, q, i, v = tok
            key = ("d", q, i)
            if self.seen[eng].get(key, 0) >= v:
                return
            self.seen[eng][key] = v
            self.E[eng].wait_ge(self.dsem[q][i], v)

    @staticmethod
    def _bufs(xs):
        out = []
        for x in xs:
            b = x.b if isinstance(x, Tile) else x
            if isinstance(b, (list, tuple)):
                out.extend(b)
            else:
                out.append(b)
        return out

    def _deps(self, reads, writes):
        deps = []
        for b in reads:
            deps.extend(b.w)
            if b.excl:
                deps.extend(b.r)
        for b in writes:
            deps.extend(b.w)
            deps.extend(b.r)
        return deps

    def _commit(self, tok, reads, writes):
        for b in writes:
            b.w = [tok]
            b.r = []
        for b in reads:
            if b not in writes:
                b.r.append(tok)
                if len(b.r) > 48:
                    best = {}
                    for t in b.r:
                        k = t[:2] if t[0] == "c" else t[:3]
                        if k not in best or best[k][-1] < t[-1]:
                            best[k] = t
                    b.r = list(best.values())

    def op(self, eng, fn, reads=(), writes=()):
        reads = self._bufs(reads)
        writes = self._bufs(writes)
        for tok in self._deps(reads, writes):
            self._wait_tok(eng, tok)
        ins = fn()
        self.cnt[eng] += 1
        n = self.cnt[eng]
        ins.then_inc(self.csem[eng][(n - 1) % self.KS], 1)
        self.ninst += 1
        self._commit(("c", eng, n), reads, writes)
        return ins

    def dma(self, q, out, in_, reads=(), writes=(), fn=None, **kw):
        reads = self._bufs(reads)
        writes = self._bufs(writes)
        for tok in self._deps(reads, writes):
            self._wait_tok(q, tok)
        i = self.dnext[q]
        self.dnext[q] = (i + 1) % self.NDQ[q]
        if self.dval[q][i] > 0:
            self._wait_tok(q, ("d", q, i, self.dval[q][i]))
        ins = self.E[q].dma_start(out=out, in_=in_, **kw) if fn is None else fn()
        ins.then_inc(self.dsem[q][i], 16)
        self.dval[q][i] += 16
        self.ninst += 1
        self._commit(("d", q, i, self.dval[q][i]), reads, writes)
        return ins

    def barrier(self):
        for e in self.E:
            for e2, n in self.cnt.items():
                if n > 0:
                    self._wait_tok(e, ("c", e2, n))
            for q in self.dsem:
                for i, v in enumerate(self.dval[q]):
                    if v > 0:
                        self._wait_tok(e, ("d", q, i, v))


class Cfg:
    def __init__(s, **kw):
        s.__dict__.update(kw)
        s.INNER = s.H * s.PH
        s.HPG = s.H // s.G
        s.XBC = s.INNER + 2 * s.G * s.NS
        s.G5 = s.W5 // s.C5
        s.AW = 3 * s.HG * s.DH
        s.INW = s.INNER + s.XBC + 2 * s.H + s.W5 + 3 * s.AW + 3 * s.D
        s.o_z = 0
        s.o_xbc = s.INNER
        s.o_dt = s.o_xbc + s.XBC
        s.o_u = s.o_dt + 2 * s.H
        s.o_q = s.o_u + s.W5
        s.o_k = s.o_q + s.AW
        s.o_v = s.o_k + s.AW
        s.o_g = s.o_v + s.AW
        s.KC = s.D // 128
        s.NCH = s.NT // 128
        s.BND = s.NT // 2
        s.PADT = [max(1, -(-(h * d) // 128)) for (h, d) in s.PAT]


FULL = Cfg(D=2048, NT=8192, TG=2048, H=24, G=4, PH=64, NS=128, CONV=5, W5=1024, C5=16, P5=64,
           DH=128, HG=4, PAT=((64, 1), (64, 4), (64, 16)), NBK=32, MAXD=1024, DFF=5504, FTG=1024,
           DEPTH=2, EPS=1e-6, SC=512, S5C=512, ATG=2)


def t5_bucket(rel, nb, maxd):
    half = nb // 2
    exact = half // 2
    sign = (rel > 0).astype(np.int32) * half
    n = np.abs(rel)
    large = exact + (np.log(np.maximum(n, 1) / exact) / np.log(maxd / exact) * (half - exact)).astype(np.int32)
    large = np.minimum(large, half - 1)
    return sign + np.where(n < exact, n, large)


def host_consts(c):
    k = {}
    k["c_ident"] = np.eye(128, dtype=np.float32)
    s = np.arange(128)[:, None]
    l = np.arange(128)[None, :]
    k["c_triu"] = (s <= l).astype(np.float32)
    k["c_tril"] = (s >= l).astype(np.float32)
    k["c_ones"] = np.ones((128, 128), np.float32)
    k["c_iota"] = np.broadcast_to(np.arange(c.S5C, dtype=np.float32), (128, c.S5C)).copy()
    for p, (half, dil) in enumerate(c.PAT):
        padt = c.PADT[p]
        W = (2 * padt + 1) * 128
        nrel = W + 127
        rel = np.arange(nrel) - 127 - padt * 128
        ok = (rel % dil == 0) & (np.abs(rel) <= half * dil)
        bk = t5_bucket(rel, c.NBK, c.MAXD)
        oh = np.zeros((c.NBK + 1, nrel), np.float32)
        for r in range(nrel):
            if ok[r]:
                oh[bk[r], r] = 1.0
            else:
                oh[c.NBK, r] = 1.0
        k[f"c_oh{p}"] = np.ascontiguousarray(oh[:, ::-1])
    return k


class Builder:
    def __init__(self, c, ncores):
        self.c = c
        nc = bass.Bass("TRN2", target_bir_lowering=False)
        self.nc = nc
        self.P = Prog(nc)
        self.din = {}
        self.dbufs = {}

    def inp(self, name, shape, dt=F32):
        self.din[name] = self.nc.dram_tensor(name, list(shape), dt, kind="ExternalInput").ap()
        return self.din[name]

    def scratch(self, name, shape, dt):
        return self.nc.dram_tensor(name, list(shape), dt).ap()

    def db(self, key):
        if key not in self.dbufs:
            self.dbufs[key] = Buf(str(key))
        return self.dbufs[key]

    def V(self, fn, r=(), w=()):
        return self.P.op("dve", fn, r, w)

    def A(self, fn, r=(), w=()):
        return self.P.op("act", fn, r, w)

    def G(self, fn, r=(), w=()):
        return self.P.op("pool", fn, r, w)

    def T(self, fn, r=(), w=()):
        return self.P.op("pe", fn, r, w)

    def mm(self, out, pairs, r, w, start=True, stop=True):
        nc = self.nc

        def fn():
            ins = None
            n = len(pairs)
            for i, (lt, rh) in enumerate(pairs):
                ins = nc.tensor.matmul(out, lhsT=lt, rhs=rh, start=(start and i == 0), stop=(stop and i == n - 1))
            return ins
        return self.P.op("pe", fn, r, w)

    def sb(self, st, name, shape, dt):
        self.P.nalloc += 1
        h = st.enter_context(self.nc.sbuf_tensor(f"{name}_{self.P.nalloc}", list(shape), dt))
        return Tile(h.ap() if hasattr(h, "ap") and callable(h.ap) else h[:], Buf(name))

    def bcast_load(self, tile, src_row_ap, n):
        self.P.dma("sp", tile.ap, src_row_ap.partition_broadcast(128), writes=[tile])

    def norm_T(self, st, xsrc, xkey, gain, t0, ntok, hT, col0=0, scale_tile=None):
        c, nc, P = self.c, self.nc, self.P
        if not hasattr(self, "_nt"):
            self._nt = None
        nt = self._norm_tiles
        ntile = -(-ntok // 128)
        for ti in range(ntile):
            rows = min(128, ntok - ti * 128)
            xt = nt["x"][self._nti % 2]
            xn = nt["xn"][self._nti % 2]
            self._nti += 1
            r0 = t0 + ti * 128
            P.dma("sp", xt.ap[0:rows, :], xsrc[r0:r0 + rows, :], reads=[self.db((xkey, r0 // 128))], writes=[xt])
            sq, ss, rt, rs = nt["sq"], nt["ss"], nt["rt"], nt["rs"]
            self.A(lambda: nc.scalar.activation(out=sq.ap[0:rows, :], in_=xt.ap[0:rows, :], func=AF.Square), [xt], [sq])
            self.V(lambda: nc.vector.reduce_sum(out=ss.ap[0:rows, :], in_=sq.ap[0:rows, :], axis=AX.X), [sq], [ss])
            self.A(lambda: nc.scalar.activation(out=rt.ap[0:rows, :], in_=ss.ap[0:rows, :], func=AF.Sqrt,
                                                scale=1.0 / c.D, bias=nt["eps"].ap[0:rows, :]), [ss, nt["eps"]], [rt])
            self.V(lambda: nc.vector.reciprocal(out=rs.ap[0:rows, :], in_=rt.ap[0:rows, :]), [rt], [rs])
            if scale_tile is not None:
                self.V(lambda: nc.vector.tensor_tensor(out=rs.ap[0:rows, :], in0=rs.ap[0:rows, :],
                                                       in1=scale_tile.ap[0:rows, :], op=ALU.mult), [rs, scale_tile], [rs])
            self.V(lambda: nc.vector.scalar_tensor_tensor(out=xn.ap[0:rows, :], in0=xt.ap[0:rows, :], scalar=rs.ap[0:rows, 0:1],
                                                          in1=gain.ap[0:rows, :], op0=ALU.mult, op1=ALU.mult),
                   [xt, rs, gain], [xn])
            pst = self.pT
            def tr():
                ins = None
                for kc in range(c.KC):
                    ins = nc.tensor.transpose(pst.ap[:, kc * 128: kc * 128 + rows], xn.ap[0:rows, kc * 128:(kc + 1) * 128],
                                              self.identb.ap[0:rows, 0:rows])
                return ins
            self.T(tr, [xn, self.identb], [pst])
            src = pst.ap[:, 0:c.KC * 128].rearrange("p (k t) -> p k t", t=128)[:, :, 0:rows]
            dst = hT.ap[:, :, col0 + ti * 128: col0 + ti * 128 + rows]
            self.A(lambda: nc.scalar.copy(out=dst, in_=src), [pst], [hT])

    def norm_setup(self, st, nbuf=2):
        c = self.c
        self._norm_tiles = {
            "x": [self.sb(st, "nx", [128, c.D], F32) for _ in range(nbuf)] * (2 // nbuf),
            "xn": [self.sb(st, "nxn", [128, c.D], BF16) for _ in range(nbuf)] * (2 // nbuf),
            "sq": self.sb(st, "nsq", [128, c.D], F32),
            "ss": self.sb(st, "nss", [128, 1], F32),
            "rt": self.sb(st, "nrt", [128, 1], F32),
            "rs": self.sb(st, "nrs", [128, 1], F32),
            "eps": self.sb(st, "neps", [128, 1], F32),
        }
        self._nti = 0
        e = self._norm_tiles["eps"]
        self.V(lambda: self.nc.vector.memset(e.ap, c.EPS), [], [e])

    def wt_setup(self, st, kmax, cw, nbf=3):
        self._w = {"st": [self.sb(st, "wst", [128, kmax, cw], F32) for _ in range(2)],
                   "bf": [self.sb(st, "wbf", [128, kmax, cw], BF16) for _ in range(nbf)], "i": 0, "kmax": kmax, "cw": cw, "nbf": nbf}

    def load_w(self, Wl, k0, kn, c0, cn):
        nc, P = self.nc, self.P
        w = self._w
        ws, wb = w["st"][w["i"] % 2], w["bf"][w["i"] % w["nbf"]]
        w["i"] += 1
        src = Wl[k0 * 128:(k0 + kn) * 128, c0:c0 + cn].rearrange("(k p) n -> p k n", p=128)
        P.dma("sp", ws.ap[:, 0:kn, 0:cn], src, writes=[ws])
        self.G(lambda: nc.gpsimd.tensor_copy(out=wb.ap[:, 0:kn, 0:cn], in_=ws.ap[:, 0:kn, 0:cn]), [ws], [wb])
        return wb

    def dense_feat(self, actT, KC, ntok, Wl, c0, ncols, evac, tokoff=0):
        nc = self.nc
        cw = self._w["cw"]
        ntt = -(-ntok // 512)
        chunks = [(cc, min(cw, c0 + ncols - cc)) for cc in range(c0, c0 + ncols, cw)]
        depth = 2 if self._w["nbf"] >= 4 else 1
        q = [self.load_w(Wl, 0, KC, ch[0], ch[1]) for ch in chunks[:depth]]
        for ci, (cc, cn) in enumerate(chunks):
            wb = q.pop(0)
            if ci + depth < len(chunks):
                q.append(self.load_w(Wl, 0, KC, chunks[ci + depth][0], chunks[ci + depth][1]))
            for b0 in range(0, cn, 128):
                nb = min(128, cn - b0)
                for tt in range(ntt):
                    tw = min(512, ntok - tt * 512)
                    bank = self._bank % 8
                    self._bank += 1
                    pt = self.psb[bank]
                    pairs = [(wb.ap[:, kc, b0:b0 + nb], actT.ap[:, kc, tokoff + tt * 512: tokoff + tt * 512 + tw]) for kc in range(KC)]
                    self.mm(pt.ap[0:nb, 0:tw], pairs, [wb, actT], self.pbufs(bank))
                    evac(pt.ap[0:nb, 0:tw], cc + b0, nb, tt * 512, tw, self.pbufs(bank))

    def dense_tok(self, actT, KC, ntok, Wl, c0, ncols, CW, evac, tokoff=0):
        ntile = -(-ntok // 128)
        kmax = self._w["kmax"]
        assert CW <= self._w["cw"]
        per_bank = 512 // CW
        assert ntile <= 4 * per_bank, (ntile, CW)
        kgs = list(range(0, KC, kmax))
        groups = [(cc, min(CW, c0 + ncols - cc)) for cc in range(c0, c0 + ncols, CW)]
        prefetch = self._w["nbf"] >= 2 * len(kgs)
        assert self._w["nbf"] >= len(kgs)
        ldg = lambda g: [(kg, min(kmax, KC - kg), self.load_w(Wl, kg, min(kmax, KC - kg), g[0], g[1])) for kg in kgs]
        nxt = ldg(groups[0])
        for gi, (cc, cn) in enumerate(groups):
            base = (self._bank % 2) * 4
            self._bank += 1
            wbs = nxt if (prefetch or gi == 0) else ldg(groups[gi])
            if prefetch and gi + 1 < len(groups):
                nxt = ldg(groups[gi + 1])
            for ti in range(ntile):
                rows = min(128, ntok - ti * 128)
                bank = base + ti // per_bank
                off = (ti % per_bank) * CW
                pt = self.psb[bank]
                pairs = []
                for (kg, kn, wb) in wbs:
                    pairs += [(actT.ap[:, kg + k, tokoff + ti * 128: tokoff + ti * 128 + rows], wb.ap[:, k, 0:cn]) for k in range(kn)]
                hb = self.phalf[bank]
                self.mm(pt.ap[0:rows, off:off + cn], pairs, [w_[2] for w_ in wbs] + [actT], hb)
                evac(pt.ap[0:rows, off:off + cn], ti * 128, rows, cc, cn, hb)

    def pbufs(self, bank):
        return self.phalf[bank]


import contextlib


def weight_specs(c):
    L = c.DEPTH
    return [
        ("norm_mix", [L, c.D]), ("w_in", [L, c.D, c.INW]), ("ssm_cw", [L, c.XBC, 8]),
        ("ssm_a_log", [L, 2 * c.H]), ("ssm_dt_bias", [L, 2 * c.H]), ("ssm_dcol", [L, 128, c.INNER // 128]),
        ("ssm_ng", [L, 128, c.INNER // 128]), ("ssm_w_out", [L, c.INNER, c.D]),
        ("s5_par", [L, 2, 128, c.TS, 3]), ("s5_b", [L, 128, c.TS, 2, c.C5]), ("s5_c", [L, 2, 128, c.TS, 2, c.C5]),
        ("s5_dcol", [L, 128, c.W5 // 128]), ("s5_w_glu", [L, c.W5, 2 * c.D]), ("att_w_out", [L, c.HG * c.DH, c.D]),
        ("w_o", [L, c.D, c.D]), ("norm_ffn", [L, c.D]), ("w_up", [L, c.D, 2 * c.DFF]),
        ("ffn_cw", [L, 128, 2 * c.DFF // 128, 4]), ("w_down", [L, c.DFF, c.D]), ("final_norm", [1, c.D]),
        ("rel_bias", [c.NBK, 3 * c.HG]),
    ]


def build(c, stop_after=None, dbg=()):
    c.TS = c.G5 * c.P5 // 128
    B = Builder(c, 1)
    nc, P = B.nc, B.P
    x_in = B.inp("x", [c.NT, c.D])
    y_out = nc.dram_tensor("y", [c.NT, c.D], F32, kind="ExternalOutput").ap()
    link = B.inp("link", [128, 1])
    nlink = B.inp("nlink", [128, 1])
    W = {n: B.inp(n, s) for n, s in weight_specs(c)}
    K = {n: B.inp(n, list(v.shape)) for n, v in host_consts(c).items()}
    S = {
        "xres": B.scratch("xres", [c.NT, c.D], F32),
        "xmid": B.scratch("xmid", [c.NT, c.D], F32),
        "zT": B.scratch("zT", [c.INNER, c.NT], BF16),
        "xbcT": B.scratch("xbcT", [c.XBC, c.NT], BF16),
        "dt": B.scratch("dt_tok", [c.NT, 2 * c.H], F32),
        "uT": B.scratch("uT", [c.W5, c.NT], BF16),
        "qT": B.scratch("qT", [c.AW, c.NT], BF16),
        "kT": B.scratch("kT", [c.AW, c.NT], BF16),
        "v": B.scratch("v_tok", [c.NT, c.AW], BF16),
        "gT": B.scratch("gT", [3 * c.D, c.NT], BF16),
        "ynT": B.scratch("ynT", [c.INNER, c.NT], BF16),
        "s5T": B.scratch("s5T", [c.W5, c.NT], BF16),
        "attT": B.scratch("attT", [c.HG * c.DH, c.NT], BF16),
        "hin": B.scratch("hin", [2, c.NCH, 128, c.INNER], BF16),
    }
    B.S, B.W, B.K = S, W, K
    B.stop_after = stop_after

    psall = nc.alloc_psum_tensor("psall", [128, 8, 512], F32).ap()
    B.phalf = [[b_, b_] for b_ in [Buf(f"ps{i}", excl=True) for i in range(8)]]
    B.psb = [Tile(psall[:, i, :], B.phalf[i]) for i in range(8)]
    B.pT = Tile(psall[:, 6:8, :].bitcast(BF16).rearrange("p a b -> p (a b)"), B.phalf[6] + B.phalf[7])
    B._bank = 0
    gst = contextlib.ExitStack()
    identf = B.sb(gst, "identf", [128, 128], F32)
    B.identb = B.sb(gst, "identb", [128, 128], BF16)
    onesf = B.sb(gst, "onesf", [128, 128], F32)
    onesb = B.sb(gst, "onesb", [128, 128], BF16)
    triu = B.sb(gst, "triu", [128, 128], F32)
    tril = B.sb(gst, "tril", [128, 128], F32)
    linkt = B.sb(gst, "linkt", [128, 1], F32)
    nlinkt = B.sb(gst, "nlinkt", [128, 1], F32)
    B.identf, B.onesf, B.onesb, B.triu, B.tril, B.linkt, B.nlinkt = identf, onesf, onesb, triu, tril, linkt, nlinkt
    P.dma("sp", identf.ap, K["c_ident"], writes=[identf])
    P.dma("sp", onesf.ap, K["c_ones"], writes=[onesf])
    P.dma("sp", triu.ap, K["c_triu"], writes=[triu])
    P.dma("sp", tril.ap, K["c_tril"], writes=[tril])
    P.dma("sp", linkt.ap, link, writes=[linkt])
    P.dma("sp", nlinkt.ap, nlink, writes=[nlinkt])
    B.V(lambda: nc.vector.tensor_copy(out=B.identb.ap, in_=identf.ap), [identf], [B.identb])
    B.V(lambda: nc.vector.tensor_copy(out=onesb.ap, in_=onesf.ap), [onesf], [onesb])

    class PTile(Tile):
        pass
    B.pT_bufs = B.phalf[6] + B.phalf[7]

    attn_setup(B)
    for l in range(c.DEPTH):
        xsrc, xkey = (x_in, "xin") if l == 0 else (S["xres"], "xres")
        phase_inproj(B, l, xsrc, xkey)
        if stop_after == ("inproj", l):
            break
        phase_ssd(B, l)
        if stop_after in (("ssd", l), ("ssd0", l), ("ssdA", l)):
            break
        phase_s5(B, l)
        if stop_after == ("s5", l):
            break
        phase_attn(B, l)
        if stop_after == ("attn", l):
            break
        phase_merge(B, l, xsrc, xkey)
        if stop_after == ("merge", l):
            break
        phase_ffn(B, l)
        if stop_after == ("ffn", l):
            break
    else:
        phase_final(B, y_out)
    for name in dbg:
        src = S[name]
        o = nc.dram_tensor("dbg_" + name, list(src.shape), src.dtype, kind="ExternalOutput").ap()
        P.dma("sp", o, src)
    P.barrier()
    gst.close()
    return nc, B


def phase_inproj(B, l, xsrc, xkey):
    c, nc, P, S, W = B.c, B.nc, B.P, B.S, B.W
    with contextlib.ExitStack() as st:
        gain = B.sb(st, "gain", [128, c.D], F32)
        B.bcast_load(gain, W["norm_mix"][l:l + 1, :], c.D)
        B.norm_setup(st)
        hT = B.sb(st, "hT", [128, c.KC, c.TG], BF16)
        B.wt_setup(st, c.KC, 256, nbf=4)
        stg = [B.sb(st, "stg", [128, c.TG], BF16) for _ in range(3)]
        ntile = c.TG // 128
        vst = [B.sb(st, "vst", [128, ntile, 128], BF16) for _ in range(2)]
        dst_ = B.sb(st, "dtst", [128, ntile, 2 * c.H], F32)
        segs = [("zT", c.o_z, c.INNER), ("xbcT", c.o_xbc, c.XBC), ("uT", c.o_u, c.W5), ("qT", c.o_q, c.AW),
                ("kT", c.o_k, c.AW), ("gT", c.o_g, 3 * c.D)]
        cnt = {"blk": 0, "ev": 0, "v": 0}
        ntt = -(-c.TG // 512)
        for tg in range(c.NT // c.TG):
            t0 = tg * c.TG
            B.norm_T(st, xsrc, xkey, gain, t0, c.TG, hT)
            for (dname, c0, n) in segs:
                def evac(ps, col_abs, nb, tok0, tw, pb, dname=dname, c0=c0):
                    sg = stg[cnt["blk"] % 3]
                    cnt["ev"] += 1
                    if cnt["ev"] % 2 == 0:
                        B.A(lambda: nc.scalar.copy(out=sg.ap[0:nb, tok0:tok0 + tw], in_=ps), pb, [sg])
                    else:
                        B.V(lambda: nc.vector.tensor_copy(out=sg.ap[0:nb, tok0:tok0 + tw], in_=ps), pb, [sg])
                    if tok0 + tw >= c.TG:
                        P.dma("sp", S[dname][col_abs - c0: col_abs - c0 + nb, t0:t0 + c.TG], sg.ap[0:nb, :], reads=[sg])
                        cnt["blk"] += 1
                B.dense_feat(hT, c.KC, c.TG, W["w_in"][l], c0, n, evac)

            def evac_dt(ps, tok0, rows, col_abs, cw, pb):
                ti = tok0 // 128
                B.V(lambda: nc.vector.tensor_copy(out=dst_.ap[0:rows, ti, :], in_=ps), pb, [dst_])
                if ti == ntile - 1:
                    P.dma("sp", S["dt"][t0:t0 + c.TG, :].rearrange("(t p) n -> p t n", p=128), dst_.ap, reads=[dst_])
            B.dense_tok(hT, c.KC, c.TG, W["w_in"][l], c.o_dt, 2 * c.H, 2 * c.H, evac_dt)

            def evac_v(ps, tok0, rows, col_abs, cw, pb):
                ti = tok0 // 128
                vs = vst[cnt["v"] % 2]
                B.A(lambda: nc.scalar.copy(out=vs.ap[0:rows, ti, 0:cw], in_=ps), pb, [vs])
                if ti == ntile - 1:
                    P.dma("sp", S["v"][t0:t0 + c.TG, col_abs - c.o_v: col_abs - c.o_v + cw].rearrange("(t p) n -> p t n", p=128),
                          vs.ap[:, :, 0:cw], reads=[vs])
                    cnt["v"] += 1
            B.dense_tok(hT, c.KC, c.TG, W["w_in"][l], c.o_v, c.AW, 128, evac_v)
        P.barrier()


def layout_weights(c, I):
    L = c.DEPTH
    TS = c.G5 * c.P5 // 128
    f = lambda a: np.ascontiguousarray(np.asarray(a, dtype=np.float32))
    o = {}
    o["norm_mix"] = f(I["norm_mix"])
    o["w_in"] = f(I["w_in"])
    cw = np.zeros((L, c.XBC, 8), np.float32)
    cw[:, :, 0:c.CONV] = np.asarray(I["ssm_conv_w"]).transpose(0, 2, 1)
    cw[:, :, 5] = np.asarray(I["ssm_conv_b"])
    o["ssm_cw"] = cw
    o["ssm_a_log"] = f(np.asarray(I["ssm_a_log"]).reshape(L, 2 * c.H))
    o["ssm_dt_bias"] = f(np.asarray(I["ssm_dt_bias"]).reshape(L, 2 * c.H))
    dcol = np.repeat(np.asarray(I["ssm_d"]), c.PH, axis=1)
    o["ssm_dcol"] = f(dcol.reshape(L, c.INNER // 128, 128).transpose(0, 2, 1))
    o["ssm_ng"] = f(np.asarray(I["ssm_norm"]).reshape(L, c.INNER // 128, 128).transpose(0, 2, 1))
    o["ssm_w_out"] = f(I["ssm_w_out"])
    st = lambda a: np.asarray(a).reshape(L, 2, TS, 128).transpose(0, 1, 3, 2)
    ls = np.broadcast_to(np.asarray(I["s5_log_step"])[..., None], (L, 2, c.G5, c.P5))
    o["s5_par"] = f(np.stack([st(I["s5_a_re"]), st(I["s5_a_im"]), st(ls)], axis=-1))
    sb_ = lambda b: np.asarray(b).reshape(L, TS, 128, c.C5).transpose(0, 2, 1, 3)
    o["s5_b"] = f(np.stack([sb_(I["s5_b_re"]), sb_(I["s5_b_im"])], axis=3))
    sc_ = lambda cc: np.asarray(cc).transpose(0, 1, 2, 4, 3).reshape(L, 2, TS, 128, c.C5).transpose(0, 1, 3, 2, 4)
    o["s5_c"] = f(np.stack([sc_(I["s5_c_re"]), sc_(I["s5_c_im"])], axis=4))
    o["s5_dcol"] = f(np.asarray(I["s5_d"]).reshape(L, c.W5 // 128, 128).transpose(0, 2, 1))
    o["s5_w_glu"] = f(I["s5_w_glu"])
    o["att_w_out"] = f(I["att_w_out"])
    o["w_o"] = f(I["w_o"])
    o["norm_ffn"] = f(I["norm_ffn"])
    o["w_up"] = f(I["w_up"])
    fc = np.concatenate([np.asarray(I["ffn_conv_w"]).transpose(0, 2, 1), np.asarray(I["ffn_conv_b"])[:, :, None]], axis=2)
    o["ffn_cw"] = f(fc.reshape(L, 2 * c.DFF // 128, 128, 4).transpose(0, 2, 1, 3))
    o["w_down"] = f(I["w_down"])
    o["final_norm"] = f(np.asarray(I["final_norm"]).reshape(1, c.D))
    o["rel_bias"] = f(I["rel_bias"])
    return o


def core_inputs(c, wl, consts, x_stream, link):
    m = dict(wl)
    m.update(consts)
    m["x"] = np.ascontiguousarray(x_stream, dtype=np.float32)
    m["link"] = np.full((128, 1), float(link), np.float32)
    m["nlink"] = np.full((128, 1), 0.0 if link else NEG, np.float32)
    return m


def _bc(ap, shape):
    return ap.broadcast_to(list(shape))


def phase_ssd(B, l):
    c, nc, P, S, W = B.c, B.nc, B.P, B.S, B.W
    CT = c.XBC // 128
    TI = c.INNER // 128
    H, PH, G, HPG, NS = c.H, c.PH, c.G, c.HPG, c.NS
    BND = c.BND
    SC = c.SC
    CPS = SC // 128
    NSC = c.NT // SC
    with contextlib.ExitStack() as st:
        cw = B.sb(st, "cw", [128, CT, 8], F32)
        P.dma("sp", cw.ap, W["ssm_cw"][l].rearrange("(t p) k -> p t k", p=128), writes=[cw])
        xps = [B.sb(st, "xp", [128, 2, BND + 4], BF16) for _ in range(2)]
        accs = [B.sb(st, "acc", [128, 2, BND], F32) for _ in range(2)]
        xos = [B.sb(st, "xo", [128, 2, BND], BF16) for _ in range(2)]
        for xp in xps:
            B.V(lambda: nc.vector.memset(xp.ap, 0.0), [], [xp])
        for ct in range(CT):
            xp, acc, xo = xps[ct % 2], accs[ct % 2], xos[ct % 2]
            rows = S["xbcT"][ct * 128:(ct + 1) * 128, :]
            P.dma("sp", xp.ap[:, :, 2:2 + BND], rows.rearrange("p (h t) -> p h t", h=2), writes=[xp])
            B.V(lambda: nc.vector.tensor_scalar(out=xp.ap[:, 0, BND + 2:BND + 4], in0=xp.ap[:, 1, 2:4], scalar1=B.linkt.ap[:, 0:1],
                                                scalar2=None, op0=ALU.mult), [xp, B.linkt], [xp])
            B.V(lambda: nc.vector.tensor_scalar(out=xp.ap[:, 1, 0:2], in0=xp.ap[:, 0, BND:BND + 2], scalar1=B.linkt.ap[:, 0:1],
                                                scalar2=None, op0=ALU.mult), [xp, B.linkt], [xp])
            B.V(lambda: nc.vector.tensor_scalar(out=acc.ap, in0=xp.ap[:, :, 0:BND], scalar1=cw.ap[:, ct, 0:1], scalar2=cw.ap[:, ct, 5:6],
                                                op0=ALU.mult, op1=ALU.add), [xp, cw], [acc])
            for k in range(1, c.CONV):
                B.V(lambda k=k: nc.vector.scalar_tensor_tensor(out=acc.ap, in0=xp.ap[:, :, k:k + BND], scalar=cw.ap[:, ct, k:k + 1],
                                                               in1=acc.ap, op0=ALU.mult, op1=ALU.add), [xp, cw, acc], [acc])
            B.A(lambda: nc.scalar.activation(out=xo.ap, in_=acc.ap, func=AF.Silu), [acc], [xo])
            P.dma("sp", rows.rearrange("p (h t) -> p h t", h=2), xo.ap, reads=[xo])
        P.barrier()

    if B.stop_after == ("ssd0", l):
        return

    def dt_prep(st):
        t = {}
        t["bias"] = B.sb(st, "dtb", [128, 2 * H], F32)
        t["A"] = B.sb(st, "Abc", [128, 2 * H], F32)
        B.bcast_load(t["bias"], W["ssm_dt_bias"][l:l + 1, :], 2 * H)
        B.bcast_load(t["A"], W["ssm_a_log"][l:l + 1, :], 2 * H)
        A_ = t["A"]
        B.A(lambda: nc.scalar.activation(out=A_.ap, in_=A_.ap, func=AF.Exp), [A_], [A_])
        B.V(lambda: nc.vector.tensor_scalar(out=A_.ap, in0=A_.ap, scalar1=-1.0, scalar2=None, op0=ALU.mult), [A_], [A_])
        for n in ("raw", "t", "a", "e", "dtv", "av"):
            t[n] = B.sb(st, "dt" + n, [128, CPS, 2 * H], F32)
        return t

    def dt_compute(t, sc):
        raw, tt, a, e, dtv, av = t["raw"], t["t"], t["a"], t["e"], t["dtv"], t["av"]
        P.dma("sp", raw.ap, S["dt"][sc * SC:(sc + 1) * SC, :].rearrange("(k p) n -> p k n", p=128), writes=[raw])
        bb = _bc(t["bias"].ap.unsqueeze(1), [128, CPS, 2 * H])
        B.V(lambda: nc.vector.tensor_tensor(out=tt.ap, in0=raw.ap, in1=bb, op=ALU.add), [raw, t["bias"]], [tt])
        B.A(lambda: nc.scalar.activation(out=a.ap, in_=tt.ap, func=AF.Abs), [tt], [a])
        B.A(lambda: nc.scalar.activation(out=e.ap, in_=a.ap, func=AF.Exp, scale=-1.0), [a], [e])
        B.A(lambda: nc.scalar.activation(out=e.ap, in_=e.ap, func=AF.Ln, bias=1.0), [e], [e])
        B.V(lambda: nc.vector.tensor_scalar(out=a.ap, in0=tt.ap, scalar1=0.0, scalar2=None, op0=ALU.max), [tt], [a])
        B.V(lambda: nc.vector.tensor_tensor(out=dtv.ap, in0=a.ap, in1=e.ap, op=ALU.add), [a, e], [dtv])
        ab = _bc(t["A"].ap.unsqueeze(1), [128, CPS, 2 * H])
        B.V(lambda: nc.vector.tensor_tensor(out=av.ap, in0=dtv.ap, in1=ab, op=ALU.mult), [dtv, t["A"]], [av])

    def load_xc(xc, sc):
        P.dma("sp", xc.ap, S["xbcT"][:, sc * SC:(sc + 1) * SC].rearrange("(t p) n -> p t n", p=128), writes=[xc])

    for d in range(2):
        with contextlib.ExitStack() as st:
            t = dt_prep(st)
            xcs = [B.sb(st, "xc", [128, CT, SC], BF16) for _ in range(2)]
            xsB = B.sb(st, "xsB", [128, c.INNER + G * NS], BF16)
            acs = B.sb(st, "acs", [128, H], F32)
            dte = B.sb(st, "dte", [128, H], F32)
            coef = B.sb(st, "coef", [128, H], F32)
            dec = B.sb(st, "dec", [128, H], F32)
            xdd = B.sb(st, "xdd", [128, c.INNER], BF16)
            Hst = B.sb(st, "Hst", [128, c.INNER], F32)
            hstg = [B.sb(st, "hstg", [128, c.INNER], BF16) for _ in range(2)]
            B.V(lambda: nc.vector.memset(Hst.ap, 0.0), [], [Hst])
            tri = B.triu if d == 0 else B.tril
            scs = list(range(NSC)) if d == 0 else list(range(NSC - 1, -1, -1))
            bnd_chunk = c.NCH // 2 if d == 0 else c.NCH // 2 - 1
            psS = [B.psb[0], B.psb[1], B.psb[2]]
            pm = B.psb[3]
            for si, sc in enumerate(scs):
                xc = xcs[si % 2]
                load_xc(xc, sc)
                dt_compute(t, sc)
                cks = list(range(CPS)) if d == 0 else list(range(CPS - 1, -1, -1))
                for ck in cks:
                    gck = sc * CPS + ck
                    tk = slice(ck * 128, (ck + 1) * 128)

                    def tr():
                        ins = None
                        for i in range(TI):
                            ins = nc.tensor.transpose(B.pT.ap[:, i * 128:(i + 1) * 128], xc.ap[:, i, tk], B.identb.ap)
                        for g in range(G):
                            ins = nc.tensor.transpose(B.pT.ap[:, (TI + g) * 128:(TI + g + 1) * 128], xc.ap[:, TI + g, tk], B.identb.ap)
                        return ins
                    B.T(tr, [xc, B.identb], [B.pT])
                    B.A(lambda: nc.scalar.copy(out=xsB.ap, in_=B.pT.ap[:, 0:c.INNER + G * NS]), [B.pT], [xsB])
                    a_d = t["av"].ap[:, ck, d * H:(d + 1) * H]
                    B.mm(pm.ap[:, 0:H], [(tri.ap, a_d)], [tri, t["av"]], [B.phalf[3][0]])
                    B.mm(pm.ap[:, 256:256 + H], [(B.onesf.ap, a_d)], [B.onesf, t["av"]], [B.phalf[3][1]])
                    B.A(lambda: nc.scalar.copy(out=acs.ap, in_=pm.ap[:, 0:H]), [B.phalf[3][0]], [acs])
                    B.V(lambda: nc.vector.tensor_tensor(out=dte.ap, in0=pm.ap[:, 256:256 + H], in1=acs.ap, op=ALU.subtract),
                        [B.phalf[3][1], acs], [dte])
                    B.A(lambda: nc.scalar.activation(out=dte.ap, in_=dte.ap, func=AF.Exp), [dte], [dte])
                    B.A(lambda: nc.scalar.activation(out=dec.ap, in_=pm.ap[:, 256:256 + H], func=AF.Exp), [B.phalf[3][1]], [dec])
                    B.V(lambda: nc.vector.tensor_tensor(out=coef.ap, in0=t["dtv"].ap[:, ck, d * H:(d + 1) * H], in1=dte.ap, op=ALU.mult),
                        [t["dtv"], dte], [coef])
                    B.V(lambda: nc.vector.tensor_tensor(out=xdd.ap.rearrange("p (h q) -> p h q", q=PH),
                                                        in0=xsB.ap[:, 0:c.INNER].rearrange("p (h q) -> p h q", q=PH),
                                                        in1=_bc(coef.ap.unsqueeze(2), [128, H, PH]), op=ALU.mult), [xsB, coef], [xdd])
                    for g in range(G):
                        c0 = g * HPG * PH
                        c1 = c0 + HPG * PH
                        p0 = c0
                        while p0 < c1:
                            p1 = min(c1, (p0 // 512 + 1) * 512)
                            bk = p0 // 512
                            B.mm(psS[bk].ap[:, p0 - bk * 512:p1 - bk * 512],
                                 [(xsB.ap[:, c.INNER + g * NS: c.INNER + (g + 1) * NS], xdd.ap[:, p0:p1])], [xsB, xdd], B.phalf[bk])
                            p0 = p1
                    if gck == bnd_chunk:
                        B.V(lambda: nc.vector.tensor_scalar(out=Hst.ap, in0=Hst.ap, scalar1=B.linkt.ap[:, 0:1], scalar2=None, op0=ALU.mult),
                            [Hst, B.linkt], [Hst])
                    hs = hstg[gck % 2]
                    B.A(lambda: nc.scalar.copy(out=hs.ap, in_=Hst.ap), [Hst], [hs])
                    P.dma("sp", S["hin"][d, gck], hs.ap, reads=[hs])
                    B.V(lambda: nc.vector.tensor_tensor(out=Hst.ap.rearrange("p (h q) -> p h q", q=PH),
                                                        in0=Hst.ap.rearrange("p (h q) -> p h q", q=PH),
                                                        in1=_bc(dec.ap.unsqueeze(2), [128, H, PH]), op=ALU.mult), [Hst, dec], [Hst])
                    nb = -(-c.INNER // 512)
                    for bk in range(nb):
                        w_ = min(512, c.INNER - bk * 512)
                        B.V(lambda bk=bk, w_=w_: nc.vector.tensor_tensor(out=Hst.ap[:, bk * 512:bk * 512 + w_], in0=psS[bk].ap[:, 0:w_],
                                                                         in1=Hst.ap[:, bk * 512:bk * 512 + w_], op=ALU.add),
                            B.phalf[bk] + [Hst], [Hst])
            P.barrier()

    if B.stop_after == ("ssdA", l):
        return
    HB = 3 if HPG % 3 == 0 else (2 if HPG % 2 == 0 else 1)
    with contextlib.ExitStack() as st:
        t = dt_prep(st)
        xcs = [B.sb(st, "xc", [128, CT, SC], BF16) for _ in range(2)]
        zts = [B.sb(st, "zt", [128, TI, SC], BF16) for _ in range(2)]
        yns = [B.sb(st, "ynst", [128, TI, SC], BF16) for _ in range(2)]
        dcol = B.sb(st, "dcol", [128, TI], F32)
        ng = B.sb(st, "ng", [128, TI], F32)
        P.dma("sp", dcol.ap, W["ssm_dcol"][l], writes=[dcol])
        P.dma("sp", ng.ap, W["ssm_ng"][l], writes=[ng])
        mneg = [B.sb(st, "mneg", [128, 128], F32) for _ in range(2)]
        for d, tri in enumerate((B.triu, B.tril)):
            B.V(lambda d=d, tri=tri: nc.vector.tensor_scalar(out=mneg[d].ap, in0=tri.ap, scalar1=-1.0, scalar2=-NEG, op0=ALU.add, op1=ALU.mult),
                [tri], [mneg[d]])
        epst = B.sb(st, "epst", [128, 1], F32)
        B.V(lambda: nc.vector.memset(epst.ap, c.EPS), [], [epst])
        xs_tok = B.sb(st, "xstok", [128, c.INNER], BF16)
        xd = [B.sb(st, "xd", [128, c.INNER], BF16) for _ in range(2)]
        acs2 = B.sb(st, "acs2", [128, 2 * H], F32)
        cbt = B.sb(st, "cbt", [128, G, 128], F32)
        hin = [[B.sb(st, "hin", [128, c.INNER], BF16) for _ in range(2)] for _ in range(2)]
        MT = [B.sb(st, "MT", [128, H, 128], BF16) for _ in range(2)]
        CsT = [B.sb(st, "CsT", [128, H, 128], BF16) for _ in range(2)]
        amask = [B.sb(st, "amask", [128, HB, 128], F32) for _ in range(2)]
        expA = [B.sb(st, "expA", [128, HB, 128], F32) for _ in range(2)]
        T1 = [B.sb(st, "T1", [128, HB, 128], F32) for _ in range(2)]
        yv = B.sb(st, "yv", [128, TI, 128], F32)
        sz = B.sb(st, "sz", [128, TI, 128], F32)
        sq = B.sb(st, "sq", [128, TI, 128], BF16)
        rt = B.sb(st, "rt", [128, 128], F32)
        psy = [B.psb[1], B.psb[2], B.psb[3]]
        pbc = [B.psb[0], B.psb[4]]
        pcb = B.psb[5]
        pmisc = B.psb[7]
        pmb = [B.phalf[7][1]]
        it = 0
        for sc in range(NSC):
            xc, zt, ynst = xcs[sc % 2], zts[sc % 2], yns[sc % 2]
            load_xc(xc, sc)
            P.dma("sp", zt.ap, S["zT"][:, sc * SC:(sc + 1) * SC].rearrange("(t p) n -> p t n", p=128), writes=[zt])
            dt_compute(t, sc)
            for ck in range(CPS):
                gck = sc * CPS + ck
                tk = slice(ck * 128, (ck + 1) * 128)

                def tr():
                    ins = None
                    for i in range(TI):
                        ins = nc.tensor.transpose(B.pT.ap[:, i * 128:(i + 1) * 128], xc.ap[:, i, tk], B.identb.ap)
                    return ins
                B.T(tr, [xc, B.identb], [B.phalf[6][0], B.phalf[6][1], B.phalf[7][0]])
                B.A(lambda: nc.scalar.copy(out=xs_tok.ap, in_=B.pT.ap[:, 0:c.INNER]), [B.phalf[6][0], B.phalf[6][1], B.phalf[7][0]], [xs_tok])
                for d in range(2):
                    B.V(lambda d=d: nc.vector.tensor_tensor(out=xd[d].ap.rearrange("p (h q) -> p h q", q=PH),
                                                            in0=xs_tok.ap.rearrange("p (h q) -> p h q", q=PH),
                                                            in1=_bc(t["dtv"].ap[:, ck, d * H:(d + 1) * H].unsqueeze(2), [128, H, PH]), op=ALU.mult),
                        [xs_tok, t["dtv"]], [xd[d]])
                    P.dma("sp", hin[d][gck % 2].ap, S["hin"][d, gck], writes=[hin[d][gck % 2]])
                B.mm(pmisc.ap[:, 256:256 + H], [(B.triu.ap, t["av"].ap[:, ck, 0:H])], [B.triu, t["av"]], pmb)
                B.mm(pmisc.ap[:, 256 + H:256 + 2 * H], [(B.tril.ap, t["av"].ap[:, ck, H:2 * H])], [B.tril, t["av"]], pmb)
                B.A(lambda: nc.scalar.copy(out=acs2.ap, in_=pmisc.ap[:, 256:256 + 2 * H]), pmb, [acs2])
                for g in range(G):
                    B.mm(pcb.ap[:, g * 128:(g + 1) * 128], [(xc.ap[:, TI + g, tk], xc.ap[:, TI + G + g, tk])], [xc], B.phalf[5])
                B.A(lambda: nc.scalar.copy(out=cbt.ap.rearrange("p g l -> p (g l)"), in_=pcb.ap[:, 0:G * 128]), B.phalf[5], [cbt])
                import os as _os
                for d in range(2):
                    if _os.environ.get("SSD_SKIP") == "3":
                        continue
                    tri = B.triu if d == 0 else B.tril
                    for hb in range(H // HB):
                        h0 = hb * HB
                        g = h0 // HPG
                        i2 = it % 2
                        it += 1
                        am, ea, t1, pb = amask[i2], expA[i2], T1[i2], pbc[i2]
                        pbb = B.phalf[0] if i2 == 0 else B.phalf[4]
                        a_sl = t["av"].ap[:, ck, d * H + h0: d * H + h0 + HB]
                        B.V(lambda: nc.vector.tensor_tensor(out=am.ap, in0=_bc(tri.ap.unsqueeze(1), [128, HB, 128]),
                                                            in1=_bc(a_sl.unsqueeze(2), [128, HB, 128]), op=ALU.mult), [tri, t["av"]], [am])
                        _lv = _os.environ.get("SSD_SKIP")
                        if _lv == "5":
                            continue
                        B.mm(pb.ap[:, 0:HB * 128], [(B.onesf.ap, am.ap.rearrange("p h l -> p (h l)"))], [B.onesf, am], pbb)
                        if _lv == "6":
                            continue
                        pv = pb.ap[:, 0:HB * 128].rearrange("p (h l) -> p h l", l=128)
                        B.A(lambda: nc.scalar.activation(out=ea.ap, in_=pv, func=AF.Exp), pbb, [ea])
                        if _lv == "7":
                            continue
                        B.V(lambda: nc.vector.tensor_tensor(out=t1.ap, in0=pv, in1=_bc(mneg[d].ap.unsqueeze(1), [128, HB, 128]), op=ALU.add),
                            pbb + [mneg[d]], [t1])
                        if _lv == "8":
                            continue
                        B.V(lambda: nc.vector.tensor_tensor(out=t1.ap, in0=t1.ap,
                                                            in1=_bc(acs2.ap[:, d * H + h0:d * H + h0 + HB].unsqueeze(2), [128, HB, 128]),
                                                            op=ALU.subtract), [t1, acs2], [t1])
                        B.A(lambda: nc.scalar.activation(out=t1.ap, in_=t1.ap, func=AF.Exp), [t1], [t1])
                        if _lv == "9":
                            continue
                        B.V(lambda: nc.vector.tensor_tensor(out=MT[d].ap[:, h0:h0 + HB, :], in0=t1.ap,
                                                            in1=_bc(cbt.ap[:, g:g + 1, :], [128, HB, 128]), op=ALU.mult), [t1, cbt], [MT[d]])
                        if _os.environ.get("SSD_SKIP") == "4":
                            continue
                        B.G(lambda: nc.gpsimd.tensor_tensor(out=CsT[d].ap[:, h0:h0 + HB, :], in0=ea.ap,
                                                            in1=_bc(xc.ap[:, TI + G + g, tk].unsqueeze(1), [128, HB, 128]), op=ALU.mult),
                            [ea, xc], [CsT[d]])
                if _os.environ.get("SSD_SKIP") in ("1", "3", "4", "5", "6", "7", "8", "9"):
                    continue
                hf, hb_ = hin[0][gck % 2], hin[1][gck % 2]
                for h in range(H):
                    tl = (h * PH) // 128
                    po = (h * PH) % 128
                    bk = (tl * 128) // 512
                    co = tl * 128 - bk * 512
                    hs = slice(h * PH, (h + 1) * PH)
                    B.mm(psy[bk].ap[po:po + PH, co:co + 128],
                         [(xd[0].ap[:, hs], MT[0].ap[:, h, :]), (xd[1].ap[:, hs], MT[1].ap[:, h, :]),
                          (hf.ap[:, hs], CsT[0].ap[:, h, :]), (hb_.ap[:, hs], CsT[1].ap[:, h, :])],
                         [xd[0], xd[1], MT[0], MT[1], hf, hb_, CsT[0], CsT[1]], B.phalf[1 + bk])
                if _os.environ.get("SSD_SKIP") == "2":
                    continue
                for i in range(TI):
                    bk = (i * 128) // 512
                    co = i * 128 - bk * 512
                    B.V(lambda i=i, bk=bk, co=co: nc.vector.scalar_tensor_tensor(out=yv.ap[:, i, :], in0=xc.ap[:, i, tk], scalar=dcol.ap[:, i:i + 1],
                                                                                  in1=psy[bk].ap[:, co:co + 128], op0=ALU.mult, op1=ALU.add),
                        [xc, dcol] + B.phalf[1 + bk], [yv])
                B.A(lambda: nc.scalar.activation(out=sz.ap, in_=zt.ap[:, :, tk], func=AF.Silu), [zt], [sz])
                B.V(lambda: nc.vector.tensor_tensor(out=yv.ap, in0=yv.ap, in1=sz.ap, op=ALU.mult), [yv, sz], [yv])
                B.A(lambda: nc.scalar.activation(out=sq.ap, in_=yv.ap, func=AF.Square), [yv], [sq])
                B.mm(pmisc.ap[:, 384:512], [(B.onesb.ap, sq.ap[:, i, :]) for i in range(TI)], [B.onesb, sq], pmb)
                B.A(lambda: nc.scalar.activation(out=rt.ap, in_=pmisc.ap[:, 384:512], func=AF.Sqrt, scale=1.0 / c.INNER, bias=epst.ap[:, 0:1]),
                    pmb + [epst], [rt])
                B.V(lambda: nc.vector.reciprocal(out=rt.ap, in_=rt.ap), [rt], [rt])
                B.V(lambda: nc.vector.tensor_tensor(out=yv.ap, in0=yv.ap, in1=_bc(rt.ap.unsqueeze(1), [128, TI, 128]), op=ALU.mult), [yv, rt], [yv])
                B.V(lambda: nc.vector.tensor_tensor(out=ynst.ap[:, :, tk], in0=yv.ap, in1=_bc(ng.ap.unsqueeze(2), [128, TI, 128]), op=ALU.mult),
                    [yv, ng], [ynst])
            P.dma("sp", S["ynT"][:, sc * SC:(sc + 1) * SC].rearrange("(t p) n -> p t n", p=128), ynst.ap, reads=[ynst])
        P.barrier()


def phase_s5(B, l):
    c, nc, P, S, W = B.c, B.nc, B.P, B.S, B.W
    TS, C5, NT, LC = c.TS, c.C5, c.NT, c.S5C
    KT5 = c.W5 // 128
    NLC = NT // LC
    PI = math.pi
    with contextlib.ExitStack() as st:
        par = B.sb(st, "s5par", [128, 2, TS, 3], F32)
        P.dma("sp", par.ap, W["s5_par"][l].rearrange("d p t k -> p d t k"), writes=[par])
        bb = B.sb(st, "s5b", [128, TS, 2, C5], F32)
        P.dma("sp", bb.ap, W["s5_b"][l], writes=[bb])
        cc = B.sb(st, "s5c", [128, 2, TS, 2, C5], F32)
        P.dma("sp", cc.ap, W["s5_c"][l].rearrange("d p t k o -> p d t k o"), writes=[cc])
        dcol = B.sb(st, "s5d", [128, KT5], F32)
        P.dma("sp", dcol.ap, W["s5_dcol"][l], writes=[dcol])
        iota = B.sb(st, "iota", [128, LC], F32)
        riota = B.sb(st, "riota", [128, LC], F32)
        P.dma("sp", iota.ap, B.K["c_iota"], writes=[iota])
        B.V(lambda: nc.vector.tensor_scalar(out=riota.ap, in0=iota.ap, scalar1=-1.0, scalar2=float(LC - 1), op0=ALU.mult, op1=ALU.add),
            [iota], [riota])
        hpi = B.sb(st, "hpi", [128, 1], F32)
        B.V(lambda: nc.vector.memset(hpi.ap, PI / 2), [], [hpi])
        n2 = 2 * TS
        mk = lambda n, w=n2: B.sb(st, n, [128, w], F32)
        are, aim, step, lr, th, rr, cosl, sinl = [mk(n) for n in ("are", "aim", "step", "lr", "th", "rr", "cosl", "sinl")]
        tA, tB, tC, tD = [mk(n) for n in ("tA", "tB", "tC", "tD")]
        ki = B.sb(st, "ki", [128, max(n2, LC)], I32)
        tabt = [B.sb(st, "tabt", [128, LC], F32) for _ in range(3)]

        def red_sincos(x_ap, n, sin_out, cos_out, tmp, deps, outs):
            y, g, a = tmp
            B.V(lambda: nc.vector.tensor_scalar(out=y, in0=x_ap, scalar1=1.0 / TWO_PI, scalar2=None, op0=ALU.mult), deps, [tmpb])
            B.V(lambda: nc.vector.tensor_copy(out=ki.ap[:, 0:n], in_=y), [tmpb], [ki])
            B.V(lambda: nc.vector.tensor_copy(out=y, in_=ki.ap[:, 0:n]), [ki], [tmpb])
            B.V(lambda: nc.vector.scalar_tensor_tensor(out=y, in0=y, scalar=-TWO_PI, in1=x_ap, op0=ALU.mult, op1=ALU.add), deps + [tmpb], [tmpb])
            B.V(lambda: nc.vector.tensor_single_scalar(out=g, in_=y, scalar=PI, op=ALU.is_gt), [tmpb], [tmpb])
            B.V(lambda: nc.vector.scalar_tensor_tensor(out=y, in0=g, scalar=-TWO_PI, in1=y, op0=ALU.mult, op1=ALU.add), [tmpb], [tmpb])
            B.V(lambda: nc.vector.tensor_single_scalar(out=g, in_=y, scalar=-PI, op=ALU.is_lt), [tmpb], [tmpb])
            B.V(lambda: nc.vector.scalar_tensor_tensor(out=y, in0=g, scalar=TWO_PI, in1=y, op0=ALU.mult, op1=ALU.add), [tmpb], [tmpb])
            B.V(lambda: nc.vector.tensor_scalar(out=y, in0=y, scalar1=-PI, scalar2=PI, op0=ALU.max, op1=ALU.min), [tmpb], [tmpb])
            B.A(lambda: nc.scalar.activation(out=sin_out, in_=y, func=AF.Sin), [tmpb], outs)
            B.A(lambda: nc.scalar.activation(out=a, in_=y, func=AF.Abs), [tmpb], [tmpb])
            B.A(lambda: nc.scalar.activation(out=cos_out, in_=a, func=AF.Sin, scale=-1.0, bias=hpi.ap[:, 0:1]), [tmpb, hpi], outs)
        tmpb = Buf("s5tmp")
        f2 = lambda k: par.ap[:, :, :, k]
        v2 = lambda t: t.ap.rearrange("p (d t) -> p d t", d=2)
        B.V(lambda: nc.vector.tensor_copy(out=v2(are), in_=f2(0)), [par], [are])
        B.V(lambda: nc.vector.tensor_copy(out=v2(aim), in_=f2(1)), [par], [aim])
        B.A(lambda: nc.scalar.activation(out=v2(step), in_=f2(2), func=AF.Exp), [par], [step])
        B.V(lambda: nc.vector.tensor_tensor(out=lr.ap, in0=are.ap, in1=step.ap, op=ALU.mult), [are, step], [lr])
        B.V(lambda: nc.vector.tensor_tensor(out=th.ap, in0=aim.ap, in1=step.ap, op=ALU.mult), [aim, step], [th])
        B.A(lambda: nc.scalar.activation(out=rr.ap, in_=lr.ap, func=AF.Exp), [lr], [rr])
        red_sincos(th.ap, n2, sinl.ap, cosl.ap, (tA.ap, tB.ap, tC.ap), [th], [sinl, cosl])
        cT, sT, cTl, sTl, thL = [mk(n) for n in ("cT", "sT", "cTl", "sTl", "thL")]
        B.V(lambda: nc.vector.tensor_scalar(out=thL.ap, in0=th.ap, scalar1=float(LC), scalar2=None, op0=ALU.mult), [th], [thL])
        red_sincos(thL.ap, n2, sT.ap, cT.ap, (tA.ap, tB.ap, tC.ap), [thL], [sT, cT])
        B.V(lambda: nc.vector.tensor_scalar(out=cTl.ap, in0=cT.ap, scalar1=B.linkt.ap[:, 0:1], scalar2=None, op0=ALU.mult), [cT, B.linkt], [cTl])
        B.V(lambda: nc.vector.tensor_scalar(out=sTl.ap, in0=sT.ap, scalar1=B.linkt.ap[:, 0:1], scalar2=None, op0=ALU.mult), [sT, B.linkt], [sTl])
        lbr, lbi, cr, ci = [mk(n) for n in ("lbr", "lbi", "cr", "ci")]
        B.V(lambda: nc.vector.tensor_tensor(out=lbr.ap, in0=rr.ap, in1=cosl.ap, op=ALU.mult), [rr, cosl], [lbr])
        B.V(lambda: nc.vector.tensor_tensor(out=lbi.ap, in0=rr.ap, in1=sinl.ap, op=ALU.mult), [rr, sinl], [lbi])
        B.V(lambda: nc.vector.tensor_scalar(out=lbr.ap, in0=lbr.ap, scalar1=-1.0, scalar2=None, op0=ALU.add), [lbr], [lbr])
        B.V(lambda: nc.vector.tensor_tensor(out=tA.ap, in0=are.ap, in1=are.ap, op=ALU.mult), [are, tmpb], [tmpb])
        B.V(lambda: nc.vector.tensor_tensor(out=tB.ap, in0=aim.ap, in1=aim.ap, op=ALU.mult), [aim, tmpb], [tmpb])
        B.V(lambda: nc.vector.tensor_tensor(out=tA.ap, in0=tA.ap, in1=tB.ap, op=ALU.add), [tmpb], [tmpb])
        B.V(lambda: nc.vector.reciprocal(out=tD.ap, in_=tA.ap), [tmpb], [tD])
        B.V(lambda: nc.vector.tensor_tensor(out=tA.ap, in0=lbr.ap, in1=are.ap, op=ALU.mult), [lbr, are, tmpb], [tmpb])
        B.V(lambda: nc.vector.tensor_tensor(out=tB.ap, in0=lbi.ap, in1=aim.ap, op=ALU.mult), [lbi, aim, tmpb], [tmpb])
        B.V(lambda: nc.vector.tensor_tensor(out=tA.ap, in0=tA.ap, in1=tB.ap, op=ALU.add), [tmpb], [tmpb])
        B.V(lambda: nc.vector.tensor_tensor(out=cr.ap, in0=tA.ap, in1=tD.ap, op=ALU.mult), [tmpb, tD], [cr])
        B.V(lambda: nc.vector.tensor_tensor(out=tA.ap, in0=lbi.ap, in1=are.ap, op=ALU.mult), [lbi, are, tmpb], [tmpb])
        B.V(lambda: nc.vector.tensor_tensor(out=tB.ap, in0=lbr.ap, in1=aim.ap, op=ALU.mult), [lbr, aim, tmpb], [tmpb])
        B.V(lambda: nc.vector.tensor_tensor(out=tA.ap, in0=tA.ap, in1=tB.ap, op=ALU.subtract), [tmpb], [tmpb])
        B.V(lambda: nc.vector.tensor_tensor(out=ci.ap, in0=tA.ap, in1=tD.ap, op=ALU.mult), [tmpb, tD], [ci])
        BT = B.sb(st, "BT", [128, 2, TS, 2, 128], BF16)
        Cp = B.sb(st, "Cp", [128, 2, TS, 2, 128], BF16)
        B.V(lambda: nc.vector.memset(Cp.ap, 0.0), [], [Cp])
        Bbr = B.sb(st, "Bbr", [128, TS, C5], F32)
        Bbi = B.sb(st, "Bbi", [128, TS, C5], F32)
        tE = B.sb(st, "tE", [128, TS, C5], F32)
        pad = [B.sb(st, "pad", [128, 128], BF16) for _ in range(2)]
        ipad = 0
        for d in range(2):
            crb = _bc(cr.ap[:, d * TS:(d + 1) * TS].unsqueeze(2), [128, TS, C5])
            cib = _bc(ci.ap[:, d * TS:(d + 1) * TS].unsqueeze(2), [128, TS, C5])
            bre, bim = bb.ap[:, :, 0, :], bb.ap[:, :, 1, :]
            B.V(lambda: nc.vector.tensor_tensor(out=Bbr.ap, in0=bre, in1=crb, op=ALU.mult), [bb, cr], [Bbr])
            B.V(lambda: nc.vector.tensor_tensor(out=tE.ap, in0=bim, in1=cib, op=ALU.mult), [bb, ci], [tE])
            B.V(lambda: nc.vector.tensor_tensor(out=Bbr.ap, in0=Bbr.ap, in1=tE.ap, op=ALU.subtract), [Bbr, tE], [Bbr])
            B.V(lambda: nc.vector.tensor_tensor(out=Bbi.ap, in0=bim, in1=crb, op=ALU.mult), [bb, cr], [Bbi])
            B.V(lambda: nc.vector.tensor_tensor(out=tE.ap, in0=bre, in1=cib, op=ALU.mult), [bb, ci], [tE])
            B.V(lambda: nc.vector.tensor_tensor(out=Bbi.ap, in0=Bbi.ap, in1=tE.ap, op=ALU.add), [Bbi, tE], [Bbi])
            for m in range(TS):
                off = (m % 4) * 2 * C5
                for k, src in enumerate((Bbr, Bbi)):
                    pd = pad[ipad % 2]
                    ipad += 1
                    B.V(lambda: nc.vector.memset(pd.ap, 0.0), [], [pd])
                    B.V(lambda: nc.vector.tensor_copy(out=pd.ap[0:64, off:off + C5], in_=src.ap[0:64, m, :]), [src], [pd])
                    B.V(lambda: nc.vector.tensor_copy(out=pd.ap[64:128, off + C5:off + 2 * C5], in_=src.ap[64:128, m, :]), [src], [pd])
                    B.T(lambda: nc.tensor.transpose(B.pT.ap[:, 0:128], pd.ap, B.identb.ap), [pd, B.identb], B.phalf[6])
                    B.A(lambda: nc.scalar.copy(out=BT.ap[:, d, m, k, :], in_=B.pT.ap[:, 0:128]), B.phalf[6], [BT])
                B.V(lambda: nc.vector.tensor_copy(out=Cp.ap[0:64, d, m, 0, off:off + C5], in_=cc.ap[0:64, d, m, 0, :]), [cc], [Cp])
                B.V(lambda: nc.vector.tensor_copy(out=Cp.ap[64:128, d, m, 0, off + C5:off + 2 * C5], in_=cc.ap[64:128, d, m, 0, :]), [cc], [Cp])
                B.V(lambda: nc.vector.tensor_scalar(out=Cp.ap[0:64, d, m, 1, off:off + C5], in0=cc.ap[0:64, d, m, 1, :], scalar1=-1.0, scalar2=None,
                                                    op0=ALU.mult), [cc], [Cp])
                B.V(lambda: nc.vector.tensor_scalar(out=Cp.ap[64:128, d, m, 1, off + C5:off + 2 * C5], in0=cc.ap[64:128, d, m, 1, :], scalar1=-1.0,
                                                    scalar2=None, op0=ALU.mult), [cc], [Cp])
        uT = B.sb(st, "uT", [128, NT], BF16)
        yacc = B.sb(st, "yacc", [128, NT], F32)
        nm = min(4, TS)
        cs = [B.sb(st, "cs", [128, LC], F32) for _ in range(nm)]
        sn = [B.sb(st, "sn", [128, LC], F32) for _ in range(nm)]
        z0 = [[B.sb(st, "z0", [128, 1], F32) for _ in range(2)] for _ in range(nm)]
        wr, wi, zr, zi = [[B.sb(st, n, [128, LC], F32) for _ in range(2)] for n in ("wr", "wi", "zr", "zi")]
        q1, q2 = [[B.sb(st, n, [128, LC], F32) for _ in range(2)] for n in ("q1", "q2")]
        xr = [B.sb(st, "xr", [128, LC], BF16) for _ in range(nm)]
        xi = [B.sb(st, "xi", [128, LC], BF16) for _ in range(nm)]
        c1 = B.sb(st, "c1", [128, 1], F32)
        it = 0
        for kt in range(KT5):
            ms = [m for m in range(4 * kt, 4 * kt + 4) if m < TS]
            P.dma("sp", uT.ap, S["uT"][kt * 128:(kt + 1) * 128, :], writes=[uT])
            for d in range(2):
                for j, m in enumerate(ms):
                    ang = tabt[0]
                    io = iota if d == 0 else riota
                    B.V(lambda: nc.vector.tensor_scalar(out=ang.ap, in0=io.ap, scalar1=th.ap[:, d * TS + m:d * TS + m + 1], scalar2=None, op0=ALU.mult),
                        [io, th], [ang])
                    red_sincos(ang.ap, LC, sn[j].ap, cs[j].ap, (tabt[1].ap, tabt[2].ap, tabt[2].ap), [ang], [sn[j], cs[j]])
                    for k in range(2):
                        B.V(lambda k=k: nc.vector.memset(z0[j][k].ap, 0.0), [], [z0[j][k]])
                order = list(range(NLC)) if d == 0 else list(range(NLC - 1, -1, -1))
                bnd_c = (c.BND // LC) if d == 0 else (c.BND // LC - 1)
                for ch in order:
                    tk = slice(ch * LC, (ch + 1) * LC)
                    for j, m in enumerate(ms):
                        i2 = it % 2
                        it += 1
                        pa, pb_ = B.psb[2 * i2], B.psb[2 * i2 + 1]
                        ba, bbuf = B.phalf[2 * i2], B.phalf[2 * i2 + 1]
                        B.mm(pa.ap[:, 0:LC], [(BT.ap[:, d, m, 0, :], uT.ap[:, tk])], [BT, uT], ba)
                        B.mm(pb_.ap[:, 0:LC], [(BT.ap[:, d, m, 1, :], uT.ap[:, tk])], [BT, uT], bbuf)
                        w_r, w_i, z_r, z_i, a1, a2 = wr[i2], wi[i2], zr[i2], zi[i2], q1[i2], q2[i2]
                        B.V(lambda: nc.vector.tensor_tensor(out=a1.ap, in0=pa.ap[:, 0:LC], in1=cs[j].ap, op=ALU.mult), ba + [cs[j]], [a1])
                        B.V(lambda: nc.vector.tensor_tensor(out=a2.ap, in0=pb_.ap[:, 0:LC], in1=sn[j].ap, op=ALU.mult), bbuf + [sn[j]], [a2])
                        B.V(lambda: nc.vector.tensor_tensor(out=w_r.ap, in0=a1.ap, in1=a2.ap, op=ALU.add), [a1, a2], [w_r])
                        B.V(lambda: nc.vector.tensor_tensor(out=a1.ap, in0=pb_.ap[:, 0:LC], in1=cs[j].ap, op=ALU.mult), bbuf + [cs[j]], [a1])
                        B.V(lambda: nc.vector.tensor_tensor(out=a2.ap, in0=pa.ap[:, 0:LC], in1=sn[j].ap, op=ALU.mult), ba + [sn[j]], [a2])
                        B.V(lambda: nc.vector.tensor_tensor(out=w_i.ap, in0=a1.ap, in1=a2.ap, op=ALU.subtract), [a1, a2], [w_i])
                        if ch == bnd_c:
                            for k in range(2):
                                B.V(lambda k=k: nc.vector.tensor_scalar(out=z0[j][k].ap, in0=z0[j][k].ap, scalar1=B.linkt.ap[:, 0:1], scalar2=None,
                                                                        op0=ALU.mult), [z0[j][k], B.linkt], [z0[j][k]])
                        rb = _bc(rr.ap[:, d * TS + m:d * TS + m + 1], [128, LC])
                        sl = (lambda a: a) if d == 0 else (lambda a: a[:, ::-1])
                        B.V(lambda: nc.vector.tensor_tensor_scan(out=sl(z_r.ap), data0=rb, data1=sl(w_r.ap), initial=z0[j][0].ap[:, 0:1],
                                                                 op0=ALU.mult, op1=ALU.add), [rr, w_r, z0[j][0]], [z_r])
                        B.V(lambda: nc.vector.tensor_tensor_scan(out=sl(z_i.ap), data0=rb, data1=sl(w_i.ap), initial=z0[j][1].ap[:, 0:1],
                                                                 op0=ALU.mult, op1=ALU.add), [rr, w_i, z0[j][1]], [z_i])
                        B.G(lambda: nc.gpsimd.tensor_tensor(out=a1.ap, in0=z_r.ap, in1=cs[j].ap, op=ALU.mult), [z_r, cs[j]], [a1])
                        B.G(lambda: nc.gpsimd.tensor_tensor(out=a2.ap, in0=z_i.ap, in1=sn[j].ap, op=ALU.mult), [z_i, sn[j]], [a2])
                        B.G(lambda: nc.gpsimd.tensor_tensor(out=xr[j].ap, in0=a1.ap, in1=a2.ap, op=ALU.subtract), [a1, a2], [xr[j]])
                        B.G(lambda: nc.gpsimd.tensor_tensor(out=a1.ap, in0=z_i.ap, in1=cs[j].ap, op=ALU.mult), [z_i, cs[j]], [a1])
                        B.G(lambda: nc.gpsimd.tensor_tensor(out=a2.ap, in0=z_r.ap, in1=sn[j].ap, op=ALU.mult), [z_r, sn[j]], [a2])
                        B.G(lambda: nc.gpsimd.tensor_tensor(out=xi[j].ap, in0=a1.ap, in1=a2.ap, op=ALU.add), [a1, a2], [xi[j]])
                        last = LC - 1 if d == 0 else 0
                        col = d * TS + m
                        lk = (ch == (bnd_c - 1 if d == 0 else bnd_c + 1))
                        zl_r, zl_i = z_r.ap[:, last:last + 1], z_i.ap[:, last:last + 1]
                        B.V(lambda: nc.vector.tensor_scalar(out=c1.ap, in0=zl_i, scalar1=sT.ap[:, col:col + 1], scalar2=None, op0=ALU.mult), [z_i, sT], [c1])
                        B.V(lambda: nc.vector.scalar_tensor_tensor(out=z0[j][0].ap, in0=zl_r, scalar=cT.ap[:, col:col + 1], in1=c1.ap,
                                                                   op0=ALU.mult, op1=ALU.subtract), [z_r, cT, c1], [z0[j][0]])
                        B.V(lambda: nc.vector.tensor_scalar(out=c1.ap, in0=zl_r, scalar1=sT.ap[:, col:col + 1], scalar2=None, op0=ALU.mult), [z_r, sT], [c1])
                        B.V(lambda: nc.vector.scalar_tensor_tensor(out=z0[j][1].ap, in0=zl_i, scalar=cT.ap[:, col:col + 1], in1=c1.ap,
                                                                   op0=ALU.mult, op1=ALU.add), [z_i, cT, c1], [z0[j][1]])
                    py = B.psb[4 + (ch % 2)]
                    pyb = B.phalf[4 + (ch % 2)]
                    pairs = []
                    rd = [Cp]
                    for j, m in enumerate(ms):
                        pairs += [(Cp.ap[:, d, m, 0, :], xr[j].ap), (Cp.ap[:, d, m, 1, :], xi[j].ap)]
                        rd += [xr[j], xi[j]]
                    B.mm(py.ap[:, 0:LC], pairs, rd, pyb)
                    if d == 0:
                        B.A(lambda: nc.scalar.copy(out=yacc.ap[:, tk], in_=py.ap[:, 0:LC]), pyb, [yacc])
                    else:
                        B.V(lambda: nc.vector.tensor_tensor(out=yacc.ap[:, tk], in0=py.ap[:, 0:LC], in1=yacc.ap[:, tk], op=ALU.add), pyb + [yacc], [yacc])
            for ch in range(NLC):
                tk = slice(ch * LC, (ch + 1) * LC)
                ya, tq, yoc = wr[ch % 2], q1[ch % 2], xr[ch % min(2, nm)]
                B.V(lambda: nc.vector.scalar_tensor_tensor(out=ya.ap, in0=uT.ap[:, tk], scalar=dcol.ap[:, kt:kt + 1], in1=yacc.ap[:, tk],
                                                           op0=ALU.mult, op1=ALU.add), [uT, dcol, yacc], [ya])
                B.V(lambda: nc.vector.tensor_tensor(out=tq.ap, in0=ya.ap, in1=ya.ap, op=ALU.mult), [ya], [tq])
                B.V(lambda: nc.vector.tensor_scalar(out=tq.ap, in0=tq.ap, scalar1=0.044715, scalar2=1.0, op0=ALU.mult, op1=ALU.add), [tq], [tq])
                B.V(lambda: nc.vector.tensor_tensor(out=tq.ap, in0=tq.ap, in1=ya.ap, op=ALU.mult), [tq, ya], [tq])
                B.A(lambda: nc.scalar.activation(out=tq.ap, in_=tq.ap, func=AF.Sigmoid, scale=1.5957691216057308), [tq], [tq])
                B.V(lambda: nc.vector.tensor_tensor(out=yoc.ap, in0=tq.ap, in1=ya.ap, op=ALU.mult), [tq, ya], [yoc])
                P.dma("sp", S["s5T"][kt * 128:(kt + 1) * 128, tk], yoc.ap, reads=[yoc])
        P.barrier()


def attn_setup(B):
    c, nc, P, W, K = B.c, B.nc, B.P, B.W, B.K
    NH = 3 * c.HG
    B.fd = []
    with contextlib.ExitStack() as st:
        tbl = B.sb(st, "tbl", [c.NBK + 1, NH], F32)
        P.dma("sp", tbl.ap[0:c.NBK, :], W["rel_bias"], writes=[tbl])
        B.V(lambda: nc.vector.memset(tbl.ap[c.NBK:c.NBK + 1, :], NEG), [], [tbl])
        for p in range(3):
            Wp = (2 * c.PADT[p] + 1) * 128
            nrel = Wp + 127
            fd = B.scratch(f"fd{p}", [NH, nrel], F32)
            B.fd.append(fd)
            oh = B.sb(st, "oh", [c.NBK + 1, nrel], F32)
            fs = B.sb(st, "fs", [NH, nrel], F32)
            P.dma("sp", oh.ap, K[f"c_oh{p}"], writes=[oh])
            for i, c0 in enumerate(range(0, nrel, 512)):
                w_ = min(512, nrel - c0)
                bk = i % 2
                B.mm(B.psb[bk].ap[0:NH, 0:w_], [(tbl.ap, oh.ap[:, c0:c0 + w_])], [tbl, oh], B.phalf[bk])
                B.A(lambda: nc.scalar.copy(out=fs.ap[:, c0:c0 + w_], in_=B.psb[bk].ap[0:NH, 0:w_]), B.phalf[bk], [fs])
            P.dma("sp", fd, fs.ap, reads=[fs])
        P.barrier()


def phase_attn(B, l):
    c, nc, P, S, W = B.c, B.nc, B.P, B.S, B.W
    BND, NT, HG = c.BND, c.NT, c.HG
    NQ = BND // 128
    Wp = [(2 * pt + 1) * 128 for pt in c.PADT]
    off = [0, Wp[0], Wp[0] + Wp[1]]
    Wtot = sum(Wp)
    NKT = Wtot // 128
    scale = float(c.DH) ** -0.5
    with contextlib.ExitStack() as st:
        kT = [B.sb(st, "kT", [128, BND + 2 * c.PADT[p] * 128], BF16) for p in range(3)]
        Vt = [B.sb(st, "Vt", [128, NQ + 2 * c.PADT[p], 128], BF16) for p in range(3)]
        qT = [B.sb(st, "qT", [128, BND], BF16) for p in range(3)]
        MB = [B.sb(st, "MB", [128, Wp[p]], F32) for p in range(3)]
        MBr = [B.sb(st, "MBr", [128, Wp[p]], F32) for p in range(3)]
        Sp = [B.sb(st, "Sp", [128, Wtot], F32) for _ in range(2)]
        Pm = [B.sb(st, "Pm", [128, Wtot], BF16) for _ in range(2)]
        PT = [B.sb(st, "PT", [128, NKT, 128], BF16) for _ in range(2)]
        negm = B.sb(st, "negm", [128, 1], F32)
        rs = B.sb(st, "ars", [128, 128], F32)
        ast_ = [B.sb(st, "ast", [128, BND], BF16) for _ in range(2)]
        it = 0
        for j in range(HG):
            for p in range(3):
                h = p * HG + j
                nrel = Wp[p] + 127
                src = bass.AP(tensor=B.fd[p].tensor, offset=h * nrel, ap=[[1, 128], [1, Wp[p]]])
                P.dma("sp", MBr[p].ap, src, writes=[MBr[p]])
                B.V(lambda p=p: nc.vector.tensor_copy(out=MB[p].ap, in_=MBr[p].ap[:, ::-1]), [MBr[p]], [MB[p]])
            for hf in range(2):
                hs = hf * BND
                for p in range(3):
                    h = p * HG + j
                    pad = c.PADT[p] * 128
                    lo, hi = hs - pad, hs + BND + pad
                    slo, shi = max(lo, 0), min(hi, NT)
                    B.V(lambda p=p: nc.vector.memset(kT[p].ap, 0.0), [], [kT[p]])
                    B.G(lambda p=p: nc.gpsimd.memset(Vt[p].ap, 0.0), [], [Vt[p]])
                    P.dma("sp", kT[p].ap[:, slo - lo: shi - lo], S["kT"][h * 128:(h + 1) * 128, slo:shi], writes=[kT[p]])
                    P.dma("sp", Vt[p].ap[:, (slo - lo) // 128:(shi - lo) // 128, :],
                          S["v"][slo:shi, h * 128:(h + 1) * 128].rearrange("(t p) n -> p t n", p=128), writes=[Vt[p]])
                    P.dma("sp", qT[p].ap, S["qT"][h * 128:(h + 1) * 128, hs:hs + BND], writes=[qT[p]])
                asg = ast_[(j * 2 + hf) % 2]
                for i in range(NQ):
                    sp, pm, pt = Sp[i % 2], Pm[i % 2], PT[i % 2]
                    for p in range(3):
                        pad = c.PADT[p] * 128
                        for c0 in range(0, Wp[p], 512):
                            pw = min(512, Wp[p] - c0)
                            bk = it % 2
                            it += 1
                            B.mm(B.psb[bk].ap[:, 0:pw], [(qT[p].ap[:, i * 128:(i + 1) * 128], kT[p].ap[:, i * 128 + c0: i * 128 + c0 + pw])],
                                 [qT[p], kT[p]], B.phalf[bk])
                            B.V(lambda p=p, c0=c0, pw=pw, bk=bk: nc.vector.scalar_tensor_tensor(
                                out=sp.ap[:, off[p] + c0: off[p] + c0 + pw], in0=B.psb[bk].ap[:, 0:pw], scalar=scale,
                                in1=MB[p].ap[:, c0:c0 + pw], op0=ALU.mult, op1=ALU.add), B.phalf[bk] + [MB[p]], [sp])
                        base = hs - pad + i * 128
                        if base < 0:
                            n0 = min(Wp[p], -base)
                            B.V(lambda p=p, n0=n0: nc.vector.tensor_scalar(out=sp.ap[:, off[p]:off[p] + n0], in0=sp.ap[:, off[p]:off[p] + n0],
                                                                           scalar1=NEG, scalar2=None, op0=ALU.add), [sp], [sp])
                        if base + Wp[p] > NT:
                            n0 = max(0, NT - base)
                            B.V(lambda p=p, n0=n0: nc.vector.tensor_scalar(out=sp.ap[:, off[p] + n0:off[p] + Wp[p]], in0=sp.ap[:, off[p] + n0:off[p] + Wp[p]],
                                                                           scalar1=NEG, scalar2=None, op0=ALU.add), [sp], [sp])
                        if hf == 0 and base + Wp[p] > BND:
                            n0 = max(0, BND - base)
                            B.V(lambda p=p, n0=n0: nc.vector.tensor_scalar(out=sp.ap[:, off[p] + n0:off[p] + Wp[p]], in0=sp.ap[:, off[p] + n0:off[p] + Wp[p]],
                                                                           scalar1=B.nlinkt.ap[:, 0:1], scalar2=None, op0=ALU.add), [sp, B.nlinkt], [sp])
                        if hf == 1 and base < BND:
                            n0 = min(Wp[p], BND - base)
                            B.V(lambda p=p, n0=n0: nc.vector.tensor_scalar(out=sp.ap[:, off[p]:off[p] + n0], in0=sp.ap[:, off[p]:off[p] + n0],
                                                                           scalar1=B.nlinkt.ap[:, 0:1], scalar2=None, op0=ALU.add), [sp, B.nlinkt], [sp])
                    B.V(lambda: nc.vector.tensor_reduce(out=negm.ap, in_=sp.ap, axis=AX.X, op=ALU.max, negate=True), [sp], [negm])
                    B.A(lambda: nc.scalar.activation(out=pm.ap, in_=sp.ap, func=AF.Exp, bias=negm.ap[:, 0:1]), [sp, negm], [pm])
                    for r0 in range(0, NKT, 16):
                        rn = min(16, NKT - r0)

                        def tr(r0=r0, rn=rn):
                            ins = None
                            for t_ in range(rn):
                                ins = nc.tensor.transpose(B.pT.ap[:, t_ * 128:(t_ + 1) * 128], pm.ap[:, (r0 + t_) * 128:(r0 + t_ + 1) * 128], B.identb.ap)
                            return ins
                        B.T(tr, [pm, B.identb], [B.pT])
                        if (r0 // 16) % 2 == 0:
                            B.A(lambda r0=r0, rn=rn: nc.scalar.copy(out=pt.ap[:, r0:r0 + rn, :].rearrange("p t q -> p (t q)"), in_=B.pT.ap[:, 0:rn * 128]),
                                [B.pT], [pt])
                        else:
                            B.V(lambda r0=r0, rn=rn: nc.vector.tensor_copy(out=pt.ap[:, r0:r0 + rn, :].rearrange("p t q -> p (t q)"), in_=B.pT.ap[:, 0:rn * 128]),
                                [B.pT], [pt])
                    bo, bs_ = (2, 3) if i % 2 == 0 else (4, 5)
                    pv, ps_ = [], []
                    for p in range(3):
                        for kt in range(Wp[p] // 128):
                            tile_ = off[p] // 128 + kt
                            pv.append((Vt[p].ap[:, i + kt, :], pt.ap[:, tile_, :]))
                            ps_.append((B.onesb.ap, pt.ap[:, tile_, :]))
                    B.mm(B.psb[bo].ap[:, 0:128], pv, [pt] + Vt, B.phalf[bo])
                    B.mm(B.psb[bs_].ap[:, 0:128], ps_, [pt, B.onesb], B.phalf[bs_])
                    B.V(lambda bs_=bs_: nc.vector.reciprocal(out=rs.ap, in_=B.psb[bs_].ap[:, 0:128]), B.phalf[bs_], [rs])
                    B.V(lambda bo=bo: nc.vector.tensor_tensor(out=asg.ap[:, i * 128:(i + 1) * 128], in0=B.psb[bo].ap[:, 0:128], in1=rs.ap, op=ALU.mult),
                        B.phalf[bo] + [rs], [asg])
                P.dma("sp", S["attT"][j * 128:(j + 1) * 128, hs:hs + BND], asg.ap, reads=[asg])
        P.barrier()


def phase_merge(B, l, xsrc, xkey):
    c, nc, P, S, W = B.c, B.nc, B.P, B.S, B.W
    TG, D, KC = c.FTG, c.D, c.KC
    TI, K5, KA = c.INNER // 128, c.W5 // 128, c.HG * c.DH // 128
    with contextlib.ExitStack() as st:
        B.wt_setup(st, 16, 128, nbf=6)
        act = B.sb(st, "act", [128, max(TI, K5, KA), TG], BF16)
        mT = B.sb(st, "mT", [128, KC, TG], BF16)
        gts = [B.sb(st, "gt", [128, TG], BF16) for _ in range(2)]
        sgs = [B.sb(st, "sg", [128, TG], F32) for _ in range(2)]
        tmp = [B.sb(st, "mtmp", [128, 512], F32) for _ in range(2)]
        tmp2 = [B.sb(st, "mtmp2", [128, 512], F32) for _ in range(2)]
        xts = [B.sb(st, "mxt", [128, 128], F32) for _ in range(4)]
        cnt = {"g": 0, "t": 0, "x": 0}
        for tg in range(c.NT // TG):
            t0 = tg * TG

            def gate_tile(b, col_abs):
                gt, sg = gts[cnt["g"] % 2], sgs[cnt["g"] % 2]
                cnt["g"] += 1
                P.dma("sp", gt.ap, S["gT"][b * D + col_abs: b * D + col_abs + 128, t0:t0 + TG], writes=[gt])
                B.A(lambda: nc.scalar.activation(out=sg.ap, in_=gt.ap, func=AF.Sigmoid), [gt], [sg])
                return sg

            def load_act(name, kn):
                P.dma("sp", act.ap[:, 0:kn, :], S[name][:, t0:t0 + TG].rearrange("(t p) n -> p t n", p=128), writes=[act])
            load_act("ynT", TI)
            cur = {}

            def evac_a(ps, col_abs, nb, tok0, tw, pb):
                if tok0 == 0:
                    cur["sg"] = gate_tile(0, col_abs)
                sg = cur["sg"]
                B.V(lambda: nc.vector.tensor_tensor(out=mT.ap[:, col_abs // 128, tok0:tok0 + tw], in0=ps, in1=sg.ap[:, tok0:tok0 + tw], op=ALU.mult),
                    pb + [sg], [mT])
            B.dense_feat(act, TI, TG, W["ssm_w_out"][l], 0, D, evac_a)
            load_act("s5T", K5)
            ldp = lambda db_: (B.load_w(W["s5_w_glu"][l], 0, K5, db_ * 128, 128), B.load_w(W["s5_w_glu"][l], 0, K5, D + db_ * 128, 128))
            nxt = ldp(0)
            for db in range(KC):
                wv, wg = nxt
                if db + 1 < KC:
                    nxt = ldp(db + 1)
                sg = gate_tile(1, db * 128)
                for tt in range(-(-TG // 512)):
                    tw = min(512, TG - tt * 512)
                    tks = slice(tt * 512, tt * 512 + tw)
                    ba, bb_ = (0, 1) if cnt["t"] % 2 == 0 else (2, 3)
                    tp, tp2 = tmp[cnt["t"] % 2], tmp2[cnt["t"] % 2]
                    cnt["t"] += 1
                    B.mm(B.psb[ba].ap[:, 0:tw], [(wv.ap[:, k, 0:128], act.ap[:, k, tks]) for k in range(K5)], [wv, act], B.phalf[ba])
                    B.mm(B.psb[bb_].ap[:, 0:tw], [(wg.ap[:, k, 0:128], act.ap[:, k, tks]) for k in range(K5)], [wg, act], B.phalf[bb_])
                    B.A(lambda: nc.scalar.activation(out=tp.ap[:, 0:tw], in_=B.psb[bb_].ap[:, 0:tw], func=AF.Sigmoid), B.phalf[bb_], [tp])
                    B.V(lambda: nc.vector.tensor_tensor(out=tp2.ap[:, 0:tw], in0=B.psb[ba].ap[:, 0:tw], in1=tp.ap[:, 0:tw], op=ALU.mult), B.phalf[ba] + [tp], [tp2])
                    B.V(lambda: nc.vector.tensor_tensor(out=tp2.ap[:, 0:tw], in0=tp2.ap[:, 0:tw], in1=sg.ap[:, tks], op=ALU.mult), [tp2, sg], [tp2])
                    B.V(lambda: nc.vector.tensor_tensor(out=mT.ap[:, db, tks], in0=tp2.ap[:, 0:tw], in1=mT.ap[:, db, tks], op=ALU.add), [tp2, mT], [mT])
            load_act("attT", KA)

            def evac_c(ps, col_abs, nb, tok0, tw, pb):
                if tok0 == 0:
                    cur["sg"] = gate_tile(2, col_abs)
                sg = cur["sg"]
                tp = tmp[cnt["t"] % 2]
                cnt["t"] += 1
                B.V(lambda: nc.vector.tensor_tensor(out=tp.ap[:, 0:tw], in0=ps, in1=sg.ap[:, tok0:tok0 + tw], op=ALU.mult), pb + [sg], [tp])
                B.V(lambda: nc.vector.tensor_tensor(out=mT.ap[:, col_abs // 128, tok0:tok0 + tw], in0=tp.ap[:, 0:tw],
                                                    in1=mT.ap[:, col_abs // 128, tok0:tok0 + tw], op=ALU.add), [tp, mT], [mT])
            B.dense_feat(act, KA, TG, W["att_w_out"][l], 0, D, evac_c)

            def evac_o(ps, tok0, rows, col_abs, cw, pb):
                xt = xts[cnt["x"] % 4]
                cnt["x"] += 1
                P.dma("sp", xt.ap[0:rows, 0:cw], xsrc[t0 + tok0:t0 + tok0 + rows, col_abs:col_abs + cw], writes=[xt])
                B.V(lambda: nc.vector.tensor_tensor(out=xt.ap[0:rows, 0:cw], in0=ps, in1=xt.ap[0:rows, 0:cw], op=ALU.add), pb + [xt], [xt])
                P.dma("sp", S["xmid"][t0 + tok0:t0 + tok0 + rows, col_abs:col_abs + cw], xt.ap[0:rows, 0:cw], reads=[xt])
            B.dense_tok(mT, KC, TG, W["w_o"][l], 0, D, 128, evac_o)
        P.barrier()


def phase_ffn(B, l):
    c, nc, P, S, W = B.c, B.nc, B.P, B.S, B.W
    TG, D, KC, DFF = c.FTG, c.D, c.KC, c.DFF
    KF = DFF // 128
    with contextlib.ExitStack() as st:
        gain = B.sb(st, "gain2", [128, D], F32)
        B.bcast_load(gain, W["norm_ffn"][l:l + 1, :], D)
        h2T = B.sb(st, "h2T", [128, KC, TG + 2], BF16)
        gvT = B.sb(st, "gvT", [128, KF, TG], BF16)
        cwt = B.sb(st, "fcw", [128, 2 * KF, 4], F32)
        P.dma("sp", cwt.ap, W["ffn_cw"][l], writes=[cwt])
        ups = [B.sb(st, "ups", [128, TG + 2], F32) for _ in range(2)]
        accg = B.sb(st, "accg", [128, TG], F32)
        accv = B.sb(st, "accv", [128, TG], F32)
        xts = [B.sb(st, "fxt", [128, 128], F32) for _ in range(4)]
        cnt = {"x": 0, "b": 0}
        for tg in range(c.NT // TG):
            t0 = tg * TG
            with contextlib.ExitStack() as st1:
                B.norm_setup(st1, nbuf=1)
                B.norm_T(st1, S["xmid"], "xmid", gain, t0, TG, h2T, col0=1)
                for (tok, col) in ((t0 - 1, 0), (t0 + TG, TG + 1)):
                    if tok < 0 or tok >= c.NT:
                        B.V(lambda col=col: nc.vector.memset(h2T.ap[:, :, col:col + 1], 0.0), [], [h2T])
                    else:
                        cross = (tok == c.BND - 1 and col == 0) or (tok == c.BND and col == TG + 1)
                        B.norm_T(st1, S["xmid"], "xmid", gain, tok, 1, h2T, col0=col, scale_tile=(B.linkt if cross else None))
                P.barrier()
            st2 = contextlib.ExitStack()
            B.wt_setup(st2, 16, 128, nbf=6)
            loads = [(jb, which, cbase) for jb in range(KF) for which, cbase in enumerate((jb * 128, DFF + jb * 128))]
            q = [B.load_w(W["w_up"][l], 0, KC, ld[2], 128) for ld in loads[:2]]
            for li, (jb, which, cbase) in enumerate(loads):
                accs = (accg, accv)
                if True:
                    wb = q.pop(0)
                    if li + 2 < len(loads):
                        q.append(B.load_w(W["w_up"][l], 0, KC, loads[li + 2][2], 128))
                    up = ups[which]
                    ti_ = which * KF + jb
                    pieces = [(1 + q0, min(512, TG - q0)) for q0 in range(0, TG, 512)] + [(0, 1), (TG + 1, 1)]
                    for (cs_, n_) in pieces:
                        bk = cnt["b"] % 6
                        cnt["b"] += 1
                        B.mm(B.psb[bk].ap[:, 0:n_], [(wb.ap[:, k, 0:128], h2T.ap[:, k, cs_:cs_ + n_]) for k in range(KC)], [wb, h2T], B.phalf[bk])
                        if cnt["b"] % 2 == 0:
                            B.A(lambda bk=bk, cs_=cs_, n_=n_: nc.scalar.copy(out=up.ap[:, cs_:cs_ + n_], in_=B.psb[bk].ap[:, 0:n_]), B.phalf[bk], [up])
                        else:
                            B.V(lambda bk=bk, cs_=cs_, n_=n_: nc.vector.tensor_copy(out=up.ap[:, cs_:cs_ + n_], in_=B.psb[bk].ap[:, 0:n_]), B.phalf[bk], [up])
                    acc = accs[which]
                    B.V(lambda: nc.vector.tensor_scalar(out=acc.ap, in0=up.ap[:, 0:TG], scalar1=cwt.ap[:, ti_, 0:1], scalar2=cwt.ap[:, ti_, 3:4],
                                                        op0=ALU.mult, op1=ALU.add), [up, cwt], [acc])
                    for k in (1, 2):
                        B.V(lambda k=k: nc.vector.scalar_tensor_tensor(out=acc.ap, in0=up.ap[:, k:k + TG], scalar=cwt.ap[:, ti_, k:k + 1], in1=acc.ap,
                                                                       op0=ALU.mult, op1=ALU.add), [up, cwt, acc], [acc])
                if which == 1:
                    B.A(lambda: nc.scalar.activation(out=accg.ap, in_=accg.ap, func=AF.Silu), [accg], [accg])
                    B.V(lambda: nc.vector.tensor_tensor(out=gvT.ap[:, jb, :], in0=accg.ap, in1=accv.ap, op=ALU.mult), [accg, accv], [gvT])

            def evac_d(ps, tok0, rows, col_abs, cw, pb):
                xt = xts[cnt["x"] % 4]
                cnt["x"] += 1
                P.dma("sp", xt.ap[0:rows, 0:cw], S["xmid"][t0 + tok0:t0 + tok0 + rows, col_abs:col_abs + cw], writes=[xt])
                B.V(lambda: nc.vector.tensor_tensor(out=xt.ap[0:rows, 0:cw], in0=ps, in1=xt.ap[0:rows, 0:cw], op=ALU.add), pb + [xt], [xt])
                P.dma("sp", S["xres"][t0 + tok0:t0 + tok0 + rows, col_abs:col_abs + cw], xt.ap[0:rows, 0:cw], reads=[xt])
            B.dense_tok(gvT, KF, TG, W["w_down"][l], 0, D, 128, evac_d)
            P.barrier()
            st2.close()
        P.barrier()


def phase_final(B, y_out):
    c, nc, P, S, W = B.c, B.nc, B.P, B.S, B.W
    with contextlib.ExitStack() as st:
        gain = B.sb(st, "gainf", [128, c.D], F32)
        B.bcast_load(gain, W["final_norm"][0:1, :], c.D)
        B.norm_setup(st)
        nt = B._norm_tiles
        outs = [B.sb(st, "fo", [128, c.D], F32) for _ in range(2)]
        for ti in range(c.NT // 128):
            xt, o = nt["x"][ti % 2], outs[ti % 2]
            sq, ss, rt, rs = nt["sq"], nt["ss"], nt["rt"], nt["rs"]
            P.dma("sp", xt.ap, S["xres"][ti * 128:(ti + 1) * 128, :], writes=[xt])
            B.A(lambda: nc.scalar.activation(out=sq.ap, in_=xt.ap, func=AF.Square), [xt], [sq])
            B.V(lambda: nc.vector.reduce_sum(out=ss.ap, in_=sq.ap, axis=AX.X), [sq], [ss])
            B.A(lambda: nc.scalar.activation(out=rt.ap, in_=ss.ap, func=AF.Sqrt, scale=1.0 / c.D, bias=nt["eps"].ap[:, 0:1]), [ss, nt["eps"]], [rt])
            B.V(lambda: nc.vector.reciprocal(out=rs.ap, in_=rt.ap), [rt], [rs])
            B.V(lambda: nc.vector.scalar_tensor_tensor(out=o.ap, in0=xt.ap, scalar=rs.ap[:, 0:1], in1=gain.ap, op0=ALU.mult, op1=ALU.mult),
                [xt, rs, gain], [o])
            P.dma("sp", y_out[ti * 128:(ti + 1) * 128, :], o.ap, reads=[o])
        P.barrier()


_CACHE = {}


def kernel(**inputs):
    c = FULL
    c.TS = c.G5 * c.P5 // 128
    xp = np.asarray(inputs["x_prompt"], dtype=np.float32)
    xs = np.asarray(inputs["x_sample"], dtype=np.float32)
    assert xp.shape == (2, c.BND, c.D) and xs.shape == (1, c.NT, c.D)
    wl = layout_weights(c, inputs)
    consts = host_consts(c)
    maps = [core_inputs(c, wl, consts, xp.reshape(c.NT, c.D), 0),
            core_inputs(c, wl, consts, xs.reshape(c.NT, c.D), 1)]
    if "nc" not in _CACHE:
        _CACHE["nc"] = build(c)[0]
    res = run_bass_kernel_spmd(_CACHE["nc"], maps, core_ids=[0, 1])
    y_prompt = np.asarray(res.results[0]["y"], dtype=np.float32).reshape(2, c.BND, c.D)
    y_sample = np.asarray(res.results[1]["y"], dtype=np.float32).reshape(1, c.NT, c.D)
    return (y_prompt, y_sample)
```

```python
import math
import numpy as np
import concourse.bass as bass
import concourse.mybir as mybir
from concourse.bass_utils import run_bass_kernel_spmd

F32 = mybir.dt.float32
BF16 = mybir.dt.bfloat16
I32 = mybir.dt.int32
ALU = mybir.AluOpType
AF = mybir.ActivationFunctionType
AX = mybir.AxisListType
NEG = -1.0e30
TWO_PI = 2.0 * math.pi


class Buf:
    __slots__ = ("name", "w", "r", "excl")

    def __init__(self, name="", excl=False):
        self.name = name
        self.w = []
        self.r = []
        self.excl = excl


class Tile:
    def __init__(self, ap, buf):
        self.ap = ap
        self.b = buf

    def __getitem__(self, k):
        return self.ap[k]


class Prog:
    KS = 6
    NDQ = {"sp": 24, "pool": 12, "act": 8}

    def __init__(self, nc):
        self.nc = nc
        self.E = {"pe": nc.tensor, "act": nc.scalar, "dve": nc.vector, "pool": nc.gpsimd, "sp": nc.sync}
        self.csem = {e: [nc.alloc_semaphore(name=f"c_{e}{i}") for i in range(self.KS)]
                     for e in ("pe", "act", "dve", "pool")}
        self.cnt = {e: 0 for e in self.csem}
        self.dsem = {q: [nc.alloc_semaphore(name=f"d_{q}{i}") for i in range(n)] for q, n in self.NDQ.items()}
        self.dval = {q: [0] * self.NDQ[q] for q in self.dsem}
        self.dnext = {q: 0 for q in self.dsem}
        self.seen = {e: {} for e in self.E}
        self.ninst = 0
        self.nalloc = 0

    def buf(self, name=""):
        return Buf(name)

    def sb(self, name, shape, dt):
        self.nalloc += 1
        h = self.nc.alloc_sbuf_tensor(f"{name}_{self.nalloc}", list(shape), dt)
        return Tile(h.ap(), Buf(name))

    def ps(self, name, shape, dt):
        self.nalloc += 1
        h = self.nc.alloc_psum_tensor(f"{name}_{self.nalloc}", list(shape), dt)
        return Tile(h.ap(), Buf(name))

    def _wait_tok(self, eng, tok):
        if tok[0] == "c":
            _, e2, n = tok
            if e2 == eng and eng == "pe":
                return
            key = ("c", e2)
            if self.seen[eng].get(key, 0) >= n:
                return
            self.seen[eng][key] = n
            self.E[eng].wait_ge(self.csem[e2][(n - 1) % self.KS], (n - 1) // self.KS + 1)
        else:
            _, q, i, v = tok
            key = ("d", q, i)
            if self.seen[eng].get(key, 0) >= v:
                return
            self.seen[eng][key] = v
            self.E[eng].wait_ge(self.dsem[q][i], v)

    @staticmethod
    def _bufs(xs):
        out = []
        for x in xs:
            b = x.b if isinstance(x, Tile) else x
            if isinstance(b, (list, tuple)):
                out.extend(b)
            else:
                out.append(b)
        return out

    def _deps(self, reads, writes):
        deps = []
        for b in reads:
            deps.extend(b.w)
            if b.excl:
                deps.extend(b.r)
        for b in writes:
            deps.extend(b.w)
            deps.extend(b.r)
        return deps

    def _commit(self, tok, reads, writes):
        for b in writes:
            b.w = [tok]
            b.r = []
        for b in reads:
            if b not in writes:
                b.r.append(tok)
                if len(b.r) > 48:
                    best = {}
                    for t in b.r:
                        k = t[:2] if t[0] == "c" else t[:3]
                        if k not in best or best[k][-1] < t[-1]:
                            best[k] = t
                    b.r = list(best.values())

    def op(self, eng, fn, reads=(), writes=()):
        reads = self._bufs(reads)
        writes = self._bufs(writes)
        for tok in self._deps(reads, writes):
            self._wait_tok(eng, tok)
        ins = fn()
        self.cnt[eng] += 1
        n = self.cnt[eng]
        ins.then_inc(self.csem[eng][(n - 1) % self.KS], 1)
        self.ninst += 1
        self._commit(("c", eng, n), reads, writes)
        return ins

    def dma(self, q, out, in_, reads=(), writes=(), fn=None, **kw):
        reads = self._bufs(reads)
        writes = self._bufs(writes)
        for tok in self._deps(reads, writes):
            self._wait_tok(q, tok)
        i = self.dnext[q]
        self.dnext[q] = (i + 1) % self.NDQ[q]
        if self.dval[q][i] > 0:
            self._wait_tok(q, ("d", q, i, self.dval[q][i]))
        ins = self.E[q].dma_start(out=out, in_=in_, **kw) if fn is None else fn()
        ins.then_inc(self.dsem[q][i], 16)
        self.dval[q][i] += 16
        self.ninst += 1
        self._commit(("d", q, i, self.dval[q][i]), reads, writes)
        return ins

    def barrier(self):
        for e in self.E:
            for e2, n in self.cnt.items():
                if n > 0:
                    self._wait_tok(e, ("c", e2, n))
            for q in self.dsem:
                for i, v in enumerate(self.dval[q]):
                    if v > 0:
                        self._wait_tok(e, ("d", q, i, v))


class Cfg:
    def __init__(s, **kw):
        s.__dict__.update(kw)
        s.INNER = s.H * s.PH
        s.HPG = s.H // s.G
        s.XBC = s.INNER + 2 * s.G * s.NS
        s.G5 = s.W5 // s.C5
        s.AW = 3 * s.HG * s.DH
        s.INW = s.INNER + s.XBC + 2 * s.H + s.W5 + 3 * s.AW + 3 * s.D
        s.o_z = 0
        s.o_xbc = s.INNER
        s.o_dt = s.o_xbc + s.XBC
        s.o_u = s.o_dt + 2 * s.H
        s.o_q = s.o_u + s.W5
        s.o_k = s.o_q + s.AW
        s.o_v = s.o_k + s.AW
        s.o_g = s.o_v + s.AW
        s.KC = s.D // 128
        s.NCH = s.NT // 128
        s.BND = s.NT // 2
        s.PADT = [max(1, -(-(h * d) // 128)) for (h, d) in s.PAT]


FULL = Cfg(D=2048, NT=8192, TG=2048, H=24, G=4, PH=64, NS=128, CONV=5, W5=1024, C5=16, P5=64,
           DH=128, HG=4, PAT=((64, 1), (64, 4), (64, 16)), NBK=32, MAXD=1024, DFF=5504, FTG=1024,
           DEPTH=2, EPS=1e-6, SC=512, S5C=512, ATG=2)


def t5_bucket(rel, nb, maxd):
    half = nb // 2
    exact = half // 2
    sign = (rel > 0).astype(np.int32) * half
    n = np.abs(rel)
    large = exact + (np.log(np.maximum(n, 1) / exact) / np.log(maxd / exact) * (half - exact)).astype(np.int32)
    large = np.minimum(large, half - 1)
    return sign + np.where(n < exact, n, large)


def host_consts(c):
    k = {}
    k["c_ident"] = np.eye(128, dtype=np.float32)
    s = np.arange(128)[:, None]
    l = np.arange(128)[None, :]
    k["c_triu"] = (s <= l).astype(np.float32)
    k["c_tril"] = (s >= l).astype(np.float32)
    k["c_ones"] = np.ones((128, 128), np.float32)
    k["c_iota"] = np.broadcast_to(np.arange(c.S5C, dtype=np.float32), (128, c.S5C)).copy()
    for p, (half, dil) in enumerate(c.PAT):
        padt = c.PADT[p]
        W = (2 * padt + 1) * 128
        nrel = W + 127
        rel = np.arange(nrel) - 127 - padt * 128
        ok = (rel % dil == 0) & (np.abs(rel) <= half * dil)
        bk = t5_bucket(rel, c.NBK, c.MAXD)
        oh = np.zeros((c.NBK + 1, nrel), np.float32)
        for r in range(nrel):
            if ok[r]:
                oh[bk[r], r] = 1.0
            else:
                oh[c.NBK, r] = 1.0
        k[f"c_oh{p}"] = np.ascontiguousarray(oh[:, ::-1])
    return k


class Builder:
    def __init__(self, c, ncores):
        self.c = c
        nc = bass.Bass("TRN2", target_bir_lowering=False)
        self.nc = nc
        self.P = Prog(nc)
        self.din = {}
        self.dbufs = {}

    def inp(self, name, shape, dt=F32):
        self.din[name] = self.nc.dram_tensor(name, list(shape), dt, kind="ExternalInput").ap()
        return self.din[name]

    def scratch(self, name, shape, dt):
        return self.nc.dram_tensor(name, list(shape), dt).ap()

    def db(self, key):
        if key not in self.dbufs:
            self.dbufs[key] = Buf(str(key))
        return self.dbufs[key]

    def V(self, fn, r=(), w=()):
        return self.P.op("dve", fn, r, w)

    def A(self, fn, r=(), w=()):
        return self.P.op("act", fn, r, w)

    def G(self, fn, r=(), w=()):
        return self.P.op("pool", fn, r, w)

    def T(self, fn, r=(), w=()):
        return self.P.op("pe", fn, r, w)

    def mm(self, out, pairs, r, w, start=True, stop=True):
        nc = self.nc

        def fn():
            ins = None
            n = len(pairs)
            for i, (lt, rh) in enumerate(pairs):
                ins = nc.tensor.matmul(out, lhsT=lt, rhs=rh, start=(start and i == 0), stop=(stop and i == n - 1))
            return ins
        return self.P.op("pe", fn, r, w)

    def sb(self, st, name, shape, dt):
        self.P.nalloc += 1
        h = st.enter_context(self.nc.sbuf_tensor(f"{name}_{self.P.nalloc}", list(shape), dt))
        return Tile(h.ap() if hasattr(h, "ap") and callable(h.ap) else h[:], Buf(name))

    def bcast_load(self, tile, src_row_ap, n):
        self.P.dma("sp", tile.ap, src_row_ap.partition_broadcast(128), writes=[tile])

    def norm_T(self, st, xsrc, xkey, gain, t0, ntok, hT, col0=0, scale_tile=None):
        c, nc, P = self.c, self.nc, self.P
        if not hasattr(self, "_nt"):
            self._nt = None
        nt = self._norm_tiles
        ntile = -(-ntok // 128)
        for ti in range(ntile):
            rows = min(128, ntok - ti * 128)
            xt = nt["x"][self._nti % 2]
            xn = nt["xn"][self._nti % 2]
            self._nti += 1
            r0 = t0 + ti * 128
            P.dma("sp", xt.ap[0:rows, :], xsrc[r0:r0 + rows, :], reads=[self.db((xkey, r0 // 128))], writes=[xt])
            sq, ss, rt, rs = nt["sq"], nt["ss"], nt["rt"], nt["rs"]
            self.A(lambda: nc.scalar.activation(out=sq.ap[0:rows, :], in_=xt.ap[0:rows, :], func=AF.Square), [xt], [sq])
            self.V(lambda: nc.vector.reduce_sum(out=ss.ap[0:rows, :], in_=sq.ap[0:rows, :], axis=AX.X), [sq], [ss])
            self.A(lambda: nc.scalar.activation(out=rt.ap[0:rows, :], in_=ss.ap[0:rows, :], func=AF.Sqrt,
                                                scale=1.0 / c.D, bias=nt["eps"].ap[0:rows, :]), [ss, nt["eps"]], [rt])
            self.V(lambda: nc.vector.reciprocal(out=rs.ap[0:rows, :], in_=rt.ap[0:rows, :]), [rt], [rs])
            if scale_tile is not None:
                self.V(lambda: nc.vector.tensor_tensor(out=rs.ap[0:rows, :], in0=rs.ap[0:rows, :],
                                                       in1=scale_tile.ap[0:rows, :], op=ALU.mult), [rs, scale_tile], [rs])
            self.V(lambda: nc.vector.scalar_tensor_tensor(out=xn.ap[0:rows, :], in0=xt.ap[0:rows, :], scalar=rs.ap[0:rows, 0:1],
                                                          in1=gain.ap[0:rows, :], op0=ALU.mult, op1=ALU.mult),
                   [xt, rs, gain], [xn])
            pst = self.pT
            def tr():
                ins = None
                for kc in range(c.KC):
                    ins = nc.tensor.transpose(pst.ap[:, kc * 128: kc * 128 + rows], xn.ap[0:rows, kc * 128:(kc + 1) * 128],
                                              self.identb.ap[0:rows, 0:rows])
                return ins
            self.T(tr, [xn, self.identb], [pst])
            src = pst.ap[:, 0:c.KC * 128].rearrange("p (k t) -> p k t", t=128)[:, :, 0:rows]
            dst = hT.ap[:, :, col0 + ti * 128: col0 + ti * 128 + rows]
            self.A(lambda: nc.scalar.copy(out=dst, in_=src), [pst], [hT])

    def norm_setup(self, st, nbuf=2):
        c = self.c
        self._norm_tiles = {
            "x": [self.sb(st, "nx", [128, c.D], F32) for _ in range(nbuf)] * (2 // nbuf),
            "xn": [self.sb(st, "nxn", [128, c.D], BF16) for _ in range(nbuf)] * (2 // nbuf),
            "sq": self.sb(st, "nsq", [128, c.D], F32),
            "ss": self.sb(st, "nss", [128, 1], F32),
            "rt": self.sb(st, "nrt", [128, 1], F32),
            "rs": self.sb(st, "nrs", [128, 1], F32),
            "eps": self.sb(st, "neps", [128, 1], F32),
        }
        self._nti = 0
        e = self._norm_tiles["eps"]
        self.V(lambda: self.nc.vector.memset(e.ap, c.EPS), [], [e])

    def wt_setup(self, st, kmax, cw, nbf=3):
        self._w = {"st": [self.sb(st, "wst", [128, kmax, cw], F32) for _ in range(2)],
                   "bf": [self.sb(st, "wbf", [128, kmax, cw], BF16) for _ in range(nbf)], "i": 0, "kmax": kmax, "cw": cw, "nbf": nbf}

    def load_w(self, Wl, k0, kn, c0, cn):
        nc, P = self.nc, self.P
        w = self._w
        ws, wb = w["st"][w["i"] % 2], w["bf"][w["i"] % w["nbf"]]
        w["i"] += 1
        src = Wl[k0 * 128:(k0 + kn) * 128, c0:c0 + cn].rearrange("(k p) n -> p k n", p=128)
        P.dma("sp", ws.ap[:, 0:kn, 0:cn], src, writes=[ws])
        self.G(lambda: nc.gpsimd.tensor_copy(out=wb.ap[:, 0:kn, 0:cn], in_=ws.ap[:, 0:kn, 0:cn]), [ws], [wb])
        return wb

    def dense_feat(self, actT, KC, ntok, Wl, c0, ncols, evac, tokoff=0):
        nc = self.nc
        cw = self._w["cw"]
        ntt = -(-ntok // 512)
        chunks = [(cc, min(cw, c0 + ncols - cc)) for cc in range(c0, c0 + ncols, cw)]
        depth = 2 if self._w["nbf"] >= 4 else 1
        q = [self.load_w(Wl, 0, KC, ch[0], ch[1]) for ch in chunks[:depth]]
        for ci, (cc, cn) in enumerate(chunks):
            wb = q.pop(0)
            if ci + depth < len(chunks):
                q.append(self.load_w(Wl, 0, KC, chunks[ci + depth][0], chunks[ci + depth][1]))
            for b0 in range(0, cn, 128):
                nb = min(128, cn - b0)
                for tt in range(ntt):
                    tw = min(512, ntok - tt * 512)
                    bank = self._bank % 8
                    self._bank += 1
                    pt = self.psb[bank]
                    pairs = [(wb.ap[:, kc, b0:b0 + nb], actT.ap[:, kc, tokoff + tt * 512: tokoff + tt * 512 + tw]) for kc in range(KC)]
                    self.mm(pt.ap[0:nb, 0:tw], pairs, [wb, actT], self.pbufs(bank))
                    evac(pt.ap[0:nb, 0:tw], cc + b0, nb, tt * 512, tw, self.pbufs(bank))

    def dense_tok(self, actT, KC, ntok, Wl, c0, ncols, CW, evac, tokoff=0):
        ntile = -(-ntok // 128)
        kmax = self._w["kmax"]
        assert CW <= self._w["cw"]
        per_bank = 512 // CW
        assert ntile <= 4 * per_bank, (ntile, CW)
        kgs = list(range(0, KC, kmax))
        groups = [(cc, min(CW, c0 + ncols - cc)) for cc in range(c0, c0 + ncols, CW)]
        prefetch = self._w["nbf"] >= 2 * len(kgs)
        assert self._w["nbf"] >= len(kgs)
        ldg = lambda g: [(kg, min(kmax, KC - kg), self.load_w(Wl, kg, min(kmax, KC - kg), g[0], g[1])) for kg in kgs]
        nxt = ldg(groups[0])
        for gi, (cc, cn) in enumerate(groups):
            base = (self._bank % 2) * 4
            self._bank += 1
            wbs = nxt if (prefetch or gi == 0) else ldg(groups[gi])
            if prefetch and gi + 1 < len(groups):
                nxt = ldg(groups[gi + 1])
            for ti in range(ntile):
                rows = min(128, ntok - ti * 128)
                bank = base + ti // per_bank
                off = (ti % per_bank) * CW
                pt = self.psb[bank]
                pairs = []
                for (kg, kn, wb) in wbs:
                    pairs += [(actT.ap[:, kg + k, tokoff + ti * 128: tokoff + ti * 128 + rows], wb.ap[:, k, 0:cn]) for k in range(kn)]
                hb = self.phalf[bank]
                self.mm(pt.ap[0:rows, off:off + cn], pairs, [w_[2] for w_ in wbs] + [actT], hb)
                evac(pt.ap[0:rows, off:off + cn], ti * 128, rows, cc, cn, hb)

    def pbufs(self, bank):
        return self.phalf[bank]


import contextlib


def weight_specs(c):
    L = c.DEPTH
    return [
        ("norm_mix", [L, c.D]), ("w_in", [L, c.D, c.INW]), ("ssm_cw", [L, c.XBC, 8]),
        ("ssm_a_log", [L, 2 * c.H]), ("ssm_dt_bias", [L, 2 * c.H]), ("ssm_dcol", [L, 128, c.INNER // 128]),
        ("ssm_ng", [L, 128, c.INNER // 128]), ("ssm_w_out", [L, c.INNER, c.D]),
        ("s5_par", [L, 2, 128, c.TS, 3]), ("s5_b", [L, 128, c.TS, 2, c.C5]), ("s5_c", [L, 2, 128, c.TS, 2, c.C5]),
        ("s5_dcol", [L, 128, c.W5 // 128]), ("s5_w_glu", [L, c.W5, 2 * c.D]), ("att_w_out", [L, c.HG * c.DH, c.D]),
        ("w_o", [L, c.D, c.D]), ("norm_ffn", [L, c.D]), ("w_up", [L, c.D, 2 * c.DFF]),
        ("ffn_cw", [L, 128, 2 * c.DFF // 128, 4]), ("w_down", [L, c.DFF, c.D]), ("final_norm", [1, c.D]),
        ("rel_bias", [c.NBK, 3 * c.HG]),
    ]


def build(c, stop_after=None, dbg=()):
    c.TS = c.G5 * c.P5 // 128
    B = Builder(c, 1)
    nc, P = B.nc, B.P
    x_in = B.inp("x", [c.NT, c.D])
    y_out = nc.dram_tensor("y", [c.NT, c.D], F32, kind="ExternalOutput").ap()
    link = B.inp("link", [128, 1])
    nlink = B.inp("nlink", [128, 1])
    W = {n: B.inp(n, s) for n, s in weight_specs(c)}
    K = {n: B.inp(n, list(v.shape)) for n, v in host_consts(c).items()}
    S = {
        "xres": B.scratch("xres", [c.NT, c.D], F32),
        "xmid": B.scratch("xmid", [c.NT, c.D], F32),
        "zT": B.scratch("zT", [c.INNER, c.NT], BF16),
        "xbcT": B.scratch("xbcT", [c.XBC, c.NT], BF16),
        "dt": B.scratch("dt_tok", [c.NT, 2 * c.H], F32),
        "uT": B.scratch("uT", [c.W5, c.NT], BF16),
        "qT": B.scratch("qT", [c.AW, c.NT], BF16),
        "kT": B.scratch("kT", [c.AW, c.NT], BF16),
        "v": B.scratch("v_tok", [c.NT, c.AW], BF16),
        "gT": B.scratch("gT", [3 * c.D, c.NT], BF16),
        "ynT": B.scratch("ynT", [c.INNER, c.NT], BF16),
        "s5T": B.scratch("s5T", [c.W5, c.NT], BF16),
        "attT": B.scratch("attT", [c.HG * c.DH, c.NT], BF16),
        "hin": B.scratch("hin", [2, c.NCH, 128, c.INNER], BF16),
    }
    B.S, B.W, B.K = S, W, K
    B.stop_after = stop_after

    psall = nc.alloc_psum_tensor("psall", [128, 8, 512], F32).ap()
    B.phalf = [[b_, b_] for b_ in [Buf(f"ps{i}", excl=True) for i in range(8)]]
    B.psb = [Tile(psall[:, i, :], B.phalf[i]) for i in range(8)]
    B.pT = Tile(psall[:, 6:8, :].bitcast(BF16).rearrange("p a b -> p (a b)"), B.phalf[6] + B.phalf[7])
    B._bank = 0
    gst = contextlib.ExitStack()
    identf = B.sb(gst, "identf", [128, 128], F32)
    B.identb = B.sb(gst, "identb", [128, 128], BF16)
    onesf = B.sb(gst, "onesf", [128, 128], F32)
    onesb = B.sb(gst, "onesb", [128, 128], BF16)
    triu = B.sb(gst, "triu", [128, 128], F32)
    tril = B.sb(gst, "tril", [128, 128], F32)
    linkt = B.sb(gst, "linkt", [128, 1], F32)
    nlinkt = B.sb(gst, "nlinkt", [128, 1], F32)
    B.identf, B.onesf, B.onesb, B.triu, B.tril, B.linkt, B.nlinkt = identf, onesf, onesb, triu, tril, linkt, nlinkt
    P.dma("sp", identf.ap, K["c_ident"], writes=[identf])
    P.dma("sp", onesf.ap, K["c_ones"], writes=[onesf])
    P.dma("sp", triu.ap, K["c_triu"], writes=[triu])
    P.dma("sp", tril.ap, K["c_tril"], writes=[tril])
    P.dma("sp", linkt.ap, link, writes=[linkt])
    P.dma("sp", nlinkt.ap, nlink, writes=[nlinkt])
    B.V(lambda: nc.vector.tensor_copy(out=B.identb.ap, in_=identf.ap), [identf], [B.identb])
    B.V(lambda: nc.vector.tensor_copy(out=onesb.ap, in_=onesf.ap), [onesf], [onesb])

    class PTile(Tile):
        pass
    B.pT_bufs = B.phalf[6] + B.phalf[7]

    attn_setup(B)
    for l in range(c.DEPTH):
        xsrc, xkey = (x_in, "xin") if l == 0 else (S["xres"], "xres")
        phase_inproj(B, l, xsrc, xkey)
        if stop_after == ("inproj", l):
            break
        phase_ssd(B, l)
        if stop_after in (("ssd", l), ("ssd0", l), ("ssdA", l)):
            break
        phase_s5(B, l)
        if stop_after == ("s5", l):
            break
        phase_attn(B, l)
        if stop_after == ("attn", l):
            break
        phase_merge(B, l, xsrc, xkey)
        if stop_after == ("merge", l):
            break
        phase_ffn(B, l)
        if stop_after == ("ffn", l):
            break
    else:
        phase_final(B, y_out)
    for name in dbg:
        src = S[name]
        o = nc.dram_tensor("dbg_" + name, list(src.shape), src.dtype, kind="ExternalOutput").ap()
        P.dma("sp", o, src)
    P.barrier()
    gst.close()
    return nc, B


def phase_inproj(B, l, xsrc, xkey):
    c, nc, P, S, W = B.c, B.nc, B.P, B.S, B.W
    with contextlib.ExitStack() as st:
        gain = B.sb(st, "gain", [128, c.D], F32)
        B.bcast_load(gain, W["norm_mix"][l:l + 1, :], c.D)
        B.norm_setup(st)
        hT = B.sb(st, "hT", [128, c.KC, c.TG], BF16)
        B.wt_setup(st, c.KC, 256, nbf=4)
        stg = [B.sb(st, "stg", [128, c.TG], BF16) for _ in range(3)]
        ntile = c.TG // 128
        vst = [B.sb(st, "vst", [128, ntile, 128], BF16) for _ in range(2)]
        dst_ = B.sb(st, "dtst", [128, ntile, 2 * c.H], F32)
        segs = [("zT", c.o_z, c.INNER), ("xbcT", c.o_xbc, c.XBC), ("uT", c.o_u, c.W5), ("qT", c.o_q, c.AW),
                ("kT", c.o_k, c.AW), ("gT", c.o_g, 3 * c.D)]
        cnt = {"blk": 0, "ev": 0, "v": 0}
        ntt = -(-c.TG // 512)
        for tg in range(c.NT // c.TG):
            t0 = tg * c.TG
            B.norm_T(st, xsrc, xkey, gain, t0, c.TG, hT)
            for (dname, c0, n) in segs:
                def evac(ps, col_abs, nb, tok0, tw, pb, dname=dname, c0=c0):
                    sg = stg[cnt["blk"] % 3]
                    cnt["ev"] += 1
                    if cnt["ev"] % 2 == 0:
                        B.A(lambda: nc.scalar.copy(out=sg.ap[0:nb, tok0:tok0 + tw], in_=ps), pb, [sg])
                    else:
                        B.V(lambda: nc.vector.tensor_copy(out=sg.ap[0:nb, tok0:tok0 + tw], in_=ps), pb, [sg])
                    if tok0 + tw >= c.TG:
                        P.dma("sp", S[dname][col_abs - c0: col_abs - c0 + nb, t0:t0 + c.TG], sg.ap[0:nb, :], reads=[sg])
                        cnt["blk"] += 1
                B.dense_feat(hT, c.KC, c.TG, W["w_in"][l], c0, n, evac)

            def evac_dt(ps, tok0, rows, col_abs, cw, pb):
                ti = tok0 // 128
                B.V(lambda: nc.vector.tensor_copy(out=dst_.ap[0:rows, ti, :], in_=ps), pb, [dst_])
                if ti == ntile - 1:
                    P.dma("sp", S["dt"][t0:t0 + c.TG, :].rearrange("(t p) n -> p t n", p=128), dst_.ap, reads=[dst_])
            B.dense_tok(hT, c.KC, c.TG, W["w_in"][l], c.o_dt, 2 * c.H, 2 * c.H, evac_dt)

            def evac_v(ps, tok0, rows, col_abs, cw, pb):
                ti = tok0 // 128
                vs = vst[cnt["v"] % 2]
                B.A(lambda: nc.scalar.copy(out=vs.ap[0:rows, ti, 0:cw], in_=ps), pb, [vs])
                if ti == ntile - 1:
                    P.dma("sp", S["v"][t0:t0 + c.TG, col_abs - c.o_v: col_abs - c.o_v + cw].rearrange("(t p) n -> p t n", p=128),
                          vs.ap[:, :, 0:cw], reads=[vs])
                    cnt["v"] += 1
            B.dense_tok(hT, c.KC, c.TG, W["w_in"][l], c.o_v, c.AW, 128, evac_v)
        P.barrier()


def layout_weights(c, I):
    L = c.DEPTH
    TS = c.G5 * c.P5 // 128
    f = lambda a: np.ascontiguousarray(np.asarray(a, dtype=np.float32))
    o = {}
    o["norm_mix"] = f(I["norm_mix"])
    o["w_in"] = f(I["w_in"])
    cw = np.zeros((L, c.XBC, 8), np.float32)
    cw[:, :, 0:c.CONV] = np.asarray(I["ssm_conv_w"]).transpose(0, 2, 1)
    cw[:, :, 5] = np.asarray(I["ssm_conv_b"])
    o["ssm_cw"] = cw
    o["ssm_a_log"] = f(np.asarray(I["ssm_a_log"]).reshape(L, 2 * c.H))
    o["ssm_dt_bias"] = f(np.asarray(I["ssm_dt_bias"]).reshape(L, 2 * c.H))
    dcol = np.repeat(np.asarray(I["ssm_d"]), c.PH, axis=1)
    o["ssm_dcol"] = f(dcol.reshape(L, c.INNER // 128, 128).transpose(0, 2, 1))
    o["ssm_ng"] = f(np.asarray(I["ssm_norm"]).reshape(L, c.INNER // 128, 128).transpose(0, 2, 1))
    o["ssm_w_out"] = f(I["ssm_w_out"])
    st = lambda a: np.asarray(a).reshape(L, 2, TS, 128).transpose(0, 1, 3, 2)
    ls = np.broadcast_to(np.asarray(I["s5_log_step"])[..., None], (L, 2, c.G5, c.P5))
    o["s5_par"] = f(np.stack([st(I["s5_a_re"]), st(I["s5_a_im"]), st(ls)], axis=-1))
    sb_ = lambda b: np.asarray(b).reshape(L, TS, 128, c.C5).transpose(0, 2, 1, 3)
    o["s5_b"] = f(np.stack([sb_(I["s5_b_re"]), sb_(I["s5_b_im"])], axis=3))
    sc_ = lambda cc: np.asarray(cc).transpose(0, 1, 2, 4, 3).reshape(L, 2, TS, 128, c.C5).transpose(0, 1, 3, 2, 4)
    o["s5_c"] = f(np.stack([sc_(I["s5_c_re"]), sc_(I["s5_c_im"])], axis=4))
    o["s5_dcol"] = f(np.asarray(I["s5_d"]).reshape(L, c.W5 // 128, 128).transpose(0, 2, 1))
    o["s5_w_glu"] = f(I["s5_w_glu"])
    o["att_w_out"] = f(I["att_w_out"])
    o["w_o"] = f(I["w_o"])
    o["norm_ffn"] = f(I["norm_ffn"])
    o["w_up"] = f(I["w_up"])
    fc = np.concatenate([np.asarray(I["ffn_conv_w"]).transpose(0, 2, 1), np.asarray(I["ffn_conv_b"])[:, :, None]], axis=2)
    o["ffn_cw"] = f(fc.reshape(L, 2 * c.DFF // 128, 128, 4).transpose(0, 2, 1, 3))
    o["w_down"] = f(I["w_down"])
    o["final_norm"] = f(np.asarray(I["final_norm"]).reshape(1, c.D))
    o["rel_bias"] = f(I["rel_bias"])
    return o


def core_inputs(c, wl, consts, x_stream, link):
    m = dict(wl)
    m.update(consts)
    m["x"] = np.ascontiguousarray(x_stream, dtype=np.float32)
    m["link"] = np.full((128, 1), float(link), np.float32)
    m["nlink"] = np.full((128, 1), 0.0 if link else NEG, np.float32)
    return m


def _bc(ap, shape):
    return ap.broadcast_to(list(shape))


def phase_ssd(B, l):
    c, nc, P, S, W = B.c, B.nc, B.P, B.S, B.W
    CT = c.XBC // 128
    TI = c.INNER // 128
    H, PH, G, HPG, NS = c.H, c.PH, c.G, c.HPG, c.NS
    BND = c.BND
    SC = c.SC
    CPS = SC // 128
    NSC = c.NT // SC
    with contextlib.ExitStack() as st:
        cw = B.sb(st, "cw", [128, CT, 8], F32)
        P.dma("sp", cw.ap, W["ssm_cw"][l].rearrange("(t p) k -> p t k", p=128), writes=[cw])
        xps = [B.sb(st, "xp", [128, 2, BND + 4], BF16) for _ in range(2)]
        accs = [B.sb(st, "acc", [128, 2, BND], F32) for _ in range(2)]
        xos = [B.sb(st, "xo", [128, 2, BND], BF16) for _ in range(2)]
        for xp in xps:
            B.V(lambda: nc.vector.memset(xp.ap, 0.0), [], [xp])
        for ct in range(CT):
            xp, acc, xo = xps[ct % 2], accs[ct % 2], xos[ct % 2]
            rows = S["xbcT"][ct * 128:(ct + 1) * 128, :]
            P.dma("sp", xp.ap[:, :, 2:2 + BND], rows.rearrange("p (h t) -> p h t", h=2), writes=[xp])
            B.V(lambda: nc.vector.tensor_scalar(out=xp.ap[:, 0, BND + 2:BND + 4], in0=xp.ap[:, 1, 2:4], scalar1=B.linkt.ap[:, 0:1],
                                                scalar2=None, op0=ALU.mult), [xp, B.linkt], [xp])
            B.V(lambda: nc.vector.tensor_scalar(out=xp.ap[:, 1, 0:2], in0=xp.ap[:, 0, BND:BND + 2], scalar1=B.linkt.ap[:, 0:1],
                                                scalar2=None, op0=ALU.mult), [xp, B.linkt], [xp])
            B.V(lambda: nc.vector.tensor_scalar(out=acc.ap, in0=xp.ap[:, :, 0:BND], scalar1=cw.ap[:, ct, 0:1], scalar2=cw.ap[:, ct, 5:6],
                                                op0=ALU.mult, op1=ALU.add), [xp, cw], [acc])
            for k in range(1, c.CONV):
                B.V(lambda k=k: nc.vector.scalar_tensor_tensor(out=acc.ap, in0=xp.ap[:, :, k:k + BND], scalar=cw.ap[:, ct, k:k + 1],
                                                               in1=acc.ap, op0=ALU.mult, op1=ALU.add), [xp, cw, acc], [acc])
            B.A(lambda: nc.scalar.activation(out=xo.ap, in_=acc.ap, func=AF.Silu), [acc], [xo])
            P.dma("sp", rows.rearrange("p (h t) -> p h t", h=2), xo.ap, reads=[xo])
        P.barrier()

    if B.stop_after == ("ssd0", l):
        return

    def dt_prep(st):
        t = {}
        t["bias"] = B.sb(st, "dtb", [128, 2 * H], F32)
        t["A"] = B.sb(st, "Abc", [128, 2 * H], F32)
        B.bcast_load(t["bias"], W["ssm_dt_bias"][l:l + 1, :], 2 * H)
        B.bcast_load(t["A"], W["ssm_a_log"][l:l + 1, :], 2 * H)
        A_ = t["A"]
        B.A(lambda: nc.scalar.activation(out=A_.ap, in_=A_.ap, func=AF.Exp), [A_], [A_])
        B.V(lambda: nc.vector.tensor_scalar(out=A_.ap, in0=A_.ap, scalar1=-1.0, scalar2=None, op0=ALU.mult), [A_], [A_])
        for n in ("raw", "t", "a", "e", "dtv", "av"):
            t[n] = B.sb(st, "dt" + n, [128, CPS, 2 * H], F32)
        return t

    def dt_compute(t, sc):
        raw, tt, a, e, dtv, av = t["raw"], t["t"], t["a"], t["e"], t["dtv"], t["av"]
        P.dma("sp", raw.ap, S["dt"][sc * SC:(sc + 1) * SC, :].rearrange("(k p) n -> p k n", p=128), writes=[raw])
        bb = _bc(t["bias"].ap.unsqueeze(1), [128, CPS, 2 * H])
        B.V(lambda: nc.vector.tensor_tensor(out=tt.ap, in0=raw.ap, in1=bb, op=ALU.add), [raw, t["bias"]], [tt])
        B.A(lambda: nc.scalar.activation(out=a.ap, in_=tt.ap, func=AF.Abs), [tt], [a])
        B.A(lambda: nc.scalar.activation(out=e.ap, in_=a.ap, func=AF.Exp, scale=-1.0), [a], [e])
        B.A(lambda: nc.scalar.activation(out=e.ap, in_=e.ap, func=AF.Ln, bias=1.0), [e], [e])
        B.V(lambda: nc.vector.tensor_scalar(out=a.ap, in0=tt.ap, scalar1=0.0, scalar2=None, op0=ALU.max), [tt], [a])
        B.V(lambda: nc.vector.tensor_tensor(out=dtv.ap, in0=a.ap, in1=e.ap, op=ALU.add), [a, e], [dtv])
        ab = _bc(t["A"].ap.unsqueeze(1), [128, CPS, 2 * H])
        B.V(lambda: nc.vector.tensor_tensor(out=av.ap, in0=dtv.ap, in1=ab, op=ALU.mult), [dtv, t["A"]], [av])

    def load_xc(xc, sc):
        P.dma("sp", xc.ap, S["xbcT"][:, sc * SC:(sc + 1) * SC].rearrange("(t p) n -> p t n", p=128), writes=[xc])

    for d in range(2):
        with contextlib.ExitStack() as st:
            t = dt_prep(st)
            xcs = [B.sb(st, "xc", [128, CT, SC], BF16) for _ in range(2)]
            xsB = B.sb(st, "xsB", [128, c.INNER + G * NS], BF16)
            acs = B.sb(st, "acs", [128, H], F32)
            dte = B.sb(st, "dte", [128, H], F32)
            coef = B.sb(st, "coef", [128, H], F32)
            dec = B.sb(st, "dec", [128, H], F32)
            xdd = B.sb(st, "xdd", [128, c.INNER], BF16)
            Hst = B.sb(st, "Hst", [128, c.INNER], F32)
            hstg = [B.sb(st, "hstg", [128, c.INNER], BF16) for _ in range(2)]
            B.V(lambda: nc.vector.memset(Hst.ap, 0.0), [], [Hst])
            tri = B.triu if d == 0 else B.tril
            scs = list(range(NSC)) if d == 0 else list(range(NSC - 1, -1, -1))
            bnd_chunk = c.NCH // 2 if d == 0 else c.NCH // 2 - 1
            psS = [B.psb[0], B.psb[1], B.psb[2]]
            pm = B.psb[3]
            for si, sc in enumerate(scs):
                xc = xcs[si % 2]
                load_xc(xc, sc)
                dt_compute(t, sc)
                cks = list(range(CPS)) if d == 0 else list(range(CPS - 1, -1, -1))
                for ck in cks:
                    gck = sc * CPS + ck
                    tk = slice(ck * 128, (ck + 1) * 128)

                    def tr():
                        ins = None
                        for i in range(TI):
                            ins = nc.tensor.transpose(B.pT.ap[:, i * 128:(i + 1) * 128], xc.ap[:, i, tk], B.identb.ap)
                        for g in range(G):
                            ins = nc.tensor.transpose(B.pT.ap[:, (TI + g) * 128:(TI + g + 1) * 128], xc.ap[:, TI + g, tk], B.identb.ap)
                        return ins
                    B.T(tr, [xc, B.identb], [B.pT])
                    B.A(lambda: nc.scalar.copy(out=xsB.ap, in_=B.pT.ap[:, 0:c.INNER + G * NS]), [B.pT], [xsB])
                    a_d = t["av"].ap[:, ck, d * H:(d + 1) * H]
                    B.mm(pm.ap[:, 0:H], [(tri.ap, a_d)], [tri, t["av"]], [B.phalf[3][0]])
                    B.mm(pm.ap[:, 256:256 + H], [(B.onesf.ap, a_d)], [B.onesf, t["av"]], [B.phalf[3][1]])
                    B.A(lambda: nc.scalar.copy(out=acs.ap, in_=pm.ap[:, 0:H]), [B.phalf[3][0]], [acs])
                    B.V(lambda: nc.vector.tensor_tensor(out=dte.ap, in0=pm.ap[:, 256:256 + H], in1=acs.ap, op=ALU.subtract),
                        [B.phalf[3][1], acs], [dte])
                    B.A(lambda: nc.scalar.activation(out=dte.ap, in_=dte.ap, func=AF.Exp), [dte], [dte])
                    B.A(lambda: nc.scalar.activation(out=dec.ap, in_=pm.ap[:, 256:256 + H], func=AF.Exp), [B.phalf[3][1]], [dec])
                    B.V(lambda: nc.vector.tensor_tensor(out=coef.ap, in0=t["dtv"].ap[:, ck, d * H:(d + 1) * H], in1=dte.ap, op=ALU.mult),
                        [t["dtv"], dte], [coef])
                    B.V(lambda: nc.vector.tensor_tensor(out=xdd.ap.rearrange("p (h q) -> p h q", q=PH),
                                                        in0=xsB.ap[:, 0:c.INNER].rearrange("p (h q) -> p h q", q=PH),
                                                        in1=_bc(coef.ap.unsqueeze(2), [128, H, PH]), op=ALU.mult), [xsB, coef], [xdd])
                    for g in range(G):
                        c0 = g * HPG * PH
                        c1 = c0 + HPG * PH
                        p0 = c0
                        while p0 < c1:
                            p1 = min(c1, (p0 // 512 + 1) * 512)
                            bk = p0 // 512
                            B.mm(psS[bk].ap[:, p0 - bk * 512:p1 - bk * 512],
                                 [(xsB.ap[:, c.INNER + g * NS: c.INNER + (g + 1) * NS], xdd.ap[:, p0:p1])], [xsB, xdd], B.phalf[bk])
                            p0 = p1
                    if gck == bnd_chunk:
                        B.V(lambda: nc.vector.tensor_scalar(out=Hst.ap, in0=Hst.ap, scalar1=B.linkt.ap[:, 0:1], scalar2=None, op0=ALU.mult),
                            [Hst, B.linkt], [Hst])
                    hs = hstg[gck % 2]
                    B.A(lambda: nc.scalar.copy(out=hs.ap, in_=Hst.ap), [Hst], [hs])
                    P.dma("sp", S["hin"][d, gck], hs.ap, reads=[hs])
                    B.V(lambda: nc.vector.tensor_tensor(out=Hst.ap.rearrange("p (h q) -> p h q", q=PH),
                                                        in0=Hst.ap.rearrange("p (h q) -> p h q", q=PH),
                                                        in1=_bc(dec.ap.unsqueeze(2), [128, H, PH]), op=ALU.mult), [Hst, dec], [Hst])
                    nb = -(-c.INNER // 512)
                    for bk in range(nb):
                        w_ = min(512, c.INNER - bk * 512)
                        B.V(lambda bk=bk, w_=w_: nc.vector.tensor_tensor(out=Hst.ap[:, bk * 512:bk * 512 + w_], in0=psS[bk].ap[:, 0:w_],
                                                                         in1=Hst.ap[:, bk * 512:bk * 512 + w_], op=ALU.add),
                            B.phalf[bk] + [Hst], [Hst])
            P.barrier()

    if B.stop_after == ("ssdA", l):
        return
    HB = 3 if HPG % 3 == 0 else (2 if HPG % 2 == 0 else 1)
    with contextlib.ExitStack() as st:
        t = dt_prep(st)
        xcs = [B.sb(st, "xc", [128, CT, SC], BF16) for _ in range(2)]
        zts = [B.sb(st, "zt", [128, TI, SC], BF16) for _ in range(2)]
        yns = [B.sb(st, "ynst", [128, TI, SC], BF16) for _ in range(2)]
        dcol = B.sb(st, "dcol", [128, TI], F32)
        ng = B.sb(st, "ng", [128, TI], F32)
        P.dma("sp", dcol.ap, W["ssm_dcol"][l], writes=[dcol])
        P.dma("sp", ng.ap, W["ssm_ng"][l], writes=[ng])
        mneg = [B.sb(st, "mneg", [128, 128], F32) for _ in range(2)]
        for d, tri in enumerate((B.triu, B.tril)):
            B.V(lambda d=d, tri=tri: nc.vector.tensor_scalar(out=mneg[d].ap, in0=tri.ap, scalar1=-1.0, scalar2=-NEG, op0=ALU.add, op1=ALU.mult),
                [tri], [mneg[d]])
        epst = B.sb(st, "epst", [128, 1], F32)
        B.V(lambda: nc.vector.memset(epst.ap, c.EPS), [], [epst])
        xs_tok = B.sb(st, "xstok", [128, c.INNER], BF16)
        xd = [B.sb(st, "xd", [128, c.INNER], BF16) for _ in range(2)]
        acs2 = B.sb(st, "acs2", [128, 2 * H], F32)
        cbt = B.sb(st, "cbt", [128, G, 128], F32)
        hin = [[B.sb(st, "hin", [128, c.INNER], BF16) for _ in range(2)] for _ in range(2)]
        MT = [B.sb(st, "MT", [128, H, 128], BF16) for _ in range(2)]
        CsT = [B.sb(st, "CsT", [128, H, 128], BF16) for _ in range(2)]
        amask = [B.sb(st, "amask", [128, HB, 128], F32) for _ in range(2)]
        expA = [B.sb(st, "expA", [128, HB, 128], F32) for _ in range(2)]
        T1 = [B.sb(st, "T1", [128, HB, 128], F32) for _ in range(2)]
        yv = B.sb(st, "yv", [128, TI, 128], F32)
        sz = B.sb(st, "sz", [128, TI, 128], F32)
        sq = B.sb(st, "sq", [128, TI, 128], BF16)
        rt = B.sb(st, "rt", [128, 128], F32)
        psy = [B.psb[1], B.psb[2], B.psb[3]]
        pbc = [B.psb[0], B.psb[4]]
        pcb = B.psb[5]
        pmisc = B.psb[7]
        pmb = [B.phalf[7][1]]
        it = 0
        for sc in range(NSC):
            xc, zt, ynst = xcs[sc % 2], zts[sc % 2], yns[sc % 2]
            load_xc(xc, sc)
            P.dma("sp", zt.ap, S["zT"][:, sc * SC:(sc + 1) * SC].rearrange("(t p) n -> p t n", p=128), writes=[zt])
            dt_compute(t, sc)
            for ck in range(CPS):
                gck = sc * CPS + ck
                tk = slice(ck * 128, (ck + 1) * 128)

                def tr():
                    ins = None
                    for i in range(TI):
                        ins = nc.tensor.transpose(B.pT.ap[:, i * 128:(i + 1) * 128], xc.ap[:, i, tk], B.identb.ap)
                    return ins
                B.T(tr, [xc, B.identb], [B.phalf[6][0], B.phalf[6][1], B.phalf[7][0]])
                B.A(lambda: nc.scalar.copy(out=xs_tok.ap, in_=B.pT.ap[:, 0:c.INNER]), [B.phalf[6][0], B.phalf[6][1], B.phalf[7][0]], [xs_tok])
                for d in range(2):
                    B.V(lambda d=d: nc.vector.tensor_tensor(out=xd[d].ap.rearrange("p (h q) -> p h q", q=PH),
                                                            in0=xs_tok.ap.rearrange("p (h q) -> p h q", q=PH),
                                                            in1=_bc(t["dtv"].ap[:, ck, d * H:(d + 1) * H].unsqueeze(2), [128, H, PH]), op=ALU.mult),
                        [xs_tok, t["dtv"]], [xd[d]])
                    P.dma("sp", hin[d][gck % 2].ap, S["hin"][d, gck], writes=[hin[d][gck % 2]])
                B.mm(pmisc.ap[:, 256:256 + H], [(B.triu.ap, t["av"].ap[:, ck, 0:H])], [B.triu, t["av"]], pmb)
                B.mm(pmisc.ap[:, 256 + H:256 + 2 * H], [(B.tril.ap, t["av"].ap[:, ck, H:2 * H])], [B.tril, t["av"]], pmb)
                B.A(lambda: nc.scalar.copy(out=acs2.ap, in_=pmisc.ap[:, 256:256 + 2 * H]), pmb, [acs2])
                for g in range(G):
                    B.mm(pcb.ap[:, g * 128:(g + 1) * 128], [(xc.ap[:, TI + g, tk], xc.ap[:, TI + G + g, tk])], [xc], B.phalf[5])
                B.A(lambda: nc.scalar.copy(out=cbt.ap.rearrange("p g l -> p (g l)"), in_=pcb.ap[:, 0:G * 128]), B.phalf[5], [cbt])
                import os as _os
                for d in range(2):
                    if _os.environ.get("SSD_SKIP") == "3":
                        continue
                    tri = B.triu if d == 0 else B.tril
                    for hb in range(H // HB):
                        h0 = hb * HB
                        g = h0 // HPG
                        i2 = it % 2
                        it += 1
                        am, ea, t1, pb = amask[i2], expA[i2], T1[i2], pbc[i2]
                        pbb = B.phalf[0] if i2 == 0 else B.phalf[4]
                        a_sl = t["av"].ap[:, ck, d * H + h0: d * H + h0 + HB]
                        B.V(lambda: nc.vector.tensor_tensor(out=am.ap, in0=_bc(tri.ap.unsqueeze(1), [128, HB, 128]),
                                                            in1=_bc(a_sl.unsqueeze(2), [128, HB, 128]), op=ALU.mult), [tri, t["av"]], [am])
                        _lv = _os.environ.get("SSD_SKIP")
                        if _lv == "5":
                            continue
                        B.mm(pb.ap[:, 0:HB * 128], [(B.onesf.ap, am.ap.rearrange("p h l -> p (h l)"))], [B.onesf, am], pbb)
                        if _lv == "6":
                            continue
                        pv = pb.ap[:, 0:HB * 128].rearrange("p (h l) -> p h l", l=128)
                        B.A(lambda: nc.scalar.activation(out=ea.ap, in_=pv, func=AF.Exp), pbb, [ea])
                        if _lv == "7":
                            continue
                        B.V(lambda: nc.vector.tensor_tensor(out=t1.ap, in0=pv, in1=_bc(mneg[d].ap.unsqueeze(1), [128, HB, 128]), op=ALU.add),
                            pbb + [mneg[d]], [t1])
                        if _lv == "8":
                            continue
                        B.V(lambda: nc.vector.tensor_tensor(out=t1.ap, in0=t1.ap,
                                                            in1=_bc(acs2.ap[:, d * H + h0:d * H + h0 + HB].unsqueeze(2), [128, HB, 128]),
                                                            op=ALU.subtract), [t1, acs2], [t1])
                        B.A(lambda: nc.scalar.activation(out=t1.ap, in_=t1.ap, func=AF.Exp), [t1], [t1])
                        if _lv == "9":
                            continue
                        B.V(lambda: nc.vector.tensor_tensor(out=MT[d].ap[:, h0:h0 + HB, :], in0=t1.ap,
                                                            in1=_bc(cbt.ap[:, g:g + 1, :], [128, HB, 128]), op=ALU.mult), [t1, cbt], [MT[d]])
                        if _os.environ.get("SSD_SKIP") == "4":
                            continue
                        B.G(lambda: nc.gpsimd.tensor_tensor(out=CsT[d].ap[:, h0:h0 + HB, :], in0=ea.ap,
                                                            in1=_bc(xc.ap[:, TI + G + g, tk].unsqueeze(1), [128, HB, 128]), op=ALU.mult),
                            [ea, xc], [CsT[d]])
                if _os.environ.get("SSD_SKIP") in ("1", "3", "4", "5", "6", "7", "8", "9"):
                    continue
                hf, hb_ = hin[0][gck % 2], hin[1][gck % 2]
                for h in range(H):
                    tl = (h * PH) // 128
                    po = (h * PH) % 128
                    bk = (tl * 128) // 512
                    co = tl * 128 - bk * 512
                    hs = slice(h * PH, (h + 1) * PH)
                    B.mm(psy[bk].ap[po:po + PH, co:co + 128],
                         [(xd[0].ap[:, hs], MT[0].ap[:, h, :]), (xd[1].ap[:, hs], MT[1].ap[:, h, :]),
                          (hf.ap[:, hs], CsT[0].ap[:, h, :]), (hb_.ap[:, hs], CsT[1].ap[:, h, :])],
                         [xd[0], xd[1], MT[0], MT[1], hf, hb_, CsT[0], CsT[1]], B.phalf[1 + bk])
                if _os.environ.get("SSD_SKIP") == "2":
                    continue
                for i in range(TI):
                    bk = (i * 128) // 512
                    co = i * 128 - bk * 512
                    B.V(lambda i=i, bk=bk, co=co: nc.vector.scalar_tensor_tensor(out=yv.ap[:, i, :], in0=xc.ap[:, i, tk], scalar=dcol.ap[:, i:i + 1],
                                                                                  in1=psy[bk].ap[:, co:co + 128], op0=ALU.mult, op1=ALU.add),
                        [xc, dcol] + B.phalf[1 + bk], [yv])
                B.A(lambda: nc.scalar.activation(out=sz.ap, in_=zt.ap[:, :, tk], func=AF.Silu), [zt], [sz])
                B.V(lambda: nc.vector.tensor_tensor(out=yv.ap, in0=yv.ap, in1=sz.ap, op=ALU.mult), [yv, sz], [yv])
                B.A(lambda: nc.scalar.activation(out=sq.ap, in_=yv.ap, func=AF.Square), [yv], [sq])
                B.mm(pmisc.ap[:, 384:512], [(B.onesb.ap, sq.ap[:, i, :]) for i in range(TI)], [B.onesb, sq], pmb)
                B.A(lambda: nc.scalar.activation(out=rt.ap, in_=pmisc.ap[:, 384:512], func=AF.Sqrt, scale=1.0 / c.INNER, bias=epst.ap[:, 0:1]),
                    pmb + [epst], [rt])
                B.V(lambda: nc.vector.reciprocal(out=rt.ap, in_=rt.ap), [rt], [rt])
                B.V(lambda: nc.vector.tensor_tensor(out=yv.ap, in0=yv.ap, in1=_bc(rt.ap.unsqueeze(1), [128, TI, 128]), op=ALU.mult), [yv, rt], [yv])
                B.V(lambda: nc.vector.tensor_tensor(out=ynst.ap[:, :, tk], in0=yv.ap, in1=_bc(ng.ap.unsqueeze(2), [128, TI, 128]), op=ALU.mult),
                    [yv, ng], [ynst])
            P.dma("sp", S["ynT"][:, sc * SC:(sc + 1) * SC].rearrange("(t p) n -> p t n", p=128), ynst.ap, reads=[ynst])
        P.barrier()


def phase_s5(B, l):
    c, nc, P, S, W = B.c, B.nc, B.P, B.S, B.W
    TS, C5, NT, LC = c.TS, c.C5, c.NT, c.S5C
    KT5 = c.W5 // 128
    NLC = NT // LC
    PI = math.pi
    with contextlib.ExitStack() as st:
        par = B.sb(st, "s5par", [128, 2, TS, 3], F32)
        P.dma("sp", par.ap, W["s5_par"][l].rearrange("d p t k -> p d t k"), writes=[par])
        bb = B.sb(st, "s5b", [128, TS, 2, C5], F32)
        P.dma("sp", bb.ap, W["s5_b"][l], writes=[bb])
        cc = B.sb(st, "s5c", [128, 2, TS, 2, C5], F32)
        P.dma("sp", cc.ap, W["s5_c"][l].rearrange("d p t k o -> p d t k o"), writes=[cc])
        dcol = B.sb(st, "s5d", [128, KT5], F32)
        P.dma("sp", dcol.ap, W["s5_dcol"][l], writes=[dcol])
        iota = B.sb(st, "iota", [128, LC], F32)
        riota = B.sb(st, "riota", [128, LC], F32)
        P.dma("sp", iota.ap, B.K["c_iota"], writes=[iota])
        B.V(lambda: nc.vector.tensor_scalar(out=riota.ap, in0=iota.ap, scalar1=-1.0, scalar2=float(LC - 1), op0=ALU.mult, op1=ALU.add),
            [iota], [riota])
        hpi = B.sb(st, "hpi", [128, 1], F32)
        B.V(lambda: nc.vector.memset(hpi.ap, PI / 2), [], [hpi])
        n2 = 2 * TS
        mk = lambda n, w=n2: B.sb(st, n, [128, w], F32)
        are, aim, step, lr, th, rr, cosl, sinl = [mk(n) for n in ("are", "aim", "step", "lr", "th", "rr", "cosl", "sinl")]
        tA, tB, tC, tD = [mk(n) for n in ("tA", "tB", "tC", "tD")]
        ki = B.sb(st, "ki", [128, max(n2, LC)], I32)
        tabt = [B.sb(st, "tabt", [128, LC], F32) for _ in range(3)]

        def red_sincos(x_ap, n, sin_out, cos_out, tmp, deps, outs):
            y, g, a = tmp
            B.V(lambda: nc.vector.tensor_scalar(out=y, in0=x_ap, scalar1=1.0 / TWO_PI, scalar2=None, op0=ALU.mult), deps, [tmpb])
            B.V(lambda: nc.vector.tensor_copy(out=ki.ap[:, 0:n], in_=y), [tmpb], [ki])
            B.V(lambda: nc.vector.tensor_copy(out=y, in_=ki.ap[:, 0:n]), [ki], [tmpb])
            B.V(lambda: nc.vector.scalar_tensor_tensor(out=y, in0=y, scalar=-TWO_PI, in1=x_ap, op0=ALU.mult, op1=ALU.add), deps + [tmpb], [tmpb])
            B.V(lambda: nc.vector.tensor_single_scalar(out=g, in_=y, scalar=PI, op=ALU.is_gt), [tmpb], [tmpb])
            B.V(lambda: nc.vector.scalar_tensor_tensor(out=y, in0=g, scalar=-TWO_PI, in1=y, op0=ALU.mult, op1=ALU.add), [tmpb], [tmpb])
            B.V(lambda: nc.vector.tensor_single_scalar(out=g, in_=y, scalar=-PI, op=ALU.is_lt), [tmpb], [tmpb])
            B.V(lambda: nc.vector.scalar_tensor_tensor(out=y, in0=g, scalar=TWO_PI, in1=y, op0=ALU.mult, op1=ALU.add), [tmpb], [tmpb])
            B.V(lambda: nc.vector.tensor_scalar(out=y, in0=y, scalar1=-PI, scalar2=PI, op0=ALU.max, op1=ALU.min), [tmpb], [tmpb])
            B.A(lambda: nc.scalar.activation(out=sin_out, in_=y, func=AF.Sin), [tmpb], outs)
            B.A(lambda: nc.scalar.activation(out=a, in_=y, func=AF.Abs), [tmpb], [tmpb])
            B.A(lambda: nc.scalar.activation(out=cos_out, in_=a, func=AF.Sin, scale=-1.0, bias=hpi.ap[:, 0:1]), [tmpb, hpi], outs)
        tmpb = Buf("s5tmp")
        f2 = lambda k: par.ap[:, :, :, k]
        v2 = lambda t: t.ap.rearrange("p (d t) -> p d t", d=2)
        B.V(lambda: nc.vector.tensor_copy(out=v2(are), in_=f2(0)), [par], [are])
        B.V(lambda: nc.vector.tensor_copy(out=v2(aim), in_=f2(1)), [par], [aim])
        B.A(lambda: nc.scalar.activation(out=v2(step), in_=f2(2), func=AF.Exp), [par], [step])
        B.V(lambda: nc.vector.tensor_tensor(out=lr.ap, in0=are.ap, in1=step.ap, op=ALU.mult), [are, step], [lr])
        B.V(lambda: nc.vector.tensor_tensor(out=th.ap, in0=aim.ap, in1=step.ap, op=ALU.mult), [aim, step], [th])
        B.A(lambda: nc.scalar.activation(out=rr.ap, in_=lr.ap, func=AF.Exp), [lr], [rr])
        red_sincos(th.ap, n2, sinl.ap, cosl.ap, (tA.ap, tB.ap, tC.ap), [th], [sinl, cosl])
        cT, sT, cTl, sTl, thL = [mk(n) for n in ("cT", "sT", "cTl", "sTl", "thL")]
        B.V(lambda: nc.vector.tensor_scalar(out=thL.ap, in0=th.ap, scalar1=float(LC), scalar2=None, op0=ALU.mult), [th], [thL])
        red_sincos(thL.ap, n2, sT.ap, cT.ap, (tA.ap, tB.ap, tC.ap), [thL], [sT, cT])
        B.V(lambda: nc.vector.tensor_scalar(out=cTl.ap, in0=cT.ap, scalar1=B.linkt.ap[:, 0:1], scalar2=None, op0=ALU.mult), [cT, B.linkt], [cTl])
        B.V(lambda: nc.vector.tensor_scalar(out=sTl.ap, in0=sT.ap, scalar1=B.linkt.ap[:, 0:1], scalar2=None, op0=ALU.mult), [sT, B.linkt], [sTl])
        lbr, lbi, cr, ci = [mk(n) for n in ("lbr", "lbi", "cr", "ci")]
        B.V(lambda: nc.vector.tensor_tensor(out=lbr.ap, in0=rr.ap, in1=cosl.ap, op=ALU.mult), [rr, cosl], [lbr])
        B.V(lambda: nc.vector.tensor_tensor(out=lbi.ap, in0=rr.ap, in1=sinl.ap, op=ALU.mult), [rr, sinl], [lbi])
        B.V(lambda: nc.vector.tensor_scalar(out=lbr.ap, in0=lbr.ap, scalar1=-1.0, scalar2=None, op0=ALU.add), [lbr], [lbr])
        B.V(lambda: nc.vector.tensor_tensor(out=tA.ap, in0=are.ap, in1=are.ap, op=ALU.mult), [are, tmpb], [tmpb])
        B.V(lambda: nc.vector.tensor_tensor(out=tB.ap, in0=aim.ap, in1=aim.ap, op=ALU.mult), [aim, tmpb], [tmpb])
        B.V(lambda: nc.vector.tensor_tensor(out=tA.ap, in0=tA.ap, in1=tB.ap, op=ALU.add), [tmpb], [tmpb])
        B.V(lambda: nc.vector.reciprocal(out=tD.ap, in_=tA.ap), [tmpb], [tD])
        B.V(lambda: nc.vector.tensor_tensor(out=tA.ap, in0=lbr.ap, in1=are.ap, op=ALU.mult), [lbr, are, tmpb], [tmpb])
        B.V(lambda: nc.vector.tensor_tensor(out=tB.ap, in0=lbi.ap, in1=aim.ap, op=ALU.mult), [lbi, aim, tmpb], [tmpb])
        B.V(lambda: nc.vector.tensor_tensor(out=tA.ap, in0=tA.ap, in1=tB.ap, op=ALU.add), [tmpb], [tmpb])
        B.V(lambda: nc.vector.tensor_tensor(out=cr.ap, in0=tA.ap, in1=tD.ap, op=ALU.mult), [tmpb, tD], [cr])
        B.V(lambda: nc.vector.tensor_tensor(out=tA.ap, in0=lbi.ap, in1=are.ap, op=ALU.mult), [lbi, are, tmpb], [tmpb])
        B.V(lambda: nc.vector.tensor_tensor(out=tB.ap, in0=lbr.ap, in1=aim.ap, op=ALU.mult), [lbr, aim, tmpb], [tmpb])
        B.V(lambda: nc.vector.tensor_tensor(out=tA.ap, in0=tA.ap, in1=tB.ap, op=ALU.subtract), [tmpb], [tmpb])
        B.V(lambda: nc.vector.tensor_tensor(out=ci.ap, in0=tA.ap, in1=tD.ap, op=ALU.mult), [tmpb, tD], [ci])
        BT = B.sb(st, "BT", [128, 2, TS, 2, 128], BF16)
        Cp = B.sb(st, "Cp", [128, 2, TS, 2, 128], BF16)
        B.V(lambda: nc.vector.memset(Cp.ap, 0.0), [], [Cp])
        Bbr = B.sb(st, "Bbr", [128, TS, C5], F32)
        Bbi = B.sb(st, "Bbi", [128, TS, C5], F32)
        tE = B.sb(st, "tE", [128, TS, C5], F32)
        pad = [B.sb(st, "pad", [128, 128], BF16) for _ in range(2)]
        ipad = 0
        for d in range(2):
            crb = _bc(cr.ap[:, d * TS:(d + 1) * TS].unsqueeze(2), [128, TS, C5])
            cib = _bc(ci.ap[:, d * TS:(d + 1) * TS].unsqueeze(2), [128, TS, C5])
            bre, bim = bb.ap[:, :, 0, :], bb.ap[:, :, 1, :]
            B.V(lambda: nc.vector.tensor_tensor(out=Bbr.ap, in0=bre, in1=crb, op=ALU.mult), [bb, cr], [Bbr])
            B.V(lambda: nc.vector.tensor_tensor(out=tE.ap, in0=bim, in1=cib, op=ALU.mult), [bb, ci], [tE])
            B.V(lambda: nc.vector.tensor_tensor(out=Bbr.ap, in0=Bbr.ap, in1=tE.ap, op=ALU.subtract), [Bbr, tE], [Bbr])
            B.V(lambda: nc.vector.tensor_tensor(out=Bbi.ap, in0=bim, in1=crb, op=ALU.mult), [bb, cr], [Bbi])
            B.V(lambda: nc.vector.tensor_tensor(out=tE.ap, in0=bre, in1=cib, op=ALU.mult), [bb, ci], [tE])
            B.V(lambda: nc.vector.tensor_tensor(out=Bbi.ap, in0=Bbi.ap, in1=tE.ap, op=ALU.add), [Bbi, tE], [Bbi])
            for m in range(TS):
                off = (m % 4) * 2 * C5
                for k, src in enumerate((Bbr, Bbi)):
                    pd = pad[ipad % 2]
                    ipad += 1
                    B.V(lambda: nc.vector.memset(pd.ap, 0.0), [], [pd])
                    B.V(lambda: nc.vector.tensor_copy(out=pd.ap[0:64, off:off + C5], in_=src.ap[0:64, m, :]), [src], [pd])
                    B.V(lambda: nc.vector.tensor_copy(out=pd.ap[64:128, off + C5:off + 2 * C5], in_=src.ap[64:128, m, :]), [src], [pd])
                    B.T(lambda: nc.tensor.transpose(B.pT.ap[:, 0:128], pd.ap, B.identb.ap), [pd, B.identb], B.phalf[6])
                    B.A(lambda: nc.scalar.copy(out=BT.ap[:, d, m, k, :], in_=B.pT.ap[:, 0:128]), B.phalf[6], [BT])
                B.V(lambda: nc.vector.tensor_copy(out=Cp.ap[0:64, d, m, 0, off:off + C5], in_=cc.ap[0:64, d, m, 0, :]), [cc], [Cp])
                B.V(lambda: nc.vector.tensor_copy(out=Cp.ap[64:128, d, m, 0, off + C5:off + 2 * C5], in_=cc.ap[64:128, d, m, 0, :]), [cc], [Cp])
                B.V(lambda: nc.vector.tensor_scalar(out=Cp.ap[0:64, d, m, 1, off:off + C5], in0=cc.ap[0:64, d, m, 1, :], scalar1=-1.0, scalar2=None,
                                                    op0=ALU.mult), [cc], [Cp])
                B.V(lambda: nc.vector.tensor_scalar(out=Cp.ap[64:128, d, m, 1, off + C5:off + 2 * C5], in0=cc.ap[64:128, d, m, 1, :], scalar1=-1.0,
                                                    scalar2=None, op0=ALU.mult), [cc], [Cp])
        uT = B.sb(st, "uT", [128, NT], BF16)
        yacc = B.sb(st, "yacc", [128, NT], F32)
        nm = min(4, TS)
        cs = [B.sb(st, "cs", [128, LC], F32) for _ in range(nm)]
        sn = [B.sb(st, "sn", [128, LC], F32) for _ in range(nm)]
        z0 = [[B.sb(st, "z0", [128, 1], F32) for _ in range(2)] for _ in range(nm)]
        wr, wi, zr, zi = [[B.sb(st, n, [128, LC], F32) for _ in range(2)] for n in ("wr", "wi", "zr", "zi")]
        q1, q2 = [[B.sb(st, n, [128, LC], F32) for _ in range(2)] for n in ("q1", "q2")]
        xr = [B.sb(st, "xr", [128, LC], BF16) for _ in range(nm)]
        xi = [B.sb(st, "xi", [128, LC], BF16) for _ in range(nm)]
        c1 = B.sb(st, "c1", [128, 1], F32)
        it = 0
        for kt in range(KT5):
            ms = [m for m in range(4 * kt, 4 * kt + 4) if m < TS]
            P.dma("sp", uT.ap, S["uT"][kt * 128:(kt + 1) * 128, :], writes=[uT])
            for d in range(2):
                for j, m in enumerate(ms):
                    ang = tabt[0]
                    io = iota if d == 0 else riota
                    B.V(lambda: nc.vector.tensor_scalar(out=ang.ap, in0=io.ap, scalar1=th.ap[:, d * TS + m:d * TS + m + 1], scalar2=None, op0=ALU.mult),
                        [io, th], [ang])
                    red_sincos(ang.ap, LC, sn[j].ap, cs[j].ap, (tabt[1].ap, tabt[2].ap, tabt[2].ap), [ang], [sn[j], cs[j]])
                    for k in range(2):
                        B.V(lambda k=k: nc.vector.memset(z0[j][k].ap, 0.0), [], [z0[j][k]])
                order = list(range(NLC)) if d == 0 else list(range(NLC - 1, -1, -1))
                bnd_c = (c.BND // LC) if d == 0 else (c.BND // LC - 1)
                for ch in order:
                    tk = slice(ch * LC, (ch + 1) * LC)
                    for j, m in enumerate(ms):
                        i2 = it % 2
                        it += 1
                        pa, pb_ = B.psb[2 * i2], B.psb[2 * i2 + 1]
                        ba, bbuf = B.phalf[2 * i2], B.phalf[2 * i2 + 1]
                        B.mm(pa.ap[:, 0:LC], [(BT.ap[:, d, m, 0, :], uT.ap[:, tk])], [BT, uT], ba)
                        B.mm(pb_.ap[:, 0:LC], [(BT.ap[:, d, m, 1, :], uT.ap[:, tk])], [BT, uT], bbuf)
                        w_r, w_i, z_r, z_i, a1, a2 = wr[i2], wi[i2], zr[i2], zi[i2], q1[i2], q2[i2]
                        B.V(lambda: nc.vector.tensor_tensor(out=a1.ap, in0=pa.ap[:, 0:LC], in1=cs[j].ap, op=ALU.mult), ba + [cs[j]], [a1])
                        B.V(lambda: nc.vector.tensor_tensor(out=a2.ap, in0=pb_.ap[:, 0:LC], in1=sn[j].ap, op=ALU.mult), bbuf + [sn[j]], [a2])
                        B.V(lambda: nc.vector.tensor_tensor(out=w_r.ap, in0=a1.ap, in1=a2.ap, op=ALU.add), [a1, a2], [w_r])
                        B.V(lambda: nc.vector.tensor_tensor(out=a1.ap, in0=pb_.ap[:, 0:LC], in1=cs[j].ap, op=ALU.mult), bbuf + [cs[j]], [a1])
                        B.V(lambda: nc.vector.tensor_tensor(out=a2.ap, in0=pa.ap[:, 0:LC], in1=sn[j].ap, op=ALU.mult), ba + [sn[j]], [a2])
                        B.V(lambda: nc.vector.tensor_tensor(out=w_i.ap, in0=a1.ap, in1=a2.ap, op=ALU.subtract), [a1, a2], [w_i])
                        if ch == bnd_c:
                            for k in range(2):
                                B.V(lambda k=k: nc.vector.tensor_scalar(out=z0[j][k].ap, in0=z0[j][k].ap, scalar1=B.linkt.ap[:, 0:1], scalar2=None,
                                                                        op0=ALU.mult), [z0[j][k], B.linkt], [z0[j][k]])
                        rb = _bc(rr.ap[:, d * TS + m:d * TS + m + 1], [128, LC])
                        sl = (lambda a: a) if d == 0 else (lambda a: a[:, ::-1])
                        B.V(lambda: nc.vector.tensor_tensor_scan(out=sl(z_r.ap), data0=rb, data1=sl(w_r.ap), initial=z0[j][0].ap[:, 0:1],
                                                                 op0=ALU.mult, op1=ALU.add), [rr, w_r, z0[j][0]], [z_r])
                        B.V(lambda: nc.vector.tensor_tensor_scan(out=sl(z_i.ap), data0=rb, data1=sl(w_i.ap), initial=z0[j][1].ap[:, 0:1],
                                                                 op0=ALU.mult, op1=ALU.add), [rr, w_i, z0[j][1]], [z_i])
                        B.G(lambda: nc.gpsimd.tensor_tensor(out=a1.ap, in0=z_r.ap, in1=cs[j].ap, op=ALU.mult), [z_r, cs[j]], [a1])
                        B.G(lambda: nc.gpsimd.tensor_tensor(out=a2.ap, in0=z_i.ap, in1=sn[j].ap, op=ALU.mult), [z_i, sn[j]], [a2])
                        B.G(lambda: nc.gpsimd.tensor_tensor(out=xr[j].ap, in0=a1.ap, in1=a2.ap, op=ALU.subtract), [a1, a2], [xr[j]])
                        B.G(lambda: nc.gpsimd.tensor_tensor(out=a1.ap, in0=z_i.ap, in1=cs[j].ap, op=ALU.mult), [z_i, cs[j]], [a1])
                        B.G(lambda: nc.gpsimd.tensor_tensor(out=a2.ap, in0=z_r.ap, in1=sn[j].ap, op=ALU.mult), [z_r, sn[j]], [a2])
                        B.G(lambda: nc.gpsimd.tensor_tensor(out=xi[j].ap, in0=a1.ap, in1=a2.ap, op=ALU.add), [a1, a2], [xi[j]])
                        last = LC - 1 if d == 0 else 0
                        col = d * TS + m
                        lk = (ch == (bnd_c - 1 if d == 0 else bnd_c + 1))
                        zl_r, zl_i = z_r.ap[:, last:last + 1], z_i.ap[:, last:last + 1]
                        B.V(lambda: nc.vector.tensor_scalar(out=c1.ap, in0=zl_i, scalar1=sT.ap[:, col:col + 1], scalar2=None, op0=ALU.mult), [z_i, sT], [c1])
                        B.V(lambda: nc.vector.scalar_tensor_tensor(out=z0[j][0].ap, in0=zl_r, scalar=cT.ap[:, col:col + 1], in1=c1.ap,
                                                                   op0=ALU.mult, op1=ALU.subtract), [z_r, cT, c1], [z0[j][0]])
                        B.V(lambda: nc.vector.tensor_scalar(out=c1.ap, in0=zl_r, scalar1=sT.ap[:, col:col + 1], scalar2=None, op0=ALU.mult), [z_r, sT], [c1])
                        B.V(lambda: nc.vector.scalar_tensor_tensor(out=z0[j][1].ap, in0=zl_i, scalar=cT.ap[:, col:col + 1], in1=c1.ap,
                                                                   op0=ALU.mult, op1=ALU.add), [z_i, cT, c1], [z0[j][1]])
                    py = B.psb[4 + (ch % 2)]
                    pyb = B.phalf[4 + (ch % 2)]
                    pairs = []
                    rd = [Cp]
                    for j, m in enumerate(ms):
                        pairs += [(Cp.ap[:, d, m, 0, :], xr[j].ap), (Cp.ap[:, d, m, 1, :], xi[j].ap)]
                        rd += [xr[j], xi[j]]
                    B.mm(py.ap[:, 0:LC], pairs, rd, pyb)
                    if d == 0:
                        B.A(lambda: nc.scalar.copy(out=yacc.ap[:, tk], in_=py.ap[:, 0:LC]), pyb, [yacc])
                    else:
                        B.V(lambda: nc.vector.tensor_tensor(out=yacc.ap[:, tk], in0=py.ap[:, 0:LC], in1=yacc.ap[:, tk], op=ALU.add), pyb + [yacc], [yacc])
            for ch in range(NLC):
                tk = slice(ch * LC, (ch + 1) * LC)
                ya, tq, yoc = wr[ch % 2], q1[ch % 2], xr[ch % min(2, nm)]
                B.V(lambda: nc.vector.scalar_tensor_tensor(out=ya.ap, in0=uT.ap[:, tk], scalar=dcol.ap[:, kt:kt + 1], in1=yacc.ap[:, tk],
                                                           op0=ALU.mult, op1=ALU.add), [uT, dcol, yacc], [ya])
                B.V(lambda: nc.vector.tensor_tensor(out=tq.ap, in0=ya.ap, in1=ya.ap, op=ALU.mult), [ya], [tq])
                B.V(lambda: nc.vector.tensor_scalar(out=tq.ap, in0=tq.ap, scalar1=0.044715, scalar2=1.0, op0=ALU.mult, op1=ALU.add), [tq], [tq])
                B.V(lambda: nc.vector.tensor_tensor(out=tq.ap, in0=tq.ap, in1=ya.ap, op=ALU.mult), [tq, ya], [tq])
                B.A(lambda: nc.scalar.activation(out=tq.ap, in_=tq.ap, func=AF.Sigmoid, scale=1.5957691216057308), [tq], [tq])
                B.V(lambda: nc.vector.tensor_tensor(out=yoc.ap, in0=tq.ap, in1=ya.ap, op=ALU.mult), [tq, ya], [yoc])
                P.dma("sp", S["s5T"][kt * 128:(kt + 1) * 128, tk], yoc.ap, reads=[yoc])
        P.barrier()


def attn_setup(B):
    c, nc, P, W, K = B.c, B.nc, B.P, B.W, B.K
    NH = 3 * c.HG
    B.fd = []
    with contextlib.ExitStack() as st:
        tbl = B.sb(st, "tbl", [c.NBK + 1, NH], F32)
        P.dma("sp", tbl.ap[0:c.NBK, :], W["rel_bias"], writes=[tbl])
        B.V(lambda: nc.vector.memset(tbl.ap[c.NBK:c.NBK + 1, :], NEG), [], [tbl])
        for p in range(3):
            Wp = (2 * c.PADT[p] + 1) * 128
            nrel = Wp + 127
            fd = B.scratch(f"fd{p}", [NH, nrel], F32)
            B.fd.append(fd)
            oh = B.sb(st, "oh", [c.NBK + 1, nrel], F32)
            fs = B.sb(st, "fs", [NH, nrel], F32)
            P.dma("sp", oh.ap, K[f"c_oh{p}"], writes=[oh])
            for i, c0 in enumerate(range(0, nrel, 512)):
                w_ = min(512, nrel - c0)
                bk = i % 2
                B.mm(B.psb[bk].ap[0:NH, 0:w_], [(tbl.ap, oh.ap[:, c0:c0 + w_])], [tbl, oh], B.phalf[bk])
                B.A(lambda: nc.scalar.copy(out=fs.ap[:, c0:c0 + w_], in_=B.psb[bk].ap[0:NH, 0:w_]), B.phalf[bk], [fs])
            P.dma("sp", fd, fs.ap, reads=[fs])
        P.barrier()


def phase_attn(B, l):
    c, nc, P, S, W = B.c, B.nc, B.P, B.S, B.W
    BND, NT, HG = c.BND, c.NT, c.HG
    NQ = BND // 128
    Wp = [(2 * pt + 1) * 128 for pt in c.PADT]
    off = [0, Wp[0], Wp[0] + Wp[1]]
    Wtot = sum(Wp)
    NKT = Wtot // 128
    scale = float(c.DH) ** -0.5
    with contextlib.ExitStack() as st:
        kT = [B.sb(st, "kT", [128, BND + 2 * c.PADT[p] * 128], BF16) for p in range(3)]
        Vt = [B.sb(st, "Vt", [128, NQ + 2 * c.PADT[p], 128], BF16) for p in range(3)]
        qT = [B.sb(st, "qT", [128, BND], BF16) for p in range(3)]
        MB = [B.sb(st, "MB", [128, Wp[p]], F32) for p in range(3)]
        MBr = [B.sb(st, "MBr", [128, Wp[p]], F32) for p in range(3)]
        Sp = [B.sb(st, "Sp", [128, Wtot], F32) for _ in range(2)]
        Pm = [B.sb(st, "Pm", [128, Wtot], BF16) for _ in range(2)]
        PT = [B.sb(st, "PT", [128, NKT, 128], BF16) for _ in range(2)]
        negm = B.sb(st, "negm", [128, 1], F32)
        rs = B.sb(st, "ars", [128, 128], F32)
        ast_ = [B.sb(st, "ast", [128, BND], BF16) for _ in range(2)]
        it = 0
        for j in range(HG):
            for p in range(3):
                h = p * HG + j
                nrel = Wp[p] + 127
                src = bass.AP(tensor=B.fd[p].tensor, offset=h * nrel, ap=[[1, 128], [1, Wp[p]]])
                P.dma("sp", MBr[p].ap, src, writes=[MBr[p]])
                B.V(lambda p=p: nc.vector.tensor_copy(out=MB[p].ap, in_=MBr[p].ap[:, ::-1]), [MBr[p]], [MB[p]])
            for hf in range(2):
                hs = hf * BND
                for p in range(3):
                    h = p * HG + j
                    pad = c.PADT[p] * 128
                    lo, hi = hs - pad, hs + BND + pad
                    slo, shi = max(lo, 0), min(hi, NT)
                    B.V(lambda p=p: nc.vector.memset(kT[p].ap, 0.0), [], [kT[p]])
                    B.G(lambda p=p: nc.gpsimd.memset(Vt[p].ap, 0.0), [], [Vt[p]])
                    P.dma("sp", kT[p].ap[:, slo - lo: shi - lo], S["kT"][h * 128:(h + 1) * 128, slo:shi], writes=[kT[p]])
                    P.dma("sp", Vt[p].ap[:, (slo - lo) // 128:(shi - lo) // 128, :],
                          S["v"][slo:shi, h * 128:(h + 1) * 128].rearrange("(t p) n -> p t n", p=128), writes=[Vt[p]])
                    P.dma("sp", qT[p].ap, S["qT"][h * 128:(h + 1) * 128, hs:hs + BND], writes=[qT[p]])
                asg = ast_[(j * 2 + hf) % 2]
                for i in range(NQ):
                    sp, pm, pt = Sp[i % 2], Pm[i % 2], PT[i % 2]
                    for p in range(3):
                        pad = c.PADT[p] * 128
                        for c0 in range(0, Wp[p], 512):
                            pw = min(512, Wp[p] - c0)
                            bk = it % 2
                            it += 1
                            B.mm(B.psb[bk].ap[:, 0:pw], [(qT[p].ap[:, i * 128:(i + 1) * 128], kT[p].ap[:, i * 128 + c0: i * 128 + c0 + pw])],
                                 [qT[p], kT[p]], B.phalf[bk])
                            B.V(lambda p=p, c0=c0, pw=pw, bk=bk: nc.vector.scalar_tensor_tensor(
                                out=sp.ap[:, off[p] + c0: off[p] + c0 + pw], in0=B.psb[bk].ap[:, 0:pw], scalar=scale,
                                in1=MB[p].ap[:, c0:c0 + pw], op0=ALU.mult, op1=ALU.add), B.phalf[bk] + [MB[p]], [sp])
                        base = hs - pad + i * 128
                        if base < 0:
                            n0 = min(Wp[p], -base)
                            B.V(lambda p=p, n0=n0: nc.vector.tensor_scalar(out=sp.ap[:, off[p]:off[p] + n0], in0=sp.ap[:, off[p]:off[p] + n0],
                                                                           scalar1=NEG, scalar2=None, op0=ALU.add), [sp], [sp])
                        if base + Wp[p] > NT:
                            n0 = max(0, NT - base)
                            B.V(lambda p=p, n0=n0: nc.vector.tensor_scalar(out=sp.ap[:, off[p] + n0:off[p] + Wp[p]], in0=sp.ap[:, off[p] + n0:off[p] + Wp[p]],
                                                                           scalar1=NEG, scalar2=None, op0=ALU.add), [sp], [sp])
                        if hf == 0 and base + Wp[p] > BND:
                            n0 = max(0, BND - base)
                            B.V(lambda p=p, n0=n0: nc.vector.tensor_scalar(out=sp.ap[:, off[p] + n0:off[p] + Wp[p]], in0=sp.ap[:, off[p] + n0:off[p] + Wp[p]],
                                                                           scalar1=B.nlinkt.ap[:, 0:1], scalar2=None, op0=ALU.add), [sp, B.nlinkt], [sp])
                        if hf == 1 and base < BND:
                            n0 = min(Wp[p], BND - base)
                            B.V(lambda p=p, n0=n0: nc.vector.tensor_scalar(out=sp.ap[:, off[p]:off[p] + n0], in0=sp.ap[:, off[p]:off[p] + n0],
                                                                           scalar1=B.nlinkt.ap[:, 0:1], scalar2=None, op0=ALU.add), [sp, B.nlinkt], [sp])
                    B.V(lambda: nc.vector.tensor_reduce(out=negm.ap, in_=sp.ap, axis=AX.X, op=ALU.max, negate=True), [sp], [negm])
                    B.A(lambda: nc.scalar.activation(out=pm.ap, in_=sp.ap, func=AF.Exp, bias=negm.ap[:, 0:1]), [sp, negm], [pm])
                    for r0 in range(0, NKT, 16):
                        rn = min(16, NKT - r0)

                        def tr(r0=r0, rn=rn):
                            ins = None
                            for t_ in range(rn):
                                ins = nc.tensor.transpose(B.pT.ap[:, t_ * 128:(t_ + 1) * 128], pm.ap[:, (r0 + t_) * 128:(r0 + t_ + 1) * 128], B.identb.ap)
                            return ins
                        B.T(tr, [pm, B.identb], [B.pT])
                        if (r0 // 16) % 2 == 0:
                            B.A(lambda r0=r0, rn=rn: nc.scalar.copy(out=pt.ap[:, r0:r0 + rn, :].rearrange("p t q -> p (t q)"), in_=B.pT.ap[:, 0:rn * 128]),
                                [B.pT], [pt])
                        else:
                            B.V(lambda r0=r0, rn=rn: nc.vector.tensor_copy(out=pt.ap[:, r0:r0 + rn, :].rearrange("p t q -> p (t q)"), in_=B.pT.ap[:, 0:rn * 128]),
                                [B.pT], [pt])
                    bo, bs_ = (2, 3) if i % 2 == 0 else (4, 5)
                    pv, ps_ = [], []
                    for p in range(3):
                        for kt in range(Wp[p] // 128):
                            tile_ = off[p] // 128 + kt
                            pv.append((Vt[p].ap[:, i + kt, :], pt.ap[:, tile_, :]))
                            ps_.append((B.onesb.ap, pt.ap[:, tile_, :]))
                    B.mm(B.psb[bo].ap[:, 0:128], pv, [pt] + Vt, B.phalf[bo])
                    B.mm(B.psb[bs_].ap[:, 0:128], ps_, [pt, B.onesb], B.phalf[bs_])
                    B.V(lambda bs_=bs_: nc.vector.reciprocal(out=rs.ap, in_=B.psb[bs_].ap[:, 0:128]), B.phalf[bs_], [rs])
                    B.V(lambda bo=bo: nc.vector.tensor_tensor(out=asg.ap[:, i * 128:(i + 1) * 128], in0=B.psb[bo].ap[:, 0:128], in1=rs.ap, op=ALU.mult),
                        B.phalf[bo] + [rs], [asg])
                P.dma("sp", S["attT"][j * 128:(j + 1) * 128, hs:hs + BND], asg.ap, reads=[asg])
        P.barrier()


def phase_merge(B, l, xsrc, xkey):
    c, nc, P, S, W = B.c, B.nc, B.P, B.S, B.W
    TG, D, KC = c.FTG, c.D, c.KC
    TI, K5, KA = c.INNER // 128, c.W5 // 128, c.HG * c.DH // 128
    with contextlib.ExitStack() as st:
        B.wt_setup(st, 16, 128, nbf=6)
        act = B.sb(st, "act", [128, max(TI, K5, KA), TG], BF16)
        mT = B.sb(st, "mT", [128, KC, TG], BF16)
        gts = [B.sb(st, "gt", [128, TG], BF16) for _ in range(2)]
        sgs = [B.sb(st, "sg", [128, TG], F32) for _ in range(2)]
        tmp = [B.sb(st, "mtmp", [128, 512], F32) for _ in range(2)]
        tmp2 = [B.sb(st, "mtmp2", [128, 512], F32) for _ in range(2)]
        xts = [B.sb(st, "mxt", [128, 128], F32) for _ in range(4)]
        cnt = {"g": 0, "t": 0, "x": 0}
        for tg in range(c.NT // TG):
            t0 = tg * TG

            def gate_tile(b, col_abs):
                gt, sg = gts[cnt["g"] % 2], sgs[cnt["g"] % 2]
                cnt["g"] += 1
                P.dma("sp", gt.ap, S["gT"][b * D + col_abs: b * D + col_abs + 128, t0:t0 + TG], writes=[gt])
                B.A(lambda: nc.scalar.activation(out=sg.ap, in_=gt.ap, func=AF.Sigmoid), [gt], [sg])
                return sg

            def load_act(name, kn):
                P.dma("sp", act.ap[:, 0:kn, :], S[name][:, t0:t0 + TG].rearrange("(t p) n -> p t n", p=128), writes=[act])
            load_act("ynT", TI)
            cur = {}

            def evac_a(ps, col_abs, nb, tok0, tw, pb):
                if tok0 == 0:
                    cur["sg"] = gate_tile(0, col_abs)
                sg = cur["sg"]
                B.V(lambda: nc.vector.tensor_tensor(out=mT.ap[:, col_abs // 128, tok0:tok0 + tw], in0=ps, in1=sg.ap[:, tok0:tok0 + tw], op=ALU.mult),
                    pb + [sg], [mT])
            B.dense_feat(act, TI, TG, W["ssm_w_out"][l], 0, D, evac_a)
            load_act("s5T", K5)
            ldp = lambda db_: (B.load_w(W["s5_w_glu"][l], 0, K5, db_ * 128, 128), B.load_w(W["s5_w_glu"][l], 0, K5, D + db_ * 128, 128))
            nxt = ldp(0)
            for db in range(KC):
                wv, wg = nxt
                if db + 1 < KC:
                    nxt = ldp(db + 1)
                sg = gate_tile(1, db * 128)
                for tt in range(-(-TG // 512)):
                    tw = min(512, TG - tt * 512)
                    tks = slice(tt * 512, tt * 512 + tw)
                    ba, bb_ = (0, 1) if cnt["t"] % 2 == 0 else (2, 3)
                    tp, tp2 = tmp[cnt["t"] % 2], tmp2[cnt["t"] % 2]
                    cnt["t"] += 1
                    B.mm(B.psb[ba].ap[:, 0:tw], [(wv.ap[:, k, 0:128], act.ap[:, k, tks]) for k in range(K5)], [wv, act], B.phalf[ba])
                    B.mm(B.psb[bb_].ap[:, 0:tw], [(wg.ap[:, k, 0:128], act.ap[:, k, tks]) for k in range(K5)], [wg, act], B.phalf[bb_])
                    B.A(lambda: nc.scalar.activation(out=tp.ap[:, 0:tw], in_=B.psb[bb_].ap[:, 0:tw], func=AF.Sigmoid), B.phalf[bb_], [tp])
                    B.V(lambda: nc.vector.tensor_tensor(out=tp2.ap[:, 0:tw], in0=B.psb[ba].ap[:, 0:tw], in1=tp.ap[:, 0:tw], op=ALU.mult), B.phalf[ba] + [tp], [tp2])
                    B.V(lambda: nc.vector.tensor_tensor(out=tp2.ap[:, 0:tw], in0=tp2.ap[:, 0:tw], in1=sg.ap[:, tks], op=ALU.mult), [tp2, sg], [tp2])
                    B.V(lambda: nc.vector.tensor_tensor(out=mT.ap[:, db, tks], in0=tp2.ap[:, 0:tw], in1=mT.ap[:, db, tks], op=ALU.add), [tp2, mT], [mT])
            load_act("attT", KA)

            def evac_c(ps, col_abs, nb, tok0, tw, pb):
                if tok0 == 0:
                    cur["sg"] = gate_tile(2, col_abs)
                sg = cur["sg"]
                tp = tmp[cnt["t"] % 2]
                cnt["t"] += 1
                B.V(lambda: nc.vector.tensor_tensor(out=tp.ap[:, 0:tw], in0=ps, in1=sg.ap[:, tok0:tok0 + tw], op=ALU.mult), pb + [sg], [tp])
                B.V(lambda: nc.vector.tensor_tensor(out=mT.ap[:, col_abs // 128, tok0:tok0 + tw], in0=tp.ap[:, 0:tw],
                                                    in1=mT.ap[:, col_abs // 128, tok0:tok0 + tw], op=ALU.add), [tp, mT], [mT])
            B.dense_feat(act, KA, TG, W["att_w_out"][l], 0, D, evac_c)

            def evac_o(ps, tok0, rows, col_abs, cw, pb):
                xt = xts[cnt["x"] % 4]
                cnt["x"] += 1
                P.dma("sp", xt.ap[0:rows, 0:cw], xsrc[t0 + tok0:t0 + tok0 + rows, col_abs:col_abs + cw], writes=[xt])
                B.V(lambda: nc.vector.tensor_tensor(out=xt.ap[0:rows, 0:cw], in0=ps, in1=xt.ap[0:rows, 0:cw], op=ALU.add), pb + [xt], [xt])
                P.dma("sp", S["xmid"][t0 + tok0:t0 + tok0 + rows, col_abs:col_abs + cw], xt.ap[0:rows, 0:cw], reads=[xt])
            B.dense_tok(mT, KC, TG, W["w_o"][l], 0, D, 128, evac_o)
        P.barrier()


def phase_ffn(B, l):
    c, nc, P, S, W = B.c, B.nc, B.P, B.S, B.W
    TG, D, KC, DFF = c.FTG, c.D, c.KC, c.DFF
    KF = DFF // 128
    with contextlib.ExitStack() as st:
        gain = B.sb(st, "gain2", [128, D], F32)
        B.bcast_load(gain, W["norm_ffn"][l:l + 1, :], D)
        h2T = B.sb(st, "h2T", [128, KC, TG + 2], BF16)
        gvT = B.sb(st, "gvT", [128, KF, TG], BF16)
        cwt = B.sb(st, "fcw", [128, 2 * KF, 4], F32)
        P.dma("sp", cwt.ap, W["ffn_cw"][l], writes=[cwt])
        ups = [B.sb(st, "ups", [128, TG + 2], F32) for _ in range(2)]
        accg = B.sb(st, "accg", [128, TG], F32)
        accv = B.sb(st, "accv", [128, TG], F32)
        xts = [B.sb(st, "fxt", [128, 128], F32) for _ in range(4)]
        cnt = {"x": 0, "b": 0}
        for tg in range(c.NT // TG):
            t0 = tg * TG
            with contextlib.ExitStack() as st1:
                B.norm_setup(st1, nbuf=1)
                B.norm_T(st1, S["xmid"], "xmid", gain, t0, TG, h2T, col0=1)
                for (tok, col) in ((t0 - 1, 0), (t0 + TG, TG + 1)):
                    if tok < 0 or tok >= c.NT:
                        B.V(lambda col=col: nc.vector.memset(h2T.ap[:, :, col:col + 1], 0.0), [], [h2T])
                    else:
                        cross = (tok == c.BND - 1 and col == 0) or (tok == c.BND and col == TG + 1)
                        B.norm_T(st1, S["xmid"], "xmid", gain, tok, 1, h2T, col0=col, scale_tile=(B.linkt if cross else None))
                P.barrier()
            st2 = contextlib.ExitStack()
            B.wt_setup(st2, 16, 128, nbf=6)
            loads = [(jb, which, cbase) for jb in range(KF) for which, cbase in enumerate((jb * 128, DFF + jb * 128))]
            q = [B.load_w(W["w_up"][l], 0, KC, ld[2], 128) for ld in loads[:2]]
            for li, (jb, which, cbase) in enumerate(loads):
                accs = (accg, accv)
                if True:
                    wb = q.pop(0)
                    if li + 2 < len(loads):
                        q.append(B.load_w(W["w_up"][l], 0, KC, loads[li + 2][2], 128))
                    up = ups[which]
                    ti_ = which * KF + jb
                    npc = -(-(TG + 2) // 512)
                    psz = -(-(TG + 2) // npc)
                    psz += psz % 2
                    pieces = [(q0, min(psz, TG + 2 - q0)) for q0 in range(0, TG + 2, psz)]
                    for (cs_, n_) in pieces:
                        bk = cnt["b"] % 6
                        cnt["b"] += 1
                        B.mm(B.psb[bk].ap[:, 0:n_], [(wb.ap[:, k, 0:128], h2T.ap[:, k, cs_:cs_ + n_]) for k in range(KC)], [wb, h2T], B.phalf[bk])
                        if cnt["b"] % 2 == 0:
                            B.A(lambda bk=bk, cs_=cs_, n_=n_: nc.scalar.copy(out=up.ap[:, cs_:cs_ + n_], in_=B.psb[bk].ap[:, 0:n_]), B.phalf[bk], [up])
                        else:
                            B.V(lambda bk=bk, cs_=cs_, n_=n_: nc.vector.tensor_copy(out=up.ap[:, cs_:cs_ + n_], in_=B.psb[bk].ap[:, 0:n_]), B.phalf[bk], [up])
                    acc = accs[which]
                    B.V(lambda: nc.vector.tensor_scalar(out=acc.ap, in0=up.ap[:, 0:TG], scalar1=cwt.ap[:, ti_, 0:1], scalar2=cwt.ap[:, ti_, 3:4],
                                                        op0=ALU.mult, op1=ALU.add), [up, cwt], [acc])
                    for k in (1, 2):
                        B.V(lambda k=k: nc.vector.scalar_tensor_tensor(out=acc.ap, in0=up.ap[:, k:k + TG], scalar=cwt.ap[:, ti_, k:k + 1], in1=acc.ap,
                                                                       op0=ALU.mult, op1=ALU.add), [up, cwt, acc], [acc])
                if which == 1:
                    B.A(lambda: nc.scalar.activation(out=accg.ap, in_=accg.ap, func=AF.Silu), [accg], [accg])
                    B.V(lambda: nc.vector.tensor_tensor(out=gvT.ap[:, jb, :], in0=accg.ap, in1=accv.ap, op=ALU.mult), [accg, accv], [gvT])

            def evac_d(ps, tok0, rows, col_abs, cw, pb):
                xt = xts[cnt["x"] % 4]
                cnt["x"] += 1
                P.dma("sp", xt.ap[0:rows, 0:cw], S["xmid"][t0 + tok0:t0 + tok0 + rows, col_abs:col_abs + cw], writes=[xt])
                B.V(lambda: nc.vector.tensor_tensor(out=xt.ap[0:rows, 0:cw], in0=ps, in1=xt.ap[0:rows, 0:cw], op=ALU.add), pb + [xt], [xt])
                P.dma("sp", S["xres"][t0 + tok0:t0 + tok0 + rows, col_abs:col_abs + cw], xt.ap[0:rows, 0:cw], reads=[xt])
            B.dense_tok(gvT, KF, TG, W["w_down"][l], 0, D, 128, evac_d)
            P.barrier()
            st2.close()
        P.barrier()


def phase_final(B, y_out):
    c, nc, P, S, W = B.c, B.nc, B.P, B.S, B.W
    with contextlib.ExitStack() as st:
        gain = B.sb(st, "gainf", [128, c.D], F32)
        B.bcast_load(gain, W["final_norm"][0:1, :], c.D)
        B.norm_setup(st)
        nt = B._norm_tiles
        outs = [B.sb(st, "fo", [128, c.D], F32) for _ in range(2)]
        for ti in range(c.NT // 128):
            xt, o = nt["x"][ti % 2], outs[ti % 2]
            sq, ss, rt, rs = nt["sq"], nt["ss"], nt["rt"], nt["rs"]
            P.dma("sp", xt.ap, S["xres"][ti * 128:(ti + 1) * 128, :], writes=[xt])
            B.A(lambda: nc.scalar.activation(out=sq.ap, in_=xt.ap, func=AF.Square), [xt], [sq])
            B.V(lambda: nc.vector.reduce_sum(out=ss.ap, in_=sq.ap, axis=AX.X), [sq], [ss])
            B.A(lambda: nc.scalar.activation(out=rt.ap, in_=ss.ap, func=AF.Sqrt, scale=1.0 / c.D, bias=nt["eps"].ap[:, 0:1]), [ss, nt["eps"]], [rt])
            B.V(lambda: nc.vector.reciprocal(out=rs.ap, in_=rt.ap), [rt], [rs])
            B.V(lambda: nc.vector.scalar_tensor_tensor(out=o.ap, in0=xt.ap, scalar=rs.ap[:, 0:1], in1=gain.ap, op0=ALU.mult, op1=ALU.mult),
                [xt, rs, gain], [o])
            P.dma("sp", y_out[ti * 128:(ti + 1) * 128, :], o.ap, reads=[o])
        P.barrier()


_CACHE = {}


def kernel(**inputs):
    c = FULL
    c.TS = c.G5 * c.P5 // 128
    xp = np.asarray(inputs["x_prompt"], dtype=np.float32)
    xs = np.asarray(inputs["x_sample"], dtype=np.float32)
    assert xp.shape == (2, c.BND, c.D) and xs.shape == (1, c.NT, c.D)
    wl = layout_weights(c, inputs)
    consts = host_consts(c)
    maps = [core_inputs(c, wl, consts, xp.reshape(c.NT, c.D), 0),
            core_inputs(c, wl, consts, xs.reshape(c.NT, c.D), 1)]
    if "nc" not in _CACHE:
        _CACHE["nc"] = build(c)[0]
    res = run_bass_kernel_spmd(_CACHE["nc"], maps, core_ids=[0, 1])
    y_prompt = np.asarray(res.results[0]["y"], dtype=np.float32).reshape(2, c.BND, c.D)
    y_sample = np.asarray(res.results[1]["y"], dtype=np.float32).reshape(1, c.NT, c.D)
    return (y_prompt, y_sample)
```
